# Optimizing a Trainium2 kernel written in Bass

```python
import math
import jax
import jax.numpy as jnp
from jax import lax
import numpy as np

D_MODEL = 1024
BATCH = 4
SEQ = 4096
DEPTH = 4

N_MIXERS = 2
N_A_LAYERS = (DEPTH + N_MIXERS - 1) // N_MIXERS
N_B_LAYERS = DEPTH // N_MIXERS
A_HEADS = 8
A_HEAD_DIM = D_MODEL // A_HEADS
A_Q_RANK = 384
A_KV_RANK = 256
IDX_HEADS = 8
IDX_DIM = 64
TOPK_MAX = 256
Q_BLOCK = 128
A_IN_COLS = A_Q_RANK + A_KV_RANK + IDX_DIM + IDX_HEADS
B_HEADS = 8
B_KEY_DIM = D_MODEL // B_HEADS
B_VAL_DIM = D_MODEL // B_HEADS
B_CHUNK = 64
D_FF = 2816
CONV_WIDTH = 3
DN_ALPHA = (2 * DEPTH) ** 0.25
DN_BETA = (8 * DEPTH) ** -0.25
LN_EPS = 1e-5
RMS_EPS = 1e-6

kernel_name = 'dsa_hgrn2_convffn_deepnorm_hybrid'


def layer_norm(x, g, b):
    xf = x.astype(jnp.float32)
    mu = jnp.mean(xf, axis=-1, keepdims=True)
    var = jnp.mean(jnp.square(xf - mu), axis=-1, keepdims=True)
    return ((xf - mu) * lax.rsqrt(var + LN_EPS) * g + b).astype(x.dtype)


def rms_norm(x, g):
    xf = x.astype(jnp.float32)
    y = xf * lax.rsqrt(jnp.mean(jnp.square(xf), axis=-1, keepdims=True) + RMS_EPS)
    return (y * g).astype(x.dtype)


def dsa_mixer(x, w_in, g_q, g_kv, w_q_lat, w_q_idx, g_kidx, b_kidx, w_uv, w_out):
    bsz, seq, _ = x.shape
    top_k = min(TOPK_MAX, seq // 4)
    n_blk = seq // Q_BLOCK
    proj = x @ w_in
    o1 = A_Q_RANK
    o2 = o1 + A_KV_RANK
    o3 = o2 + IDX_DIM
    c_q = rms_norm(proj[..., :o1], g_q)
    c_kv = rms_norm(proj[..., o1:o2], g_kv)
    k_idx = layer_norm(proj[..., o2:o3], g_kidx, b_kidx)
    w_idx = proj[..., o3:] * (IDX_HEADS ** -0.5 * IDX_DIM ** -0.5)
    q_lat = (c_q @ w_q_lat).reshape(bsz, seq, A_HEADS, A_KV_RANK)
    q_idx = (c_q @ w_q_idx).reshape(bsz, seq, IDX_HEADS, IDX_DIM)

    def blocks(t):
        return jnp.swapaxes(t.reshape((bsz, n_blk, Q_BLOCK) + t.shape[2:]), 0, 1)

    key_pos = jnp.arange(seq, dtype=jnp.int32)
    scale = A_KV_RANK ** -0.5

    def attend_block(args):
        q_b, qi_b, wi_b, start = args
        q_pos = start + jnp.arange(Q_BLOCK, dtype=jnp.int32)
        rel = jax.nn.relu(jnp.einsum('bqhd,bsd->bqhs', qi_b, k_idx))
        score = jnp.einsum('bqh,bqhs->bqs', wi_b, rel).astype(jnp.float32)
        score = jnp.where(key_pos[None, None, :] <= q_pos[None, :, None], score, -jnp.inf)
        _, idx = lax.top_k(score, top_k)
        valid = idx <= q_pos[None, :, None]
        kv_sel = jax.vmap(lambda c, i: c[i])(c_kv, idx)
        logits = jnp.einsum('bqhr,bqkr->bhqk', q_b, kv_sel).astype(jnp.float32) * scale
        logits = jnp.where(valid[:, None], logits, -jnp.inf)
        p = jax.nn.softmax(logits, axis=-1).astype(kv_sel.dtype)
        return jnp.einsum('bhqk,bqkr->bqhr', p, kv_sel)

    starts = jnp.arange(n_blk, dtype=jnp.int32) * Q_BLOCK
    o_lat = lax.map(attend_block, (blocks(q_lat), blocks(q_idx), blocks(w_idx), starts))
    o_lat = jnp.swapaxes(o_lat, 0, 1).reshape(bsz, seq, A_HEADS, A_KV_RANK)
    o = jnp.einsum('bshr,hrd->bshd', o_lat, w_uv).reshape(bsz, seq, A_HEADS * A_HEAD_DIM)
    return o @ w_out


def hgrn2_mixer(x, w_in, lb, g_o, w_out):
    bsz, seq, _ = x.shape
    n_chunk = seq // B_CHUNK
    f32 = jnp.float32
    proj = x @ w_in
    q_raw, f_raw, i_in, g_raw = jnp.split(proj, 4, axis=-1)
    f_raw = f_raw.astype(f32)
    forget = lb + (1.0 - lb) * jax.nn.sigmoid(f_raw)
    log_f = jnp.log(forget)
    key = (1.0 - lb) * jax.nn.sigmoid(-f_raw)
    q = jax.nn.silu(q_raw.astype(f32))
    v = i_in.astype(f32)

    def chunks(t):
        return t.reshape(bsz, n_chunk, B_CHUNK, B_HEADS, -1).transpose(1, 0, 3, 2, 4)

    causal = jnp.tril(jnp.ones((B_CHUNK, B_CHUNK), dtype=bool))[:, :, None]

    def step(state, inp):
        qc, kc, vc, gc = inp
        b = jnp.cumsum(gc, axis=2)
        diff = jnp.where(causal, b[:, :, :, None, :] - b[:, :, None, :, :], -jnp.inf)
        scores = jnp.einsum('bhtd,bhsd,bhtsd->bhts', qc, kc, jnp.exp(diff))
        o = (jnp.einsum('bhts,bhse->bhte', scores, vc)
             + jnp.einsum('bhtd,bhde->bhte', qc * jnp.exp(b), state))
        b_last = b[:, :, -1:, :]
        state = (jnp.exp(b_last[:, :, 0, :, None]) * state
                 + jnp.einsum('bhsd,bhse->bhde', kc * jnp.exp(b_last - b), vc))
        return state, o

    s0 = jnp.zeros((bsz, B_HEADS, B_KEY_DIM, B_VAL_DIM), f32)
    _, o = lax.scan(step, s0, (chunks(q), chunks(key), chunks(v), chunks(log_f)))
    o = o.transpose(1, 0, 3, 2, 4).reshape(bsz, seq, B_HEADS, B_VAL_DIM)
    gate = jax.nn.sigmoid(g_raw.astype(f32)).reshape(bsz, seq, B_HEADS, B_VAL_DIM)
    o = rms_norm(o * gate, g_o.reshape(B_HEADS, B_VAL_DIM)).astype(x.dtype).reshape(bsz, seq, D_MODEL)
    return o @ w_out


def conv_ffn(x, w_up, conv_w, conv_b, w_down):
    h = x @ w_up
    h = lax.conv_general_dilated(h, conv_w.astype(h.dtype), window_strides=(1,),
                                 padding=[(CONV_WIDTH - 1, 0)],
                                 dimension_numbers=('NWC', 'WIO', 'NWC'),
                                 feature_group_count=2 * D_FF) + conv_b
    a, u = jnp.split(h, 2, axis=-1)
    return (jax.nn.silu(a) * u) @ w_down


def setup_inputs(seed: int = 0) -> dict:
    key = jax.random.key(seed)
    ks = jax.random.split(key, 22)
    f32 = jnp.float32
    nrm = lambda k, shape, s: jax.random.normal(k, shape, f32) * s
    gain = lambda k, shape: 1.0 + 0.01 * jax.random.normal(k, shape, f32)
    beta = DN_BETA
    b_col_scale = jnp.concatenate([jnp.ones((2 * D_MODEL,), f32), jnp.full((D_MODEL,), beta, f32),
                                   jnp.ones((D_MODEL,), f32)])
    return {
        'x': nrm(ks[0], (BATCH, SEQ, D_MODEL), 1.0),
        'a_w_in': nrm(ks[1], (N_A_LAYERS, D_MODEL, A_IN_COLS), D_MODEL ** -0.5),
        'a_g_q': gain(ks[2], (N_A_LAYERS, A_Q_RANK)),
        'a_g_kv': gain(ks[3], (N_A_LAYERS, A_KV_RANK)),
        'a_w_q_lat': nrm(ks[4], (N_A_LAYERS, A_Q_RANK, A_HEADS * A_KV_RANK), A_Q_RANK ** -0.5),
        'a_w_q_idx': nrm(ks[5], (N_A_LAYERS, A_Q_RANK, IDX_HEADS * IDX_DIM), A_Q_RANK ** -0.5),
        'a_g_kidx': gain(ks[6], (N_A_LAYERS, IDX_DIM)),
        'a_b_kidx': nrm(ks[7], (N_A_LAYERS, IDX_DIM), 0.01),
        'a_w_uv': nrm(ks[8], (N_A_LAYERS, A_HEADS, A_KV_RANK, A_HEAD_DIM), A_KV_RANK ** -0.5 * beta),
        'a_w_out': nrm(ks[9], (N_A_LAYERS, A_HEADS * A_HEAD_DIM, D_MODEL), (A_HEADS * A_HEAD_DIM) ** -0.5 * beta),
        'b_w_in': nrm(ks[10], (N_B_LAYERS, D_MODEL, 4 * D_MODEL), D_MODEL ** -0.5) * b_col_scale,
        'b_lb_logits': nrm(ks[11], (DEPTH, B_HEADS * B_KEY_DIM), 0.1),
        'b_g_o': gain(ks[12], (N_B_LAYERS, D_MODEL)),
        'b_w_out': nrm(ks[13], (N_B_LAYERS, D_MODEL, D_MODEL), D_MODEL ** -0.5 * beta),
        'ln1_g': gain(ks[14], (DEPTH, D_MODEL)),
        'ln1_b': nrm(ks[15], (DEPTH, D_MODEL), 0.01),
        'f_w_up': nrm(ks[16], (DEPTH, D_MODEL, 2 * D_FF), D_MODEL ** -0.5 * beta),
        'f_conv_w': nrm(ks[17], (DEPTH, CONV_WIDTH, 1, 2 * D_FF), CONV_WIDTH ** -0.5),
        'f_conv_b': nrm(ks[18], (DEPTH, 2 * D_FF), 0.01),
        'f_w_down': nrm(ks[19], (DEPTH, D_FF, D_MODEL), D_FF ** -0.5 * beta),
        'ln2_g': gain(ks[20], (DEPTH, D_MODEL)),
        'ln2_b': nrm(ks[21], (DEPTH, D_MODEL), 0.01),
    }


def reference(x, a_w_in, a_g_q, a_g_kv, a_w_q_lat, a_w_q_idx, a_g_kidx, a_b_kidx, a_w_uv, a_w_out,
              b_w_in, b_lb_logits, b_g_o, b_w_out,
              ln1_g, ln1_b, f_w_up, f_conv_w, f_conv_b, f_w_down, ln2_g, ln2_b):
    c = jnp.cumsum(jax.nn.softmax(b_lb_logits.astype(jnp.float32), axis=0), axis=0)
    lower_bounds = c - c[0:1]
    for layer in range(DEPTH):
        j = layer // N_MIXERS
        if layer % N_MIXERS == 0:
            h = dsa_mixer(x, a_w_in[j], a_g_q[j], a_g_kv[j], a_w_q_lat[j], a_w_q_idx[j],
                          a_g_kidx[j], a_b_kidx[j], a_w_uv[j], a_w_out[j])
        else:
            h = hgrn2_mixer(x, b_w_in[j], lower_bounds[layer], b_g_o[j], b_w_out[j])
        x = layer_norm(DN_ALPHA * x + h, ln1_g[layer], ln1_b[layer])
        h = conv_ffn(x, f_w_up[layer], f_conv_w[layer], f_conv_b[layer], f_w_down[layer])
        x = layer_norm(DN_ALPHA * x + h, ln2_g[layer], ln2_b[layer])
    return x
```

```python
from contextlib import ExitStack
import numpy as np
import ml_dtypes
import concourse.bass as bass
import concourse.mybir as mybir
from concourse.bass_utils import run_bass_kernel_spmd

F32 = mybir.dt.float32
BF16 = mybir.dt.bfloat16
ALU = mybir.AluOpType
AF = mybir.ActivationFunctionType
AX = mybir.AxisListType

D = 1024
DFF = 2816
NFC = DFF // 128
DEPTH = 4
ALPHA = (2 * DEPTH) ** 0.25
LN_EPS = 1e-5
RMS_EPS = 1e-6
QR, KVR, IDXD, IDXH, AH = 384, 256, 64, 8, 8
AIN = QR + KVR + IDXD + IDXH

ENGS = ['tensor', 'vector', 'scalar', 'gpsimd', 'sync']
DMAQ = ['sync', 'scalar', 'gpsimd']


class Res:
    __slots__ = ('w', 'r', 'name', 'excl')

    def __init__(self, name='', excl=False):
        self.w = None
        self.r = []
        self.name = name
        self.excl = excl


class Prog:
    def __init__(self, nc, stack, n_dma_slots=4):
        self.nc = nc
        self.sem = {e: stack.enter_context(nc.semaphore(f"s_{e}")) for e in ENGS}
        self.cnt = {e: 0 for e in ENGS}
        self.nslots = n_dma_slots
        self.dsem, self.dcnt, self.dnext = {}, {}, {}
        for qn in DMAQ:
            for s in range(n_dma_slots):
                self.dsem[(qn, s)] = stack.enter_context(nc.semaphore(f"d_{qn}_{s}"))
                self.dcnt[(qn, s)] = 0
            self.dnext[qn] = 0
        self.seen = {e: {} for e in ENGS}
        self.ninst = 0

    def _wait(self, eng, prod, val):
        if self.seen[eng].get(prod, 0) >= val:
            return
        sem = self.sem[prod] if isinstance(prod, str) else self.dsem[prod]
        getattr(self.nc, eng).wait_ge(sem, val)
        self.seen[eng][prod] = val

    def _deps(self, eng, reads, writes):
        deps = []
        for b in reads:
            if b.w is not None:
                deps.append(b.w)
            if b.excl:
                deps.extend(b.r)
        for b in writes:
            if b.w is not None:
                if not (b.excl and eng == 'tensor' and b.w[0] == 'tensor'):
                    deps.append(b.w)
            deps.extend(b.r)
        for (p, v) in deps:
            self._wait(eng, p, v)

    def _commit(self, me, reads, writes):
        for b in reads:
            b.r.append(me)
        for b in writes:
            b.w = me
            b.r = []

    def op(self, eng, fn, reads=(), writes=(), inc=True):
        self._deps(eng, reads, writes)
        if inc:
            self.cnt[eng] += 1
            fn(getattr(self.nc, eng)).then_inc(self.sem[eng], 1)
            self._commit((eng, self.cnt[eng]), reads, writes)
        else:
            fn(getattr(self.nc, eng))
            self._commit((eng, self.cnt[eng] + 1), reads, writes)
        self.ninst += 1

    def dma(self, qn, out, in_, reads=(), writes=(), **kw):
        s = self.dnext[qn]
        self.dnext[qn] = (s + 1) % self.nslots
        key = (qn, s)
        prev = 16 * self.dcnt[key]
        if prev:
            self._wait(qn, key, prev)
        self._deps(qn, reads, writes)
        self.dcnt[key] += 1
        getattr(self.nc, qn).dma_start(out=out, in_=in_, **kw).then_inc(self.dsem[key], 16)
        self._commit((key, 16 * self.dcnt[key]), reads, writes)
        self.ninst += 1

    def barrier(self):
        for e in ENGS:
            for p in ENGS:
                if p != e and self.cnt[p]:
                    self._wait(e, p, self.cnt[p])
            for k, c in self.dcnt.items():
                if c:
                    self._wait(e, k, 16 * c)


class Rot:
    def __init__(self, tiles):
        self.t = [(t, Res()) for t in tiles]
        self.i = 0

    def next(self):
        r = self.t[self.i]
        self.i = (self.i + 1) % len(self.t)
        return r


def build_program(S, layers=(0, 1, 2, 3), first=True):
    layers = tuple(layers)
    NB = S // 128
    TOPK = min(256, S // 4)
    nc = bass.Bass("TRN2", target_bir_lowering=False)

    def din(name, shape, dt=F32):
        return nc.dram_tensor(name, list(shape), dt, kind="ExternalInput").ap()

    x_in = din("x", [S, D])
    Wn = {}
    for name, shape in WSHAPES.items():
        Wn[name] = din(name, shape)
    ident_in = din("ident", [128, 128], BF16)
    hconst_in = din("hconst", [128, 4, 128])
    hcind_in = din("hcind", [128, 4])
    maskall_in = din("maskall", [128, 8, 128])
    rep_in = din("rep", [8, 128], BF16)
    cm_in = din("cm", [128, 128])
    pw_in = din("pw", [128, 32])
    y_out = nc.dram_tensor("y", [S, D], F32, kind="ExternalOutput").ap()
    X32 = nc.dram_tensor("X32", [S, D], F32).ap()
    XTH = nc.dram_tensor("XTH", [NB + 1, 128, 8, 130], BF16).ap()
    rX32 = [Res() for _ in range(NB)]
    rXTH = [Res() for _ in range(NB + 1)]
    rXTHh = [Res() for _ in range(NB + 1)]
    rY = Res()
    WUb = nc.dram_tensor("WUb", [4, NFC, 128, 8, 256], BF16).ap()
    rWb = [Res() for _ in range(4)]

    with ExitStack() as top:
        P = Prog(nc, top)
        uid = [0]

        def sb(st, name, shape, dt):
            uid[0] += 1
            return st.enter_context(nc.sbuf_tensor(f"t{uid[0]}_{name}", list(shape), dt))
        pbank = [(top.enter_context(nc.psum_tensor(f"pb{i}", [128, 512], F32)), Res(excl=True)) for i in range(6)]
        ptb = [(top.enter_context(nc.psum_tensor(f"pt{i}", [128, 1024], BF16)), Res(excl=True)) for i in range(2)]
        pbi = [0]
        pti = [0]

        def PB():
            r = pbank[pbi[0]]
            pbi[0] = (pbi[0] + 1) % 4
            return r

        def PT():
            r = ptb[pti[0]]
            pti[0] = (pti[0] + 1) % len(ptb)
            return r

        epsT = sb(top, "epsT", [128, 2], F32)
        reps = Res()
        P.op('vector', lambda e: e.memset(epsT[:, 0:1], LN_EPS), [], [reps])
        P.op('vector', lambda e: e.memset(epsT[:, 1:2], RMS_EPS), [], [reps])

        def rsqrt(out, in_, scale, which, reads, writes):
            P.op('scalar', lambda e: e.activation(out=out, in_=in_, func=AF.Sqrt, scale=scale, bias=epsT[:, which:which + 1]),
                 list(reads) + [reps], writes)
            P.op('vector', lambda e: e.reciprocal(out=out, in_=out), writes, writes)

        ident = sb(top, "ident_sb", [128, 128], BF16)
        rid = Res()
        P.dma('sync', ident[:], ident_in[:, :], writes=[rid])

        def convert_wu(l, j):
            for part in range(2):
                col = part * DFF + j * 128
                P.dma('gpsimd', WUb[l, j, :, :, part * 128:(part + 1) * 128],
                      Wn['f_w_up'][l, :, col:col + 128].rearrange("(c p) n -> p c n", p=128), [], [rWb[l]])

        for j in range(NFC):
            convert_wu(layers[0], j)

        lnst = ExitStack()
        top.enter_context(lnst)
        yb_rot = Rot([sb(top, f"yb{i}", [128, D], BF16) for i in range(1)])
        xtb_rot = Rot([sb(top, f"xtb{i}", [128, 8, 128], BF16) for i in range(2)])
        yn_rot = Rot([sb(top, f"yn{i}", [128, D], F32) for i in range(1)])
        sq_rot = Rot([sb(top, f"sq{i}", [128, D], F32) for i in range(1)])
        st_rot = Rot([sb(top, f"st{i}", [128, 8], F32) for i in range(2)])
        eng_flip = [0]

        def transpose_store(b, y32, ry, halo):
            yb, ryb = yb_rot.next()
            P.op('scalar', lambda e: e.copy(out=yb[:], in_=y32[:]), [ry], [ryb])
            pt, rpt = PT()
            for c in range(8):
                P.op('tensor', lambda e, c=c: e.transpose(out=pt[:, c * 128:(c + 1) * 128],
                                                           in_=yb[:, c * 128:(c + 1) * 128], identity=ident[:]),
                     [ryb, rid], [rpt])
            xtb, rxtb = xtb_rot.next()
            P.op('vector', lambda e: e.tensor_copy(out=xtb[:].rearrange("p c t -> p (c t)"), in_=pt[:]), [rpt], [rxtb])
            P.dma('gpsimd', XTH[b, :, :, 2:130], xtb[:], [rxtb], [rXTH[b]])
            if halo:
                P.dma('gpsimd', XTH[b + 1, :, :, 0:2], xtb[:, :, 126:128], [rxtb], [rXTHh[b + 1]])

        def ln_tail(b, z, rz, gam, bet, rgb, to_out, halo):
            st_, rst = st_rot.next()
            sq, rsq = sq_rot.next()
            yn, ryn = yn_rot.next()
            s1, s2, mean, msq, var, rstd, nmr = [st_[:, i:i + 1] for i in range(7)]
            P.op('vector', lambda e: e.tensor_scalar(out=yn[:], in0=z[:], scalar1=1.0, scalar2=None, op0=ALU.mult,
                                                     op1=ALU.add, accum_out=s1), [rz], [ryn, rst])
            P.op('scalar', lambda e: e.activation(out=sq[:], in_=z[:], func=AF.Square), [rz], [rsq])
            P.op('vector', lambda e: e.tensor_scalar(out=sq[:], in0=sq[:], scalar1=1.0, scalar2=None, op0=ALU.mult,
                                                     op1=ALU.add, accum_out=s2), [rsq], [rsq, rst])
            P.op('vector', lambda e: e.tensor_scalar(out=mean, in0=s1, scalar1=1.0 / D, scalar2=None, op0=ALU.mult), [rst], [rst])
            P.op('vector', lambda e: e.tensor_tensor(out=msq, in0=mean, in1=mean, op=ALU.mult), [rst], [rst])
            P.op('vector', lambda e: e.scalar_tensor_tensor(out=var, in0=s2, scalar=1.0 / D, in1=msq, op0=ALU.mult,
                                                            op1=ALU.subtract), [rst], [rst])
            rsqrt(rstd, var, 1.0, 0, [rst], [rst])
            P.op('vector', lambda e: e.scalar_tensor_tensor(out=nmr, in0=mean, scalar=-1.0, in1=rstd, op0=ALU.mult,
                                                            op1=ALU.mult), [rst], [rst])
            P.op('scalar', lambda e: e.activation(out=yn[:], in_=z[:], func=AF.Identity, scale=rstd, bias=nmr),
                 [rz, rst], [ryn])
            P.op('gpsimd', lambda e: e.tensor_tensor(out=yn[:], in0=yn[:], in1=gam[:], op=ALU.mult), [ryn] + list(rgb), [ryn])
            P.op('vector', lambda e: e.tensor_tensor(out=yn[:], in0=yn[:], in1=bet[:], op=ALU.add), [ryn] + list(rgb), [ryn])
            if to_out:
                P.dma('gpsimd', y_out[b * 128:(b + 1) * 128, :], yn[:], [ryn], [rY])
            else:
                P.dma('gpsimd', X32[b * 128:(b + 1) * 128, :], yn[:], [ryn], [rX32[b]])
                transpose_store(b, yn, ryn, halo)

        if first:
            with ExitStack() as ph:
                zt = sb(ph, "zt", [128, 8, 2], BF16)
                rzt = Res()
                P.op('vector', lambda e: e.memset(zt[:], 0.0), [], [rzt])
                P.dma('sync', XTH[0, :, :, 0:2], zt[:], [rzt], [rXTHh[0]])
                xl_rot = Rot([sb(ph, f"xl{i}", [128, D], F32) for i in range(2)])
                for b in range(NB):
                    xl, rxl = xl_rot.next()
                    P.dma('scalar', xl[:], x_in[b * 128:(b + 1) * 128, :], [], [rxl])
                    transpose_store(b, xl, rxl, False)
                P.barrier()

        def xsrc(layer):
            return x_in if (first and layer == layers[0]) else X32

        def load_bc(st, name, src_row):
            t = sb(st, name, [128, src_row.shape[-1]], F32)
            r = Res()
            P.dma('sync', t[:], src_row.to_broadcast([128, src_row.shape[-1]]), [], [r])
            return t, r

        TT = 512 if NB % 4 == 0 else 128 * NB
        NBT = TT // 128

        def ffn_phase(layer, last):
            with ExitStack() as ph:
                wd = sb(ph, "wd", [128, NFC, D], BF16)
                rw = Res()
                for j in range(0, NFC, 2):
                    P.dma('gpsimd', wd[:, j:j + 2, :],
                          Wn['f_w_down'][layer, j * 128:(j + 2) * 128, :].rearrange("(c p) n -> p c n", p=128), [], [rw])
                cw = sb(ph, "cw", [128, 3, 2 * NFC], F32)
                cb = sb(ph, "cb", [128, 2 * NFC], F32)
                for k in range(3):
                    P.dma('scalar', cw[:, k, :], Wn['f_conv_w'][layer, k, 0, :].rearrange("(j p) -> p j", p=128), [], [rw],
                          allow_slow_non_contiguous=True)
                P.dma('scalar', cb[:], Wn['f_conv_b'][layer, :].rearrange("(j p) -> p j", p=128), [], [rw],
                      allow_slow_non_contiguous=True)
                gam = sb(ph, "gam2", [128, D], F32)
                bet = sb(ph, "bet2", [128, D], F32)
                P.dma('sync', gam[:], Wn['ln2_g'][layer:layer + 1, :].to_broadcast([128, D]), [], [rw])
                P.dma('sync', bet[:], Wn['ln2_b'][layer:layer + 1, :].to_broadcast([128, D]), [], [rw])
                carry = sb(ph, "carry", [128, 2 * NFC, 2], F32)
                rcar = [Res() for _ in range(2 * NFC)]
                P.op('gpsimd', lambda e: e.memset(carry[:], 0.0), [], rcar)
                wu_rot = Rot([sb(ph, f"wu{i}", [128, 8, 256], BF16) for i in range(3)])
                xw_rot = Rot([sb(ph, f"xw{i}", [128, 8, TT], BF16) for i in range(2)])
                g_rot = Rot([sb(ph, f"g{i}", [128, NFC, TT], BF16) for i in range(2)])
                hs_rot = Rot([sb(ph, f"hs{i}", [128, TT + 2], F32) for i in range(4)])
                acc_rot = Rot([sb(ph, f"acc{i}", [128, TT], F32) for i in range(6)])
                sa_rot = Rot([sb(ph, f"sa{i}", [128, TT], F32) for i in range(2)])
                x32_rot = Rot([sb(ph, f"x32{i}", [128, D], F32) for i in range(4 if NBT == 4 else NBT)])
                z_rot = Rot([sb(ph, f"z{i}", [128, D], F32) for i in range(2)])
                for t in range(NB // NBT):
                    b0 = t * NBT
                    xw, rxw = xw_rot.next()
                    for i in range(NBT):
                        P.dma('sync', xw[:, :, i * 128:(i + 1) * 128], XTH[b0 + i, :, :, 2:130], [rXTH[b0 + i]], [rxw])
                    g, rg = g_rot.next()
                    pend_gate = None
                    x32s = []
                    for i in range(NBT):
                        x32, rx32 = x32_rot.next()
                        P.dma('sync', x32[:], X32[(b0 + i) * 128:(b0 + i + 1) * 128, :], [rX32[b0 + i]], [rx32])
                        x32s.append((x32, rx32))

                    def emit_gate(accs, j, g, rg):
                        sa, rsa = sa_rot.next()
                        P.op('scalar', lambda e, sa=sa, a=accs[0][0]: e.activation(out=sa[:], in_=a[:], func=AF.Silu),
                             [accs[0][1]], [rsa])
                        P.op('vector', lambda e, sa=sa, u=accs[1][0], g=g, j=j: e.tensor_tensor(
                            out=g[:, j, :], in0=sa[:], in1=u[:], op=ALU.mult), [rsa, accs[1][1]], [rg])

                    for j in range(NFC):
                        if t == 0 and layer != layers[-1]:
                            convert_wu(layers[layers.index(layer) + 1], j)
                        wu, rwu = wu_rot.next()
                        P.dma('sync', wu[:], WUb[layer, j], [rWb[layer]], [rwu])
                        accs = []
                        for part in range(2):
                            jj = part * NFC + j
                            ps, rps = PB()
                            for c in range(8):
                                P.op('tensor', lambda e, c=c, part=part, ps=ps, wu=wu, xw=xw: e.matmul(
                                    ps[:, 0:TT], lhsT=wu[:, c, part * 128:(part + 1) * 128], rhs=xw[:, c, :],
                                    start=(c == 0), stop=(c == 7)), [rwu, rxw], [rps], inc=(c == 7))
                            hs, rhs_ = hs_rot.next()
                            P.op('gpsimd', lambda e, hs=hs, jj=jj: e.tensor_copy(out=hs[:, 0:2], in_=carry[:, jj, :]), [rcar[jj]], [rhs_])
                            P.op('scalar', lambda e, hs=hs, ps=ps: e.copy(out=hs[:, 2:TT + 2], in_=ps[:, 0:TT]), [rps], [rhs_])
                            P.op('gpsimd', lambda e, hs=hs, jj=jj: e.tensor_copy(out=carry[:, jj, :], in_=hs[:, TT:TT + 2]), [rhs_], [rcar[jj]])
                            acc, racc = acc_rot.next()
                            eng = 'vector' if part == 0 else 'gpsimd'
                            P.op(eng, lambda e, acc=acc, hs=hs, jj=jj: e.tensor_scalar(
                                out=acc[:], in0=hs[:, 0:TT], scalar1=cw[:, 0, jj:jj + 1], scalar2=cb[:, jj:jj + 1],
                                op0=ALU.mult, op1=ALU.add), [rhs_, rw], [racc])
                            for k in (1, 2):
                                P.op('vector', lambda e, acc=acc, hs=hs, jj=jj, k=k: e.scalar_tensor_tensor(
                                    out=acc[:], in0=hs[:, k:k + TT], scalar=cw[:, k, jj:jj + 1], in1=acc[:],
                                    op0=ALU.mult, op1=ALU.add), [rhs_, rw, racc], [racc])
                            accs.append((acc, racc))
                        if pend_gate is not None:
                            emit_gate(*pend_gate)
                        pend_gate = (accs, j, g, rg)
                    emit_gate(*pend_gate)
                    pend_gate = None
                    for i in range(NBT):
                        b = b0 + i
                        x32, rx32 = x32s[i]
                        z, rz = z_rot.next()
                        for half in range(2):
                            po, rpo = PB()
                            for j in range(NFC):
                                P.op('tensor', lambda e, j=j, half=half, po=po, g=g, i=i: e.matmul(
                                    po[:, :], lhsT=g[:, j, i * 128:(i + 1) * 128], rhs=wd[:, j, half * 512:(half + 1) * 512],
                                    start=(j == 0), stop=(j == NFC - 1)), [rg, rw], [rpo], inc=(j == NFC - 1))
                            P.op('vector', lambda e, half=half, po=po, z=z, x32=x32: e.scalar_tensor_tensor(
                                out=z[:, half * 512:(half + 1) * 512], in0=x32[:, half * 512:(half + 1) * 512], scalar=ALPHA,
                                in1=po[:, :], op0=ALU.mult, op1=ALU.add), [rpo, rx32], [rz])
                        ln_tail(b, z, rz, gam, bet, [rw], last, False)
                P.barrier()

        ph_hs = [sb(top, "hsb", [128, 130], F32)]
        rhs = Res()

        def identity_mixer_phase(layer):
            with ExitStack() as ph:
                gam, rg1 = load_bc(ph, "gam1", Wn['ln1_g'][layer:layer + 1, :])
                bet, rb1 = load_bc(ph, "bet1", Wn['ln1_b'][layer:layer + 1, :])
                rgb = Res()
                P.op('vector', lambda e: e.memset(ph_hs[0][:, 0:1], 0.0), [rg1, rb1, rhs], [rgb, rhs])
                x32_rot = Rot([sb(ph, f"mx32{i}", [128, D], F32) for i in range(2)])
                z_rot = Rot([sb(ph, f"mz{i}", [128, D], F32) for i in range(2)])
                for b in range(NB):
                    x32, rx32 = x32_rot.next()
                    P.dma('sync', x32[:], xsrc(layer)[b * 128:(b + 1) * 128, :], [rX32[b]], [rx32])
                    z, rz = z_rot.next()
                    P.op('vector', lambda e, z=z, x32=x32: e.tensor_scalar(out=z[:], in0=x32[:], scalar1=ALPHA, scalar2=None,
                                                                        op0=ALU.mult), [rx32], [rz])
                    ln_tail(b, z, rz, gam, bet, [rgb], False, True)
                P.barrier()


        def hgrn_phase(layer):
            j = layer // 2
            with ExitStack() as ph:
                win = sb(ph, "hwin", [128, 8, 4 * D], BF16)
                wout = sb(ph, "hwout", [128, 8, D], BF16)
                rw = Res()
                for c in range(8):
                    P.dma('gpsimd', win[:, c, :], Wn['b_w_in'][j, c * 128:(c + 1) * 128, :], [], [rw])
                P.dma('gpsimd', wout[:], Wn['b_w_out'][j].rearrange("(c p) n -> p c n", p=128), [], [rw])
                gam, rg1 = load_bc(ph, "hgam1", Wn['ln1_g'][layer:layer + 1, :])
                bet, rb1 = load_bc(ph, "hbet1", Wn['ln1_b'][layer:layer + 1, :])
                go, rgo = load_bc(ph, "hgo", Wn['b_g_o'][j:j + 1, :])
                cst = sb(ph, "hcst", [128, 3, 128], F32)
                cind = sb(ph, "hcind", [128, 4], F32)
                cmask = sb(ph, "hcmask", [128, 128], F32)
                cmask4 = sb(ph, "hcmask4", [128, 4, 128], F32)
                rc = Res()
                P.dma('sync', cst[:], hconst_in[:, 0:3, :], [], [rc])
                P.dma('sync', cmask[:], hconst_in[:, 3, :], [], [rc])
                for i4 in range(4):
                    P.dma('sync', cmask4[:, i4, :], hconst_in[:, 3, :], [], [rc])
                P.dma('sync', cind[:], hcind_in[:, :], [], [rc])
                lb = sb(ph, "hlb", [128, D], F32)
                oml = sb(ph, "homl", [128, D], F32)
                rl = Res()
                tmp = ExitStack()
                lg = [load_bc(tmp, f"hlg{l}", Wn['b_lb_logits'][l:l + 1, :]) for l in range(4)]
                mx = sb(tmp, "hmx", [128, D], F32)
                den = sb(tmp, "hden", [128, D], F32)
                P.op('vector', lambda e: e.tensor_tensor(out=mx[:], in0=lg[0][0][:], in1=lg[1][0][:], op=ALU.max),
                     [lg[0][1], lg[1][1]], [rl])
                for l in (2, 3):
                    P.op('vector', lambda e, l=l: e.tensor_tensor(out=mx[:], in0=mx[:], in1=lg[l][0][:], op=ALU.max),
                         [lg[l][1], rl], [rl])
                for l in range(4):
                    P.op('vector', lambda e, l=l: e.tensor_tensor(out=lg[l][0][:], in0=lg[l][0][:], in1=mx[:], op=ALU.subtract),
                         [rl, lg[l][1]], [lg[l][1]])
                    P.op('scalar', lambda e, l=l: e.activation(out=lg[l][0][:], in_=lg[l][0][:], func=AF.Exp),
                         [lg[l][1]], [lg[l][1]])
                P.op('vector', lambda e: e.tensor_tensor(out=den[:], in0=lg[0][0][:], in1=lg[1][0][:], op=ALU.add),
                     [lg[0][1], lg[1][1]], [rl])
                for l in (2, 3):
                    P.op('vector', lambda e, l=l: e.tensor_tensor(out=den[:], in0=den[:], in1=lg[l][0][:], op=ALU.add),
                         [lg[l][1], rl], [rl])
                P.op('vector', lambda e: e.tensor_copy(out=lb[:], in_=lg[1][0][:]), [lg[1][1], rl], [rl])
                for l in range(2, layer + 1):
                    P.op('vector', lambda e, l=l: e.tensor_tensor(out=lb[:], in0=lb[:], in1=lg[l][0][:], op=ALU.add),
                         [lg[l][1], rl], [rl])
                P.op('vector', lambda e: e.reciprocal(out=den[:], in_=den[:]), [rl], [rl])
                P.op('vector', lambda e: e.tensor_tensor(out=lb[:], in0=lb[:], in1=den[:], op=ALU.mult), [rl], [rl])
                P.op('vector', lambda e: e.tensor_scalar(out=oml[:], in0=lb[:], scalar1=-1.0, scalar2=1.0, op0=ALU.mult,
                                                         op1=ALU.add), [rl], [rl])
                P.barrier()
                tmp.close()
                Sf = sb(ph, "hSf", [128, 8, 128], F32)
                SbA = sb(ph, "hSbA", [128, 8, 128], BF16)
                SbB = sb(ph, "hSbB", [128, 8, 128], BF16)
                q0T = sb(ph, "hq0T", [128, 8, 128], BF16)
                q1T = sb(ph, "hq1T", [128, 8, 128], BF16)
                kT = sb(ph, "hkT", [128, 8, 128], BF16)
                qT = sb(ph, "hqT", [128, 8, 128], BF16)
                rS = Res()
                rq = Res()
                P.op('vector', lambda e: e.memset(Sf[:], 0.0), [], [rS])
                P.op('gpsimd', lambda e: e.memset(q0T[:], 0.0), [], [rq])
                P.op('gpsimd', lambda e: e.memset(q1T[:], 0.0), [], [rq])
                xh_rot = Rot([sb(ph, f"hxh{i}", [128, 8, 130], BF16) for i in range(2)])
                x32_rot = Rot([sb(ph, f"hx32{i}", [128, D], F32) for i in range(2)])
                F = lambda n: sb(ph, n, [128, D], F32)
                sig, lf, key, bb, bl, qs, og = F("hsig"), F("hlf"), F("hkey"), F("hbb"), F("hbl"), F("hqs"), F("hog")
                zt_ = qs
                B16 = lambda n: sb(ph, n, [128, D], BF16)
                qtl, ktl, k0, k1, vv, onb = B16("hqtl"), B16("hktl"), B16("hk0"), B16("hk1"), B16("hvv"), B16("honb")
                onT = sb(ph, "honT", [128, 8, 128], BF16)
                scT = sb(ph, "hscT", [128, 8, 128], BF16)
                rscT = Res()
                rog = Res()
                rSb = {id(SbA): Res(), id(SbB): Res()}
                P.op('vector', lambda e: e.memset(SbA[:], 0.0), [], [rSb[id(SbA)]])
                dec = sb(ph, "hdec", [128, 8, 2], F32)
                ss = sb(ph, "hss", [128, 8], F32)
                rt = Res()

                def proj(part, xh, rxh):
                    outs = []
                    for half in range(2):
                        ps, rps = PB()
                        col = part * D + half * 512
                        for c in range(8):
                            P.op('tensor', lambda e, c=c, col=col, ps=ps: e.matmul(
                                ps[:, :], lhsT=xh[:, c, 2:130], rhs=win[:, c, col:col + 512],
                                start=(c == 0), stop=(c == 7)), [rw, rxh], [rps], inc=(c == 7))
                        outs.append((ps, rps))
                    return outs

                for b in range(NB):
                    xh, rxh = xh_rot.next()
                    P.dma('sync', xh[:, :, 2:130], XTH[b, :, :, 2:130], [rXTH[b]], [rxh])
                    x32, rx32 = x32_rot.next()
                    P.dma('sync', x32[:], xsrc(layer)[b * 128:(b + 1) * 128, :], [rX32[b]], [rx32])
                    for half, (ps, rps) in enumerate(proj(1, xh, rxh)):
                        hs = slice(half * 512, (half + 1) * 512)
                        P.op('scalar', lambda e, ps=ps, hs=hs: e.activation(out=sig[:, hs], in_=ps[:, :], func=AF.Sigmoid),
                             [rps], [rt])
                    P.op('vector', lambda e: e.tensor_tensor(out=lf[:], in0=sig[:], in1=oml[:], op=ALU.mult), [rt, rl], [rt])
                    P.op('vector', lambda e: e.tensor_tensor(out=key[:], in0=oml[:], in1=lf[:], op=ALU.subtract), [rt, rl], [rt])
                    P.op('vector', lambda e: e.tensor_tensor(out=lf[:], in0=lf[:], in1=lb[:], op=ALU.add), [rt, rl], [rt])
                    P.op('scalar', lambda e: e.activation(out=lf[:], in_=lf[:], func=AF.Ln), [rt], [rt])
                    for half in range(2):
                        hs = slice(half * 512, (half + 1) * 512)
                        ps, rps = PB()
                        P.op('tensor', lambda e, ps=ps, hs=hs: e.matmul(ps[:, :], lhsT=cst[:, 0, :], rhs=lf[:, hs],
                                                                       start=True, stop=True), [rt, rc], [rps])
                        P.op('vector', lambda e, ps=ps, hs=hs: e.tensor_copy(out=bb[:, hs], in_=ps[:, :]), [rps], [rt])
                        ps, rps = PB()
                        P.op('tensor', lambda e, ps=ps, hs=hs: e.matmul(ps[:, :], lhsT=cst[:, 1, :], rhs=lf[:, hs],
                                                                       start=True, stop=True), [rt, rc], [rps])
                        P.op('vector', lambda e, ps=ps, hs=hs: e.tensor_copy(out=bl[:, hs], in_=ps[:, :]), [rps], [rt])
                    ps, rps = PB()
                    for h in range(8):
                        P.op('tensor', lambda e, h=h, ps=ps: e.matmul(ps[:, h * 2:h * 2 + 2], lhsT=lf[:, h * 128:(h + 1) * 128],
                                                                     rhs=cind[:, 0:2], start=True, stop=True), [rt, rc], [rps])
                    P.op('scalar', lambda e, ps=ps: e.activation(out=dec[:].rearrange("p h c -> p (h c)"), in_=ps[:, 0:16],
                                                                 func=AF.Exp), [rps], [rt])
                    P.op('vector', lambda e: e.tensor_tensor(out=bl[:], in0=bl[:], in1=bb[:], op=ALU.subtract), [rt], [rt])
                    P.op('scalar', lambda e: e.activation(out=bl[:], in_=bl[:], func=AF.Exp), [rt], [rt])
                    P.op('vector', lambda e: e.tensor_tensor(out=bl[:], in0=bl[:], in1=key[:], op=ALU.mult), [rt], [rt])
                    P.op('vector', lambda e: e.tensor_scalar(out=k0[:], in0=bl[:], scalar1=cind[:, 0:1], scalar2=None,
                                                             op0=ALU.mult), [rt, rc], [rt])
                    P.op('vector', lambda e: e.tensor_scalar(out=k1[:], in0=bl[:], scalar1=cind[:, 1:2], scalar2=None,
                                                             op0=ALU.mult), [rt, rc], [rt])
                    P.op('scalar', lambda e: e.activation(out=sig[:], in_=bb[:], func=AF.Exp, scale=-1.0), [rt], [rt])
                    P.op('vector', lambda e: e.tensor_tensor(out=ktl[:], in0=sig[:], in1=key[:], op=ALU.mult), [rt], [rt])
                    P.op('scalar', lambda e: e.activation(out=bb[:], in_=bb[:], func=AF.Exp), [rt], [rt])
                    for half, (ps, rps) in enumerate(proj(0, xh, rxh)):
                        hs = slice(half * 512, (half + 1) * 512)
                        P.op('scalar', lambda e, ps=ps, hs=hs: e.activation(out=qs[:, hs], in_=ps[:, :], func=AF.Silu),
                             [rps], [rt])
                    P.op('vector', lambda e: e.tensor_tensor(out=qtl[:], in0=qs[:], in1=bb[:], op=ALU.mult), [rt], [rt])
                    for half, (ps, rps) in enumerate(proj(2, xh, rxh)):
                        hs = slice(half * 512, (half + 1) * 512)
                        P.op('vector', lambda e, ps=ps, hs=hs: e.tensor_copy(out=vv[:, hs], in_=ps[:, :]), [rps], [rt])
                    for half, (ps, rps) in enumerate(proj(3, xh, rxh)):
                        hs = slice(half * 512, (half + 1) * 512)
                        P.op('scalar', lambda e, ps=ps, hs=hs: e.activation(out=sig[:, hs], in_=ps[:, :], func=AF.Sigmoid),
                             [rps], [rt])
                    for (src, dsts) in ((qtl, 'q'), (ktl, 'k')):
                        pt, rpt = PT()
                        for h in range(8):
                            P.op('tensor', lambda e, h=h, pt=pt, src=src: e.transpose(
                                out=pt[:, h * 128:(h + 1) * 128], in_=src[:, h * 128:(h + 1) * 128], identity=ident[:]),
                                [rt, rid], [rpt])
                        ptv = pt[:].rearrange("p (h t) -> p h t", h=8)
                        if dsts == 'q':
                            P.op('vector', lambda e, ptv=ptv: e.tensor_copy(out=qT[:], in_=ptv), [rpt], [rq])
                            P.op('vector', lambda e, ptv=ptv: e.tensor_copy(out=q0T[:, :, 0:64], in_=ptv[:, :, 0:64]), [rpt], [rq])
                            P.op('vector', lambda e, ptv=ptv: e.tensor_copy(out=q1T[:, :, 64:128], in_=ptv[:, :, 64:128]), [rpt], [rq])
                        else:
                            P.op('vector', lambda e, ptv=ptv: e.tensor_copy(out=kT[:], in_=ptv), [rpt], [rq])
                    pu = [pbank[0], pbank[1]]
                    psc = [pbank[2], pbank[3]]
                    po = [pbank[4], pbank[5]]
                    Sfv = Sf[:].rearrange("p h e -> p (h e)")

                    def state_update(kk, ci, Sb):
                        for h in range(8):
                            hsl = slice(h * 128, (h + 1) * 128)
                            pb_, rpb = pu[h // 4]
                            P.op('tensor', lambda e, pb_=pb_, h=h, hsl=hsl: e.matmul(
                                pb_[:, (h % 4) * 128:(h % 4 + 1) * 128], lhsT=kk[:, hsl], rhs=vv[:, hsl], start=True, stop=True),
                                [rt], [rpb])
                        P.op('vector', lambda e: e.tensor_tensor(out=Sf[:], in0=Sf[:], in1=dec[:, :, ci:ci + 1].to_broadcast([128, 8, 128]),
                                                                 op=ALU.mult), [rt, rS], [rS])
                        for hh in range(2):
                            pb_, rpb = pu[hh]
                            P.op('vector', lambda e, pb_=pb_, hh=hh: e.tensor_tensor(
                                out=Sfv[:, hh * 512:(hh + 1) * 512], in0=Sfv[:, hh * 512:(hh + 1) * 512], in1=pb_[:, :], op=ALU.add),
                                [rpb, rS], [rS])
                        P.op('scalar', lambda e: e.copy(out=Sb[:], in_=Sf[:]), [rS], [rSb[id(Sb)]])

                    state_update(k0, 0, SbB)
                    for h in range(8):
                        pb_, rpb = psc[h // 4]
                        P.op('tensor', lambda e, pb_=pb_, h=h: e.matmul(pb_[:, (h % 4) * 128:(h % 4 + 1) * 128], lhsT=kT[:, h, :],
                                                                       rhs=qT[:, h, :], start=True, stop=True), [rq], [rpb])
                    for hh in range(2):
                        pb_, rpb = psc[hh]
                        P.op('vector', lambda e, pb_=pb_, hh=hh: e.tensor_tensor(
                            out=scT[:, hh * 4:(hh + 1) * 4, :], in0=pb_[:, :].rearrange("p (h t) -> p h t", h=4), in1=cmask4[:],
                            op=ALU.mult), [rpb, rc], [rscT])
                    for h in range(8):
                        hsl = slice(h * 128, (h + 1) * 128)
                        pb_, rpb = po[h // 4]
                        osl = slice((h % 4) * 128, (h % 4 + 1) * 128)
                        P.op('tensor', lambda e, pb_=pb_, h=h, hsl=hsl, osl=osl: e.matmul(
                            pb_[:, osl], lhsT=scT[:, h, :], rhs=vv[:, hsl], start=True, stop=False), [rscT, rt], [rpb])
                        P.op('tensor', lambda e, pb_=pb_, h=h, osl=osl: e.matmul(
                            pb_[:, osl], lhsT=q0T[:, h, :], rhs=SbA[:, h, :], start=False, stop=False), [rq, rSb[id(SbA)]], [rpb])
                        P.op('tensor', lambda e, pb_=pb_, h=h, osl=osl: e.matmul(
                            pb_[:, osl], lhsT=q1T[:, h, :], rhs=SbB[:, h, :], start=False, stop=True), [rq, rSb[id(SbB)]], [rpb])
                    for hh in range(2):
                        pb_, rpb = po[hh]
                        P.op('vector', lambda e, pb_=pb_, hh=hh: e.tensor_tensor(
                            out=og[:, hh * 512:(hh + 1) * 512], in0=pb_[:, :], in1=sig[:, hh * 512:(hh + 1) * 512], op=ALU.mult),
                            [rpb, rt], [rog])
                    state_update(k1, 1, SbA)
                    P.op('scalar', lambda e: e.activation(out=qs[:], in_=og[:], func=AF.Square), [rt, rog], [rt])
                    P.op('vector', lambda e: e.tensor_reduce(out=ss[:], in_=qs[:].rearrange("p (h e) -> p h e", h=8),
                                                             axis=AX.X, op=ALU.add), [rt], [rt])
                    rsqrt(ss[:], ss[:], 1.0 / 128, 1, [rt], [rt])
                    for h in range(8):
                        hsl = slice(h * 128, (h + 1) * 128)
                        P.op('vector', lambda e, h=h, hsl=hsl: e.scalar_tensor_tensor(
                            out=onb[:, hsl], in0=og[:, hsl], scalar=ss[:, h:h + 1], in1=go[:, hsl], op0=ALU.mult, op1=ALU.mult),
                            [rt, rgo, rog], [rt])
                    pt, rpt = PT()
                    for c in range(8):
                        P.op('tensor', lambda e, c=c, pt=pt: e.transpose(out=pt[:, c * 128:(c + 1) * 128],
                                                                        in_=onb[:, c * 128:(c + 1) * 128], identity=ident[:]),
                             [rt, rid], [rpt])
                    P.op('vector', lambda e, pt=pt: e.tensor_copy(out=onT[:].rearrange("p c t -> p (c t)"), in_=pt[:]), [rpt], [rt])
                    for half in range(2):
                        po, rpo = PB()
                        for c in range(8):
                            P.op('tensor', lambda e, c=c, half=half, po=po: e.matmul(
                                po[:, :], lhsT=onT[:, c, :], rhs=wout[:, c, half * 512:(half + 1) * 512],
                                start=(c == 0), stop=(c == 7)), [rt, rw], [rpo], inc=(c == 7))
                        P.op('vector', lambda e, half=half, po=po, x32=x32: e.scalar_tensor_tensor(
                            out=zt_[:, half * 512:(half + 1) * 512], in0=x32[:, half * 512:(half + 1) * 512], scalar=ALPHA,
                            in1=po[:, :], op0=ALU.mult, op1=ALU.add), [rpo, rx32], [rt])
                    ln_tail(b, zt_, rt, gam, bet, [rg1, rb1], False, False)
                P.barrier()


        NIT = 18

        def dsa_phase(layer):
            j = layer // 2
            with ExitStack() as ph:
                win = sb(ph, "awin", [128, 8, AIN], BF16)
                wql = sb(ph, "awql", [128, 3, AH * KVR], BF16)
                wqi = sb(ph, "awqi", [128, 3, IDXH * IDXD], BF16)
                wuv = sb(ph, "awuv", [128, 2 * AH, 128], BF16)
                wout = sb(ph, "awout", [128, 8, D], BF16)
                rw = Res()
                P.dma('gpsimd', win[:], Wn['a_w_in'][j].rearrange("(c p) n -> p c n", p=128), [], [rw])
                P.dma('gpsimd', wql[:], Wn['a_w_q_lat'][j].rearrange("(c p) n -> p c n", p=128), [], [rw])
                P.dma('gpsimd', wqi[:], Wn['a_w_q_idx'][j].rearrange("(c p) n -> p c n", p=128), [], [rw])
                P.dma('gpsimd', wuv[:], Wn['a_w_uv'][j].rearrange("h (rc p) d -> p (h rc) d", p=128), [], [rw])
                P.dma('gpsimd', wout[:], Wn['a_w_out'][j].rearrange("(c p) n -> p c n", p=128), [], [rw])
                gam, rg1 = load_bc(ph, "agam1", Wn['ln1_g'][layer:layer + 1, :])
                bet, rb1 = load_bc(ph, "abet1", Wn['ln1_b'][layer:layer + 1, :])
                gq, rgq = load_bc(ph, "agq", Wn['a_g_q'][j:j + 1, :])
                gkv, rgkv = load_bc(ph, "agkv", Wn['a_g_kv'][j:j + 1, :])
                gki, rgki = load_bc(ph, "agki", Wn['a_g_kidx'][j:j + 1, :])
                bki, rbki = load_bc(ph, "abki", Wn['a_b_kidx'][j:j + 1, :])
                maskall = sb(ph, "amaskall", [128, 8, 128], F32)
                rep = sb(ph, "arep", [8, 128], BF16)
                cm = sb(ph, "acm", [128, 128], F32)
                pw = sb(ph, "apw", [128, NIT], F32)
                rc = Res()
                P.dma('sync', maskall[:], maskall_in[:, :, :], [], [rc])
                P.dma('sync', rep[:], rep_in[:, :], [], [rc])
                P.dma('sync', cm[:], cm_in[:, :], [], [rc])
                P.dma('sync', pw[:], pw_in[:, 0:NIT], [], [rc])
                CKV = sb(ph, "aCKV", [128, NB, KVR + 1], BF16)
                CKVT = sb(ph, "aCKVT", [128, 2, S], BF16)
                KIT = sb(ph, "aKIT", [64, S], BF16)
                rK = Res()
                P.op('vector', lambda e: e.memset(CKV[:, :, KVR:KVR + 1], 1.0), [], [rK])
                score = sb(ph, "ascore", [128, S], F32)
                msk = sb(ph, "amsk", [128, S], BF16)
                junk = msk
                zA = sb(ph, "azA", [128, 328], F32)
                mskT_rot = Rot([sb(ph, f"amskT{i}", [128, NB, 128], BF16) for i in range(2)])
                x32_rot = Rot([sb(ph, f"ax32{i}", [128, D], F32) for i in range(1)])
                xh_rot = Rot([sb(ph, f"axh{i}", [128, 8, 130], BF16) for i in range(2)])
                sqa = sb(ph, "asqa", [128, QR], F32)
                cq = sb(ph, "acq", [128, QR], BF16)
                kix = sb(ph, "akix", [128, IDXD], F32)
                kib = sb(ph, "akib", [128, IDXD], BF16)
                wq = sb(ph, "awq", [128, 8], BF16)
                st = sb(ph, "ast", [128, 16], F32)
                cqT_rot = Rot([sb(ph, f"acqT{i}", [128, 3, 128], BF16) for i in range(2)])
                wT = sb(ph, "awT", [8, 128], BF16)
                QIT = sb(ph, "aQIT", [64, 8, 8, 16], BF16)
                Wsel = sb(ph, "aWsel", [128, 8, 128], BF16)
                R_rot = Rot([sb(ph, f"aR{i}", [128, 512], BF16) for i in range(3)])
                mm = sb(ph, "amm", [128, 2, 8], F32)
                bs = sb(ph, "abs", [128, 8], F32)
                Wt = sb(ph, "aWt", [128, NIT], F32)
                qlT4 = sb(ph, "aqlT4", [128, 2, 512], BF16)
                PT_rot = Rot([sb(ph, f"aPT{i}", [128, 4, 128], BF16) for i in range(3)])
                olb = sb(ph, "aolb", [128, AH, KVR], BF16)
                olT = sb(ph, "aolT8", [128, AH, 2, 128], BF16)
                dcol = sb(ph, "adcol", [128, AH], F32)
                rd = sb(ph, "ard8", [128, AH], F32)
                rolb, rolT, rrd = Res(), Res(), Res()
                oh = sb(ph, "aoh", [128, D], BF16)
                ohT = sb(ph, "aohT", [128, 8, 128], BF16)
                rden = sb(ph, "arden", [128, 1], F32)
                z = sb(ph, "az", [128, D], F32)
                rt = Res()
                r_z, r_q, r_cq, r_kv, r_ki, r_w, r_cqT, r_wT, r_qit, r_wsel, r_mm, r_bs = [Res() for _ in range(12)]
                sqa_kv = sb(ph, "asqakv", [128, KVR], F32)
                sqa_ki = sb(ph, "asqaki", [128, IDXD], F32)
                rsc = Res()
                rmk = Res()
                rql = Res()
                ro = Res()

                def rmsn(ps, rps0, c0, n, gtile, rg, out_ap, idx):
                    ss, rs = st[:, idx:idx + 1], st[:, idx + 1:idx + 2]
                    P.op('scalar', lambda e: e.activation(out=sqa[:, 0:n], in_=ps[:, c0:c0 + n], func=AF.Square), [rps0], [r_q])
                    P.op('vector', lambda e: e.tensor_scalar(out=sqa[:, 0:n], in0=sqa[:, 0:n], scalar1=1.0, scalar2=None,
                                                             op0=ALU.mult, op1=ALU.add, accum_out=ss), [r_q], [r_q])
                    rsqrt(rs, ss, 1.0 / n, 1, [r_q], [r_q])
                    P.op('vector', lambda e: e.scalar_tensor_tensor(out=out_ap, in0=ps[:, c0:c0 + n], scalar=rs, in1=gtile[:, 0:n],
                                                                    op0=ALU.mult, op1=ALU.mult), [rps0, r_q, rg], [r_cq])

                blk = {}
                r_zo = Res()

                def SB(b):
                    cqT, r_cqT = cqT_rot.next()
                    mskT, rmkT = mskT_rot.next()
                    L = (b + 1) * 128
                    xh, rxh = xh_rot.next()
                    P.dma('sync', xh[:, :, 2:130], XTH[b, :, :, 2:130], [rXTH[b]], [rxh])
                    ps0, rps0 = PB()
                    ps1, rps1 = PB()
                    for c in range(8):
                        P.op('tensor', lambda e, c=c, ps0=ps0: e.matmul(ps0[:, :], lhsT=xh[:, c, 2:130], rhs=win[:, c, 0:512],
                                                                       start=(c == 0), stop=(c == 7)), [rw, rxh], [rps0], inc=(c == 7))
                    for c in range(8):
                        P.op('tensor', lambda e, c=c, ps1=ps1: e.matmul(ps1[:, 0:AIN - 512], lhsT=xh[:, c, 2:130], rhs=win[:, c, 512:AIN],
                                                                       start=(c == 0), stop=(c == 7)), [rw, rxh], [rps1], inc=(c == 7))
                    rmsn(ps0, rps0, 0, QR, gq, rgq, cq[:], 0)
                    P.op('scalar', lambda e, ps0=ps0: e.copy(out=zA[:, 0:128], in_=ps0[:, QR:512]), [rps0], [r_z])
                    P.op('scalar', lambda e, ps1=ps1: e.copy(out=zA[:, 128:128 + 200], in_=ps1[:, 0:200]), [rps1], [r_z])
                    ss, rs = st[:, 2:3], st[:, 3:4]
                    P.op('scalar', lambda e: e.activation(out=sqa_kv[:, 0:KVR], in_=zA[:, 0:KVR], func=AF.Square), [r_z], [r_kv])
                    P.op('vector', lambda e: e.tensor_scalar(out=sqa_kv[:, 0:KVR], in0=sqa_kv[:, 0:KVR], scalar1=1.0, scalar2=None,
                                                             op0=ALU.mult, op1=ALU.add, accum_out=ss), [r_kv], [r_kv])
                    rsqrt(rs, ss, 1.0 / KVR, 1, [r_kv], [r_kv])
                    P.op('vector', lambda e, b=b: e.scalar_tensor_tensor(out=CKV[:, b, 0:KVR], in0=zA[:, 0:KVR], scalar=rs, in1=gkv[:, :],
                                                                         op0=ALU.mult, op1=ALU.mult), [r_kv, r_z, rgkv], [rK])
                    s1, s2, mean, msq, var, rstd, nmr = [st[:, 4 + i:5 + i] for i in range(7)]
                    ki = zA[:, 256:320]
                    P.op('vector', lambda e: e.tensor_scalar(out=kix[:], in0=ki, scalar1=1.0, scalar2=None, op0=ALU.mult,
                                                             op1=ALU.add, accum_out=s1), [r_z], [r_ki])
                    P.op('scalar', lambda e: e.activation(out=sqa_ki[:, 0:IDXD], in_=ki, func=AF.Square), [r_z], [r_ki])
                    P.op('vector', lambda e: e.tensor_scalar(out=sqa_ki[:, 0:IDXD], in0=sqa_ki[:, 0:IDXD], scalar1=1.0, scalar2=None,
                                                             op0=ALU.mult, op1=ALU.add, accum_out=s2), [r_ki], [r_ki])
                    P.op('vector', lambda e: e.tensor_scalar(out=mean, in0=s1, scalar1=1.0 / IDXD, scalar2=None, op0=ALU.mult), [r_ki], [r_ki])
                    P.op('vector', lambda e: e.tensor_tensor(out=msq, in0=mean, in1=mean, op=ALU.mult), [r_ki], [r_ki])
                    P.op('vector', lambda e: e.scalar_tensor_tensor(out=var, in0=s2, scalar=1.0 / IDXD, in1=msq, op0=ALU.mult,
                                                                    op1=ALU.subtract), [r_ki], [r_ki])
                    rsqrt(rstd, var, 1.0, 0, [r_ki], [r_ki])
                    P.op('vector', lambda e: e.scalar_tensor_tensor(out=nmr, in0=mean, scalar=-1.0, in1=rstd, op0=ALU.mult,
                                                                    op1=ALU.mult), [r_ki], [r_ki])
                    P.op('scalar', lambda e: e.activation(out=kix[:], in_=ki, func=AF.Identity, scale=rstd, bias=nmr), [r_ki, r_z], [r_ki])
                    P.op('vector', lambda e: e.tensor_tensor(out=kix[:], in0=kix[:], in1=gki[:], op=ALU.mult), [r_ki, rgki], [r_ki])
                    P.op('vector', lambda e: e.tensor_tensor(out=kib[:], in0=kix[:], in1=bki[:], op=ALU.add), [r_ki, rbki], [r_ki])
                    P.op('vector', lambda e: e.tensor_scalar(out=wq[:], in0=zA[:, 320:328], scalar1=float((IDXH * IDXD) ** -0.5),
                                                             scalar2=None, op0=ALU.mult), [r_z], [r_w])
                    pt, rpt = PT()
                    for c in range(3):
                        P.op('tensor', lambda e, c=c, pt=pt: e.transpose(out=pt[:, c * 128:(c + 1) * 128], in_=cq[:, c * 128:(c + 1) * 128],
                                                                        identity=ident[:]), [r_cq, rid], [rpt])
                    for c in range(2):
                        P.op('tensor', lambda e, c=c, pt=pt, b=b: e.transpose(out=pt[:, (3 + c) * 128:(4 + c) * 128],
                                                                             in_=CKV[:, b, c * 128:(c + 1) * 128], identity=ident[:]),
                             [rK, rid], [rpt])
                    P.op('tensor', lambda e, pt=pt: e.transpose(out=pt[0:64, 5 * 128:6 * 128], in_=kib[:, :], identity=ident[:]),
                         [r_ki, rid], [rpt])
                    P.op('tensor', lambda e, pt=pt: e.transpose(out=pt[0:8, 6 * 128:7 * 128], in_=wq[:, :], identity=ident[:]),
                         [r_w, rid], [rpt])
                    P.op('vector', lambda e, pt=pt: e.tensor_copy(out=cqT[:].rearrange("p c t -> p (c t)"), in_=pt[:, 0:384]), [rpt], [r_cqT])
                    for c in range(2):
                        P.op('vector', lambda e, pt=pt, c=c, b=b: e.tensor_copy(out=CKVT[:, c, b * 128:(b + 1) * 128],
                                                                               in_=pt[:, (3 + c) * 128:(4 + c) * 128]), [rpt], [rK])
                    P.op('vector', lambda e, pt=pt, b=b: e.tensor_copy(out=KIT[:, b * 128:(b + 1) * 128], in_=pt[0:64, 640:768]), [rpt], [rK])
                    P.op('vector', lambda e, pt=pt: e.tensor_copy(out=wT[:], in_=pt[0:8, 768:896]), [rpt], [r_wT])
                    for hh in range(2):
                        pq, rpq = PB()
                        for h4 in range(4):
                            h = hh * 4 + h4
                            for kc in range(3):
                                P.op('tensor', lambda e, pq=pq, h=h, h4=h4, kc=kc: e.matmul(
                                    pq[0:64, h4 * 128:(h4 + 1) * 128], lhsT=wqi[:, kc, h * 64:(h + 1) * 64], rhs=cqT[:, kc, :],
                                    start=(kc == 0), stop=(kc == 2)), [rw, r_cqT], [rpq], inc=(kc == 2))
                        P.op('scalar', lambda e, pq=pq, hh=hh: e.copy(
                            out=QIT[:, :, hh * 4:(hh + 1) * 4, :].rearrange("p g h t -> p h g t"),
                            in_=pq[0:64, :].rearrange("p (h g t) -> p h g t", h=4, g=8)), [rpq], [r_qit])
                    pe_, rpe = PB()
                    P.op('tensor', lambda e, pe_=pe_: e.matmul(pe_[:, 0:128], lhsT=rep[:, :], rhs=wT[:, :], start=True, stop=True),
                         [rc, r_wT], [rpe])
                    for g in range(8):
                        P.op('vector', lambda e, g=g, pe_=pe_: e.tensor_tensor(out=Wsel[:, g, :], in0=pe_[:, 0:128], in1=maskall[:, g, :],
                                                                              op=ALU.mult), [rpe, rc], [r_wsel])
                    nkc = (L + 511) // 512
                    for kc in range(nkc):
                        k0_ = kc * 512
                        n = min(512, L - k0_)
                        psc, rpsc = pbank[4]
                        pend = []

                        def emit_mm1(g, k0_=k0_, n=n):
                            p1, rp1 = PB()
                            P.op('tensor', lambda e, p1=p1, g=g: e.matmul(
                                p1[:, 0:n], lhsT=QIT[:, g, :, :].rearrange("p h t -> p (h t)"), rhs=KIT[:, k0_:k0_ + n], start=True, stop=True),
                                [r_qit, rK], [rp1])
                            R, rR = R_rot.next()
                            if g % 2 == 0:
                                P.op('scalar', lambda e, p1=p1, R=R: e.activation(out=R[:, 0:n], in_=p1[:, 0:n], func=AF.Relu),
                                     [rp1], [rR])
                            else:
                                P.op('vector', lambda e, p1=p1, R=R: e.tensor_scalar(out=R[:, 0:n], in0=p1[:, 0:n], scalar1=0.0,
                                                                                    scalar2=None, op0=ALU.max), [rp1], [rR])
                            pend.append((R, rR))

                        emit_mm1(0)
                        emit_mm1(1)
                        for g in range(8):
                            if g + 2 < 8:
                                emit_mm1(g + 2)
                            R, rR = pend.pop(0)
                            P.op('tensor', lambda e, psc=psc, g=g, R=R, n=n: e.matmul(
                                psc[:, 0:n], lhsT=Wsel[:, g, :], rhs=R[:, 0:n], start=(g == 0), stop=(g == 7)), [r_wsel, rR], [rpsc])
                        P.op('scalar', lambda e, psc=psc, k0_=k0_, n=n: e.copy(out=score[:, k0_:k0_ + n], in_=psc[:, 0:n]), [rpsc], [rsc])
                        P.op('vector', lambda e, psc=psc, kc=kc, n=n: e.tensor_reduce(out=mm[:, 0, kc:kc + 1], in_=psc[:, 0:n], axis=AX.X,
                                                                                     op=ALU.min), [rpsc], [r_mm])
                        P.op('vector', lambda e, psc=psc, kc=kc, n=n: e.tensor_reduce(out=mm[:, 1, kc:kc + 1], in_=psc[:, 0:n], axis=AX.X,
                                                                                     op=ALU.max), [rpsc], [r_mm])
                    P.op('vector', lambda e, L=L: e.tensor_tensor(out=score[:, L - 128:L], in0=score[:, L - 128:L], in1=cm[:], op=ALU.add),
                         [rsc, rc], [rsc])
                    lo, hi, w0, mid, cnt, stp = [bs[:, i:i + 1] for i in range(6)]
                    P.op('vector', lambda e, nkc=nkc: e.tensor_reduce(out=lo, in_=mm[:, 0, 0:nkc], axis=AX.X, op=ALU.min), [r_mm], [r_bs])
                    P.op('vector', lambda e, nkc=nkc: e.tensor_reduce(out=hi, in_=mm[:, 1, 0:nkc], axis=AX.X, op=ALU.max), [r_mm], [r_bs])
                    P.op('vector', lambda e: e.tensor_tensor(out=w0, in0=hi, in1=lo, op=ALU.subtract), [r_bs], [r_bs])
                    P.op('vector', lambda e: e.tensor_scalar(out=w0, in0=w0, scalar1=1.0001, scalar2=1e-6, op0=ALU.mult, op1=ALU.add), [r_bs], [r_bs])
                    P.op('vector', lambda e: e.tensor_scalar(out=Wt[:], in0=pw[:], scalar1=w0, scalar2=None, op0=ALU.mult), [r_bs, rc], [r_bs])
                    if L > TOPK:
                        P.op('vector', lambda e: e.tensor_tensor(out=mid, in0=lo, in1=Wt[:, 0:1], op=ALU.add), [r_bs], [r_bs])
                        for k in range(NIT):
                            P.op('vector', lambda e, L=L: e.tensor_scalar(out=junk[:, 0:L], in0=score[:, 0:L], scalar1=mid, scalar2=None,
                                                                         op0=ALU.is_ge, op1=ALU.add, accum_out=cnt), [r_bs, rsc], [r_bs, rmk])
                            P.op('vector', lambda e, k=k: e.scalar_tensor_tensor(out=stp, in0=cnt, scalar=float(TOPK) - 0.5, in1=Wt[:, k:k + 1],
                                                                                op0=ALU.is_ge, op1=ALU.mult), [r_bs], [r_bs])
                            if k < NIT - 1:
                                P.op('vector', lambda e, k=k: e.scalar_tensor_tensor(out=mid, in0=mid, scalar=Wt[:, k + 1:k + 2], in1=stp,
                                                                                    op0=ALU.subtract, op1=ALU.add), [r_bs], [r_bs])
                            else:
                                P.op('vector', lambda e, k=k: e.scalar_tensor_tensor(out=lo, in0=mid, scalar=Wt[:, k:k + 1], in1=stp,
                                                                                    op0=ALU.subtract, op1=ALU.add), [r_bs], [r_bs])
                    P.op('vector', lambda e, L=L: e.tensor_scalar(out=msk[:, 0:L], in0=score[:, 0:L], scalar1=lo, scalar2=None,
                                                                 op0=ALU.is_ge), [r_bs, rsc], [rmk])
                    for kb0 in range(0, b + 1, 8):
                        nk = min(8, b + 1 - kb0)
                        pt, rpt = PT()
                        for i in range(nk):
                            P.op('tensor', lambda e, pt=pt, i=i, kb0=kb0: e.transpose(
                                out=pt[:, i * 128:(i + 1) * 128], in_=msk[:, (kb0 + i) * 128:(kb0 + i + 1) * 128], identity=ident[:]),
                                [rmk, rid], [rpt])
                        P.op('vector', lambda e, pt=pt, nk=nk, kb0=kb0: e.tensor_copy(
                            out=mskT[:, kb0:kb0 + nk, :].rearrange("p k t -> p (k t)"), in_=pt[:, 0:nk * 128]), [rpt], [rmkT])
                    blk[b] = (cqT, r_cqT, mskT, rmkT)

                def ATT(b):
                    cqT, r_cqT, mskT, rmkT = blk.pop(b)
                    x32, rx32 = x32_rot.next()
                    P.dma('sync', x32[:], xsrc(layer)[b * 128:(b + 1) * 128, :], [rX32[b]], [rx32])
                    qk_i = [0]

                    def QKB():
                        r = pbank[4 + qk_i[0]]
                        qk_i[0] ^= 1
                        return r

                    for hg in range(2):
                        for rcx in range(2):
                            pq, rpq = QKB()
                            for h4 in range(4):
                                ch = (hg * 4 + h4) * 2 + rcx
                                for kc in range(3):
                                    P.op('tensor', lambda e, pq=pq, h4=h4, ch=ch, kc=kc: e.matmul(
                                        pq[:, h4 * 128:(h4 + 1) * 128], lhsT=wql[:, kc, ch * 128:(ch + 1) * 128], rhs=cqT[:, kc, :],
                                        start=(kc == 0), stop=(kc == 2)), [rw, r_cqT], [rpq], inc=(kc == 2))
                            P.op('scalar', lambda e, pq=pq, rcx=rcx: e.activation(out=qlT4[:, rcx, :], in_=pq[:, :], func=AF.Copy,
                                                                                scale=float(KVR ** -0.5)), [rpq], [rql])
                        pendq = []

                        def emit_qk(kb):
                            pst, rpst = QKB()
                            for rcx in range(2):
                                P.op('tensor', lambda e, pst=pst, rcx=rcx, kb=kb: e.matmul(
                                    pst[:, :], lhsT=CKVT[:, rcx, kb * 128:(kb + 1) * 128], rhs=qlT4[:, rcx, :],
                                    start=(rcx == 0), stop=(rcx == 1)), [rK, rql], [rpst], inc=(rcx == 1))
                            PTt, rPT = PT_rot.next()
                            P.op('scalar', lambda e, pst=pst, PTt=PTt: e.activation(
                                out=PTt[:].rearrange("p k t -> p (k t)"), in_=pst[:, :], func=AF.Exp), [rpst], [rPT])
                            P.op('gpsimd', lambda e, PTt=PTt, kb=kb: e.tensor_tensor(
                                out=PTt[:], in0=PTt[:], in1=mskT[:, kb:kb + 1, :].to_broadcast([128, 4, 128]), op=ALU.mult), [rPT, rmkT], [rPT])
                            pendq.append((PTt, rPT))

                        emit_qk(0)
                        for kb in range(b + 1):
                            if kb + 1 <= b:
                                emit_qk(kb + 1)
                            PTt, rPT = pendq.pop(0)
                            for h4 in range(4):
                                pol, rpol = pbank[h4]
                                P.op('tensor', lambda e, pol=pol, PTt=PTt, h4=h4, kb=kb, b=b: e.matmul(
                                    pol[:, 0:KVR + 1], lhsT=PTt[:, h4, :], rhs=CKV[:, kb, :], start=(kb == 0), stop=(kb == b)),
                                    [rPT, rK], [rpol])
                        for h4 in range(4):
                            pol, rpol = pbank[h4]
                            h = hg * 4 + h4
                            P.op('scalar', lambda e, pol=pol, h=h: e.copy(out=olb[:, h, :], in_=pol[:, 0:KVR]), [rpol], [rolb])
                            P.op('scalar', lambda e, pol=pol, h=h: e.copy(out=dcol[:, h:h + 1], in_=pol[:, KVR:KVR + 1]), [rpol], [rolb])
                    P.op('vector', lambda e: e.reciprocal(out=rd[:], in_=dcol[:]), [rolb], [rrd])
                    pts = [PT(), PT()]
                    for h in range(8):
                        pt, rpt = pts[h // 4]
                        for rcx in range(2):
                            P.op('tensor', lambda e, pt=pt, h=h, rcx=rcx: e.transpose(
                                out=pt[:, ((h % 4) * 2 + rcx) * 128:((h % 4) * 2 + rcx + 1) * 128],
                                in_=olb[:, h, rcx * 128:(rcx + 1) * 128], identity=ident[:]), [rolb, rid], [rpt])
                    for hh in range(2):
                        pt, rpt = pts[hh]
                        P.op('scalar', lambda e, pt=pt, hh=hh: e.copy(
                            out=olT[:, hh * 4:(hh + 1) * 4, :, :].rearrange("p h c t -> p (h c t)"), in_=pt[:, :]), [rpt], [rolT])
                    puvs = [QKB(), QKB()]
                    for h in range(8):
                        puv, rpuv = puvs[h // 4]
                        for rcx in range(2):
                            P.op('tensor', lambda e, puv=puv, rcx=rcx, h=h: e.matmul(
                                puv[:, (h % 4) * 128:(h % 4 + 1) * 128], lhsT=olT[:, h, rcx, :], rhs=wuv[:, h * 2 + rcx, :],
                                start=(rcx == 0), stop=(rcx == 1)), [rolT, rw], [rpuv], inc=(rcx == 1))
                    for hh in range(2):
                        puv, rpuv = puvs[hh]
                        P.op('vector', lambda e, puv=puv, hh=hh: e.tensor_tensor(
                            out=oh[:, hh * 512:(hh + 1) * 512].rearrange("p (h d) -> p h d", h=4),
                            in0=puv[:, :].rearrange("p (h d) -> p h d", h=4),
                            in1=rd[:, hh * 4:(hh + 1) * 4].rearrange("p (h o) -> p h o", o=1).to_broadcast([128, 4, 128]),
                            op=ALU.mult), [rpuv, rrd], [ro])
                    pt, rpt = PT()
                    for c in range(8):
                        P.op('tensor', lambda e, c=c, pt=pt: e.transpose(out=pt[:, c * 128:(c + 1) * 128],
                                                                        in_=oh[:, c * 128:(c + 1) * 128], identity=ident[:]),
                             [ro, rid], [rpt])
                    P.op('vector', lambda e, pt=pt: e.tensor_copy(out=ohT[:].rearrange("p c t -> p (c t)"), in_=pt[:]), [rpt], [ro])
                    for half in range(2):
                        po, rpo = PB()
                        for c in range(8):
                            P.op('tensor', lambda e, c=c, half=half, po=po: e.matmul(
                                po[:, :], lhsT=ohT[:, c, :], rhs=wout[:, c, half * 512:(half + 1) * 512],
                                start=(c == 0), stop=(c == 7)), [ro, rw], [rpo], inc=(c == 7))
                        P.op('vector', lambda e, half=half, po=po, x32=x32: e.scalar_tensor_tensor(
                            out=z[:, half * 512:(half + 1) * 512], in0=x32[:, half * 512:(half + 1) * 512], scalar=ALPHA,
                            in1=po[:, :], op0=ALU.mult, op1=ALU.add), [rpo, rx32], [r_zo])
                    ln_tail(b, z, r_zo, gam, bet, [rg1, rb1], False, False)

                SB(0)
                for b in range(NB):
                    if b + 1 < NB:
                        SB(b + 1)
                    ATT(b)
                P.barrier()

        def mixer_phase(layer):
            if MIXERS[layer % 2] is None:
                identity_mixer_phase(layer)
            elif layer % 2 == 0:
                dsa_phase(layer)
            else:
                hgrn_phase(layer)

        for layer in layers:
            mixer_phase(layer)
            ffn_phase(layer, last=(layer == layers[-1]))
        for e in ('sync',):
            if rY.w is not None:
                P._wait(e, rY.w[0], rY.w[1])
        P.barrier()
        print("instructions:", P.ninst)
    return nc


MIXERS = [True, True]

WSHAPES = {
    'a_w_in': [2, D, AIN], 'a_g_q': [2, QR], 'a_g_kv': [2, KVR], 'a_w_q_lat': [2, QR, AH * KVR],
    'a_w_q_idx': [2, QR, IDXH * IDXD], 'a_g_kidx': [2, IDXD], 'a_b_kidx': [2, IDXD],
    'a_w_uv': [2, AH, KVR, 128], 'a_w_out': [2, D, D], 'b_w_in': [2, D, 4 * D], 'b_lb_logits': [4, D],
    'b_g_o': [2, D], 'b_w_out': [2, D, D], 'ln1_g': [4, D], 'ln1_b': [4, D], 'f_w_up': [4, D, 2 * DFF],
    'f_conv_w': [4, 3, 1, 2 * DFF], 'f_conv_b': [4, 2 * DFF], 'f_w_down': [4, DFF, D], 'ln2_g': [4, D], 'ln2_b': [4, D],
}


def host_consts():
    ident = np.eye(128, dtype=np.float32).astype(ml_dtypes.bfloat16)
    s_ = np.arange(128)[:, None]
    t_ = np.arange(128)[None, :]
    same = (s_ // 64) == (t_ // 64)
    hconst = np.zeros((128, 4, 128), np.float32)
    hconst[:, 0, :] = (same & (s_ <= t_))
    hconst[:, 1, :] = same
    hconst[:, 3, :] = (same & (s_ <= t_))
    hcind = np.zeros((128, 4), np.float32)
    hcind[:64, 0] = 1.0
    hcind[64:, 1] = 1.0
    p_ = np.arange(128)
    maskall = np.zeros((128, 8, 128), np.float32)
    for g in range(8):
        maskall[p_, g, 16 * g + (p_ % 16)] = 1.0
    rep = np.zeros((8, 128), np.float32)
    rep[p_ // 16, p_] = 1.0
    cm = np.where(np.arange(128)[None, :] <= np.arange(128)[:, None], 0.0, -1e30).astype(np.float32)
    pw = np.tile((0.5 ** np.arange(1, 33, dtype=np.float64)).astype(np.float32)[None, :], (128, 1))
    return {'ident': ident, 'hconst': hconst, 'hcind': hcind, 'maskall': maskall,
            'rep': rep.astype(ml_dtypes.bfloat16), 'cm': cm, 'pw': pw}


def kernel(**inputs):
    x = np.ascontiguousarray(np.asarray(inputs['x'], dtype=np.float32))
    B, S, _ = x.shape
    nc = build_program(S)
    consts = host_consts()
    wmaps = {k: np.ascontiguousarray(np.asarray(inputs[k], dtype=np.float32)) for k in WSHAPES}
    in_maps = []
    for c in range(8):
        m = dict(wmaps)
        m['x'] = x[(c // 2) % B]
        m.update(consts)
        in_maps.append(m)
    res = run_bass_kernel_spmd(nc, in_maps, core_ids=list(range(8)))
    out = np.stack([res.results[2 * b]['y'] for b in range(B)], axis=0)
    return out.astype(np.float32)
```

```python
from contextlib import ExitStack
import numpy as np
import ml_dtypes
import concourse.bass as bass
import concourse.mybir as mybir
from concourse.bass_utils import run_bass_kernel_spmd

F32 = mybir.dt.float32
BF16 = mybir.dt.bfloat16
ALU = mybir.AluOpType
AF = mybir.ActivationFunctionType
AX = mybir.AxisListType

D = 1024
DFF = 2816
NFC = DFF // 128
DEPTH = 4
ALPHA = (2 * DEPTH) ** 0.25
LN_EPS = 1e-5
RMS_EPS = 1e-6
QR, KVR, IDXD, IDXH, AH = 384, 256, 64, 8, 8
AIN = QR + KVR + IDXD + IDXH

ENGS = ['tensor', 'vector', 'scalar', 'gpsimd', 'sync']
DMAQ = ['sync', 'scalar', 'gpsimd']


class Res:
    __slots__ = ('w', 'r', 'name', 'excl')

    def __init__(self, name='', excl=False):
        self.w = None
        self.r = []
        self.name = name
        self.excl = excl


class Prog:
    def __init__(self, nc, stack, n_dma_slots=4):
        self.nc = nc
        self.sem = {e: stack.enter_context(nc.semaphore(f"s_{e}")) for e in ENGS}
        self.cnt = {e: 0 for e in ENGS}
        self.nslots = n_dma_slots
        self.dsem, self.dcnt, self.dnext = {}, {}, {}
        for qn in DMAQ:
            for s in range(n_dma_slots):
                self.dsem[(qn, s)] = stack.enter_context(nc.semaphore(f"d_{qn}_{s}"))
                self.dcnt[(qn, s)] = 0
            self.dnext[qn] = 0
        self.seen = {e: {} for e in ENGS}
        self.ninst = 0

    def _wait(self, eng, prod, val):
        if self.seen[eng].get(prod, 0) >= val:
            return
        sem = self.sem[prod] if isinstance(prod, str) else self.dsem[prod]
        getattr(self.nc, eng).wait_ge(sem, val)
        self.seen[eng][prod] = val

    def _deps(self, eng, reads, writes):
        deps = []
        for b in reads:
            if b.w is not None:
                deps.append(b.w)
            if b.excl:
                deps.extend(b.r)
        for b in writes:
            if b.w is not None:
                if not (b.excl and eng == 'tensor' and b.w[0] == 'tensor'):
                    deps.append(b.w)
            deps.extend(b.r)
        for (p, v) in deps:
            self._wait(eng, p, v)

    def _commit(self, me, reads, writes):
        for b in reads:
            b.r.append(me)
        for b in writes:
            b.w = me
            b.r = []

    def op(self, eng, fn, reads=(), writes=(), inc=True):
        self._deps(eng, reads, writes)
        if inc:
            self.cnt[eng] += 1
            fn(getattr(self.nc, eng)).then_inc(self.sem[eng], 1)
            self._commit((eng, self.cnt[eng]), reads, writes)
        else:
            fn(getattr(self.nc, eng))
            self._commit((eng, self.cnt[eng] + 1), reads, writes)
        self.ninst += 1

    def dma(self, qn, out, in_, reads=(), writes=(), **kw):
        s = self.dnext[qn]
        self.dnext[qn] = (s + 1) % self.nslots
        key = (qn, s)
        prev = 16 * self.dcnt[key]
        if prev:
            self._wait(qn, key, prev)
        self._deps(qn, reads, writes)
        self.dcnt[key] += 1
        getattr(self.nc, qn).dma_start(out=out, in_=in_, **kw).then_inc(self.dsem[key], 16)
        self._commit((key, 16 * self.dcnt[key]), reads, writes)
        self.ninst += 1

    def barrier(self):
        for e in ENGS:
            for p in ENGS:
                if p != e and self.cnt[p]:
                    self._wait(e, p, self.cnt[p])
            for k, c in self.dcnt.items():
                if c:
                    self._wait(e, k, 16 * c)


class Rot:
    def __init__(self, tiles):
        self.t = [(t, Res()) for t in tiles]
        self.i = 0

    def next(self):
        r = self.t[self.i]
        self.i = (self.i + 1) % len(self.t)
        return r


def build_program(S, layers=(0, 1, 2, 3), first=True):
    layers = tuple(layers)
    NB = S // 128
    TOPK = min(256, S // 4)
    nc = bass.Bass("TRN2", target_bir_lowering=False)

    def din(name, shape, dt=F32):
        return nc.dram_tensor(name, list(shape), dt, kind="ExternalInput").ap()

    x_in = din("x", [S, D])
    Wn = {}
    for name, shape in WSHAPES.items():
        Wn[name] = din(name, shape)
    ident_in = din("ident", [128, 128], BF16)
    hconst_in = din("hconst", [128, 4, 128])
    hcind_in = din("hcind", [128, 4])
    maskall_in = din("maskall", [128, 8, 128])
    rep_in = din("rep", [8, 128], BF16)
    cm_in = din("cm", [128, 128])
    pw_in = din("pw", [128, 32])
    y_out = nc.dram_tensor("y", [S, D], F32, kind="ExternalOutput").ap()
    X32 = nc.dram_tensor("X32", [S, D], F32).ap()
    XTH = nc.dram_tensor("XTH", [NB + 1, 128, 8, 130], BF16).ap()
    rX32 = [Res() for _ in range(NB)]
    rXTH = [Res() for _ in range(NB + 1)]
    rXTHh = [Res() for _ in range(NB + 1)]
    rY = Res()
    WUb = nc.dram_tensor("WUb", [4, NFC, 128, 8, 256], BF16).ap()
    rWb = [Res() for _ in range(4)]

    with ExitStack() as top:
        P = Prog(nc, top)
        uid = [0]

        def sb(st, name, shape, dt):
            uid[0] += 1
            return st.enter_context(nc.sbuf_tensor(f"t{uid[0]}_{name}", list(shape), dt))
        pbank = [(top.enter_context(nc.psum_tensor(f"pb{i}", [128, 512], F32)), Res(excl=True)) for i in range(6)]
        ptb = [(top.enter_context(nc.psum_tensor(f"pt{i}", [128, 1024], BF16)), Res(excl=True)) for i in range(2)]
        pbi = [0]
        pti = [0]

        def PB():
            r = pbank[pbi[0]]
            pbi[0] = (pbi[0] + 1) % 4
            return r

        def PT():
            r = ptb[pti[0]]
            pti[0] = (pti[0] + 1) % len(ptb)
            return r

        epsT = sb(top, "epsT", [128, 2], F32)
        reps = Res()
        P.op('vector', lambda e: e.memset(epsT[:, 0:1], LN_EPS), [], [reps])
        P.op('vector', lambda e: e.memset(epsT[:, 1:2], RMS_EPS), [], [reps])

        def rsqrt(out, in_, scale, which, reads, writes):
            P.op('scalar', lambda e: e.activation(out=out, in_=in_, func=AF.Sqrt, scale=scale, bias=epsT[:, which:which + 1]),
                 list(reads) + [reps], writes)
            P.op('vector', lambda e: e.reciprocal(out=out, in_=out), writes, writes)

        ident = sb(top, "ident_sb", [128, 128], BF16)
        rid = Res()
        P.dma('sync', ident[:], ident_in[:, :], writes=[rid])

        def convert_wu(l, j):
            for part in range(2):
                col = part * DFF + j * 128
                P.dma('gpsimd', WUb[l, j, :, :, part * 128:(part + 1) * 128],
                      Wn['f_w_up'][l, :, col:col + 128].rearrange("(c p) n -> p c n", p=128), [], [rWb[l]])

        for j in range(NFC):
            convert_wu(layers[0], j)

        lnst = ExitStack()
        top.enter_context(lnst)
        yb_rot = Rot([sb(top, f"yb{i}", [128, D], BF16) for i in range(1)])
        xtb_rot = Rot([sb(top, f"xtb{i}", [128, 8, 128], BF16) for i in range(2)])
        yn_rot = Rot([sb(top, f"yn{i}", [128, D], F32) for i in range(1)])
        sq_rot = Rot([sb(top, f"sq{i}", [128, D], F32) for i in range(1)])
        st_rot = Rot([sb(top, f"st{i}", [128, 8], F32) for i in range(2)])
        eng_flip = [0]

        def transpose_store(b, y32, ry, halo):
            yb, ryb = yb_rot.next()
            P.op('scalar', lambda e: e.copy(out=yb[:], in_=y32[:]), [ry], [ryb])
            pt, rpt = PT()
            for c in range(8):
                P.op('tensor', lambda e, c=c: e.transpose(out=pt[:, c * 128:(c + 1) * 128],
                                                           in_=yb[:, c * 128:(c + 1) * 128], identity=ident[:]),
                     [ryb, rid], [rpt])
            xtb, rxtb = xtb_rot.next()
            P.op('vector', lambda e: e.tensor_copy(out=xtb[:].rearrange("p c t -> p (c t)"), in_=pt[:]), [rpt], [rxtb])
            P.dma('gpsimd', XTH[b, :, :, 2:130], xtb[:], [rxtb], [rXTH[b]])
            if halo:
                P.dma('gpsimd', XTH[b + 1, :, :, 0:2], xtb[:, :, 126:128], [rxtb], [rXTHh[b + 1]])

        def ln_tail(b, z, rz, gam, bet, rgb, to_out, halo):
            st_, rst = st_rot.next()
            sq, rsq = sq_rot.next()
            yn, ryn = yn_rot.next()
            s1, s2, mean, msq, var, rstd, nmr = [st_[:, i:i + 1] for i in range(7)]
            P.op('vector', lambda e: e.tensor_scalar(out=yn[:], in0=z[:], scalar1=1.0, scalar2=None, op0=ALU.mult,
                                                     op1=ALU.add, accum_out=s1), [rz], [ryn, rst])
            P.op('scalar', lambda e: e.activation(out=sq[:], in_=z[:], func=AF.Square), [rz], [rsq])
            P.op('vector', lambda e: e.tensor_scalar(out=sq[:], in0=sq[:], scalar1=1.0, scalar2=None, op0=ALU.mult,
                                                     op1=ALU.add, accum_out=s2), [rsq], [rsq, rst])
            P.op('vector', lambda e: e.tensor_scalar(out=mean, in0=s1, scalar1=1.0 / D, scalar2=None, op0=ALU.mult), [rst], [rst])
            P.op('vector', lambda e: e.tensor_tensor(out=msq, in0=mean, in1=mean, op=ALU.mult), [rst], [rst])
            P.op('vector', lambda e: e.scalar_tensor_tensor(out=var, in0=s2, scalar=1.0 / D, in1=msq, op0=ALU.mult,
                                                            op1=ALU.subtract), [rst], [rst])
            rsqrt(rstd, var, 1.0, 0, [rst], [rst])
            P.op('vector', lambda e: e.scalar_tensor_tensor(out=nmr, in0=mean, scalar=-1.0, in1=rstd, op0=ALU.mult,
                                                            op1=ALU.mult), [rst], [rst])
            P.op('scalar', lambda e: e.activation(out=yn[:], in_=z[:], func=AF.Identity, scale=rstd, bias=nmr),
                 [rz, rst], [ryn])
            P.op('gpsimd', lambda e: e.tensor_tensor(out=yn[:], in0=yn[:], in1=gam[:], op=ALU.mult), [ryn] + list(rgb), [ryn])
            P.op('vector', lambda e: e.tensor_tensor(out=yn[:], in0=yn[:], in1=bet[:], op=ALU.add), [ryn] + list(rgb), [ryn])
            if to_out:
                P.dma('gpsimd', y_out[b * 128:(b + 1) * 128, :], yn[:], [ryn], [rY])
            else:
                P.dma('gpsimd', X32[b * 128:(b + 1) * 128, :], yn[:], [ryn], [rX32[b]])
                transpose_store(b, yn, ryn, halo)

        if first:
            with ExitStack() as ph:
                zt = sb(ph, "zt", [128, 8, 2], BF16)
                rzt = Res()
                P.op('vector', lambda e: e.memset(zt[:], 0.0), [], [rzt])
                P.dma('sync', XTH[0, :, :, 0:2], zt[:], [rzt], [rXTHh[0]])
                xl_rot = Rot([sb(ph, f"xl{i}", [128, D], F32) for i in range(2)])
                for b in range(NB):
                    xl, rxl = xl_rot.next()
                    P.dma('scalar', xl[:], x_in[b * 128:(b + 1) * 128, :], [], [rxl])
                    transpose_store(b, xl, rxl, False)
                P.barrier()

        def xsrc(layer):
            return x_in if (first and layer == layers[0]) else X32

        def load_bc(st, name, src_row):
            t = sb(st, name, [128, src_row.shape[-1]], F32)
            r = Res()
            P.dma('sync', t[:], src_row.to_broadcast([128, src_row.shape[-1]]), [], [r])
            return t, r

        TT = 512 if NB % 4 == 0 else 128 * NB
        NBT = TT // 128

        def ffn_phase(layer, last):
            with ExitStack() as ph:
                wd = sb(ph, "wd", [128, NFC, D], BF16)
                rw = Res()
                for j in range(0, NFC, 2):
                    P.dma('gpsimd', wd[:, j:j + 2, :],
                          Wn['f_w_down'][layer, j * 128:(j + 2) * 128, :].rearrange("(c p) n -> p c n", p=128), [], [rw])
                cw = sb(ph, "cw", [128, 3, 2 * NFC], F32)
                cb = sb(ph, "cb", [128, 2 * NFC], F32)
                for k in range(3):
                    P.dma('scalar', cw[:, k, :], Wn['f_conv_w'][layer, k, 0, :].rearrange("(j p) -> p j", p=128), [], [rw],
                          allow_slow_non_contiguous=True)
                P.dma('scalar', cb[:], Wn['f_conv_b'][layer, :].rearrange("(j p) -> p j", p=128), [], [rw],
                      allow_slow_non_contiguous=True)
                gam = sb(ph, "gam2", [128, D], F32)
                bet = sb(ph, "bet2", [128, D], F32)
                P.dma('sync', gam[:], Wn['ln2_g'][layer:layer + 1, :].to_broadcast([128, D]), [], [rw])
                P.dma('sync', bet[:], Wn['ln2_b'][layer:layer + 1, :].to_broadcast([128, D]), [], [rw])
                carry = sb(ph, "carry", [128, 2 * NFC, 2], F32)
                rcar = [Res() for _ in range(2 * NFC)]
                P.op('gpsimd', lambda e: e.memset(carry[:], 0.0), [], rcar)
                wu_rot = Rot([sb(ph, f"wu{i}", [128, 8, 256], BF16) for i in range(3)])
                xw_rot = Rot([sb(ph, f"xw{i}", [128, 8, TT], BF16) for i in range(2)])
                g_rot = Rot([sb(ph, f"g{i}", [128, NFC, TT], BF16) for i in range(2)])
                hs_rot = Rot([sb(ph, f"hs{i}", [128, TT + 2], F32) for i in range(4)])
                acc_rot = Rot([sb(ph, f"acc{i}", [128, TT], F32) for i in range(6)])
                sa_rot = Rot([sb(ph, f"sa{i}", [128, TT], F32) for i in range(2)])
                x32_rot = Rot([sb(ph, f"x32{i}", [128, D], F32) for i in range(4 if NBT == 4 else NBT)])
                z_rot = Rot([sb(ph, f"z{i}", [128, D], F32) for i in range(2)])
                for t in range(NB // NBT):
                    b0 = t * NBT
                    xw, rxw = xw_rot.next()
                    for i in range(NBT):
                        P.dma('sync', xw[:, :, i * 128:(i + 1) * 128], XTH[b0 + i, :, :, 2:130], [rXTH[b0 + i]], [rxw])
                    g, rg = g_rot.next()
                    pend_gate = None
                    x32s = []
                    for i in range(NBT):
                        x32, rx32 = x32_rot.next()
                        P.dma('sync', x32[:], X32[(b0 + i) * 128:(b0 + i + 1) * 128, :], [rX32[b0 + i]], [rx32])
                        x32s.append((x32, rx32))

                    def emit_gate(accs, j, g, rg):
                        sa, rsa = sa_rot.next()
                        P.op('scalar', lambda e, sa=sa, a=accs[0][0]: e.activation(out=sa[:], in_=a[:], func=AF.Silu),
                             [accs[0][1]], [rsa])
                        P.op('vector', lambda e, sa=sa, u=accs[1][0], g=g, j=j: e.tensor_tensor(
                            out=g[:, j, :], in0=sa[:], in1=u[:], op=ALU.mult), [rsa, accs[1][1]], [rg])

                    for j in range(NFC):
                        if t == 0 and layer != layers[-1]:
                            convert_wu(layers[layers.index(layer) + 1], j)
                        wu, rwu = wu_rot.next()
                        P.dma('sync', wu[:], WUb[layer, j], [rWb[layer]], [rwu])
                        accs = []
                        for part in range(2):
                            jj = part * NFC + j
                            ps, rps = PB()
                            for c in range(8):
                                P.op('tensor', lambda e, c=c, part=part, ps=ps, wu=wu, xw=xw: e.matmul(
                                    ps[:, 0:TT], lhsT=wu[:, c, part * 128:(part + 1) * 128], rhs=xw[:, c, :],
                                    start=(c == 0), stop=(c == 7)), [rwu, rxw], [rps], inc=(c == 7))
                            hs, rhs_ = hs_rot.next()
                            P.op('gpsimd', lambda e, hs=hs, jj=jj: e.tensor_copy(out=hs[:, 0:2], in_=carry[:, jj, :]), [rcar[jj]], [rhs_])
                            P.op('scalar', lambda e, hs=hs, ps=ps: e.copy(out=hs[:, 2:TT + 2], in_=ps[:, 0:TT]), [rps], [rhs_])
                            P.op('gpsimd', lambda e, hs=hs, jj=jj: e.tensor_copy(out=carry[:, jj, :], in_=hs[:, TT:TT + 2]), [rhs_], [rcar[jj]])
                            acc, racc = acc_rot.next()
                            eng = 'vector' if part == 0 else 'gpsimd'
                            P.op(eng, lambda e, acc=acc, hs=hs, jj=jj: e.tensor_scalar(
                                out=acc[:], in0=hs[:, 0:TT], scalar1=cw[:, 0, jj:jj + 1], scalar2=cb[:, jj:jj + 1],
                                op0=ALU.mult, op1=ALU.add), [rhs_, rw], [racc])
                            for k in (1, 2):
                                P.op('vector', lambda e, acc=acc, hs=hs, jj=jj, k=k: e.scalar_tensor_tensor(
                                    out=acc[:], in0=hs[:, k:k + TT], scalar=cw[:, k, jj:jj + 1], in1=acc[:],
                                    op0=ALU.mult, op1=ALU.add), [rhs_, rw, racc], [racc])
                            accs.append((acc, racc))
                        if pend_gate is not None:
                            emit_gate(*pend_gate)
                        pend_gate = (accs, j, g, rg)
                    emit_gate(*pend_gate)
                    pend_gate = None
                    for i in range(NBT):
                        b = b0 + i
                        x32, rx32 = x32s[i]
                        z, rz = z_rot.next()
                        for half in range(2):
                            po, rpo = PB()
                            for j in range(NFC):
                                P.op('tensor', lambda e, j=j, half=half, po=po, g=g, i=i: e.matmul(
                                    po[:, :], lhsT=g[:, j, i * 128:(i + 1) * 128], rhs=wd[:, j, half * 512:(half + 1) * 512],
                                    start=(j == 0), stop=(j == NFC - 1)), [rg, rw], [rpo], inc=(j == NFC - 1))
                            P.op('vector', lambda e, half=half, po=po, z=z, x32=x32: e.scalar_tensor_tensor(
                                out=z[:, half * 512:(half + 1) * 512], in0=x32[:, half * 512:(half + 1) * 512], scalar=ALPHA,
                                in1=po[:, :], op0=ALU.mult, op1=ALU.add), [rpo, rx32], [rz])
                        ln_tail(b, z, rz, gam, bet, [rw], last, False)
                P.barrier()

        ph_hs = [sb(top, "hsb", [128, 130], F32)]
        rhs = Res()

        def identity_mixer_phase(layer):
            with ExitStack() as ph:
                gam, rg1 = load_bc(ph, "gam1", Wn['ln1_g'][layer:layer + 1, :])
                bet, rb1 = load_bc(ph, "bet1", Wn['ln1_b'][layer:layer + 1, :])
                rgb = Res()
                P.op('vector', lambda e: e.memset(ph_hs[0][:, 0:1], 0.0), [rg1, rb1, rhs], [rgb, rhs])
                x32_rot = Rot([sb(ph, f"mx32{i}", [128, D], F32) for i in range(2)])
                z_rot = Rot([sb(ph, f"mz{i}", [128, D], F32) for i in range(2)])
                for b in range(NB):
                    x32, rx32 = x32_rot.next()
                    P.dma('sync', x32[:], xsrc(layer)[b * 128:(b + 1) * 128, :], [rX32[b]], [rx32])
                    z, rz = z_rot.next()
                    P.op('vector', lambda e, z=z, x32=x32: e.tensor_scalar(out=z[:], in0=x32[:], scalar1=ALPHA, scalar2=None,
                                                                        op0=ALU.mult), [rx32], [rz])
                    ln_tail(b, z, rz, gam, bet, [rgb], False, True)
                P.barrier()


        def hgrn_phase(layer):
            j = layer // 2
            with ExitStack() as ph:
                win = sb(ph, "hwin", [128, 8, 4 * D], BF16)
                wout = sb(ph, "hwout", [128, 8, D], BF16)
                rw = Res()
                for c in range(8):
                    P.dma('gpsimd', win[:, c, :], Wn['b_w_in'][j, c * 128:(c + 1) * 128, :], [], [rw])
                P.dma('gpsimd', wout[:], Wn['b_w_out'][j].rearrange("(c p) n -> p c n", p=128), [], [rw])
                gam, rg1 = load_bc(ph, "hgam1", Wn['ln1_g'][layer:layer + 1, :])
                bet, rb1 = load_bc(ph, "hbet1", Wn['ln1_b'][layer:layer + 1, :])
                go, rgo = load_bc(ph, "hgo", Wn['b_g_o'][j:j + 1, :])
                cst = sb(ph, "hcst", [128, 3, 128], F32)
                cind = sb(ph, "hcind", [128, 4], F32)
                cmask = sb(ph, "hcmask", [128, 128], F32)
                cmask4 = sb(ph, "hcmask4", [128, 4, 128], F32)
                rc = Res()
                P.dma('sync', cst[:], hconst_in[:, 0:3, :], [], [rc])
                P.dma('sync', cmask[:], hconst_in[:, 3, :], [], [rc])
                for i4 in range(4):
                    P.dma('sync', cmask4[:, i4, :], hconst_in[:, 3, :], [], [rc])
                P.dma('sync', cind[:], hcind_in[:, :], [], [rc])
                lb = sb(ph, "hlb", [128, D], F32)
                oml = sb(ph, "homl", [128, D], F32)
                rl = Res()
                tmp = ExitStack()
                lg = [load_bc(tmp, f"hlg{l}", Wn['b_lb_logits'][l:l + 1, :]) for l in range(4)]
                mx = sb(tmp, "hmx", [128, D], F32)
                den = sb(tmp, "hden", [128, D], F32)
                P.op('vector', lambda e: e.tensor_tensor(out=mx[:], in0=lg[0][0][:], in1=lg[1][0][:], op=ALU.max),
                     [lg[0][1], lg[1][1]], [rl])
                for l in (2, 3):
                    P.op('vector', lambda e, l=l: e.tensor_tensor(out=mx[:], in0=mx[:], in1=lg[l][0][:], op=ALU.max),
                         [lg[l][1], rl], [rl])
                for l in range(4):
                    P.op('vector', lambda e, l=l: e.tensor_tensor(out=lg[l][0][:], in0=lg[l][0][:], in1=mx[:], op=ALU.subtract),
                         [rl, lg[l][1]], [lg[l][1]])
                    P.op('scalar', lambda e, l=l: e.activation(out=lg[l][0][:], in_=lg[l][0][:], func=AF.Exp),
                         [lg[l][1]], [lg[l][1]])
                P.op('vector', lambda e: e.tensor_tensor(out=den[:], in0=lg[0][0][:], in1=lg[1][0][:], op=ALU.add),
                     [lg[0][1], lg[1][1]], [rl])
                for l in (2, 3):
                    P.op('vector', lambda e, l=l: e.tensor_tensor(out=den[:], in0=den[:], in1=lg[l][0][:], op=ALU.add),
                         [lg[l][1], rl], [rl])
                P.op('vector', lambda e: e.tensor_copy(out=lb[:], in_=lg[1][0][:]), [lg[1][1], rl], [rl])
                for l in range(2, layer + 1):
                    P.op('vector', lambda e, l=l: e.tensor_tensor(out=lb[:], in0=lb[:], in1=lg[l][0][:], op=ALU.add),
                         [lg[l][1], rl], [rl])
                P.op('vector', lambda e: e.reciprocal(out=den[:], in_=den[:]), [rl], [rl])
                P.op('vector', lambda e: e.tensor_tensor(out=lb[:], in0=lb[:], in1=den[:], op=ALU.mult), [rl], [rl])
                P.op('vector', lambda e: e.tensor_scalar(out=oml[:], in0=lb[:], scalar1=-1.0, scalar2=1.0, op0=ALU.mult,
                                                         op1=ALU.add), [rl], [rl])
                P.barrier()
                tmp.close()
                Sf = sb(ph, "hSf", [128, 8, 128], F32)
                SbA = sb(ph, "hSbA", [128, 8, 128], BF16)
                SbB = sb(ph, "hSbB", [128, 8, 128], BF16)
                q0T = sb(ph, "hq0T", [128, 8, 128], BF16)
                q1T = sb(ph, "hq1T", [128, 8, 128], BF16)
                kT = sb(ph, "hkT", [128, 8, 128], BF16)
                qT = sb(ph, "hqT", [128, 8, 128], BF16)
                rS = Res()
                rq = Res()
                P.op('vector', lambda e: e.memset(Sf[:], 0.0), [], [rS])
                P.op('gpsimd', lambda e: e.memset(q0T[:], 0.0), [], [rq])
                P.op('gpsimd', lambda e: e.memset(q1T[:], 0.0), [], [rq])
                xh_rot = Rot([sb(ph, f"hxh{i}", [128, 8, 130], BF16) for i in range(2)])
                x32_rot = Rot([sb(ph, f"hx32{i}", [128, D], F32) for i in range(2)])
                F = lambda n: sb(ph, n, [128, D], F32)
                sig, lf, key, bb, bl, qs, og = F("hsig"), F("hlf"), F("hkey"), F("hbb"), F("hbl"), F("hqs"), F("hog")
                zt_ = qs
                B16 = lambda n: sb(ph, n, [128, D], BF16)
                qtl, ktl, k0, k1, vv, onb = B16("hqtl"), B16("hktl"), B16("hk0"), B16("hk1"), B16("hvv"), B16("honb")
                onT = sb(ph, "honT", [128, 8, 128], BF16)
                scT = sb(ph, "hscT", [128, 8, 128], BF16)
                rscT = Res()
                rog = Res()
                rSb = {id(SbA): Res(), id(SbB): Res()}
                P.op('vector', lambda e: e.memset(SbA[:], 0.0), [], [rSb[id(SbA)]])
                dec = sb(ph, "hdec", [128, 8, 2], F32)
                ss = sb(ph, "hss", [128, 8], F32)
                rt = Res()

                def proj(part, xh, rxh):
                    outs = []
                    for half in range(2):
                        ps, rps = PB()
                        col = part * D + half * 512
                        for c in range(8):
                            P.op('tensor', lambda e, c=c, col=col, ps=ps: e.matmul(
                                ps[:, :], lhsT=xh[:, c, 2:130], rhs=win[:, c, col:col + 512],
                                start=(c == 0), stop=(c == 7)), [rw, rxh], [rps], inc=(c == 7))
                        outs.append((ps, rps))
                    return outs

                for b in range(NB):
                    xh, rxh = xh_rot.next()
                    P.dma('sync', xh[:, :, 2:130], XTH[b, :, :, 2:130], [rXTH[b]], [rxh])
                    x32, rx32 = x32_rot.next()
                    P.dma('sync', x32[:], xsrc(layer)[b * 128:(b + 1) * 128, :], [rX32[b]], [rx32])
                    for half, (ps, rps) in enumerate(proj(1, xh, rxh)):
                        hs = slice(half * 512, (half + 1) * 512)
                        P.op('scalar', lambda e, ps=ps, hs=hs: e.activation(out=sig[:, hs], in_=ps[:, :], func=AF.Sigmoid),
                             [rps], [rt])
                    P.op('vector', lambda e: e.tensor_tensor(out=lf[:], in0=sig[:], in1=oml[:], op=ALU.mult), [rt, rl], [rt])
                    P.op('vector', lambda e: e.tensor_tensor(out=key[:], in0=oml[:], in1=lf[:], op=ALU.subtract), [rt, rl], [rt])
                    P.op('vector', lambda e: e.tensor_tensor(out=lf[:], in0=lf[:], in1=lb[:], op=ALU.add), [rt, rl], [rt])
                    P.op('scalar', lambda e: e.activation(out=lf[:], in_=lf[:], func=AF.Ln), [rt], [rt])
                    for half in range(2):
                        hs = slice(half * 512, (half + 1) * 512)
                        ps, rps = PB()
                        P.op('tensor', lambda e, ps=ps, hs=hs: e.matmul(ps[:, :], lhsT=cst[:, 0, :], rhs=lf[:, hs],
                                                                       start=True, stop=True), [rt, rc], [rps])
                        P.op('vector', lambda e, ps=ps, hs=hs: e.tensor_copy(out=bb[:, hs], in_=ps[:, :]), [rps], [rt])
                        ps, rps = PB()
                        P.op('tensor', lambda e, ps=ps, hs=hs: e.matmul(ps[:, :], lhsT=cst[:, 1, :], rhs=lf[:, hs],
                                                                       start=True, stop=True), [rt, rc], [rps])
                        P.op('vector', lambda e, ps=ps, hs=hs: e.tensor_copy(out=bl[:, hs], in_=ps[:, :]), [rps], [rt])
                    ps, rps = PB()
                    for h in range(8):
                        P.op('tensor', lambda e, h=h, ps=ps: e.matmul(ps[:, h * 2:h * 2 + 2], lhsT=lf[:, h * 128:(h + 1) * 128],
                                                                     rhs=cind[:, 0:2], start=True, stop=True), [rt, rc], [rps])
                    P.op('scalar', lambda e, ps=ps: e.activation(out=dec[:].rearrange("p h c -> p (h c)"), in_=ps[:, 0:16],
                                                                 func=AF.Exp), [rps], [rt])
                    P.op('vector', lambda e: e.tensor_tensor(out=bl[:], in0=bl[:], in1=bb[:], op=ALU.subtract), [rt], [rt])
                    P.op('scalar', lambda e: e.activation(out=bl[:], in_=bl[:], func=AF.Exp), [rt], [rt])
                    P.op('vector', lambda e: e.tensor_tensor(out=bl[:], in0=bl[:], in1=key[:], op=ALU.mult), [rt], [rt])
                    P.op('vector', lambda e: e.tensor_scalar(out=k0[:], in0=bl[:], scalar1=cind[:, 0:1], scalar2=None,
                                                             op0=ALU.mult), [rt, rc], [rt])
                    P.op('vector', lambda e: e.tensor_scalar(out=k1[:], in0=bl[:], scalar1=cind[:, 1:2], scalar2=None,
                                                             op0=ALU.mult), [rt, rc], [rt])
                    P.op('scalar', lambda e: e.activation(out=sig[:], in_=bb[:], func=AF.Exp, scale=-1.0), [rt], [rt])
                    P.op('vector', lambda e: e.tensor_tensor(out=ktl[:], in0=sig[:], in1=key[:], op=ALU.mult), [rt], [rt])
                    P.op('scalar', lambda e: e.activation(out=bb[:], in_=bb[:], func=AF.Exp), [rt], [rt])
                    for half, (ps, rps) in enumerate(proj(0, xh, rxh)):
                        hs = slice(half * 512, (half + 1) * 512)
                        P.op('scalar', lambda e, ps=ps, hs=hs: e.activation(out=qs[:, hs], in_=ps[:, :], func=AF.Silu),
                             [rps], [rt])
                    P.op('vector', lambda e: e.tensor_tensor(out=qtl[:], in0=qs[:], in1=bb[:], op=ALU.mult), [rt], [rt])
                    for half, (ps, rps) in enumerate(proj(2, xh, rxh)):
                        hs = slice(half * 512, (half + 1) * 512)
                        P.op('vector', lambda e, ps=ps, hs=hs: e.tensor_copy(out=vv[:, hs], in_=ps[:, :]), [rps], [rt])
                    for half, (ps, rps) in enumerate(proj(3, xh, rxh)):
                        hs = slice(half * 512, (half + 1) * 512)
                        P.op('scalar', lambda e, ps=ps, hs=hs: e.activation(out=sig[:, hs], in_=ps[:, :], func=AF.Sigmoid),
                             [rps], [rt])
                    for (src, dsts) in ((qtl, 'q'), (ktl, 'k')):
                        pt, rpt = PT()
                        for h in range(8):
                            P.op('tensor', lambda e, h=h, pt=pt, src=src: e.transpose(
                                out=pt[:, h * 128:(h + 1) * 128], in_=src[:, h * 128:(h + 1) * 128], identity=ident[:]),
                                [rt, rid], [rpt])
                        ptv = pt[:].rearrange("p (h t) -> p h t", h=8)
                        if dsts == 'q':
                            P.op('vector', lambda e, ptv=ptv: e.tensor_copy(out=qT[:], in_=ptv), [rpt], [rq])
                            P.op('vector', lambda e, ptv=ptv: e.tensor_copy(out=q0T[:, :, 0:64], in_=ptv[:, :, 0:64]), [rpt], [rq])
                            P.op('vector', lambda e, ptv=ptv: e.tensor_copy(out=q1T[:, :, 64:128], in_=ptv[:, :, 64:128]), [rpt], [rq])
                        else:
                            P.op('vector', lambda e, ptv=ptv: e.tensor_copy(out=kT[:], in_=ptv), [rpt], [rq])
                    pu = [pbank[0], pbank[1]]
                    psc = [pbank[2], pbank[3]]
                    po = [pbank[4], pbank[5]]
                    Sfv = Sf[:].rearrange("p h e -> p (h e)")

                    def state_update(kk, ci, Sb):
                        for h in range(8):
                            hsl = slice(h * 128, (h + 1) * 128)
                            pb_, rpb = pu[h // 4]
                            P.op('tensor', lambda e, pb_=pb_, h=h, hsl=hsl: e.matmul(
                                pb_[:, (h % 4) * 128:(h % 4 + 1) * 128], lhsT=kk[:, hsl], rhs=vv[:, hsl], start=True, stop=True),
                                [rt], [rpb])
                        P.op('vector', lambda e: e.tensor_tensor(out=Sf[:], in0=Sf[:], in1=dec[:, :, ci:ci + 1].to_broadcast([128, 8, 128]),
                                                                 op=ALU.mult), [rt, rS], [rS])
                        for hh in range(2):
                            pb_, rpb = pu[hh]
                            P.op('vector', lambda e, pb_=pb_, hh=hh: e.tensor_tensor(
                                out=Sfv[:, hh * 512:(hh + 1) * 512], in0=Sfv[:, hh * 512:(hh + 1) * 512], in1=pb_[:, :], op=ALU.add),
                                [rpb, rS], [rS])
                        P.op('scalar', lambda e: e.copy(out=Sb[:], in_=Sf[:]), [rS], [rSb[id(Sb)]])

                    state_update(k0, 0, SbB)
                    for h in range(8):
                        pb_, rpb = psc[h // 4]
                        P.op('tensor', lambda e, pb_=pb_, h=h: e.matmul(pb_[:, (h % 4) * 128:(h % 4 + 1) * 128], lhsT=kT[:, h, :],
                                                                       rhs=qT[:, h, :], start=True, stop=True), [rq], [rpb])
                    for hh in range(2):
                        pb_, rpb = psc[hh]
                        P.op('vector', lambda e, pb_=pb_, hh=hh: e.tensor_tensor(
                            out=scT[:, hh * 4:(hh + 1) * 4, :], in0=pb_[:, :].rearrange("p (h t) -> p h t", h=4), in1=cmask4[:],
                            op=ALU.mult), [rpb, rc], [rscT])
                    for h in range(8):
                        hsl = slice(h * 128, (h + 1) * 128)
                        pb_, rpb = po[h // 4]
                        osl = slice((h % 4) * 128, (h % 4 + 1) * 128)
                        P.op('tensor', lambda e, pb_=pb_, h=h, hsl=hsl, osl=osl: e.matmul(
                            pb_[:, osl], lhsT=scT[:, h, :], rhs=vv[:, hsl], start=True, stop=False), [rscT, rt], [rpb])
                        P.op('tensor', lambda e, pb_=pb_, h=h, osl=osl: e.matmul(
                            pb_[:, osl], lhsT=q0T[:, h, :], rhs=SbA[:, h, :], start=False, stop=False), [rq, rSb[id(SbA)]], [rpb])
                        P.op('tensor', lambda e, pb_=pb_, h=h, osl=osl: e.matmul(
                            pb_[:, osl], lhsT=q1T[:, h, :], rhs=SbB[:, h, :], start=False, stop=True), [rq, rSb[id(SbB)]], [rpb])
                    for hh in range(2):
                        pb_, rpb = po[hh]
                        P.op('vector', lambda e, pb_=pb_, hh=hh: e.tensor_tensor(
                            out=og[:, hh * 512:(hh + 1) * 512], in0=pb_[:, :], in1=sig[:, hh * 512:(hh + 1) * 512], op=ALU.mult),
                            [rpb, rt], [rog])
                    state_update(k1, 1, SbA)
                    P.op('scalar', lambda e: e.activation(out=qs[:], in_=og[:], func=AF.Square), [rt, rog], [rt])
                    P.op('vector', lambda e: e.tensor_reduce(out=ss[:], in_=qs[:].rearrange("p (h e) -> p h e", h=8),
                                                             axis=AX.X, op=ALU.add), [rt], [rt])
                    rsqrt(ss[:], ss[:], 1.0 / 128, 1, [rt], [rt])
                    for h in range(8):
                        hsl = slice(h * 128, (h + 1) * 128)
                        P.op('vector', lambda e, h=h, hsl=hsl: e.scalar_tensor_tensor(
                            out=onb[:, hsl], in0=og[:, hsl], scalar=ss[:, h:h + 1], in1=go[:, hsl], op0=ALU.mult, op1=ALU.mult),
                            [rt, rgo, rog], [rt])
                    pt, rpt = PT()
                    for c in range(8):
                        P.op('tensor', lambda e, c=c, pt=pt: e.transpose(out=pt[:, c * 128:(c + 1) * 128],
                                                                        in_=onb[:, c * 128:(c + 1) * 128], identity=ident[:]),
                             [rt, rid], [rpt])
                    P.op('vector', lambda e, pt=pt: e.tensor_copy(out=onT[:].rearrange("p c t -> p (c t)"), in_=pt[:]), [rpt], [rt])
                    for half in range(2):
                        po, rpo = PB()
                        for c in range(8):
                            P.op('tensor', lambda e, c=c, half=half, po=po: e.matmul(
                                po[:, :], lhsT=onT[:, c, :], rhs=wout[:, c, half * 512:(half + 1) * 512],
                                start=(c == 0), stop=(c == 7)), [rt, rw], [rpo], inc=(c == 7))
                        P.op('vector', lambda e, half=half, po=po, x32=x32: e.scalar_tensor_tensor(
                            out=zt_[:, half * 512:(half + 1) * 512], in0=x32[:, half * 512:(half + 1) * 512], scalar=ALPHA,
                            in1=po[:, :], op0=ALU.mult, op1=ALU.add), [rpo, rx32], [rt])
                    ln_tail(b, zt_, rt, gam, bet, [rg1, rb1], False, False)
                P.barrier()


        NIT = 18

        def dsa_phase(layer):
            j = layer // 2
            with ExitStack() as ph:
                win = sb(ph, "awin", [128, 8, AIN], BF16)
                wql = sb(ph, "awql", [128, 3, AH * KVR], BF16)
                wqi = sb(ph, "awqi", [128, 3, IDXH * IDXD], BF16)
                wuv = sb(ph, "awuv", [128, 2 * AH, 128], BF16)
                wout = sb(ph, "awout", [128, 8, D], BF16)
                rw = Res()
                P.dma('gpsimd', win[:], Wn['a_w_in'][j].rearrange("(c p) n -> p c n", p=128), [], [rw])
                P.dma('gpsimd', wql[:], Wn['a_w_q_lat'][j].rearrange("(c p) n -> p c n", p=128), [], [rw])
                P.dma('gpsimd', wqi[:], Wn['a_w_q_idx'][j].rearrange("(c p) n -> p c n", p=128), [], [rw])
                P.dma('gpsimd', wuv[:], Wn['a_w_uv'][j].rearrange("h (rc p) d -> p (h rc) d", p=128), [], [rw])
                P.dma('gpsimd', wout[:], Wn['a_w_out'][j].rearrange("(c p) n -> p c n", p=128), [], [rw])
                gam, rg1 = load_bc(ph, "agam1", Wn['ln1_g'][layer:layer + 1, :])
                bet, rb1 = load_bc(ph, "abet1", Wn['ln1_b'][layer:layer + 1, :])
                gq, rgq = load_bc(ph, "agq", Wn['a_g_q'][j:j + 1, :])
                gkv, rgkv = load_bc(ph, "agkv", Wn['a_g_kv'][j:j + 1, :])
                gki, rgki = load_bc(ph, "agki", Wn['a_g_kidx'][j:j + 1, :])
                bki, rbki = load_bc(ph, "abki", Wn['a_b_kidx'][j:j + 1, :])
                maskall = sb(ph, "amaskall", [128, 8, 128], F32)
                rep = sb(ph, "arep", [8, 128], BF16)
                cm = sb(ph, "acm", [128, 128], F32)
                pw = sb(ph, "apw", [128, NIT], F32)
                rc = Res()
                P.dma('sync', maskall[:], maskall_in[:, :, :], [], [rc])
                P.dma('sync', rep[:], rep_in[:, :], [], [rc])
                P.dma('sync', cm[:], cm_in[:, :], [], [rc])
                P.dma('sync', pw[:], pw_in[:, 0:NIT], [], [rc])
                CKV = sb(ph, "aCKV", [128, NB, KVR + 1], BF16)
                CKVT = sb(ph, "aCKVT", [128, 2, S], BF16)
                KIT = sb(ph, "aKIT", [64, S], BF16)
                rK = Res()
                P.op('vector', lambda e: e.memset(CKV[:, :, KVR:KVR + 1], 1.0), [], [rK])
                score = sb(ph, "ascore", [128, S], F32)
                msk = sb(ph, "amsk", [128, S], BF16)
                junk = msk
                zA = sb(ph, "azA", [128, 328], F32)
                mskT_rot = Rot([sb(ph, f"amskT{i}", [128, NB, 128], BF16) for i in range(2)])
                x32_rot = Rot([sb(ph, f"ax32{i}", [128, D], F32) for i in range(1)])
                xh_rot = Rot([sb(ph, f"axh{i}", [128, 8, 130], BF16) for i in range(2)])
                sqa = sb(ph, "asqa", [128, QR], F32)
                cq = sb(ph, "acq", [128, QR], BF16)
                kix = sb(ph, "akix", [128, IDXD], F32)
                kib = sb(ph, "akib", [128, IDXD], BF16)
                wq = sb(ph, "awq", [128, 8], BF16)
                st = sb(ph, "ast", [128, 16], F32)
                cqT_rot = Rot([sb(ph, f"acqT{i}", [128, 3, 128], BF16) for i in range(2)])
                wT = sb(ph, "awT", [8, 128], BF16)
                QIT = sb(ph, "aQIT", [64, 8, 8, 16], BF16)
                Wsel = sb(ph, "aWsel", [128, 8, 128], BF16)
                R_rot = Rot([sb(ph, f"aR{i}", [128, 512], BF16) for i in range(3)])
                mm = sb(ph, "amm", [128, 2, 8], F32)
                bs = sb(ph, "abs", [128, 8], F32)
                Wt = sb(ph, "aWt", [128, NIT], F32)
                qlT4 = sb(ph, "aqlT4", [128, 2, 512], BF16)
                PT_rot = Rot([sb(ph, f"aPT{i}", [128, 4, 128], BF16) for i in range(4)])
                olb = sb(ph, "aolb", [128, AH, KVR], BF16)
                olT = sb(ph, "aolT8", [128, AH, 2, 128], BF16)
                dcol = sb(ph, "adcol", [128, AH], F32)
                rd = sb(ph, "ard8", [128, AH], F32)
                rolb, rolT, rrd = Res(), Res(), Res()
                oh = sb(ph, "aoh", [128, D], BF16)
                ohT = sb(ph, "aohT", [128, 8, 128], BF16)
                rden = sb(ph, "arden", [128, 1], F32)
                z = sb(ph, "az", [128, D], F32)
                rt = Res()
                r_z, r_q, r_cq, r_kv, r_ki, r_w, r_cqT, r_wT, r_qit, r_wsel, r_mm, r_bs = [Res() for _ in range(12)]
                sqa_kv = sb(ph, "asqakv", [128, KVR], F32)
                sqa_ki = sb(ph, "asqaki", [128, IDXD], F32)
                rsc = Res()
                rmk = Res()
                rql = Res()
                ro = Res()

                def rmsn(ps, rps0, c0, n, gtile, rg, out_ap, idx):
                    ss, rs = st[:, idx:idx + 1], st[:, idx + 1:idx + 2]
                    P.op('scalar', lambda e: e.activation(out=sqa[:, 0:n], in_=ps[:, c0:c0 + n], func=AF.Square), [rps0], [r_q])
                    P.op('vector', lambda e: e.tensor_scalar(out=sqa[:, 0:n], in0=sqa[:, 0:n], scalar1=1.0, scalar2=None,
                                                             op0=ALU.mult, op1=ALU.add, accum_out=ss), [r_q], [r_q])
                    rsqrt(rs, ss, 1.0 / n, 1, [r_q], [r_q])
                    P.op('vector', lambda e: e.scalar_tensor_tensor(out=out_ap, in0=ps[:, c0:c0 + n], scalar=rs, in1=gtile[:, 0:n],
                                                                    op0=ALU.mult, op1=ALU.mult), [rps0, r_q, rg], [r_cq])

                blk = {}
                r_zo = Res()

                def SB(b):
                    cqT, r_cqT = cqT_rot.next()
                    mskT, rmkT = mskT_rot.next()
                    L = (b + 1) * 128
                    xh, rxh = xh_rot.next()
                    P.dma('sync', xh[:, :, 2:130], XTH[b, :, :, 2:130], [rXTH[b]], [rxh])
                    ps0, rps0 = PB()
                    ps1, rps1 = PB()
                    for c in range(8):
                        P.op('tensor', lambda e, c=c, ps0=ps0: e.matmul(ps0[:, :], lhsT=xh[:, c, 2:130], rhs=win[:, c, 0:512],
                                                                       start=(c == 0), stop=(c == 7)), [rw, rxh], [rps0], inc=(c == 7))
                    for c in range(8):
                        P.op('tensor', lambda e, c=c, ps1=ps1: e.matmul(ps1[:, 0:AIN - 512], lhsT=xh[:, c, 2:130], rhs=win[:, c, 512:AIN],
                                                                       start=(c == 0), stop=(c == 7)), [rw, rxh], [rps1], inc=(c == 7))
                    rmsn(ps0, rps0, 0, QR, gq, rgq, cq[:], 0)
                    P.op('scalar', lambda e, ps0=ps0: e.copy(out=zA[:, 0:128], in_=ps0[:, QR:512]), [rps0], [r_z])
                    P.op('scalar', lambda e, ps1=ps1: e.copy(out=zA[:, 128:128 + 200], in_=ps1[:, 0:200]), [rps1], [r_z])
                    ss, rs = st[:, 2:3], st[:, 3:4]
                    P.op('scalar', lambda e: e.activation(out=sqa_kv[:, 0:KVR], in_=zA[:, 0:KVR], func=AF.Square), [r_z], [r_kv])
                    P.op('vector', lambda e: e.tensor_scalar(out=sqa_kv[:, 0:KVR], in0=sqa_kv[:, 0:KVR], scalar1=1.0, scalar2=None,
                                                             op0=ALU.mult, op1=ALU.add, accum_out=ss), [r_kv], [r_kv])
                    rsqrt(rs, ss, 1.0 / KVR, 1, [r_kv], [r_kv])
                    P.op('vector', lambda e, b=b: e.scalar_tensor_tensor(out=CKV[:, b, 0:KVR], in0=zA[:, 0:KVR], scalar=rs, in1=gkv[:, :],
                                                                         op0=ALU.mult, op1=ALU.mult), [r_kv, r_z, rgkv], [rK])
                    s1, s2, mean, msq, var, rstd, nmr = [st[:, 4 + i:5 + i] for i in range(7)]
                    ki = zA[:, 256:320]
                    P.op('vector', lambda e: e.tensor_scalar(out=kix[:], in0=ki, scalar1=1.0, scalar2=None, op0=ALU.mult,
                                                             op1=ALU.add, accum_out=s1), [r_z], [r_ki])
                    P.op('scalar', lambda e: e.activation(out=sqa_ki[:, 0:IDXD], in_=ki, func=AF.Square), [r_z], [r_ki])
                    P.op('vector', lambda e: e.tensor_scalar(out=sqa_ki[:, 0:IDXD], in0=sqa_ki[:, 0:IDXD], scalar1=1.0, scalar2=None,
                                                             op0=ALU.mult, op1=ALU.add, accum_out=s2), [r_ki], [r_ki])
                    P.op('vector', lambda e: e.tensor_scalar(out=mean, in0=s1, scalar1=1.0 / IDXD, scalar2=None, op0=ALU.mult), [r_ki], [r_ki])
                    P.op('vector', lambda e: e.tensor_tensor(out=msq, in0=mean, in1=mean, op=ALU.mult), [r_ki], [r_ki])
                    P.op('vector', lambda e: e.scalar_tensor_tensor(out=var, in0=s2, scalar=1.0 / IDXD, in1=msq, op0=ALU.mult,
                                                                    op1=ALU.subtract), [r_ki], [r_ki])
                    rsqrt(rstd, var, 1.0, 0, [r_ki], [r_ki])
                    P.op('vector', lambda e: e.scalar_tensor_tensor(out=nmr, in0=mean, scalar=-1.0, in1=rstd, op0=ALU.mult,
                                                                    op1=ALU.mult), [r_ki], [r_ki])
                    P.op('scalar', lambda e: e.activation(out=kix[:], in_=ki, func=AF.Identity, scale=rstd, bias=nmr), [r_ki, r_z], [r_ki])
                    P.op('vector', lambda e: e.tensor_tensor(out=kix[:], in0=kix[:], in1=gki[:], op=ALU.mult), [r_ki, rgki], [r_ki])
                    P.op('vector', lambda e: e.tensor_tensor(out=kib[:], in0=kix[:], in1=bki[:], op=ALU.add), [r_ki, rbki], [r_ki])
                    P.op('vector', lambda e: e.tensor_scalar(out=wq[:], in0=zA[:, 320:328], scalar1=float((IDXH * IDXD) ** -0.5),
                                                             scalar2=None, op0=ALU.mult), [r_z], [r_w])
                    pt, rpt = PT()
                    for c in range(3):
                        P.op('tensor', lambda e, c=c, pt=pt: e.transpose(out=pt[:, c * 128:(c + 1) * 128], in_=cq[:, c * 128:(c + 1) * 128],
                                                                        identity=ident[:]), [r_cq, rid], [rpt])
                    for c in range(2):
                        P.op('tensor', lambda e, c=c, pt=pt, b=b: e.transpose(out=pt[:, (3 + c) * 128:(4 + c) * 128],
                                                                             in_=CKV[:, b, c * 128:(c + 1) * 128], identity=ident[:]),
                             [rK, rid], [rpt])
                    P.op('tensor', lambda e, pt=pt: e.transpose(out=pt[0:64, 5 * 128:6 * 128], in_=kib[:, :], identity=ident[:]),
                         [r_ki, rid], [rpt])
                    P.op('tensor', lambda e, pt=pt: e.transpose(out=pt[0:8, 6 * 128:7 * 128], in_=wq[:, :], identity=ident[:]),
                         [r_w, rid], [rpt])
                    P.op('vector', lambda e, pt=pt: e.tensor_copy(out=cqT[:].rearrange("p c t -> p (c t)"), in_=pt[:, 0:384]), [rpt], [r_cqT])
                    for c in range(2):
                        P.op('vector', lambda e, pt=pt, c=c, b=b: e.tensor_copy(out=CKVT[:, c, b * 128:(b + 1) * 128],
                                                                               in_=pt[:, (3 + c) * 128:(4 + c) * 128]), [rpt], [rK])
                    P.op('vector', lambda e, pt=pt, b=b: e.tensor_copy(out=KIT[:, b * 128:(b + 1) * 128], in_=pt[0:64, 640:768]), [rpt], [rK])
                    P.op('vector', lambda e, pt=pt: e.tensor_copy(out=wT[:], in_=pt[0:8, 768:896]), [rpt], [r_wT])
                    for hh in range(2):
                        pq, rpq = PB()
                        for h4 in range(4):
                            h = hh * 4 + h4
                            for kc in range(3):
                                P.op('tensor', lambda e, pq=pq, h=h, h4=h4, kc=kc: e.matmul(
                                    pq[0:64, h4 * 128:(h4 + 1) * 128], lhsT=wqi[:, kc, h * 64:(h + 1) * 64], rhs=cqT[:, kc, :],
                                    start=(kc == 0), stop=(kc == 2)), [rw, r_cqT], [rpq], inc=(kc == 2))
                        P.op('scalar', lambda e, pq=pq, hh=hh: e.copy(
                            out=QIT[:, :, hh * 4:(hh + 1) * 4, :].rearrange("p g h t -> p h g t"),
                            in_=pq[0:64, :].rearrange("p (h g t) -> p h g t", h=4, g=8)), [rpq], [r_qit])
                    pe_, rpe = PB()
                    P.op('tensor', lambda e, pe_=pe_: e.matmul(pe_[:, 0:128], lhsT=rep[:, :], rhs=wT[:, :], start=True, stop=True),
                         [rc, r_wT], [rpe])
                    for g in range(8):
                        P.op('vector', lambda e, g=g, pe_=pe_: e.tensor_tensor(out=Wsel[:, g, :], in0=pe_[:, 0:128], in1=maskall[:, g, :],
                                                                              op=ALU.mult), [rpe, rc], [r_wsel])
                    nkc = (L + 511) // 512
                    for kc in range(nkc):
                        k0_ = kc * 512
                        n = min(512, L - k0_)
                        psc, rpsc = pbank[4]
                        pend = []

                        def emit_mm1(g, k0_=k0_, n=n):
                            p1, rp1 = PB()
                            P.op('tensor', lambda e, p1=p1, g=g: e.matmul(
                                p1[:, 0:n], lhsT=QIT[:, g, :, :].rearrange("p h t -> p (h t)"), rhs=KIT[:, k0_:k0_ + n], start=True, stop=True),
                                [r_qit, rK], [rp1])
                            R, rR = R_rot.next()
                            if g % 2 == 0:
                                P.op('scalar', lambda e, p1=p1, R=R: e.activation(out=R[:, 0:n], in_=p1[:, 0:n], func=AF.Relu),
                                     [rp1], [rR])
                            else:
                                P.op('vector', lambda e, p1=p1, R=R: e.tensor_scalar(out=R[:, 0:n], in0=p1[:, 0:n], scalar1=0.0,
                                                                                    scalar2=None, op0=ALU.max), [rp1], [rR])
                            pend.append((R, rR))

                        emit_mm1(0)
                        emit_mm1(1)
                        for g in range(8):
                            if g + 2 < 8:
                                emit_mm1(g + 2)
                            R, rR = pend.pop(0)
                            P.op('tensor', lambda e, psc=psc, g=g, R=R, n=n: e.matmul(
                                psc[:, 0:n], lhsT=Wsel[:, g, :], rhs=R[:, 0:n], start=(g == 0), stop=(g == 7)), [r_wsel, rR], [rpsc])
                        P.op('scalar', lambda e, psc=psc, k0_=k0_, n=n: e.copy(out=score[:, k0_:k0_ + n], in_=psc[:, 0:n]), [rpsc], [rsc])
                        P.op('vector', lambda e, psc=psc, kc=kc, n=n: e.tensor_reduce(out=mm[:, 0, kc:kc + 1], in_=psc[:, 0:n], axis=AX.X,
                                                                                     op=ALU.min), [rpsc], [r_mm])
                        P.op('vector', lambda e, psc=psc, kc=kc, n=n: e.tensor_reduce(out=mm[:, 1, kc:kc + 1], in_=psc[:, 0:n], axis=AX.X,
                                                                                     op=ALU.max), [rpsc], [r_mm])
                    P.op('vector', lambda e, L=L: e.tensor_tensor(out=score[:, L - 128:L], in0=score[:, L - 128:L], in1=cm[:], op=ALU.add),
                         [rsc, rc], [rsc])
                    lo, hi, w0, mid, cnt, stp = [bs[:, i:i + 1] for i in range(6)]
                    P.op('vector', lambda e, nkc=nkc: e.tensor_reduce(out=lo, in_=mm[:, 0, 0:nkc], axis=AX.X, op=ALU.min), [r_mm], [r_bs])
                    P.op('vector', lambda e, nkc=nkc: e.tensor_reduce(out=hi, in_=mm[:, 1, 0:nkc], axis=AX.X, op=ALU.max), [r_mm], [r_bs])
                    P.op('vector', lambda e: e.tensor_tensor(out=w0, in0=hi, in1=lo, op=ALU.subtract), [r_bs], [r_bs])
                    P.op('vector', lambda e: e.tensor_scalar(out=w0, in0=w0, scalar1=1.0001, scalar2=1e-6, op0=ALU.mult, op1=ALU.add), [r_bs], [r_bs])
                    P.op('vector', lambda e: e.tensor_scalar(out=Wt[:], in0=pw[:], scalar1=w0, scalar2=None, op0=ALU.mult), [r_bs, rc], [r_bs])
                    if L > TOPK:
                        P.op('vector', lambda e: e.tensor_tensor(out=mid, in0=lo, in1=Wt[:, 0:1], op=ALU.add), [r_bs], [r_bs])
                        for k in range(NIT):
                            P.op('vector', lambda e, L=L: e.tensor_scalar(out=junk[:, 0:L], in0=score[:, 0:L], scalar1=mid, scalar2=None,
                                                                         op0=ALU.is_ge, op1=ALU.add, accum_out=cnt), [r_bs, rsc], [r_bs, rmk])
                            P.op('vector', lambda e, k=k: e.scalar_tensor_tensor(out=stp, in0=cnt, scalar=float(TOPK) - 0.5, in1=Wt[:, k:k + 1],
                                                                                op0=ALU.is_ge, op1=ALU.mult), [r_bs], [r_bs])
                            if k < NIT - 1:
                                P.op('vector', lambda e, k=k: e.scalar_tensor_tensor(out=mid, in0=mid, scalar=Wt[:, k + 1:k + 2], in1=stp,
                                                                                    op0=ALU.subtract, op1=ALU.add), [r_bs], [r_bs])
                            else:
                                P.op('vector', lambda e, k=k: e.scalar_tensor_tensor(out=lo, in0=mid, scalar=Wt[:, k:k + 1], in1=stp,
                                                                                    op0=ALU.subtract, op1=ALU.add), [r_bs], [r_bs])
                    P.op('vector', lambda e, L=L: e.tensor_scalar(out=msk[:, 0:L], in0=score[:, 0:L], scalar1=lo, scalar2=None,
                                                                 op0=ALU.is_ge), [r_bs, rsc], [rmk])
                    for kb0 in range(0, b + 1, 8):
                        nk = min(8, b + 1 - kb0)
                        pt, rpt = PT()
                        for i in range(nk):
                            P.op('tensor', lambda e, pt=pt, i=i, kb0=kb0: e.transpose(
                                out=pt[:, i * 128:(i + 1) * 128], in_=msk[:, (kb0 + i) * 128:(kb0 + i + 1) * 128], identity=ident[:]),
                                [rmk, rid], [rpt])
                        P.op('vector', lambda e, pt=pt, nk=nk, kb0=kb0: e.tensor_copy(
                            out=mskT[:, kb0:kb0 + nk, :].rearrange("p k t -> p (k t)"), in_=pt[:, 0:nk * 128]), [rpt], [rmkT])
                    blk[b] = (cqT, r_cqT, mskT, rmkT)

                def ATT(b):
                    cqT, r_cqT, mskT, rmkT = blk.pop(b)
                    x32, rx32 = x32_rot.next()
                    P.dma('sync', x32[:], xsrc(layer)[b * 128:(b + 1) * 128, :], [rX32[b]], [rx32])
                    qk_i = [0]

                    def QKB():
                        r = pbank[4 + qk_i[0]]
                        qk_i[0] ^= 1
                        return r

                    for hg in range(2):
                        for rcx in range(2):
                            pq, rpq = QKB()
                            for h4 in range(4):
                                ch = (hg * 4 + h4) * 2 + rcx
                                for kc in range(3):
                                    P.op('tensor', lambda e, pq=pq, h4=h4, ch=ch, kc=kc: e.matmul(
                                        pq[:, h4 * 128:(h4 + 1) * 128], lhsT=wql[:, kc, ch * 128:(ch + 1) * 128], rhs=cqT[:, kc, :],
                                        start=(kc == 0), stop=(kc == 2)), [rw, r_cqT], [rpq], inc=(kc == 2))
                            P.op('scalar', lambda e, pq=pq, rcx=rcx: e.activation(out=qlT4[:, rcx, :], in_=pq[:, :], func=AF.Copy,
                                                                                scale=float(KVR ** -0.5)), [rpq], [rql])
                        pendq = []

                        def emit_qk(kb):
                            pst, rpst = QKB()
                            for rcx in range(2):
                                P.op('tensor', lambda e, pst=pst, rcx=rcx, kb=kb: e.matmul(
                                    pst[:, :], lhsT=CKVT[:, rcx, kb * 128:(kb + 1) * 128], rhs=qlT4[:, rcx, :],
                                    start=(rcx == 0), stop=(rcx == 1)), [rK, rql], [rpst], inc=(rcx == 1))
                            PTt, rPT = PT_rot.next()
                            P.op('scalar', lambda e, pst=pst, PTt=PTt: e.activation(
                                out=PTt[:].rearrange("p k t -> p (k t)"), in_=pst[:, :], func=AF.Exp), [rpst], [rPT])
                            P.op('gpsimd', lambda e, PTt=PTt, kb=kb: e.tensor_tensor(
                                out=PTt[:], in0=PTt[:], in1=mskT[:, kb:kb + 1, :].to_broadcast([128, 4, 128]), op=ALU.mult), [rPT, rmkT], [rPT])
                            pendq.append((PTt, rPT))

                        emit_qk(0)
                        if b >= 1:
                            emit_qk(1)
                        for kb in range(b + 1):
                            if kb + 2 <= b:
                                emit_qk(kb + 2)
                            PTt, rPT = pendq.pop(0)
                            for h4 in range(4):
                                pol, rpol = pbank[h4]
                                P.op('tensor', lambda e, pol=pol, PTt=PTt, h4=h4, kb=kb, b=b: e.matmul(
                                    pol[:, 0:KVR + 1], lhsT=PTt[:, h4, :], rhs=CKV[:, kb, :], start=(kb == 0), stop=(kb == b)),
                                    [rPT, rK], [rpol])
                        for h4 in range(4):
                            pol, rpol = pbank[h4]
                            h = hg * 4 + h4
                            P.op('scalar', lambda e, pol=pol, h=h: e.copy(out=olb[:, h, :], in_=pol[:, 0:KVR]), [rpol], [rolb])
                            P.op('scalar', lambda e, pol=pol, h=h: e.copy(out=dcol[:, h:h + 1], in_=pol[:, KVR:KVR + 1]), [rpol], [rolb])
                    P.op('vector', lambda e: e.reciprocal(out=rd[:], in_=dcol[:]), [rolb], [rrd])
                    pts = [PT(), PT()]
                    for h in range(8):
                        pt, rpt = pts[h // 4]
                        for rcx in range(2):
                            P.op('tensor', lambda e, pt=pt, h=h, rcx=rcx: e.transpose(
                                out=pt[:, ((h % 4) * 2 + rcx) * 128:((h % 4) * 2 + rcx + 1) * 128],
                                in_=olb[:, h, rcx * 128:(rcx + 1) * 128], identity=ident[:]), [rolb, rid], [rpt])
                    for hh in range(2):
                        pt, rpt = pts[hh]
                        P.op('scalar', lambda e, pt=pt, hh=hh: e.copy(
                            out=olT[:, hh * 4:(hh + 1) * 4, :, :].rearrange("p h c t -> p (h c t)"), in_=pt[:, :]), [rpt], [rolT])
                    puvs = [QKB(), QKB()]
                    for h in range(8):
                        puv, rpuv = puvs[h // 4]
                        for rcx in range(2):
                            P.op('tensor', lambda e, puv=puv, rcx=rcx, h=h: e.matmul(
                                puv[:, (h % 4) * 128:(h % 4 + 1) * 128], lhsT=olT[:, h, rcx, :], rhs=wuv[:, h * 2 + rcx, :],
                                start=(rcx == 0), stop=(rcx == 1)), [rolT, rw], [rpuv], inc=(rcx == 1))
                    for hh in range(2):
                        puv, rpuv = puvs[hh]
                        P.op('vector', lambda e, puv=puv, hh=hh: e.tensor_tensor(
                            out=oh[:, hh * 512:(hh + 1) * 512].rearrange("p (h d) -> p h d", h=4),
                            in0=puv[:, :].rearrange("p (h d) -> p h d", h=4),
                            in1=rd[:, hh * 4:(hh + 1) * 4].rearrange("p (h o) -> p h o", o=1).to_broadcast([128, 4, 128]),
                            op=ALU.mult), [rpuv, rrd], [ro])
                    pt, rpt = PT()
                    for c in range(8):
                        P.op('tensor', lambda e, c=c, pt=pt: e.transpose(out=pt[:, c * 128:(c + 1) * 128],
                                                                        in_=oh[:, c * 128:(c + 1) * 128], identity=ident[:]),
                             [ro, rid], [rpt])
                    P.op('vector', lambda e, pt=pt: e.tensor_copy(out=ohT[:].rearrange("p c t -> p (c t)"), in_=pt[:]), [rpt], [ro])
                    for half in range(2):
                        po, rpo = PB()
                        for c in range(8):
                            P.op('tensor', lambda e, c=c, half=half, po=po: e.matmul(
                                po[:, :], lhsT=ohT[:, c, :], rhs=wout[:, c, half * 512:(half + 1) * 512],
                                start=(c == 0), stop=(c == 7)), [ro, rw], [rpo], inc=(c == 7))
                        P.op('vector', lambda e, half=half, po=po, x32=x32: e.scalar_tensor_tensor(
                            out=z[:, half * 512:(half + 1) * 512], in0=x32[:, half * 512:(half + 1) * 512], scalar=ALPHA,
                            in1=po[:, :], op0=ALU.mult, op1=ALU.add), [rpo, rx32], [r_zo])
                    ln_tail(b, z, r_zo, gam, bet, [rg1, rb1], False, False)

                SB(0)
                for b in range(NB):
                    if b + 1 < NB:
                        SB(b + 1)
                    ATT(b)
                P.barrier()

        def mixer_phase(layer):
            if MIXERS[layer % 2] is None:
                identity_mixer_phase(layer)
            elif layer % 2 == 0:
                dsa_phase(layer)
            else:
                hgrn_phase(layer)

        for layer in layers:
            mixer_phase(layer)
            ffn_phase(layer, last=(layer == layers[-1]))
        for e in ('sync',):
            if rY.w is not None:
                P._wait(e, rY.w[0], rY.w[1])
        P.barrier()
        print("instructions:", P.ninst)
    return nc


MIXERS = [True, True]

WSHAPES = {
    'a_w_in': [2, D, AIN], 'a_g_q': [2, QR], 'a_g_kv': [2, KVR], 'a_w_q_lat': [2, QR, AH * KVR],
    'a_w_q_idx': [2, QR, IDXH * IDXD], 'a_g_kidx': [2, IDXD], 'a_b_kidx': [2, IDXD],
    'a_w_uv': [2, AH, KVR, 128], 'a_w_out': [2, D, D], 'b_w_in': [2, D, 4 * D], 'b_lb_logits': [4, D],
    'b_g_o': [2, D], 'b_w_out': [2, D, D], 'ln1_g': [4, D], 'ln1_b': [4, D], 'f_w_up': [4, D, 2 * DFF],
    'f_conv_w': [4, 3, 1, 2 * DFF], 'f_conv_b': [4, 2 * DFF], 'f_w_down': [4, DFF, D], 'ln2_g': [4, D], 'ln2_b': [4, D],
}


def host_consts():
    ident = np.eye(128, dtype=np.float32).astype(ml_dtypes.bfloat16)
    s_ = np.arange(128)[:, None]
    t_ = np.arange(128)[None, :]
    same = (s_ // 64) == (t_ // 64)
    hconst = np.zeros((128, 4, 128), np.float32)
    hconst[:, 0, :] = (same & (s_ <= t_))
    hconst[:, 1, :] = same
    hconst[:, 3, :] = (same & (s_ <= t_))
    hcind = np.zeros((128, 4), np.float32)
    hcind[:64, 0] = 1.0
    hcind[64:, 1] = 1.0
    p_ = np.arange(128)
    maskall = np.zeros((128, 8, 128), np.float32)
    for g in range(8):
        maskall[p_, g, 16 * g + (p_ % 16)] = 1.0
    rep = np.zeros((8, 128), np.float32)
    rep[p_ // 16, p_] = 1.0
    cm = np.where(np.arange(128)[None, :] <= np.arange(128)[:, None], 0.0, -1e30).astype(np.float32)
    pw = np.tile((0.5 ** np.arange(1, 33, dtype=np.float64)).astype(np.float32)[None, :], (128, 1))
    return {'ident': ident, 'hconst': hconst, 'hcind': hcind, 'maskall': maskall,
            'rep': rep.astype(ml_dtypes.bfloat16), 'cm': cm, 'pw': pw}


def kernel(**inputs):
    x = np.ascontiguousarray(np.asarray(inputs['x'], dtype=np.float32))
    B, S, _ = x.shape
    nc = build_program(S)
    consts = host_consts()
    wmaps = {k: np.ascontiguousarray(np.asarray(inputs[k], dtype=np.float32)) for k in WSHAPES}
    in_maps = []
    for c in range(8):
        m = dict(wmaps)
        m['x'] = x[(c // 2) % B]
        m.update(consts)
        in_maps.append(m)
    res = run_bass_kernel_spmd(nc, in_maps, core_ids=list(range(8)))
    out = np.stack([res.results[2 * b]['y'] for b in range(B)], axis=0)
    return out.astype(np.float32)
```

```python
from contextlib import ExitStack
import numpy as np
import ml_dtypes
import concourse.bass as bass
import concourse.mybir as mybir
from concourse.bass_utils import run_bass_kernel_spmd

F32 = mybir.dt.float32
BF16 = mybir.dt.bfloat16
ALU = mybir.AluOpType
AF = mybir.ActivationFunctionType
AX = mybir.AxisListType

D = 1024
DFF = 2816
NFC = DFF // 128
DEPTH = 4
ALPHA = (2 * DEPTH) ** 0.25
LN_EPS = 1e-5
RMS_EPS = 1e-6
QR, KVR, IDXD, IDXH, AH = 384, 256, 64, 8, 8
AIN = QR + KVR + IDXD + IDXH

ENGS = ['tensor', 'vector', 'scalar', 'gpsimd', 'sync']
DMAQ = ['sync', 'scalar', 'gpsimd']


class Res:
    __slots__ = ('w', 'r', 'name', 'excl')

    def __init__(self, name='', excl=False):
        self.w = None
        self.r = []
        self.name = name
        self.excl = excl


class Prog:
    def __init__(self, nc, stack, n_dma_slots=4):
        self.nc = nc
        self.sem = {e: stack.enter_context(nc.semaphore(f"s_{e}")) for e in ENGS}
        self.cnt = {e: 0 for e in ENGS}
        self.nslots = n_dma_slots
        self.dsem, self.dcnt, self.dnext = {}, {}, {}
        for qn in DMAQ:
            for s in range(n_dma_slots):
                self.dsem[(qn, s)] = stack.enter_context(nc.semaphore(f"d_{qn}_{s}"))
                self.dcnt[(qn, s)] = 0
            self.dnext[qn] = 0
        self.seen = {e: {} for e in ENGS}
        self.ninst = 0

    def _wait(self, eng, prod, val):
        if self.seen[eng].get(prod, 0) >= val:
            return
        sem = self.sem[prod] if isinstance(prod, str) else self.dsem[prod]
        getattr(self.nc, eng).wait_ge(sem, val)
        self.seen[eng][prod] = val

    def _deps(self, eng, reads, writes):
        deps = []
        for b in reads:
            if b.w is not None:
                deps.append(b.w)
            if b.excl:
                deps.extend(b.r)
        for b in writes:
            if b.w is not None:
                if not (b.excl and eng == 'tensor' and b.w[0] == 'tensor'):
                    deps.append(b.w)
            deps.extend(b.r)
        for (p, v) in deps:
            self._wait(eng, p, v)

    def _commit(self, me, reads, writes):
        for b in reads:
            b.r.append(me)
        for b in writes:
            b.w = me
            b.r = []

    def op(self, eng, fn, reads=(), writes=(), inc=True):
        self._deps(eng, reads, writes)
        if inc:
            self.cnt[eng] += 1
            fn(getattr(self.nc, eng)).then_inc(self.sem[eng], 1)
            self._commit((eng, self.cnt[eng]), reads, writes)
        else:
            fn(getattr(self.nc, eng))
            self._commit((eng, self.cnt[eng] + 1), reads, writes)
        self.ninst += 1

    def dma(self, qn, out, in_, reads=(), writes=(), **kw):
        s = self.dnext[qn]
        self.dnext[qn] = (s + 1) % self.nslots
        key = (qn, s)
        prev = 16 * self.dcnt[key]
        if prev:
            self._wait(qn, key, prev)
        self._deps(qn, reads, writes)
        self.dcnt[key] += 1
        getattr(self.nc, qn).dma_start(out=out, in_=in_, **kw).then_inc(self.dsem[key], 16)
        self._commit((key, 16 * self.dcnt[key]), reads, writes)
        self.ninst += 1

    def barrier(self):
        for e in ENGS:
            for p in ENGS:
                if p != e and self.cnt[p]:
                    self._wait(e, p, self.cnt[p])
            for k, c in self.dcnt.items():
                if c:
                    self._wait(e, k, 16 * c)


class Rot:
    def __init__(self, tiles):
        self.t = [(t, Res()) for t in tiles]
        self.i = 0

    def next(self):
        r = self.t[self.i]
        self.i = (self.i + 1) % len(self.t)
        return r


def build_program(S, layers=(0, 1, 2, 3), first=True):
    layers = tuple(layers)
    NB = S // 128
    TOPK = min(256, S // 4)
    nc = bass.Bass("TRN2", target_bir_lowering=False)

    def din(name, shape, dt=F32):
        return nc.dram_tensor(name, list(shape), dt, kind="ExternalInput").ap()

    x_in = din("x", [S, D])
    Wn = {}
    for name, shape in WSHAPES.items():
        Wn[name] = din(name, shape)
    ident_in = din("ident", [128, 128], BF16)
    hconst_in = din("hconst", [128, 4, 128])
    hcind_in = din("hcind", [128, 4])
    maskall_in = din("maskall", [128, 8, 128])
    rep_in = din("rep", [8, 128], BF16)
    cm_in = din("cm", [128, 128])
    pw_in = din("pw", [128, 32])
    y_out = nc.dram_tensor("y", [S, D], F32, kind="ExternalOutput").ap()
    X32 = nc.dram_tensor("X32", [S, D], F32).ap()
    XTH = nc.dram_tensor("XTH", [NB + 1, 128, 8, 130], BF16).ap()
    rX32 = [Res() for _ in range(NB)]
    rXTH = [Res() for _ in range(NB + 1)]
    rXTHh = [Res() for _ in range(NB + 1)]
    rY = Res()
    WUb = nc.dram_tensor("WUb", [4, NFC, 128, 8, 256], BF16).ap()
    rWb = [Res() for _ in range(4)]

    with ExitStack() as top:
        P = Prog(nc, top)
        uid = [0]

        def sb(st, name, shape, dt):
            uid[0] += 1
            return st.enter_context(nc.sbuf_tensor(f"t{uid[0]}_{name}", list(shape), dt))
        pbank = [(top.enter_context(nc.psum_tensor(f"pb{i}", [128, 512], F32)), Res(excl=True)) for i in range(6)]
        ptb = [(top.enter_context(nc.psum_tensor(f"pt{i}", [128, 1024], BF16)), Res(excl=True)) for i in range(2)]
        pbi = [0]
        pti = [0]

        def PB():
            r = pbank[pbi[0]]
            pbi[0] = (pbi[0] + 1) % 4
            return r

        def PT():
            r = ptb[pti[0]]
            pti[0] = (pti[0] + 1) % len(ptb)
            return r

        epsT = sb(top, "epsT", [128, 2], F32)
        reps = Res()
        P.op('vector', lambda e: e.memset(epsT[:, 0:1], LN_EPS), [], [reps])
        P.op('vector', lambda e: e.memset(epsT[:, 1:2], RMS_EPS), [], [reps])

        def rsqrt(out, in_, scale, which, reads, writes):
            P.op('scalar', lambda e: e.activation(out=out, in_=in_, func=AF.Sqrt, scale=scale, bias=epsT[:, which:which + 1]),
                 list(reads) + [reps], writes)
            P.op('vector', lambda e: e.reciprocal(out=out, in_=out), writes, writes)

        ident = sb(top, "ident_sb", [128, 128], BF16)
        rid = Res()
        P.dma('sync', ident[:], ident_in[:, :], writes=[rid])

        def convert_wu(l, j):
            for part in range(2):
                col = part * DFF + j * 128
                P.dma('gpsimd', WUb[l, j, :, :, part * 128:(part + 1) * 128],
                      Wn['f_w_up'][l, :, col:col + 128].rearrange("(c p) n -> p c n", p=128), [], [rWb[l]])

        for j in range(NFC):
            convert_wu(layers[0], j)

        lnst = ExitStack()
        top.enter_context(lnst)
        yb_rot = Rot([sb(top, f"yb{i}", [128, D], BF16) for i in range(1)])
        xtb_rot = Rot([sb(top, f"xtb{i}", [128, 8, 128], BF16) for i in range(2)])
        yn_rot = Rot([sb(top, f"yn{i}", [128, D], F32) for i in range(1)])
        sq_rot = Rot([sb(top, f"sq{i}", [128, D], F32) for i in range(1)])
        st_rot = Rot([sb(top, f"st{i}", [128, 8], F32) for i in range(2)])
        eng_flip = [0]

        def transpose_store(b, y32, ry, halo):
            yb, ryb = yb_rot.next()
            P.op('scalar', lambda e: e.copy(out=yb[:], in_=y32[:]), [ry], [ryb])
            pt, rpt = PT()
            for c in range(8):
                P.op('tensor', lambda e, c=c: e.transpose(out=pt[:, c * 128:(c + 1) * 128],
                                                           in_=yb[:, c * 128:(c + 1) * 128], identity=ident[:]),
                     [ryb, rid], [rpt])
            xtb, rxtb = xtb_rot.next()
            P.op('vector', lambda e: e.tensor_copy(out=xtb[:].rearrange("p c t -> p (c t)"), in_=pt[:]), [rpt], [rxtb])
            P.dma('gpsimd', XTH[b, :, :, 2:130], xtb[:], [rxtb], [rXTH[b]])
            if halo:
                P.dma('gpsimd', XTH[b + 1, :, :, 0:2], xtb[:, :, 126:128], [rxtb], [rXTHh[b + 1]])

        def ln_tail(b, z, rz, gam, bet, rgb, to_out, halo):
            st_, rst = st_rot.next()
            sq, rsq = sq_rot.next()
            yn, ryn = yn_rot.next()
            s1, s2, mean, msq, var, rstd, nmr = [st_[:, i:i + 1] for i in range(7)]
            P.op('vector', lambda e: e.tensor_scalar(out=yn[:], in0=z[:], scalar1=1.0, scalar2=None, op0=ALU.mult,
                                                     op1=ALU.add, accum_out=s1), [rz], [ryn, rst])
            P.op('scalar', lambda e: e.activation(out=sq[:], in_=z[:], func=AF.Square), [rz], [rsq])
            P.op('vector', lambda e: e.tensor_scalar(out=sq[:], in0=sq[:], scalar1=1.0, scalar2=None, op0=ALU.mult,
                                                     op1=ALU.add, accum_out=s2), [rsq], [rsq, rst])
            P.op('vector', lambda e: e.tensor_scalar(out=mean, in0=s1, scalar1=1.0 / D, scalar2=None, op0=ALU.mult), [rst], [rst])
            P.op('vector', lambda e: e.tensor_tensor(out=msq, in0=mean, in1=mean, op=ALU.mult), [rst], [rst])
            P.op('vector', lambda e: e.scalar_tensor_tensor(out=var, in0=s2, scalar=1.0 / D, in1=msq, op0=ALU.mult,
                                                            op1=ALU.subtract), [rst], [rst])
            rsqrt(rstd, var, 1.0, 0, [rst], [rst])
            P.op('vector', lambda e: e.scalar_tensor_tensor(out=nmr, in0=mean, scalar=-1.0, in1=rstd, op0=ALU.mult,
                                                            op1=ALU.mult), [rst], [rst])
            P.op('scalar', lambda e: e.activation(out=yn[:], in_=z[:], func=AF.Identity, scale=rstd, bias=nmr),
                 [rz, rst], [ryn])
            P.op('gpsimd', lambda e: e.tensor_tensor(out=yn[:], in0=yn[:], in1=gam[:], op=ALU.mult), [ryn] + list(rgb), [ryn])
            P.op('vector', lambda e: e.tensor_tensor(out=yn[:], in0=yn[:], in1=bet[:], op=ALU.add), [ryn] + list(rgb), [ryn])
            if to_out:
                P.dma('gpsimd', y_out[b * 128:(b + 1) * 128, :], yn[:], [ryn], [rY])
            else:
                P.dma('gpsimd', X32[b * 128:(b + 1) * 128, :], yn[:], [ryn], [rX32[b]])
                transpose_store(b, yn, ryn, halo)

        if first:
            with ExitStack() as ph:
                zt = sb(ph, "zt", [128, 8, 2], BF16)
                rzt = Res()
                P.op('vector', lambda e: e.memset(zt[:], 0.0), [], [rzt])
                P.dma('sync', XTH[0, :, :, 0:2], zt[:], [rzt], [rXTHh[0]])
                xl_rot = Rot([sb(ph, f"xl{i}", [128, D], F32) for i in range(2)])
                for b in range(NB):
                    xl, rxl = xl_rot.next()
                    P.dma('scalar', xl[:], x_in[b * 128:(b + 1) * 128, :], [], [rxl])
                    transpose_store(b, xl, rxl, False)
                P.barrier()

        def xsrc(layer):
            return x_in if (first and layer == layers[0]) else X32

        def load_bc(st, name, src_row):
            t = sb(st, name, [128, src_row.shape[-1]], F32)
            r = Res()
            P.dma('sync', t[:], src_row.to_broadcast([128, src_row.shape[-1]]), [], [r])
            return t, r

        TT = 512 if NB % 4 == 0 else 128 * NB
        NBT = TT // 128

        def ffn_phase(layer, last):
            with ExitStack() as ph:
                wd = sb(ph, "wd", [128, NFC, D], BF16)
                rw = Res()
                for j in range(0, NFC, 2):
                    P.dma('gpsimd', wd[:, j:j + 2, :],
                          Wn['f_w_down'][layer, j * 128:(j + 2) * 128, :].rearrange("(c p) n -> p c n", p=128), [], [rw])
                cw = sb(ph, "cw", [128, 3, 2 * NFC], F32)
                cb = sb(ph, "cb", [128, 2 * NFC], F32)
                for k in range(3):
                    P.dma('scalar', cw[:, k, :], Wn['f_conv_w'][layer, k, 0, :].rearrange("(j p) -> p j", p=128), [], [rw],
                          allow_slow_non_contiguous=True)
                P.dma('scalar', cb[:], Wn['f_conv_b'][layer, :].rearrange("(j p) -> p j", p=128), [], [rw],
                      allow_slow_non_contiguous=True)
                gam = sb(ph, "gam2", [128, D], F32)
                bet = sb(ph, "bet2", [128, D], F32)
                P.dma('sync', gam[:], Wn['ln2_g'][layer:layer + 1, :].to_broadcast([128, D]), [], [rw])
                P.dma('sync', bet[:], Wn['ln2_b'][layer:layer + 1, :].to_broadcast([128, D]), [], [rw])
                carry = sb(ph, "carry", [128, 2 * NFC, 2], F32)
                rcar = [Res() for _ in range(2 * NFC)]
                P.op('gpsimd', lambda e: e.memset(carry[:], 0.0), [], rcar)
                wu_rot = Rot([sb(ph, f"wu{i}", [128, 8, 256], BF16) for i in range(3)])
                xw_rot = Rot([sb(ph, f"xw{i}", [128, 8, TT], BF16) for i in range(2)])
                g_rot = Rot([sb(ph, f"g{i}", [128, NFC, TT], BF16) for i in range(2)])
                hs_rot = Rot([sb(ph, f"hs{i}", [128, TT + 2], F32) for i in range(4)])
                acc_rot = Rot([sb(ph, f"acc{i}", [128, TT], F32) for i in range(6)])
                sa_rot = Rot([sb(ph, f"sa{i}", [128, TT], F32) for i in range(2)])
                x32_rot = Rot([sb(ph, f"x32{i}", [128, D], F32) for i in range(4 if NBT == 4 else NBT)])
                z_rot = Rot([sb(ph, f"z{i}", [128, D], F32) for i in range(2)])
                for t in range(NB // NBT):
                    b0 = t * NBT
                    xw, rxw = xw_rot.next()
                    for i in range(NBT):
                        P.dma('sync', xw[:, :, i * 128:(i + 1) * 128], XTH[b0 + i, :, :, 2:130], [rXTH[b0 + i]], [rxw])
                    g, rg = g_rot.next()
                    pend_gate = None
                    x32s = []
                    for i in range(NBT):
                        x32, rx32 = x32_rot.next()
                        P.dma('sync', x32[:], X32[(b0 + i) * 128:(b0 + i + 1) * 128, :], [rX32[b0 + i]], [rx32])
                        x32s.append((x32, rx32))

                    def emit_gate(accs, j, g, rg):
                        sa, rsa = sa_rot.next()
                        P.op('scalar', lambda e, sa=sa, a=accs[0][0]: e.activation(out=sa[:], in_=a[:], func=AF.Silu),
                             [accs[0][1]], [rsa])
                        P.op('vector', lambda e, sa=sa, u=accs[1][0], g=g, j=j: e.tensor_tensor(
                            out=g[:, j, :], in0=sa[:], in1=u[:], op=ALU.mult), [rsa, accs[1][1]], [rg])

                    for j in range(NFC):
                        if t == 0 and layer != layers[-1]:
                            convert_wu(layers[layers.index(layer) + 1], j)
                        wu, rwu = wu_rot.next()
                        P.dma('sync', wu[:], WUb[layer, j], [rWb[layer]], [rwu])
                        accs = []
                        for part in range(2):
                            jj = part * NFC + j
                            ps, rps = PB()
                            for c in range(8):
                                P.op('tensor', lambda e, c=c, part=part, ps=ps, wu=wu, xw=xw: e.matmul(
                                    ps[:, 0:TT], lhsT=wu[:, c, part * 128:(part + 1) * 128], rhs=xw[:, c, :],
                                    start=(c == 0), stop=(c == 7)), [rwu, rxw], [rps], inc=(c == 7))
                            hs, rhs_ = hs_rot.next()
                            P.op('gpsimd', lambda e, hs=hs, jj=jj: e.tensor_copy(out=hs[:, 0:2], in_=carry[:, jj, :]), [rcar[jj]], [rhs_])
                            P.op('scalar', lambda e, hs=hs, ps=ps: e.copy(out=hs[:, 2:TT + 2], in_=ps[:, 0:TT]), [rps], [rhs_])
                            P.op('gpsimd', lambda e, hs=hs, jj=jj: e.tensor_copy(out=carry[:, jj, :], in_=hs[:, TT:TT + 2]), [rhs_], [rcar[jj]])
                            acc, racc = acc_rot.next()
                            eng = 'vector' if part == 0 else 'gpsimd'
                            P.op(eng, lambda e, acc=acc, hs=hs, jj=jj: e.tensor_scalar(
                                out=acc[:], in0=hs[:, 0:TT], scalar1=cw[:, 0, jj:jj + 1], scalar2=cb[:, jj:jj + 1],
                                op0=ALU.mult, op1=ALU.add), [rhs_, rw], [racc])
                            for k in (1, 2):
                                P.op('vector', lambda e, acc=acc, hs=hs, jj=jj, k=k: e.scalar_tensor_tensor(
                                    out=acc[:], in0=hs[:, k:k + TT], scalar=cw[:, k, jj:jj + 1], in1=acc[:],
                                    op0=ALU.mult, op1=ALU.add), [rhs_, rw, racc], [racc])
                            accs.append((acc, racc))
                        if pend_gate is not None:
                            emit_gate(*pend_gate)
                        pend_gate = (accs, j, g, rg)
                    emit_gate(*pend_gate)
                    pend_gate = None
                    for i in range(NBT):
                        b = b0 + i
                        x32, rx32 = x32s[i]
                        z, rz = z_rot.next()
                        for half in range(2):
                            po, rpo = PB()
                            for j in range(NFC):
                                P.op('tensor', lambda e, j=j, half=half, po=po, g=g, i=i: e.matmul(
                                    po[:, :], lhsT=g[:, j, i * 128:(i + 1) * 128], rhs=wd[:, j, half * 512:(half + 1) * 512],
                                    start=(j == 0), stop=(j == NFC - 1)), [rg, rw], [rpo], inc=(j == NFC - 1))
                            P.op('vector', lambda e, half=half, po=po, z=z, x32=x32: e.scalar_tensor_tensor(
                                out=z[:, half * 512:(half + 1) * 512], in0=x32[:, half * 512:(half + 1) * 512], scalar=ALPHA,
                                in1=po[:, :], op0=ALU.mult, op1=ALU.add), [rpo, rx32], [rz])
                        ln_tail(b, z, rz, gam, bet, [rw], last, False)
                P.barrier()

        ph_hs = [sb(top, "hsb", [128, 130], F32)]
        rhs = Res()

        def identity_mixer_phase(layer):
            with ExitStack() as ph:
                gam, rg1 = load_bc(ph, "gam1", Wn['ln1_g'][layer:layer + 1, :])
                bet, rb1 = load_bc(ph, "bet1", Wn['ln1_b'][layer:layer + 1, :])
                rgb = Res()
                P.op('vector', lambda e: e.memset(ph_hs[0][:, 0:1], 0.0), [rg1, rb1, rhs], [rgb, rhs])
                x32_rot = Rot([sb(ph, f"mx32{i}", [128, D], F32) for i in range(2)])
                z_rot = Rot([sb(ph, f"mz{i}", [128, D], F32) for i in range(2)])
                for b in range(NB):
                    x32, rx32 = x32_rot.next()
                    P.dma('sync', x32[:], xsrc(layer)[b * 128:(b + 1) * 128, :], [rX32[b]], [rx32])
                    z, rz = z_rot.next()
                    P.op('vector', lambda e, z=z, x32=x32: e.tensor_scalar(out=z[:], in0=x32[:], scalar1=ALPHA, scalar2=None,
                                                                        op0=ALU.mult), [rx32], [rz])
                    ln_tail(b, z, rz, gam, bet, [rgb], False, True)
                P.barrier()


        def hgrn_phase(layer):
            j = layer // 2
            with ExitStack() as ph:
                win = sb(ph, "hwin", [128, 8, 4 * D], BF16)
                wout = sb(ph, "hwout", [128, 8, D], BF16)
                rw = Res()
                for c in range(8):
                    P.dma('gpsimd', win[:, c, :], Wn['b_w_in'][j, c * 128:(c + 1) * 128, :], [], [rw])
                P.dma('gpsimd', wout[:], Wn['b_w_out'][j].rearrange("(c p) n -> p c n", p=128), [], [rw])
                gam, rg1 = load_bc(ph, "hgam1", Wn['ln1_g'][layer:layer + 1, :])
                bet, rb1 = load_bc(ph, "hbet1", Wn['ln1_b'][layer:layer + 1, :])
                go, rgo = load_bc(ph, "hgo", Wn['b_g_o'][j:j + 1, :])
                cst = sb(ph, "hcst", [128, 3, 128], F32)
                cind = sb(ph, "hcind", [128, 4], F32)
                cmask = sb(ph, "hcmask", [128, 128], F32)
                cmask4 = sb(ph, "hcmask4", [128, 4, 128], F32)
                rc = Res()
                P.dma('sync', cst[:], hconst_in[:, 0:3, :], [], [rc])
                P.dma('sync', cmask[:], hconst_in[:, 3, :], [], [rc])
                for i4 in range(4):
                    P.dma('sync', cmask4[:, i4, :], hconst_in[:, 3, :], [], [rc])
                P.dma('sync', cind[:], hcind_in[:, :], [], [rc])
                lb = sb(ph, "hlb", [128, D], F32)
                oml = sb(ph, "homl", [128, D], F32)
                rl = Res()
                tmp = ExitStack()
                lg = [load_bc(tmp, f"hlg{l}", Wn['b_lb_logits'][l:l + 1, :]) for l in range(4)]
                mx = sb(tmp, "hmx", [128, D], F32)
                den = sb(tmp, "hden", [128, D], F32)
                P.op('vector', lambda e: e.tensor_tensor(out=mx[:], in0=lg[0][0][:], in1=lg[1][0][:], op=ALU.max),
                     [lg[0][1], lg[1][1]], [rl])
                for l in (2, 3):
                    P.op('vector', lambda e, l=l: e.tensor_tensor(out=mx[:], in0=mx[:], in1=lg[l][0][:], op=ALU.max),
                         [lg[l][1], rl], [rl])
                for l in range(4):
                    P.op('vector', lambda e, l=l: e.tensor_tensor(out=lg[l][0][:], in0=lg[l][0][:], in1=mx[:], op=ALU.subtract),
                         [rl, lg[l][1]], [lg[l][1]])
                    P.op('scalar', lambda e, l=l: e.activation(out=lg[l][0][:], in_=lg[l][0][:], func=AF.Exp),
                         [lg[l][1]], [lg[l][1]])
                P.op('vector', lambda e: e.tensor_tensor(out=den[:], in0=lg[0][0][:], in1=lg[1][0][:], op=ALU.add),
                     [lg[0][1], lg[1][1]], [rl])
                for l in (2, 3):
                    P.op('vector', lambda e, l=l: e.tensor_tensor(out=den[:], in0=den[:], in1=lg[l][0][:], op=ALU.add),
                         [lg[l][1], rl], [rl])
                P.op('vector', lambda e: e.tensor_copy(out=lb[:], in_=lg[1][0][:]), [lg[1][1], rl], [rl])
                for l in range(2, layer + 1):
                    P.op('vector', lambda e, l=l: e.tensor_tensor(out=lb[:], in0=lb[:], in1=lg[l][0][:], op=ALU.add),
                         [lg[l][1], rl], [rl])
                P.op('vector', lambda e: e.reciprocal(out=den[:], in_=den[:]), [rl], [rl])
                P.op('vector', lambda e: e.tensor_tensor(out=lb[:], in0=lb[:], in1=den[:], op=ALU.mult), [rl], [rl])
                P.op('vector', lambda e: e.tensor_scalar(out=oml[:], in0=lb[:], scalar1=-1.0, scalar2=1.0, op0=ALU.mult,
                                                         op1=ALU.add), [rl], [rl])
                P.barrier()
                tmp.close()
                Sf = sb(ph, "hSf", [128, 8, 128], F32)
                SbA = sb(ph, "hSbA", [128, 8, 128], BF16)
                SbB = sb(ph, "hSbB", [128, 8, 128], BF16)
                q0T = sb(ph, "hq0T", [128, 8, 128], BF16)
                q1T = sb(ph, "hq1T", [128, 8, 128], BF16)
                kT = sb(ph, "hkT", [128, 8, 128], BF16)
                qT = sb(ph, "hqT", [128, 8, 128], BF16)
                rS = Res()
                rq = Res()
                P.op('vector', lambda e: e.memset(Sf[:], 0.0), [], [rS])
                P.op('gpsimd', lambda e: e.memset(q0T[:], 0.0), [], [rq])
                P.op('gpsimd', lambda e: e.memset(q1T[:], 0.0), [], [rq])
                xh_rot = Rot([sb(ph, f"hxh{i}", [128, 8, 130], BF16) for i in range(2)])
                x32_rot = Rot([sb(ph, f"hx32{i}", [128, D], F32) for i in range(2)])
                F = lambda n: sb(ph, n, [128, D], F32)
                sig, lf, key, bb, bl, qs, og = F("hsig"), F("hlf"), F("hkey"), F("hbb"), F("hbl"), F("hqs"), F("hog")
                zt_ = qs
                gateT = F("hgate")
                rqs, rvv, rgate = Res(), Res(), Res()
                B16 = lambda n: sb(ph, n, [128, D], BF16)
                qtl, ktl, k0, k1, vv, onb = B16("hqtl"), B16("hktl"), B16("hk0"), B16("hk1"), B16("hvv"), B16("honb")
                onT = sb(ph, "honT", [128, 8, 128], BF16)
                scT = sb(ph, "hscT", [128, 8, 128], BF16)
                rscT = Res()
                rog = Res()
                rSb = {id(SbA): Res(), id(SbB): Res()}
                P.op('vector', lambda e: e.memset(SbA[:], 0.0), [], [rSb[id(SbA)]])
                dec = sb(ph, "hdec", [128, 8, 2], F32)
                ss = sb(ph, "hss", [128, 8], F32)
                rt = Res()

                def proj(part, xh, rxh):
                    outs = []
                    for half in range(2):
                        ps, rps = PB()
                        col = part * D + half * 512
                        for c in range(8):
                            P.op('tensor', lambda e, c=c, col=col, ps=ps: e.matmul(
                                ps[:, :], lhsT=xh[:, c, 2:130], rhs=win[:, c, col:col + 512],
                                start=(c == 0), stop=(c == 7)), [rw, rxh], [rps], inc=(c == 7))
                        outs.append((ps, rps))
                    return outs

                for b in range(NB):
                    xh, rxh = xh_rot.next()
                    P.dma('sync', xh[:, :, 2:130], XTH[b, :, :, 2:130], [rXTH[b]], [rxh])
                    x32, rx32 = x32_rot.next()
                    P.dma('sync', x32[:], xsrc(layer)[b * 128:(b + 1) * 128, :], [rX32[b]], [rx32])
                    for half, (ps, rps) in enumerate(proj(1, xh, rxh)):
                        hs = slice(half * 512, (half + 1) * 512)
                        P.op('scalar', lambda e, ps=ps, hs=hs: e.activation(out=sig[:, hs], in_=ps[:, :], func=AF.Sigmoid),
                             [rps], [rt])
                    for half, (ps, rps) in enumerate(proj(0, xh, rxh)):
                        hs = slice(half * 512, (half + 1) * 512)
                        P.op('scalar', lambda e, ps=ps, hs=hs: e.activation(out=qs[:, hs], in_=ps[:, :], func=AF.Silu),
                             [rps], [rqs])
                    for half, (ps, rps) in enumerate(proj(2, xh, rxh)):
                        hs = slice(half * 512, (half + 1) * 512)
                        P.op('vector', lambda e, ps=ps, hs=hs: e.tensor_copy(out=vv[:, hs], in_=ps[:, :]), [rps], [rvv])
                    for half, (ps, rps) in enumerate(proj(3, xh, rxh)):
                        hs = slice(half * 512, (half + 1) * 512)
                        P.op('scalar', lambda e, ps=ps, hs=hs: e.activation(out=gateT[:, hs], in_=ps[:, :], func=AF.Sigmoid),
                             [rps], [rgate])
                    P.op('vector', lambda e: e.tensor_tensor(out=lf[:], in0=sig[:], in1=oml[:], op=ALU.mult), [rt, rl], [rt])
                    P.op('vector', lambda e: e.tensor_tensor(out=key[:], in0=oml[:], in1=lf[:], op=ALU.subtract), [rt, rl], [rt])
                    P.op('vector', lambda e: e.tensor_tensor(out=lf[:], in0=lf[:], in1=lb[:], op=ALU.add), [rt, rl], [rt])
                    P.op('scalar', lambda e: e.activation(out=lf[:], in_=lf[:], func=AF.Ln), [rt], [rt])
                    for half in range(2):
                        hs = slice(half * 512, (half + 1) * 512)
                        ps, rps = PB()
                        P.op('tensor', lambda e, ps=ps, hs=hs: e.matmul(ps[:, :], lhsT=cst[:, 0, :], rhs=lf[:, hs],
                                                                       start=True, stop=True), [rt, rc], [rps])
                        P.op('vector', lambda e, ps=ps, hs=hs: e.tensor_copy(out=bb[:, hs], in_=ps[:, :]), [rps], [rt])
                        ps, rps = PB()
                        P.op('tensor', lambda e, ps=ps, hs=hs: e.matmul(ps[:, :], lhsT=cst[:, 1, :], rhs=lf[:, hs],
                                                                       start=True, stop=True), [rt, rc], [rps])
                        P.op('vector', lambda e, ps=ps, hs=hs: e.tensor_copy(out=bl[:, hs], in_=ps[:, :]), [rps], [rt])
                    ps, rps = PB()
                    for h in range(8):
                        P.op('tensor', lambda e, h=h, ps=ps: e.matmul(ps[:, h * 2:h * 2 + 2], lhsT=lf[:, h * 128:(h + 1) * 128],
                                                                     rhs=cind[:, 0:2], start=True, stop=True), [rt, rc], [rps])
                    P.op('scalar', lambda e, ps=ps: e.activation(out=dec[:].rearrange("p h c -> p (h c)"), in_=ps[:, 0:16],
                                                                 func=AF.Exp), [rps], [rt])
                    P.op('vector', lambda e: e.tensor_tensor(out=bl[:], in0=bl[:], in1=bb[:], op=ALU.subtract), [rt], [rt])
                    P.op('scalar', lambda e: e.activation(out=bl[:], in_=bl[:], func=AF.Exp), [rt], [rt])
                    P.op('vector', lambda e: e.tensor_tensor(out=bl[:], in0=bl[:], in1=key[:], op=ALU.mult), [rt], [rt])
                    P.op('vector', lambda e: e.tensor_scalar(out=k0[:], in0=bl[:], scalar1=cind[:, 0:1], scalar2=None,
                                                             op0=ALU.mult), [rt, rc], [rt])
                    P.op('vector', lambda e: e.tensor_scalar(out=k1[:], in0=bl[:], scalar1=cind[:, 1:2], scalar2=None,
                                                             op0=ALU.mult), [rt, rc], [rt])
                    P.op('scalar', lambda e: e.activation(out=sig[:], in_=bb[:], func=AF.Exp, scale=-1.0), [rt], [rt])
                    P.op('vector', lambda e: e.tensor_tensor(out=ktl[:], in0=sig[:], in1=key[:], op=ALU.mult), [rt], [rt])
                    P.op('scalar', lambda e: e.activation(out=bb[:], in_=bb[:], func=AF.Exp), [rt], [rt])
                    P.op('vector', lambda e: e.tensor_tensor(out=qtl[:], in0=qs[:], in1=bb[:], op=ALU.mult), [rt, rqs], [rt])
                    for (src, dsts) in ((qtl, 'q'), (ktl, 'k')):
                        pt, rpt = PT()
                        for h in range(8):
                            P.op('tensor', lambda e, h=h, pt=pt, src=src: e.transpose(
                                out=pt[:, h * 128:(h + 1) * 128], in_=src[:, h * 128:(h + 1) * 128], identity=ident[:]),
                                [rt, rid], [rpt])
                        ptv = pt[:].rearrange("p (h t) -> p h t", h=8)
                        if dsts == 'q':
                            P.op('vector', lambda e, ptv=ptv: e.tensor_copy(out=qT[:], in_=ptv), [rpt], [rq])
                            P.op('vector', lambda e, ptv=ptv: e.tensor_copy(out=q0T[:, :, 0:64], in_=ptv[:, :, 0:64]), [rpt], [rq])
                            P.op('vector', lambda e, ptv=ptv: e.tensor_copy(out=q1T[:, :, 64:128], in_=ptv[:, :, 64:128]), [rpt], [rq])
                        else:
                            P.op('vector', lambda e, ptv=ptv: e.tensor_copy(out=kT[:], in_=ptv), [rpt], [rq])
                    pu = [pbank[0], pbank[1]]
                    psc = [pbank[2], pbank[3]]
                    po = [pbank[4], pbank[5]]
                    Sfv = Sf[:].rearrange("p h e -> p (h e)")

                    def state_update(kk, ci, Sb):
                        for h in range(8):
                            hsl = slice(h * 128, (h + 1) * 128)
                            pb_, rpb = pu[h // 4]
                            P.op('tensor', lambda e, pb_=pb_, h=h, hsl=hsl: e.matmul(
                                pb_[:, (h % 4) * 128:(h % 4 + 1) * 128], lhsT=kk[:, hsl], rhs=vv[:, hsl], start=True, stop=True),
                                [rt, rvv], [rpb])
                        P.op('vector', lambda e: e.tensor_tensor(out=Sf[:], in0=Sf[:], in1=dec[:, :, ci:ci + 1].to_broadcast([128, 8, 128]),
                                                                 op=ALU.mult), [rt, rS], [rS])
                        for hh in range(2):
                            pb_, rpb = pu[hh]
                            P.op('vector', lambda e, pb_=pb_, hh=hh: e.tensor_tensor(
                                out=Sfv[:, hh * 512:(hh + 1) * 512], in0=Sfv[:, hh * 512:(hh + 1) * 512], in1=pb_[:, :], op=ALU.add),
                                [rpb, rS], [rS])
                        P.op('scalar', lambda e: e.copy(out=Sb[:], in_=Sf[:]), [rS], [rSb[id(Sb)]])

                    state_update(k0, 0, SbB)
                    for h in range(8):
                        pb_, rpb = psc[h // 4]
                        P.op('tensor', lambda e, pb_=pb_, h=h: e.matmul(pb_[:, (h % 4) * 128:(h % 4 + 1) * 128], lhsT=kT[:, h, :],
                                                                       rhs=qT[:, h, :], start=True, stop=True), [rq], [rpb])
                    for hh in range(2):
                        pb_, rpb = psc[hh]
                        P.op('vector', lambda e, pb_=pb_, hh=hh: e.tensor_tensor(
                            out=scT[:, hh * 4:(hh + 1) * 4, :], in0=pb_[:, :].rearrange("p (h t) -> p h t", h=4), in1=cmask4[:],
                            op=ALU.mult), [rpb, rc], [rscT])
                    for h in range(8):
                        hsl = slice(h * 128, (h + 1) * 128)
                        pb_, rpb = po[h // 4]
                        osl = slice((h % 4) * 128, (h % 4 + 1) * 128)
                        P.op('tensor', lambda e, pb_=pb_, h=h, hsl=hsl, osl=osl: e.matmul(
                            pb_[:, osl], lhsT=scT[:, h, :], rhs=vv[:, hsl], start=True, stop=False), [rscT, rt, rvv], [rpb])
                        P.op('tensor', lambda e, pb_=pb_, h=h, osl=osl: e.matmul(
                            pb_[:, osl], lhsT=q0T[:, h, :], rhs=SbA[:, h, :], start=False, stop=False), [rq, rSb[id(SbA)]], [rpb])
                        P.op('tensor', lambda e, pb_=pb_, h=h, osl=osl: e.matmul(
                            pb_[:, osl], lhsT=q1T[:, h, :], rhs=SbB[:, h, :], start=False, stop=True), [rq, rSb[id(SbB)]], [rpb])
                    for hh in range(2):
                        pb_, rpb = po[hh]
                        P.op('vector', lambda e, pb_=pb_, hh=hh: e.tensor_tensor(
                            out=og[:, hh * 512:(hh + 1) * 512], in0=pb_[:, :], in1=gateT[:, hh * 512:(hh + 1) * 512], op=ALU.mult),
                            [rpb, rgate], [rog])
                    state_update(k1, 1, SbA)
                    P.op('scalar', lambda e: e.activation(out=qs[:], in_=og[:], func=AF.Square), [rt, rog, rqs], [rt, rqs])
                    P.op('vector', lambda e: e.tensor_reduce(out=ss[:], in_=qs[:].rearrange("p (h e) -> p h e", h=8),
                                                             axis=AX.X, op=ALU.add), [rt], [rt])
                    rsqrt(ss[:], ss[:], 1.0 / 128, 1, [rt], [rt])
                    for h in range(8):
                        hsl = slice(h * 128, (h + 1) * 128)
                        P.op('vector', lambda e, h=h, hsl=hsl: e.scalar_tensor_tensor(
                            out=onb[:, hsl], in0=og[:, hsl], scalar=ss[:, h:h + 1], in1=go[:, hsl], op0=ALU.mult, op1=ALU.mult),
                            [rt, rgo, rog], [rt])
                    pt, rpt = PT()
                    for c in range(8):
                        P.op('tensor', lambda e, c=c, pt=pt: e.transpose(out=pt[:, c * 128:(c + 1) * 128],
                                                                        in_=onb[:, c * 128:(c + 1) * 128], identity=ident[:]),
                             [rt, rid], [rpt])
                    P.op('vector', lambda e, pt=pt: e.tensor_copy(out=onT[:].rearrange("p c t -> p (c t)"), in_=pt[:]), [rpt], [rt])
                    for half in range(2):
                        po, rpo = PB()
                        for c in range(8):
                            P.op('tensor', lambda e, c=c, half=half, po=po: e.matmul(
                                po[:, :], lhsT=onT[:, c, :], rhs=wout[:, c, half * 512:(half + 1) * 512],
                                start=(c == 0), stop=(c == 7)), [rt, rw], [rpo], inc=(c == 7))
                        P.op('vector', lambda e, half=half, po=po, x32=x32: e.scalar_tensor_tensor(
                            out=zt_[:, half * 512:(half + 1) * 512], in0=x32[:, half * 512:(half + 1) * 512], scalar=ALPHA,
                            in1=po[:, :], op0=ALU.mult, op1=ALU.add), [rpo, rx32, rqs], [rt, rqs])
                    ln_tail(b, zt_, rt, gam, bet, [rg1, rb1], False, False)
                P.barrier()


        NIT = 18

        def dsa_phase(layer):
            j = layer // 2
            with ExitStack() as ph:
                win = sb(ph, "awin", [128, 8, AIN], BF16)
                wql = sb(ph, "awql", [128, 3, AH * KVR], BF16)
                wqi = sb(ph, "awqi", [128, 3, IDXH * IDXD], BF16)
                wuv = sb(ph, "awuv", [128, 2 * AH, 128], BF16)
                wout = sb(ph, "awout", [128, 8, D], BF16)
                rw = Res()
                P.dma('gpsimd', win[:], Wn['a_w_in'][j].rearrange("(c p) n -> p c n", p=128), [], [rw])
                P.dma('gpsimd', wql[:], Wn['a_w_q_lat'][j].rearrange("(c p) n -> p c n", p=128), [], [rw])
                P.dma('gpsimd', wqi[:], Wn['a_w_q_idx'][j].rearrange("(c p) n -> p c n", p=128), [], [rw])
                P.dma('gpsimd', wuv[:], Wn['a_w_uv'][j].rearrange("h (rc p) d -> p (h rc) d", p=128), [], [rw])
                P.dma('gpsimd', wout[:], Wn['a_w_out'][j].rearrange("(c p) n -> p c n", p=128), [], [rw])
                gam, rg1 = load_bc(ph, "agam1", Wn['ln1_g'][layer:layer + 1, :])
                bet, rb1 = load_bc(ph, "abet1", Wn['ln1_b'][layer:layer + 1, :])
                gq, rgq = load_bc(ph, "agq", Wn['a_g_q'][j:j + 1, :])
                gkv, rgkv = load_bc(ph, "agkv", Wn['a_g_kv'][j:j + 1, :])
                gki, rgki = load_bc(ph, "agki", Wn['a_g_kidx'][j:j + 1, :])
                bki, rbki = load_bc(ph, "abki", Wn['a_b_kidx'][j:j + 1, :])
                maskall = sb(ph, "amaskall", [128, 8, 128], F32)
                rep = sb(ph, "arep", [8, 128], BF16)
                cm = sb(ph, "acm", [128, 128], F32)
                pw = sb(ph, "apw", [128, NIT], F32)
                rc = Res()
                P.dma('sync', maskall[:], maskall_in[:, :, :], [], [rc])
                P.dma('sync', rep[:], rep_in[:, :], [], [rc])
                P.dma('sync', cm[:], cm_in[:, :], [], [rc])
                P.dma('sync', pw[:], pw_in[:, 0:NIT], [], [rc])
                CKV = sb(ph, "aCKV", [128, NB, KVR + 1], BF16)
                CKVT = sb(ph, "aCKVT", [128, 2, S], BF16)
                KIT = sb(ph, "aKIT", [64, S], BF16)
                rK = Res()
                P.op('vector', lambda e: e.memset(CKV[:, :, KVR:KVR + 1], 1.0), [], [rK])
                score = sb(ph, "ascore", [128, S], F32)
                msk = sb(ph, "amsk", [128, S], BF16)
                junk = msk
                zA = sb(ph, "azA", [128, 328], F32)
                mskT_rot = Rot([sb(ph, f"amskT{i}", [128, NB, 128], BF16) for i in range(2)])
                x32_rot = Rot([sb(ph, f"ax32{i}", [128, D], F32) for i in range(1)])
                xh_rot = Rot([sb(ph, f"axh{i}", [128, 8, 130], BF16) for i in range(2)])
                sqa = sb(ph, "asqa", [128, QR], F32)
                cq = sb(ph, "acq", [128, QR], BF16)
                kix = sb(ph, "akix", [128, IDXD], F32)
                kib = sb(ph, "akib", [128, IDXD], BF16)
                wq = sb(ph, "awq", [128, 8], BF16)
                st = sb(ph, "ast", [128, 16], F32)
                cqT_rot = Rot([sb(ph, f"acqT{i}", [128, 3, 128], BF16) for i in range(2)])
                wT = sb(ph, "awT", [8, 128], BF16)
                QIT = sb(ph, "aQIT", [64, 8, 8, 16], BF16)
                Wsel = sb(ph, "aWsel", [128, 8, 128], BF16)
                R_rot = Rot([sb(ph, f"aR{i}", [128, 512], BF16) for i in range(3)])
                mm = sb(ph, "amm", [128, 2, 8], F32)
                bs = sb(ph, "abs", [128, 8], F32)
                Wt = sb(ph, "aWt", [128, NIT], F32)
                qlT4 = sb(ph, "aqlT4", [128, 2, 512], BF16)
                PT_rot = Rot([sb(ph, f"aPT{i}", [128, 4, 128], BF16) for i in range(5)])
                olb = sb(ph, "aolb", [128, AH, KVR], BF16)
                olT = sb(ph, "aolT8", [128, AH, 2, 128], BF16)
                dcol = sb(ph, "adcol", [128, AH], F32)
                rd = sb(ph, "ard8", [128, AH], F32)
                rolb, rolT, rrd = Res(), Res(), Res()
                oh = sb(ph, "aoh", [128, D], BF16)
                ohT = sb(ph, "aohT", [128, 8, 128], BF16)
                rden = sb(ph, "arden", [128, 1], F32)
                z = sb(ph, "az", [128, D], F32)
                rt = Res()
                r_z, r_q, r_cq, r_kv, r_ki, r_w, r_cqT, r_wT, r_qit, r_wsel, r_mm, r_bs = [Res() for _ in range(12)]
                sqa_kv = sb(ph, "asqakv", [128, KVR], F32)
                sqa_ki = sb(ph, "asqaki", [128, IDXD], F32)
                rsc = Res()
                rmk = Res()
                rql = Res()
                ro = Res()

                def rmsn(ps, rps0, c0, n, gtile, rg, out_ap, idx):
                    ss, rs = st[:, idx:idx + 1], st[:, idx + 1:idx + 2]
                    P.op('scalar', lambda e: e.activation(out=sqa[:, 0:n], in_=ps[:, c0:c0 + n], func=AF.Square), [rps0], [r_q])
                    P.op('vector', lambda e: e.tensor_scalar(out=sqa[:, 0:n], in0=sqa[:, 0:n], scalar1=1.0, scalar2=None,
                                                             op0=ALU.mult, op1=ALU.add, accum_out=ss), [r_q], [r_q])
                    rsqrt(rs, ss, 1.0 / n, 1, [r_q], [r_q])
                    P.op('vector', lambda e: e.scalar_tensor_tensor(out=out_ap, in0=ps[:, c0:c0 + n], scalar=rs, in1=gtile[:, 0:n],
                                                                    op0=ALU.mult, op1=ALU.mult), [rps0, r_q, rg], [r_cq])

                blk = {}
                r_zo = Res()

                def SB(b):
                    cqT, r_cqT = cqT_rot.next()
                    mskT, rmkT = mskT_rot.next()
                    L = (b + 1) * 128
                    xh, rxh = xh_rot.next()
                    P.dma('sync', xh[:, :, 2:130], XTH[b, :, :, 2:130], [rXTH[b]], [rxh])
                    ps0, rps0 = PB()
                    ps1, rps1 = PB()
                    for c in range(8):
                        P.op('tensor', lambda e, c=c, ps0=ps0: e.matmul(ps0[:, :], lhsT=xh[:, c, 2:130], rhs=win[:, c, 0:512],
                                                                       start=(c == 0), stop=(c == 7)), [rw, rxh], [rps0], inc=(c == 7))
                    for c in range(8):
                        P.op('tensor', lambda e, c=c, ps1=ps1: e.matmul(ps1[:, 0:AIN - 512], lhsT=xh[:, c, 2:130], rhs=win[:, c, 512:AIN],
                                                                       start=(c == 0), stop=(c == 7)), [rw, rxh], [rps1], inc=(c == 7))
                    rmsn(ps0, rps0, 0, QR, gq, rgq, cq[:], 0)
                    P.op('scalar', lambda e, ps0=ps0: e.copy(out=zA[:, 0:128], in_=ps0[:, QR:512]), [rps0], [r_z])
                    P.op('scalar', lambda e, ps1=ps1: e.copy(out=zA[:, 128:128 + 200], in_=ps1[:, 0:200]), [rps1], [r_z])
                    ss, rs = st[:, 2:3], st[:, 3:4]
                    P.op('scalar', lambda e: e.activation(out=sqa_kv[:, 0:KVR], in_=zA[:, 0:KVR], func=AF.Square), [r_z], [r_kv])
                    P.op('vector', lambda e: e.tensor_scalar(out=sqa_kv[:, 0:KVR], in0=sqa_kv[:, 0:KVR], scalar1=1.0, scalar2=None,
                                                             op0=ALU.mult, op1=ALU.add, accum_out=ss), [r_kv], [r_kv])
                    rsqrt(rs, ss, 1.0 / KVR, 1, [r_kv], [r_kv])
                    P.op('vector', lambda e, b=b: e.scalar_tensor_tensor(out=CKV[:, b, 0:KVR], in0=zA[:, 0:KVR], scalar=rs, in1=gkv[:, :],
                                                                         op0=ALU.mult, op1=ALU.mult), [r_kv, r_z, rgkv], [rK])
                    s1, s2, mean, msq, var, rstd, nmr = [st[:, 4 + i:5 + i] for i in range(7)]
                    ki = zA[:, 256:320]
                    P.op('vector', lambda e: e.tensor_scalar(out=kix[:], in0=ki, scalar1=1.0, scalar2=None, op0=ALU.mult,
                                                             op1=ALU.add, accum_out=s1), [r_z], [r_ki])
                    P.op('scalar', lambda e: e.activation(out=sqa_ki[:, 0:IDXD], in_=ki, func=AF.Square), [r_z], [r_ki])
                    P.op('vector', lambda e: e.tensor_scalar(out=sqa_ki[:, 0:IDXD], in0=sqa_ki[:, 0:IDXD], scalar1=1.0, scalar2=None,
                                                             op0=ALU.mult, op1=ALU.add, accum_out=s2), [r_ki], [r_ki])
                    P.op('vector', lambda e: e.tensor_scalar(out=mean, in0=s1, scalar1=1.0 / IDXD, scalar2=None, op0=ALU.mult), [r_ki], [r_ki])
                    P.op('vector', lambda e: e.tensor_tensor(out=msq, in0=mean, in1=mean, op=ALU.mult), [r_ki], [r_ki])
                    P.op('vector', lambda e: e.scalar_tensor_tensor(out=var, in0=s2, scalar=1.0 / IDXD, in1=msq, op0=ALU.mult,
                                                                    op1=ALU.subtract), [r_ki], [r_ki])
                    rsqrt(rstd, var, 1.0, 0, [r_ki], [r_ki])
                    P.op('vector', lambda e: e.scalar_tensor_tensor(out=nmr, in0=mean, scalar=-1.0, in1=rstd, op0=ALU.mult,
                                                                    op1=ALU.mult), [r_ki], [r_ki])
                    P.op('scalar', lambda e: e.activation(out=kix[:], in_=ki, func=AF.Identity, scale=rstd, bias=nmr), [r_ki, r_z], [r_ki])
                    P.op('vector', lambda e: e.tensor_tensor(out=kix[:], in0=kix[:], in1=gki[:], op=ALU.mult), [r_ki, rgki], [r_ki])
                    P.op('vector', lambda e: e.tensor_tensor(out=kib[:], in0=kix[:], in1=bki[:], op=ALU.add), [r_ki, rbki], [r_ki])
                    P.op('vector', lambda e: e.tensor_scalar(out=wq[:], in0=zA[:, 320:328], scalar1=float((IDXH * IDXD) ** -0.5),
                                                             scalar2=None, op0=ALU.mult), [r_z], [r_w])
                    pt, rpt = PT()
                    for c in range(3):
                        P.op('tensor', lambda e, c=c, pt=pt: e.transpose(out=pt[:, c * 128:(c + 1) * 128], in_=cq[:, c * 128:(c + 1) * 128],
                                                                        identity=ident[:]), [r_cq, rid], [rpt])
                    for c in range(2):
                        P.op('tensor', lambda e, c=c, pt=pt, b=b: e.transpose(out=pt[:, (3 + c) * 128:(4 + c) * 128],
                                                                             in_=CKV[:, b, c * 128:(c + 1) * 128], identity=ident[:]),
                             [rK, rid], [rpt])
                    P.op('tensor', lambda e, pt=pt: e.transpose(out=pt[0:64, 5 * 128:6 * 128], in_=kib[:, :], identity=ident[:]),
                         [r_ki, rid], [rpt])
                    P.op('tensor', lambda e, pt=pt: e.transpose(out=pt[0:8, 6 * 128:7 * 128], in_=wq[:, :], identity=ident[:]),
                         [r_w, rid], [rpt])
                    P.op('vector', lambda e, pt=pt: e.tensor_copy(out=cqT[:].rearrange("p c t -> p (c t)"), in_=pt[:, 0:384]), [rpt], [r_cqT])
                    for c in range(2):
                        P.op('vector', lambda e, pt=pt, c=c, b=b: e.tensor_copy(out=CKVT[:, c, b * 128:(b + 1) * 128],
                                                                               in_=pt[:, (3 + c) * 128:(4 + c) * 128]), [rpt], [rK])
                    P.op('vector', lambda e, pt=pt, b=b: e.tensor_copy(out=KIT[:, b * 128:(b + 1) * 128], in_=pt[0:64, 640:768]), [rpt], [rK])
                    P.op('vector', lambda e, pt=pt: e.tensor_copy(out=wT[:], in_=pt[0:8, 768:896]), [rpt], [r_wT])
                    for hh in range(2):
                        pq, rpq = PB()
                        for h4 in range(4):
                            h = hh * 4 + h4
                            for kc in range(3):
                                P.op('tensor', lambda e, pq=pq, h=h, h4=h4, kc=kc: e.matmul(
                                    pq[0:64, h4 * 128:(h4 + 1) * 128], lhsT=wqi[:, kc, h * 64:(h + 1) * 64], rhs=cqT[:, kc, :],
                                    start=(kc == 0), stop=(kc == 2)), [rw, r_cqT], [rpq], inc=(kc == 2))
                        P.op('scalar', lambda e, pq=pq, hh=hh: e.copy(
                            out=QIT[:, :, hh * 4:(hh + 1) * 4, :].rearrange("p g h t -> p h g t"),
                            in_=pq[0:64, :].rearrange("p (h g t) -> p h g t", h=4, g=8)), [rpq], [r_qit])
                    pe_, rpe = PB()
                    P.op('tensor', lambda e, pe_=pe_: e.matmul(pe_[:, 0:128], lhsT=rep[:, :], rhs=wT[:, :], start=True, stop=True),
                         [rc, r_wT], [rpe])
                    for g in range(8):
                        P.op('vector', lambda e, g=g, pe_=pe_: e.tensor_tensor(out=Wsel[:, g, :], in0=pe_[:, 0:128], in1=maskall[:, g, :],
                                                                              op=ALU.mult), [rpe, rc], [r_wsel])
                    nkc = (L + 511) // 512
                    for kc in range(nkc):
                        k0_ = kc * 512
                        n = min(512, L - k0_)
                        psc, rpsc = pbank[4]
                        pend = []

                        def emit_mm1(g, k0_=k0_, n=n):
                            p1, rp1 = PB()
                            P.op('tensor', lambda e, p1=p1, g=g: e.matmul(
                                p1[:, 0:n], lhsT=QIT[:, g, :, :].rearrange("p h t -> p (h t)"), rhs=KIT[:, k0_:k0_ + n], start=True, stop=True),
                                [r_qit, rK], [rp1])
                            R, rR = R_rot.next()
                            if g % 2 == 0:
                                P.op('scalar', lambda e, p1=p1, R=R: e.activation(out=R[:, 0:n], in_=p1[:, 0:n], func=AF.Relu),
                                     [rp1], [rR])
                            else:
                                P.op('vector', lambda e, p1=p1, R=R: e.tensor_scalar(out=R[:, 0:n], in0=p1[:, 0:n], scalar1=0.0,
                                                                                    scalar2=None, op0=ALU.max), [rp1], [rR])
                            pend.append((R, rR))

                        emit_mm1(0)
                        emit_mm1(1)
                        for g in range(8):
                            if g + 2 < 8:
                                emit_mm1(g + 2)
                            R, rR = pend.pop(0)
                            P.op('tensor', lambda e, psc=psc, g=g, R=R, n=n: e.matmul(
                                psc[:, 0:n], lhsT=Wsel[:, g, :], rhs=R[:, 0:n], start=(g == 0), stop=(g == 7)), [r_wsel, rR], [rpsc])
                        P.op('scalar', lambda e, psc=psc, k0_=k0_, n=n: e.copy(out=score[:, k0_:k0_ + n], in_=psc[:, 0:n]), [rpsc], [rsc])
                        P.op('vector', lambda e, psc=psc, kc=kc, n=n: e.tensor_reduce(out=mm[:, 0, kc:kc + 1], in_=psc[:, 0:n], axis=AX.X,
                                                                                     op=ALU.min), [rpsc], [r_mm])
                        P.op('vector', lambda e, psc=psc, kc=kc, n=n: e.tensor_reduce(out=mm[:, 1, kc:kc + 1], in_=psc[:, 0:n], axis=AX.X,
                                                                                     op=ALU.max), [rpsc], [r_mm])
                    P.op('vector', lambda e, L=L: e.tensor_tensor(out=score[:, L - 128:L], in0=score[:, L - 128:L], in1=cm[:], op=ALU.add),
                         [rsc, rc], [rsc])
                    lo, hi, w0, mid, cnt, stp = [bs[:, i:i + 1] for i in range(6)]
                    P.op('vector', lambda e, nkc=nkc: e.tensor_reduce(out=lo, in_=mm[:, 0, 0:nkc], axis=AX.X, op=ALU.min), [r_mm], [r_bs])
                    P.op('vector', lambda e, nkc=nkc: e.tensor_reduce(out=hi, in_=mm[:, 1, 0:nkc], axis=AX.X, op=ALU.max), [r_mm], [r_bs])
                    P.op('vector', lambda e: e.tensor_tensor(out=w0, in0=hi, in1=lo, op=ALU.subtract), [r_bs], [r_bs])
                    P.op('vector', lambda e: e.tensor_scalar(out=w0, in0=w0, scalar1=1.0001, scalar2=1e-6, op0=ALU.mult, op1=ALU.add), [r_bs], [r_bs])
                    P.op('vector', lambda e: e.tensor_scalar(out=Wt[:], in0=pw[:], scalar1=w0, scalar2=None, op0=ALU.mult), [r_bs, rc], [r_bs])
                    if L > TOPK:
                        P.op('vector', lambda e: e.tensor_tensor(out=mid, in0=lo, in1=Wt[:, 0:1], op=ALU.add), [r_bs], [r_bs])
                        for k in range(NIT):
                            P.op('vector', lambda e, L=L: e.tensor_scalar(out=junk[:, 0:L], in0=score[:, 0:L], scalar1=mid, scalar2=None,
                                                                         op0=ALU.is_ge, op1=ALU.add, accum_out=cnt), [r_bs, rsc], [r_bs, rmk])
                            P.op('vector', lambda e, k=k: e.scalar_tensor_tensor(out=stp, in0=cnt, scalar=float(TOPK) - 0.5, in1=Wt[:, k:k + 1],
                                                                                op0=ALU.is_ge, op1=ALU.mult), [r_bs], [r_bs])
                            if k < NIT - 1:
                                P.op('vector', lambda e, k=k: e.scalar_tensor_tensor(out=mid, in0=mid, scalar=Wt[:, k + 1:k + 2], in1=stp,
                                                                                    op0=ALU.subtract, op1=ALU.add), [r_bs], [r_bs])
                            else:
                                P.op('vector', lambda e, k=k: e.scalar_tensor_tensor(out=lo, in0=mid, scalar=Wt[:, k:k + 1], in1=stp,
                                                                                    op0=ALU.subtract, op1=ALU.add), [r_bs], [r_bs])
                    P.op('vector', lambda e, L=L: e.tensor_scalar(out=msk[:, 0:L], in0=score[:, 0:L], scalar1=lo, scalar2=None,
                                                                 op0=ALU.is_ge), [r_bs, rsc], [rmk])
                    for kb0 in range(0, b + 1, 8):
                        nk = min(8, b + 1 - kb0)
                        pt, rpt = PT()
                        for i in range(nk):
                            P.op('tensor', lambda e, pt=pt, i=i, kb0=kb0: e.transpose(
                                out=pt[:, i * 128:(i + 1) * 128], in_=msk[:, (kb0 + i) * 128:(kb0 + i + 1) * 128], identity=ident[:]),
                                [rmk, rid], [rpt])
                        P.op('vector', lambda e, pt=pt, nk=nk, kb0=kb0: e.tensor_copy(
                            out=mskT[:, kb0:kb0 + nk, :].rearrange("p k t -> p (k t)"), in_=pt[:, 0:nk * 128]), [rpt], [rmkT])
                    blk[b] = (cqT, r_cqT, mskT, rmkT)

                def ATT(b):
                    cqT, r_cqT, mskT, rmkT = blk.pop(b)
                    x32, rx32 = x32_rot.next()
                    P.dma('sync', x32[:], xsrc(layer)[b * 128:(b + 1) * 128, :], [rX32[b]], [rx32])
                    qk_i = [0]

                    def QKB():
                        r = pbank[4 + qk_i[0]]
                        qk_i[0] ^= 1
                        return r

                    for hg in range(2):
                        for rcx in range(2):
                            pq, rpq = QKB()
                            for h4 in range(4):
                                ch = (hg * 4 + h4) * 2 + rcx
                                for kc in range(3):
                                    P.op('tensor', lambda e, pq=pq, h4=h4, ch=ch, kc=kc: e.matmul(
                                        pq[:, h4 * 128:(h4 + 1) * 128], lhsT=wql[:, kc, ch * 128:(ch + 1) * 128], rhs=cqT[:, kc, :],
                                        start=(kc == 0), stop=(kc == 2)), [rw, r_cqT], [rpq], inc=(kc == 2))
                            P.op('scalar', lambda e, pq=pq, rcx=rcx: e.activation(out=qlT4[:, rcx, :], in_=pq[:, :], func=AF.Copy,
                                                                                scale=float(KVR ** -0.5)), [rpq], [rql])
                        pendq = []

                        def emit_qk(kb):
                            pst, rpst = QKB()
                            for rcx in range(2):
                                P.op('tensor', lambda e, pst=pst, rcx=rcx, kb=kb: e.matmul(
                                    pst[:, :], lhsT=CKVT[:, rcx, kb * 128:(kb + 1) * 128], rhs=qlT4[:, rcx, :],
                                    start=(rcx == 0), stop=(rcx == 1)), [rK, rql], [rpst], inc=(rcx == 1))
                            PTt, rPT = PT_rot.next()
                            P.op('scalar', lambda e, pst=pst, PTt=PTt: e.activation(
                                out=PTt[:].rearrange("p k t -> p (k t)"), in_=pst[:, :], func=AF.Exp), [rpst], [rPT])
                            P.op('gpsimd', lambda e, PTt=PTt, kb=kb: e.tensor_tensor(
                                out=PTt[:], in0=PTt[:], in1=mskT[:, kb:kb + 1, :].to_broadcast([128, 4, 128]), op=ALU.mult), [rPT, rmkT], [rPT])
                            pendq.append((PTt, rPT))

                        for kb in range(min(3, b + 1)):
                            emit_qk(kb)
                        for kb in range(b + 1):
                            if kb + 3 <= b:
                                emit_qk(kb + 3)
                            PTt, rPT = pendq.pop(0)
                            for h4 in range(4):
                                pol, rpol = pbank[h4]
                                P.op('tensor', lambda e, pol=pol, PTt=PTt, h4=h4, kb=kb, b=b: e.matmul(
                                    pol[:, 0:KVR + 1], lhsT=PTt[:, h4, :], rhs=CKV[:, kb, :], start=(kb == 0), stop=(kb == b)),
                                    [rPT, rK], [rpol])
                        for h4 in range(4):
                            pol, rpol = pbank[h4]
                            h = hg * 4 + h4
                            P.op('scalar', lambda e, pol=pol, h=h: e.copy(out=olb[:, h, :], in_=pol[:, 0:KVR]), [rpol], [rolb])
                            P.op('scalar', lambda e, pol=pol, h=h: e.copy(out=dcol[:, h:h + 1], in_=pol[:, KVR:KVR + 1]), [rpol], [rolb])
                    P.op('vector', lambda e: e.reciprocal(out=rd[:], in_=dcol[:]), [rolb], [rrd])
                    pts = [PT(), PT()]
                    for h in range(8):
                        pt, rpt = pts[h // 4]
                        for rcx in range(2):
                            P.op('tensor', lambda e, pt=pt, h=h, rcx=rcx: e.transpose(
                                out=pt[:, ((h % 4) * 2 + rcx) * 128:((h % 4) * 2 + rcx + 1) * 128],
                                in_=olb[:, h, rcx * 128:(rcx + 1) * 128], identity=ident[:]), [rolb, rid], [rpt])
                    for hh in range(2):
                        pt, rpt = pts[hh]
                        P.op('scalar', lambda e, pt=pt, hh=hh: e.copy(
                            out=olT[:, hh * 4:(hh + 1) * 4, :, :].rearrange("p h c t -> p (h c t)"), in_=pt[:, :]), [rpt], [rolT])
                    puvs = [QKB(), QKB()]
                    for h in range(8):
                        puv, rpuv = puvs[h // 4]
                        for rcx in range(2):
                            P.op('tensor', lambda e, puv=puv, rcx=rcx, h=h: e.matmul(
                                puv[:, (h % 4) * 128:(h % 4 + 1) * 128], lhsT=olT[:, h, rcx, :], rhs=wuv[:, h * 2 + rcx, :],
                                start=(rcx == 0), stop=(rcx == 1)), [rolT, rw], [rpuv], inc=(rcx == 1))
                    for hh in range(2):
                        puv, rpuv = puvs[hh]
                        P.op('vector', lambda e, puv=puv, hh=hh: e.tensor_tensor(
                            out=oh[:, hh * 512:(hh + 1) * 512].rearrange("p (h d) -> p h d", h=4),
                            in0=puv[:, :].rearrange("p (h d) -> p h d", h=4),
                            in1=rd[:, hh * 4:(hh + 1) * 4].rearrange("p (h o) -> p h o", o=1).to_broadcast([128, 4, 128]),
                            op=ALU.mult), [rpuv, rrd], [ro])
                    pt, rpt = PT()
                    for c in range(8):
                        P.op('tensor', lambda e, c=c, pt=pt: e.transpose(out=pt[:, c * 128:(c + 1) * 128],
                                                                        in_=oh[:, c * 128:(c + 1) * 128], identity=ident[:]),
                             [ro, rid], [rpt])
                    P.op('vector', lambda e, pt=pt: e.tensor_copy(out=ohT[:].rearrange("p c t -> p (c t)"), in_=pt[:]), [rpt], [ro])
                    for half in range(2):
                        po, rpo = PB()
                        for c in range(8):
                            P.op('tensor', lambda e, c=c, half=half, po=po: e.matmul(
                                po[:, :], lhsT=ohT[:, c, :], rhs=wout[:, c, half * 512:(half + 1) * 512],
                                start=(c == 0), stop=(c == 7)), [ro, rw], [rpo], inc=(c == 7))
                        P.op('vector', lambda e, half=half, po=po, x32=x32: e.scalar_tensor_tensor(
                            out=z[:, half * 512:(half + 1) * 512], in0=x32[:, half * 512:(half + 1) * 512], scalar=ALPHA,
                            in1=po[:, :], op0=ALU.mult, op1=ALU.add), [rpo, rx32], [r_zo])
                    ln_tail(b, z, r_zo, gam, bet, [rg1, rb1], False, False)

                SB(0)
                for b in range(NB):
                    if b + 1 < NB:
                        SB(b + 1)
                    ATT(b)
                P.barrier()

        def mixer_phase(layer):
            if MIXERS[layer % 2] is None:
                identity_mixer_phase(layer)
            elif layer % 2 == 0:
                dsa_phase(layer)
            else:
                hgrn_phase(layer)

        for layer in layers:
            mixer_phase(layer)
            ffn_phase(layer, last=(layer == layers[-1]))
        for e in ('sync',):
            if rY.w is not None:
                P._wait(e, rY.w[0], rY.w[1])
        P.barrier()
        print("instructions:", P.ninst)
    return nc


MIXERS = [True, True]

WSHAPES = {
    'a_w_in': [2, D, AIN], 'a_g_q': [2, QR], 'a_g_kv': [2, KVR], 'a_w_q_lat': [2, QR, AH * KVR],
    'a_w_q_idx': [2, QR, IDXH * IDXD], 'a_g_kidx': [2, IDXD], 'a_b_kidx': [2, IDXD],
    'a_w_uv': [2, AH, KVR, 128], 'a_w_out': [2, D, D], 'b_w_in': [2, D, 4 * D], 'b_lb_logits': [4, D],
    'b_g_o': [2, D], 'b_w_out': [2, D, D], 'ln1_g': [4, D], 'ln1_b': [4, D], 'f_w_up': [4, D, 2 * DFF],
    'f_conv_w': [4, 3, 1, 2 * DFF], 'f_conv_b': [4, 2 * DFF], 'f_w_down': [4, DFF, D], 'ln2_g': [4, D], 'ln2_b': [4, D],
}


def host_consts():
    ident = np.eye(128, dtype=np.float32).astype(ml_dtypes.bfloat16)
    s_ = np.arange(128)[:, None]
    t_ = np.arange(128)[None, :]
    same = (s_ // 64) == (t_ // 64)
    hconst = np.zeros((128, 4, 128), np.float32)
    hconst[:, 0, :] = (same & (s_ <= t_))
    hconst[:, 1, :] = same
    hconst[:, 3, :] = (same & (s_ <= t_))
    hcind = np.zeros((128, 4), np.float32)
    hcind[:64, 0] = 1.0
    hcind[64:, 1] = 1.0
    p_ = np.arange(128)
    maskall = np.zeros((128, 8, 128), np.float32)
    for g in range(8):
        maskall[p_, g, 16 * g + (p_ % 16)] = 1.0
    rep = np.zeros((8, 128), np.float32)
    rep[p_ // 16, p_] = 1.0
    cm = np.where(np.arange(128)[None, :] <= np.arange(128)[:, None], 0.0, -1e30).astype(np.float32)
    pw = np.tile((0.5 ** np.arange(1, 33, dtype=np.float64)).astype(np.float32)[None, :], (128, 1))
    return {'ident': ident, 'hconst': hconst, 'hcind': hcind, 'maskall': maskall,
            'rep': rep.astype(ml_dtypes.bfloat16), 'cm': cm, 'pw': pw}


def kernel(**inputs):
    x = np.ascontiguousarray(np.asarray(inputs['x'], dtype=np.float32))
    B, S, _ = x.shape
    nc = build_program(S)
    consts = host_consts()
    wmaps = {k: np.ascontiguousarray(np.asarray(inputs[k], dtype=np.float32)) for k in WSHAPES}
    in_maps = []
    for c in range(8):
        m = dict(wmaps)
        m['x'] = x[(c // 2) % B]
        m.update(consts)
        in_maps.append(m)
    res = run_bass_kernel_spmd(nc, in_maps, core_ids=list(range(8)))
    out = np.stack([res.results[2 * b]['y'] for b in range(B)], axis=0)
    return out.astype(np.float32)
```

```python
from contextlib import ExitStack
import numpy as np
import ml_dtypes
import concourse.bass as bass
import concourse.mybir as mybir
from concourse.bass_utils import run_bass_kernel_spmd

F32 = mybir.dt.float32
BF16 = mybir.dt.bfloat16
ALU = mybir.AluOpType
AF = mybir.ActivationFunctionType
AX = mybir.AxisListType

D = 1024
DFF = 2816
NFC = DFF // 128
DEPTH = 4
ALPHA = (2 * DEPTH) ** 0.25
LN_EPS = 1e-5
RMS_EPS = 1e-6
QR, KVR, IDXD, IDXH, AH = 384, 256, 64, 8, 8
AIN = QR + KVR + IDXD + IDXH

ENGS = ['tensor', 'vector', 'scalar', 'gpsimd', 'sync']
DMAQ = ['sync', 'scalar', 'gpsimd']


class Res:
    __slots__ = ('w', 'r', 'name', 'excl')

    def __init__(self, name='', excl=False):
        self.w = None
        self.r = []
        self.name = name
        self.excl = excl


class Prog:
    def __init__(self, nc, stack, n_dma_slots=4):
        self.nc = nc
        self.sem = {e: stack.enter_context(nc.semaphore(f"s_{e}")) for e in ENGS}
        self.cnt = {e: 0 for e in ENGS}
        self.nslots = n_dma_slots
        self.dsem, self.dcnt, self.dnext = {}, {}, {}
        for qn in DMAQ:
            for s in range(n_dma_slots):
                self.dsem[(qn, s)] = stack.enter_context(nc.semaphore(f"d_{qn}_{s}"))
                self.dcnt[(qn, s)] = 0
            self.dnext[qn] = 0
        self.seen = {e: {} for e in ENGS}
        self.ninst = 0

    def _wait(self, eng, prod, val):
        if self.seen[eng].get(prod, 0) >= val:
            return
        sem = self.sem[prod] if isinstance(prod, str) else self.dsem[prod]
        getattr(self.nc, eng).wait_ge(sem, val)
        self.seen[eng][prod] = val

    def _deps(self, eng, reads, writes):
        deps = []
        for b in reads:
            if b.w is not None:
                deps.append(b.w)
            if b.excl:
                deps.extend(b.r)
        for b in writes:
            if b.w is not None:
                if not (b.excl and eng == 'tensor' and b.w[0] == 'tensor'):
                    deps.append(b.w)
            deps.extend(b.r)
        for (p, v) in deps:
            self._wait(eng, p, v)

    def _commit(self, me, reads, writes):
        for b in reads:
            b.r.append(me)
        for b in writes:
            b.w = me
            b.r = []

    def op(self, eng, fn, reads=(), writes=(), inc=True):
        self._deps(eng, reads, writes)
        if inc:
            self.cnt[eng] += 1
            fn(getattr(self.nc, eng)).then_inc(self.sem[eng], 1)
            self._commit((eng, self.cnt[eng]), reads, writes)
        else:
            fn(getattr(self.nc, eng))
            self._commit((eng, self.cnt[eng] + 1), reads, writes)
        self.ninst += 1

    def dma(self, qn, out, in_, reads=(), writes=(), **kw):
        s = self.dnext[qn]
        self.dnext[qn] = (s + 1) % self.nslots
        key = (qn, s)
        prev = 16 * self.dcnt[key]
        if prev:
            self._wait(qn, key, prev)
        self._deps(qn, reads, writes)
        self.dcnt[key] += 1
        getattr(self.nc, qn).dma_start(out=out, in_=in_, **kw).then_inc(self.dsem[key], 16)
        self._commit((key, 16 * self.dcnt[key]), reads, writes)
        self.ninst += 1

    def barrier(self):
        for e in ENGS:
            for p in ENGS:
                if p != e and self.cnt[p]:
                    self._wait(e, p, self.cnt[p])
            for k, c in self.dcnt.items():
                if c:
                    self._wait(e, k, 16 * c)


class Rot:
    def __init__(self, tiles):
        self.t = [(t, Res()) for t in tiles]
        self.i = 0

    def next(self):
        r = self.t[self.i]
        self.i = (self.i + 1) % len(self.t)
        return r


def build_program(S, layers=(0, 1, 2, 3), first=True):
    layers = tuple(layers)
    NB = S // 128
    TOPK = min(256, S // 4)
    nc = bass.Bass("TRN2", target_bir_lowering=False)

    def din(name, shape, dt=F32):
        return nc.dram_tensor(name, list(shape), dt, kind="ExternalInput").ap()

    x_in = din("x", [S, D])
    Wn = {}
    for name, shape in WSHAPES.items():
        Wn[name] = din(name, shape)
    ident_in = din("ident", [128, 128], BF16)
    hconst_in = din("hconst", [128, 4, 128])
    hcind_in = din("hcind", [128, 4])
    maskall_in = din("maskall", [128, 8, 128])
    rep_in = din("rep", [8, 128], BF16)
    cm_in = din("cm", [128, 128])
    pw_in = din("pw", [128, 32])
    y_out = nc.dram_tensor("y", [S, D], F32, kind="ExternalOutput").ap()
    X32 = nc.dram_tensor("X32", [S, D], F32).ap()
    XTH = nc.dram_tensor("XTH", [NB + 1, 128, 8, 130], BF16).ap()
    rX32 = [Res() for _ in range(NB)]
    rXTH = [Res() for _ in range(NB + 1)]
    rXTHh = [Res() for _ in range(NB + 1)]
    rY = Res()
    WUb = nc.dram_tensor("WUb", [4, NFC, 128, 8, 256], BF16).ap()
    rWb = [Res() for _ in range(4)]

    with ExitStack() as top:
        P = Prog(nc, top)
        uid = [0]

        def sb(st, name, shape, dt):
            uid[0] += 1
            return st.enter_context(nc.sbuf_tensor(f"t{uid[0]}_{name}", list(shape), dt))
        pbank = [(top.enter_context(nc.psum_tensor(f"pb{i}", [128, 512], F32)), Res(excl=True)) for i in range(6)]
        ptb = [(top.enter_context(nc.psum_tensor(f"pt{i}", [128, 1024], BF16)), Res(excl=True)) for i in range(2)]
        pbi = [0]
        pti = [0]

        def PB():
            r = pbank[pbi[0]]
            pbi[0] = (pbi[0] + 1) % 4
            return r

        def PT():
            r = ptb[pti[0]]
            pti[0] = (pti[0] + 1) % len(ptb)
            return r

        epsT = sb(top, "epsT", [128, 2], F32)
        reps = Res()
        P.op('vector', lambda e: e.memset(epsT[:, 0:1], LN_EPS), [], [reps])
        P.op('vector', lambda e: e.memset(epsT[:, 1:2], RMS_EPS), [], [reps])

        def rsqrt(out, in_, scale, which, reads, writes):
            P.op('scalar', lambda e: e.activation(out=out, in_=in_, func=AF.Sqrt, scale=scale, bias=epsT[:, which:which + 1]),
                 list(reads) + [reps], writes)
            P.op('vector', lambda e: e.reciprocal(out=out, in_=out), writes, writes)

        ident = sb(top, "ident_sb", [128, 128], BF16)
        rid = Res()
        P.dma('sync', ident[:], ident_in[:, :], writes=[rid])

        def convert_wu(l, j):
            for part in range(2):
                col = part * DFF + j * 128
                P.dma('gpsimd', WUb[l, j, :, :, part * 128:(part + 1) * 128],
                      Wn['f_w_up'][l, :, col:col + 128].rearrange("(c p) n -> p c n", p=128), [], [rWb[l]])

        for j in range(NFC):
            convert_wu(layers[0], j)

        lnst = ExitStack()
        top.enter_context(lnst)
        yb_rot = Rot([sb(top, f"yb{i}", [128, D], BF16) for i in range(1)])
        xtb_rot = Rot([sb(top, f"xtb{i}", [128, 8, 128], BF16) for i in range(2)])
        yn_rot = Rot([sb(top, f"yn{i}", [128, D], F32) for i in range(1)])
        sq_rot = Rot([sb(top, f"sq{i}", [128, D], F32) for i in range(1)])
        st_rot = Rot([sb(top, f"st{i}", [128, 8], F32) for i in range(2)])
        eng_flip = [0]

        def transpose_store(b, y32, ry, halo):
            yb, ryb = yb_rot.next()
            P.op('scalar', lambda e: e.copy(out=yb[:], in_=y32[:]), [ry], [ryb])
            pt, rpt = PT()
            for c in range(8):
                P.op('tensor', lambda e, c=c: e.transpose(out=pt[:, c * 128:(c + 1) * 128],
                                                           in_=yb[:, c * 128:(c + 1) * 128], identity=ident[:]),
                     [ryb, rid], [rpt])
            xtb, rxtb = xtb_rot.next()
            P.op('vector', lambda e: e.tensor_copy(out=xtb[:].rearrange("p c t -> p (c t)"), in_=pt[:]), [rpt], [rxtb])
            P.dma('gpsimd', XTH[b, :, :, 2:130], xtb[:], [rxtb], [rXTH[b]])
            if halo:
                P.dma('gpsimd', XTH[b + 1, :, :, 0:2], xtb[:, :, 126:128], [rxtb], [rXTHh[b + 1]])

        def ln_tail(b, z, rz, gam, bet, rgb, to_out, halo):
            st_, rst = st_rot.next()
            sq, rsq = sq_rot.next()
            yn, ryn = yn_rot.next()
            s1, s2, mean, msq, var, rstd, nmr = [st_[:, i:i + 1] for i in range(7)]
            P.op('vector', lambda e: e.tensor_scalar(out=yn[:], in0=z[:], scalar1=1.0, scalar2=None, op0=ALU.mult,
                                                     op1=ALU.add, accum_out=s1), [rz], [ryn, rst])
            P.op('scalar', lambda e: e.activation(out=sq[:], in_=z[:], func=AF.Square), [rz], [rsq])
            P.op('vector', lambda e: e.tensor_scalar(out=sq[:], in0=sq[:], scalar1=1.0, scalar2=None, op0=ALU.mult,
                                                     op1=ALU.add, accum_out=s2), [rsq], [rsq, rst])
            P.op('vector', lambda e: e.tensor_scalar(out=mean, in0=s1, scalar1=1.0 / D, scalar2=None, op0=ALU.mult), [rst], [rst])
            P.op('vector', lambda e: e.tensor_tensor(out=msq, in0=mean, in1=mean, op=ALU.mult), [rst], [rst])
            P.op('vector', lambda e: e.scalar_tensor_tensor(out=var, in0=s2, scalar=1.0 / D, in1=msq, op0=ALU.mult,
                                                            op1=ALU.subtract), [rst], [rst])
            rsqrt(rstd, var, 1.0, 0, [rst], [rst])
            P.op('vector', lambda e: e.scalar_tensor_tensor(out=nmr, in0=mean, scalar=-1.0, in1=rstd, op0=ALU.mult,
                                                            op1=ALU.mult), [rst], [rst])
            P.op('scalar', lambda e: e.activation(out=yn[:], in_=z[:], func=AF.Identity, scale=rstd, bias=nmr),
                 [rz, rst], [ryn])
            P.op('gpsimd', lambda e: e.tensor_tensor(out=yn[:], in0=yn[:], in1=gam[:], op=ALU.mult), [ryn] + list(rgb), [ryn])
            P.op('vector', lambda e: e.tensor_tensor(out=yn[:], in0=yn[:], in1=bet[:], op=ALU.add), [ryn] + list(rgb), [ryn])
            if to_out:
                P.dma('gpsimd', y_out[b * 128:(b + 1) * 128, :], yn[:], [ryn], [rY])
            else:
                P.dma('gpsimd', X32[b * 128:(b + 1) * 128, :], yn[:], [ryn], [rX32[b]])
                transpose_store(b, yn, ryn, halo)

        if first:
            with ExitStack() as ph:
                zt = sb(ph, "zt", [128, 8, 2], BF16)
                rzt = Res()
                P.op('vector', lambda e: e.memset(zt[:], 0.0), [], [rzt])
                P.dma('sync', XTH[0, :, :, 0:2], zt[:], [rzt], [rXTHh[0]])
                xl_rot = Rot([sb(ph, f"xl{i}", [128, D], F32) for i in range(2)])
                for b in range(NB):
                    xl, rxl = xl_rot.next()
                    P.dma('scalar', xl[:], x_in[b * 128:(b + 1) * 128, :], [], [rxl])
                    transpose_store(b, xl, rxl, False)
                P.barrier()

        def xsrc(layer):
            return x_in if (first and layer == layers[0]) else X32

        def load_bc(st, name, src_row):
            t = sb(st, name, [128, src_row.shape[-1]], F32)
            r = Res()
            P.dma('sync', t[:], src_row.to_broadcast([128, src_row.shape[-1]]), [], [r])
            return t, r

        TT = 512 if NB % 4 == 0 else 128 * NB
        NBT = TT // 128

        def ffn_phase(layer, last):
            with ExitStack() as ph:
                wd = sb(ph, "wd", [128, NFC, D], BF16)
                rw = Res()
                for j in range(0, NFC, 2):
                    P.dma('gpsimd', wd[:, j:j + 2, :],
                          Wn['f_w_down'][layer, j * 128:(j + 2) * 128, :].rearrange("(c p) n -> p c n", p=128), [], [rw])
                cw = sb(ph, "cw", [128, 3, 2 * NFC], F32)
                cb = sb(ph, "cb", [128, 2 * NFC], F32)
                for k in range(3):
                    P.dma('scalar', cw[:, k, :], Wn['f_conv_w'][layer, k, 0, :].rearrange("(j p) -> p j", p=128), [], [rw],
                          allow_slow_non_contiguous=True)
                P.dma('scalar', cb[:], Wn['f_conv_b'][layer, :].rearrange("(j p) -> p j", p=128), [], [rw],
                      allow_slow_non_contiguous=True)
                gam = sb(ph, "gam2", [128, D], F32)
                bet = sb(ph, "bet2", [128, D], F32)
                P.dma('sync', gam[:], Wn['ln2_g'][layer:layer + 1, :].to_broadcast([128, D]), [], [rw])
                P.dma('sync', bet[:], Wn['ln2_b'][layer:layer + 1, :].to_broadcast([128, D]), [], [rw])
                carry = sb(ph, "carry", [128, 2 * NFC, 2], F32)
                rcar = [Res() for _ in range(2 * NFC)]
                P.op('gpsimd', lambda e: e.memset(carry[:], 0.0), [], rcar)
                wu_rot = Rot([sb(ph, f"wu{i}", [128, 8, 256], BF16) for i in range(3)])
                xw_rot = Rot([sb(ph, f"xw{i}", [128, 8, TT], BF16) for i in range(2)])
                g_rot = Rot([sb(ph, f"g{i}", [128, NFC, TT], BF16) for i in range(2)])
                hs_rot = Rot([sb(ph, f"hs{i}", [128, TT + 2], F32) for i in range(4)])
                acc_rot = Rot([sb(ph, f"acc{i}", [128, TT], F32) for i in range(6)])
                sa_rot = Rot([sb(ph, f"sa{i}", [128, TT], F32) for i in range(2)])
                x32_rot = Rot([sb(ph, f"x32{i}", [128, D], F32) for i in range(4 if NBT == 4 else NBT)])
                z_rot = Rot([sb(ph, f"z{i}", [128, D], F32) for i in range(2)])
                for t in range(NB // NBT):
                    b0 = t * NBT
                    xw, rxw = xw_rot.next()
                    for i in range(NBT):
                        P.dma('sync', xw[:, :, i * 128:(i + 1) * 128], XTH[b0 + i, :, :, 2:130], [rXTH[b0 + i]], [rxw])
                    g, rg = g_rot.next()
                    pend_gate = None
                    x32s = []
                    for i in range(NBT):
                        x32, rx32 = x32_rot.next()
                        P.dma('sync', x32[:], X32[(b0 + i) * 128:(b0 + i + 1) * 128, :], [rX32[b0 + i]], [rx32])
                        x32s.append((x32, rx32))

                    def emit_gate(accs, j, g, rg):
                        sa, rsa = sa_rot.next()
                        P.op('scalar', lambda e, sa=sa, a=accs[0][0]: e.activation(out=sa[:], in_=a[:], func=AF.Silu),
                             [accs[0][1]], [rsa])
                        P.op('vector', lambda e, sa=sa, u=accs[1][0], g=g, j=j: e.tensor_tensor(
                            out=g[:, j, :], in0=sa[:], in1=u[:], op=ALU.mult), [rsa, accs[1][1]], [rg])

                    for j in range(NFC):
                        if t == 0 and layer != layers[-1]:
                            convert_wu(layers[layers.index(layer) + 1], j)
                        wu, rwu = wu_rot.next()
                        P.dma('sync', wu[:], WUb[layer, j], [rWb[layer]], [rwu])
                        accs = []
                        for part in range(2):
                            jj = part * NFC + j
                            ps, rps = PB()
                            for c in range(8):
                                P.op('tensor', lambda e, c=c, part=part, ps=ps, wu=wu, xw=xw: e.matmul(
                                    ps[:, 0:TT], lhsT=wu[:, c, part * 128:(part + 1) * 128], rhs=xw[:, c, :],
                                    start=(c == 0), stop=(c == 7)), [rwu, rxw], [rps], inc=(c == 7))
                            hs, rhs_ = hs_rot.next()
                            P.op('gpsimd', lambda e, hs=hs, jj=jj: e.tensor_copy(out=hs[:, 0:2], in_=carry[:, jj, :]), [rcar[jj]], [rhs_])
                            P.op('scalar', lambda e, hs=hs, ps=ps: e.copy(out=hs[:, 2:TT + 2], in_=ps[:, 0:TT]), [rps], [rhs_])
                            P.op('gpsimd', lambda e, hs=hs, jj=jj: e.tensor_copy(out=carry[:, jj, :], in_=hs[:, TT:TT + 2]), [rhs_], [rcar[jj]])
                            acc, racc = acc_rot.next()
                            eng = 'vector' if part == 0 else 'gpsimd'
                            if part == 0:
                                P.op('scalar', lambda e, acc=acc, hs=hs, jj=jj: e.activation(
                                    out=acc[:], in_=hs[:, 0:TT], func=AF.Identity, scale=cw[:, 0, jj:jj + 1], bias=cb[:, jj:jj + 1]),
                                    [rhs_, rw], [racc])
                            else:
                                P.op(eng, lambda e, acc=acc, hs=hs, jj=jj: e.tensor_scalar(
                                    out=acc[:], in0=hs[:, 0:TT], scalar1=cw[:, 0, jj:jj + 1], scalar2=cb[:, jj:jj + 1],
                                    op0=ALU.mult, op1=ALU.add), [rhs_, rw], [racc])
                            for k in (1, 2):
                                P.op('vector', lambda e, acc=acc, hs=hs, jj=jj, k=k: e.scalar_tensor_tensor(
                                    out=acc[:], in0=hs[:, k:k + TT], scalar=cw[:, k, jj:jj + 1], in1=acc[:],
                                    op0=ALU.mult, op1=ALU.add), [rhs_, rw, racc], [racc])
                            accs.append((acc, racc))
                        if pend_gate is not None:
                            emit_gate(*pend_gate)
                        pend_gate = (accs, j, g, rg)
                    emit_gate(*pend_gate)
                    pend_gate = None
                    for i in range(NBT):
                        b = b0 + i
                        x32, rx32 = x32s[i]
                        z, rz = z_rot.next()
                        for half in range(2):
                            po, rpo = PB()
                            for j in range(NFC):
                                P.op('tensor', lambda e, j=j, half=half, po=po, g=g, i=i: e.matmul(
                                    po[:, :], lhsT=g[:, j, i * 128:(i + 1) * 128], rhs=wd[:, j, half * 512:(half + 1) * 512],
                                    start=(j == 0), stop=(j == NFC - 1)), [rg, rw], [rpo], inc=(j == NFC - 1))
                            P.op('vector', lambda e, half=half, po=po, z=z, x32=x32: e.scalar_tensor_tensor(
                                out=z[:, half * 512:(half + 1) * 512], in0=x32[:, half * 512:(half + 1) * 512], scalar=ALPHA,
                                in1=po[:, :], op0=ALU.mult, op1=ALU.add), [rpo, rx32], [rz])
                        ln_tail(b, z, rz, gam, bet, [rw], last, False)
                P.barrier()

        ph_hs = [sb(top, "hsb", [128, 130], F32)]
        rhs = Res()

        def identity_mixer_phase(layer):
            with ExitStack() as ph:
                gam, rg1 = load_bc(ph, "gam1", Wn['ln1_g'][layer:layer + 1, :])
                bet, rb1 = load_bc(ph, "bet1", Wn['ln1_b'][layer:layer + 1, :])
                rgb = Res()
                P.op('vector', lambda e: e.memset(ph_hs[0][:, 0:1], 0.0), [rg1, rb1, rhs], [rgb, rhs])
                x32_rot = Rot([sb(ph, f"mx32{i}", [128, D], F32) for i in range(2)])
                z_rot = Rot([sb(ph, f"mz{i}", [128, D], F32) for i in range(2)])
                for b in range(NB):
                    x32, rx32 = x32_rot.next()
                    P.dma('sync', x32[:], xsrc(layer)[b * 128:(b + 1) * 128, :], [rX32[b]], [rx32])
                    z, rz = z_rot.next()
                    P.op('vector', lambda e, z=z, x32=x32: e.tensor_scalar(out=z[:], in0=x32[:], scalar1=ALPHA, scalar2=None,
                                                                        op0=ALU.mult), [rx32], [rz])
                    ln_tail(b, z, rz, gam, bet, [rgb], False, True)
                P.barrier()


        def hgrn_phase(layer):
            j = layer // 2
            with ExitStack() as ph:
                win = sb(ph, "hwin", [128, 8, 4 * D], BF16)
                wout = sb(ph, "hwout", [128, 8, D], BF16)
                rw = Res()
                for c in range(8):
                    P.dma('gpsimd', win[:, c, :], Wn['b_w_in'][j, c * 128:(c + 1) * 128, :], [], [rw])
                P.dma('gpsimd', wout[:], Wn['b_w_out'][j].rearrange("(c p) n -> p c n", p=128), [], [rw])
                gam, rg1 = load_bc(ph, "hgam1", Wn['ln1_g'][layer:layer + 1, :])
                bet, rb1 = load_bc(ph, "hbet1", Wn['ln1_b'][layer:layer + 1, :])
                go, rgo = load_bc(ph, "hgo", Wn['b_g_o'][j:j + 1, :])
                cst = sb(ph, "hcst", [128, 3, 128], F32)
                cind = sb(ph, "hcind", [128, 4], F32)
                cmask = sb(ph, "hcmask", [128, 128], F32)
                cmask4 = sb(ph, "hcmask4", [128, 4, 128], F32)
                rc = Res()
                P.dma('sync', cst[:], hconst_in[:, 0:3, :], [], [rc])
                P.dma('sync', cmask[:], hconst_in[:, 3, :], [], [rc])
                for i4 in range(4):
                    P.dma('sync', cmask4[:, i4, :], hconst_in[:, 3, :], [], [rc])
                P.dma('sync', cind[:], hcind_in[:, :], [], [rc])
                lb = sb(ph, "hlb", [128, D], F32)
                oml = sb(ph, "homl", [128, D], F32)
                rl = Res()
                tmp = ExitStack()
                lg = [load_bc(tmp, f"hlg{l}", Wn['b_lb_logits'][l:l + 1, :]) for l in range(4)]
                mx = sb(tmp, "hmx", [128, D], F32)
                den = sb(tmp, "hden", [128, D], F32)
                P.op('vector', lambda e: e.tensor_tensor(out=mx[:], in0=lg[0][0][:], in1=lg[1][0][:], op=ALU.max),
                     [lg[0][1], lg[1][1]], [rl])
                for l in (2, 3):
                    P.op('vector', lambda e, l=l: e.tensor_tensor(out=mx[:], in0=mx[:], in1=lg[l][0][:], op=ALU.max),
                         [lg[l][1], rl], [rl])
                for l in range(4):
                    P.op('vector', lambda e, l=l: e.tensor_tensor(out=lg[l][0][:], in0=lg[l][0][:], in1=mx[:], op=ALU.subtract),
                         [rl, lg[l][1]], [lg[l][1]])
                    P.op('scalar', lambda e, l=l: e.activation(out=lg[l][0][:], in_=lg[l][0][:], func=AF.Exp),
                         [lg[l][1]], [lg[l][1]])
                P.op('vector', lambda e: e.tensor_tensor(out=den[:], in0=lg[0][0][:], in1=lg[1][0][:], op=ALU.add),
                     [lg[0][1], lg[1][1]], [rl])
                for l in (2, 3):
                    P.op('vector', lambda e, l=l: e.tensor_tensor(out=den[:], in0=den[:], in1=lg[l][0][:], op=ALU.add),
                         [lg[l][1], rl], [rl])
                P.op('vector', lambda e: e.tensor_copy(out=lb[:], in_=lg[1][0][:]), [lg[1][1], rl], [rl])
                for l in range(2, layer + 1):
                    P.op('vector', lambda e, l=l: e.tensor_tensor(out=lb[:], in0=lb[:], in1=lg[l][0][:], op=ALU.add),
                         [lg[l][1], rl], [rl])
                P.op('vector', lambda e: e.reciprocal(out=den[:], in_=den[:]), [rl], [rl])
                P.op('vector', lambda e: e.tensor_tensor(out=lb[:], in0=lb[:], in1=den[:], op=ALU.mult), [rl], [rl])
                P.op('vector', lambda e: e.tensor_scalar(out=oml[:], in0=lb[:], scalar1=-1.0, scalar2=1.0, op0=ALU.mult,
                                                         op1=ALU.add), [rl], [rl])
                P.barrier()
                tmp.close()
                Sf = sb(ph, "hSf", [128, 8, 128], F32)
                SbA = sb(ph, "hSbA", [128, 8, 128], BF16)
                SbB = sb(ph, "hSbB", [128, 8, 128], BF16)
                q0T = sb(ph, "hq0T", [128, 8, 128], BF16)
                q1T = sb(ph, "hq1T", [128, 8, 128], BF16)
                kT = sb(ph, "hkT", [128, 8, 128], BF16)
                qT = sb(ph, "hqT", [128, 8, 128], BF16)
                rS = Res()
                rq = Res()
                P.op('vector', lambda e: e.memset(Sf[:], 0.0), [], [rS])
                P.op('gpsimd', lambda e: e.memset(q0T[:], 0.0), [], [rq])
                P.op('gpsimd', lambda e: e.memset(q1T[:], 0.0), [], [rq])
                xh_rot = Rot([sb(ph, f"hxh{i}", [128, 8, 130], BF16) for i in range(2)])
                x32_rot = Rot([sb(ph, f"hx32{i}", [128, D], F32) for i in range(2)])
                F = lambda n: sb(ph, n, [128, D], F32)
                sig, lf, key, bb, bl, qs, og = F("hsig"), F("hlf"), F("hkey"), F("hbb"), F("hbl"), F("hqs"), F("hog")
                zt_ = qs
                gateT = F("hgate")
                rqs, rvv, rgate = Res(), Res(), Res()
                B16 = lambda n: sb(ph, n, [128, D], BF16)
                qtl, ktl, k0, k1, vv, onb = B16("hqtl"), B16("hktl"), B16("hk0"), B16("hk1"), B16("hvv"), B16("honb")
                onT = sb(ph, "honT", [128, 8, 128], BF16)
                scT = sb(ph, "hscT", [128, 8, 128], BF16)
                rscT = Res()
                rog = Res()
                rSb = {id(SbA): Res(), id(SbB): Res()}
                P.op('vector', lambda e: e.memset(SbA[:], 0.0), [], [rSb[id(SbA)]])
                dec = sb(ph, "hdec", [128, 8, 2], F32)
                ss = sb(ph, "hss", [128, 8], F32)
                rt = Res()

                def proj(part, xh, rxh):
                    outs = []
                    for half in range(2):
                        ps, rps = PB()
                        col = part * D + half * 512
                        for c in range(8):
                            P.op('tensor', lambda e, c=c, col=col, ps=ps: e.matmul(
                                ps[:, :], lhsT=xh[:, c, 2:130], rhs=win[:, c, col:col + 512],
                                start=(c == 0), stop=(c == 7)), [rw, rxh], [rps], inc=(c == 7))
                        outs.append((ps, rps))
                    return outs

                for b in range(NB):
                    xh, rxh = xh_rot.next()
                    P.dma('sync', xh[:, :, 2:130], XTH[b, :, :, 2:130], [rXTH[b]], [rxh])
                    x32, rx32 = x32_rot.next()
                    P.dma('sync', x32[:], xsrc(layer)[b * 128:(b + 1) * 128, :], [rX32[b]], [rx32])
                    for half, (ps, rps) in enumerate(proj(1, xh, rxh)):
                        hs = slice(half * 512, (half + 1) * 512)
                        P.op('scalar', lambda e, ps=ps, hs=hs: e.activation(out=sig[:, hs], in_=ps[:, :], func=AF.Sigmoid),
                             [rps], [rt])
                    for half, (ps, rps) in enumerate(proj(0, xh, rxh)):
                        hs = slice(half * 512, (half + 1) * 512)
                        P.op('scalar', lambda e, ps=ps, hs=hs: e.activation(out=qs[:, hs], in_=ps[:, :], func=AF.Silu),
                             [rps], [rqs])
                    for half, (ps, rps) in enumerate(proj(2, xh, rxh)):
                        hs = slice(half * 512, (half + 1) * 512)
                        P.op('vector', lambda e, ps=ps, hs=hs: e.tensor_copy(out=vv[:, hs], in_=ps[:, :]), [rps], [rvv])
                    for half, (ps, rps) in enumerate(proj(3, xh, rxh)):
                        hs = slice(half * 512, (half + 1) * 512)
                        P.op('scalar', lambda e, ps=ps, hs=hs: e.activation(out=gateT[:, hs], in_=ps[:, :], func=AF.Sigmoid),
                             [rps], [rgate])
                    P.op('vector', lambda e: e.tensor_tensor(out=lf[:], in0=sig[:], in1=oml[:], op=ALU.mult), [rt, rl], [rt])
                    P.op('vector', lambda e: e.tensor_tensor(out=key[:], in0=oml[:], in1=lf[:], op=ALU.subtract), [rt, rl], [rt])
                    P.op('vector', lambda e: e.tensor_tensor(out=lf[:], in0=lf[:], in1=lb[:], op=ALU.add), [rt, rl], [rt])
                    P.op('scalar', lambda e: e.activation(out=lf[:], in_=lf[:], func=AF.Ln), [rt], [rt])
                    for half in range(2):
                        hs = slice(half * 512, (half + 1) * 512)
                        ps, rps = PB()
                        P.op('tensor', lambda e, ps=ps, hs=hs: e.matmul(ps[:, :], lhsT=cst[:, 0, :], rhs=lf[:, hs],
                                                                       start=True, stop=True), [rt, rc], [rps])
                        P.op('vector', lambda e, ps=ps, hs=hs: e.tensor_copy(out=bb[:, hs], in_=ps[:, :]), [rps], [rt])
                        ps, rps = PB()
                        P.op('tensor', lambda e, ps=ps, hs=hs: e.matmul(ps[:, :], lhsT=cst[:, 1, :], rhs=lf[:, hs],
                                                                       start=True, stop=True), [rt, rc], [rps])
                        P.op('vector', lambda e, ps=ps, hs=hs: e.tensor_copy(out=bl[:, hs], in_=ps[:, :]), [rps], [rt])
                    ps, rps = PB()
                    for h in range(8):
                        P.op('tensor', lambda e, h=h, ps=ps: e.matmul(ps[:, h * 2:h * 2 + 2], lhsT=lf[:, h * 128:(h + 1) * 128],
                                                                     rhs=cind[:, 0:2], start=True, stop=True), [rt, rc], [rps])
                    P.op('scalar', lambda e, ps=ps: e.activation(out=dec[:].rearrange("p h c -> p (h c)"), in_=ps[:, 0:16],
                                                                 func=AF.Exp), [rps], [rt])
                    P.op('vector', lambda e: e.tensor_tensor(out=bl[:], in0=bl[:], in1=bb[:], op=ALU.subtract), [rt], [rt])
                    P.op('scalar', lambda e: e.activation(out=bl[:], in_=bl[:], func=AF.Exp), [rt], [rt])
                    P.op('vector', lambda e: e.tensor_tensor(out=bl[:], in0=bl[:], in1=key[:], op=ALU.mult), [rt], [rt])
                    P.op('vector', lambda e: e.tensor_scalar(out=k0[:], in0=bl[:], scalar1=cind[:, 0:1], scalar2=None,
                                                             op0=ALU.mult), [rt, rc], [rt])
                    P.op('vector', lambda e: e.tensor_scalar(out=k1[:], in0=bl[:], scalar1=cind[:, 1:2], scalar2=None,
                                                             op0=ALU.mult), [rt, rc], [rt])
                    P.op('scalar', lambda e: e.activation(out=sig[:], in_=bb[:], func=AF.Exp, scale=-1.0), [rt], [rt])
                    P.op('vector', lambda e: e.tensor_tensor(out=ktl[:], in0=sig[:], in1=key[:], op=ALU.mult), [rt], [rt])
                    P.op('scalar', lambda e: e.activation(out=bb[:], in_=bb[:], func=AF.Exp), [rt], [rt])
                    P.op('vector', lambda e: e.tensor_tensor(out=qtl[:], in0=qs[:], in1=bb[:], op=ALU.mult), [rt, rqs], [rt])
                    for (src, dsts) in ((qtl, 'q'), (ktl, 'k')):
                        pt, rpt = PT()
                        for h in range(8):
                            P.op('tensor', lambda e, h=h, pt=pt, src=src: e.transpose(
                                out=pt[:, h * 128:(h + 1) * 128], in_=src[:, h * 128:(h + 1) * 128], identity=ident[:]),
                                [rt, rid], [rpt])
                        ptv = pt[:].rearrange("p (h t) -> p h t", h=8)
                        if dsts == 'q':
                            P.op('vector', lambda e, ptv=ptv: e.tensor_copy(out=qT[:], in_=ptv), [rpt], [rq])
                            P.op('vector', lambda e, ptv=ptv: e.tensor_copy(out=q0T[:, :, 0:64], in_=ptv[:, :, 0:64]), [rpt], [rq])
                            P.op('vector', lambda e, ptv=ptv: e.tensor_copy(out=q1T[:, :, 64:128], in_=ptv[:, :, 64:128]), [rpt], [rq])
                        else:
                            P.op('vector', lambda e, ptv=ptv: e.tensor_copy(out=kT[:], in_=ptv), [rpt], [rq])
                    pu = [pbank[0], pbank[1]]
                    psc = [pbank[2], pbank[3]]
                    po = [pbank[4], pbank[5]]
                    Sfv = Sf[:].rearrange("p h e -> p (h e)")

                    def state_update(kk, ci, Sb):
                        for h in range(8):
                            hsl = slice(h * 128, (h + 1) * 128)
                            pb_, rpb = pu[h // 4]
                            P.op('tensor', lambda e, pb_=pb_, h=h, hsl=hsl: e.matmul(
                                pb_[:, (h % 4) * 128:(h % 4 + 1) * 128], lhsT=kk[:, hsl], rhs=vv[:, hsl], start=True, stop=True),
                                [rt, rvv], [rpb])
                        P.op('vector', lambda e: e.tensor_tensor(out=Sf[:], in0=Sf[:], in1=dec[:, :, ci:ci + 1].to_broadcast([128, 8, 128]),
                                                                 op=ALU.mult), [rt, rS], [rS])
                        for hh in range(2):
                            pb_, rpb = pu[hh]
                            P.op('vector', lambda e, pb_=pb_, hh=hh: e.tensor_tensor(
                                out=Sfv[:, hh * 512:(hh + 1) * 512], in0=Sfv[:, hh * 512:(hh + 1) * 512], in1=pb_[:, :], op=ALU.add),
                                [rpb, rS], [rS])
                        P.op('scalar', lambda e: e.copy(out=Sb[:], in_=Sf[:]), [rS], [rSb[id(Sb)]])

                    state_update(k0, 0, SbB)
                    for h in range(8):
                        pb_, rpb = psc[h // 4]
                        P.op('tensor', lambda e, pb_=pb_, h=h: e.matmul(pb_[:, (h % 4) * 128:(h % 4 + 1) * 128], lhsT=kT[:, h, :],
                                                                       rhs=qT[:, h, :], start=True, stop=True), [rq], [rpb])
                    for hh in range(2):
                        pb_, rpb = psc[hh]
                        P.op('vector', lambda e, pb_=pb_, hh=hh: e.tensor_tensor(
                            out=scT[:, hh * 4:(hh + 1) * 4, :], in0=pb_[:, :].rearrange("p (h t) -> p h t", h=4), in1=cmask4[:],
                            op=ALU.mult), [rpb, rc], [rscT])
                    for h in range(8):
                        hsl = slice(h * 128, (h + 1) * 128)
                        pb_, rpb = po[h // 4]
                        osl = slice((h % 4) * 128, (h % 4 + 1) * 128)
                        P.op('tensor', lambda e, pb_=pb_, h=h, hsl=hsl, osl=osl: e.matmul(
                            pb_[:, osl], lhsT=scT[:, h, :], rhs=vv[:, hsl], start=True, stop=False), [rscT, rt, rvv], [rpb])
                        P.op('tensor', lambda e, pb_=pb_, h=h, osl=osl: e.matmul(
                            pb_[:, osl], lhsT=q0T[:, h, :], rhs=SbA[:, h, :], start=False, stop=False), [rq, rSb[id(SbA)]], [rpb])
                        P.op('tensor', lambda e, pb_=pb_, h=h, osl=osl: e.matmul(
                            pb_[:, osl], lhsT=q1T[:, h, :], rhs=SbB[:, h, :], start=False, stop=True), [rq, rSb[id(SbB)]], [rpb])
                    for hh in range(2):
                        pb_, rpb = po[hh]
                        P.op('vector', lambda e, pb_=pb_, hh=hh: e.tensor_tensor(
                            out=og[:, hh * 512:(hh + 1) * 512], in0=pb_[:, :], in1=gateT[:, hh * 512:(hh + 1) * 512], op=ALU.mult),
                            [rpb, rgate], [rog])
                    state_update(k1, 1, SbA)
                    P.op('scalar', lambda e: e.activation(out=qs[:], in_=og[:], func=AF.Square), [rt, rog, rqs], [rt, rqs])
                    P.op('vector', lambda e: e.tensor_reduce(out=ss[:], in_=qs[:].rearrange("p (h e) -> p h e", h=8),
                                                             axis=AX.X, op=ALU.add), [rt], [rt])
                    rsqrt(ss[:], ss[:], 1.0 / 128, 1, [rt], [rt])
                    for h in range(8):
                        hsl = slice(h * 128, (h + 1) * 128)
                        P.op('vector', lambda e, h=h, hsl=hsl: e.scalar_tensor_tensor(
                            out=onb[:, hsl], in0=og[:, hsl], scalar=ss[:, h:h + 1], in1=go[:, hsl], op0=ALU.mult, op1=ALU.mult),
                            [rt, rgo, rog], [rt])
                    pt, rpt = PT()
                    for c in range(8):
                        P.op('tensor', lambda e, c=c, pt=pt: e.transpose(out=pt[:, c * 128:(c + 1) * 128],
                                                                        in_=onb[:, c * 128:(c + 1) * 128], identity=ident[:]),
                             [rt, rid], [rpt])
                    P.op('vector', lambda e, pt=pt: e.tensor_copy(out=onT[:].rearrange("p c t -> p (c t)"), in_=pt[:]), [rpt], [rt])
                    for half in range(2):
                        po, rpo = PB()
                        for c in range(8):
                            P.op('tensor', lambda e, c=c, half=half, po=po: e.matmul(
                                po[:, :], lhsT=onT[:, c, :], rhs=wout[:, c, half * 512:(half + 1) * 512],
                                start=(c == 0), stop=(c == 7)), [rt, rw], [rpo], inc=(c == 7))
                        P.op('vector', lambda e, half=half, po=po, x32=x32: e.scalar_tensor_tensor(
                            out=zt_[:, half * 512:(half + 1) * 512], in0=x32[:, half * 512:(half + 1) * 512], scalar=ALPHA,
                            in1=po[:, :], op0=ALU.mult, op1=ALU.add), [rpo, rx32, rqs], [rt, rqs])
                    ln_tail(b, zt_, rt, gam, bet, [rg1, rb1], False, False)
                P.barrier()


        NIT = 18

        def dsa_phase(layer):
            j = layer // 2
            with ExitStack() as ph:
                win = sb(ph, "awin", [128, 8, AIN], BF16)
                wql = sb(ph, "awql", [128, 3, AH * KVR], BF16)
                wqi = sb(ph, "awqi", [128, 3, IDXH * IDXD], BF16)
                wuv = sb(ph, "awuv", [128, 2 * AH, 128], BF16)
                wout = sb(ph, "awout", [128, 8, D], BF16)
                rw = Res()
                P.dma('gpsimd', win[:], Wn['a_w_in'][j].rearrange("(c p) n -> p c n", p=128), [], [rw])
                P.dma('gpsimd', wql[:], Wn['a_w_q_lat'][j].rearrange("(c p) n -> p c n", p=128), [], [rw])
                P.dma('gpsimd', wqi[:], Wn['a_w_q_idx'][j].rearrange("(c p) n -> p c n", p=128), [], [rw])
                P.dma('gpsimd', wuv[:], Wn['a_w_uv'][j].rearrange("h (rc p) d -> p (h rc) d", p=128), [], [rw])
                P.dma('gpsimd', wout[:], Wn['a_w_out'][j].rearrange("(c p) n -> p c n", p=128), [], [rw])
                gam, rg1 = load_bc(ph, "agam1", Wn['ln1_g'][layer:layer + 1, :])
                bet, rb1 = load_bc(ph, "abet1", Wn['ln1_b'][layer:layer + 1, :])
                gq, rgq = load_bc(ph, "agq", Wn['a_g_q'][j:j + 1, :])
                gkv, rgkv = load_bc(ph, "agkv", Wn['a_g_kv'][j:j + 1, :])
                gki, rgki = load_bc(ph, "agki", Wn['a_g_kidx'][j:j + 1, :])
                bki, rbki = load_bc(ph, "abki", Wn['a_b_kidx'][j:j + 1, :])
                maskall = sb(ph, "amaskall", [128, 8, 128], F32)
                rep = sb(ph, "arep", [8, 128], BF16)
                cm = sb(ph, "acm", [128, 128], F32)
                pw = sb(ph, "apw", [128, NIT], F32)
                rc = Res()
                P.dma('sync', maskall[:], maskall_in[:, :, :], [], [rc])
                P.dma('sync', rep[:], rep_in[:, :], [], [rc])
                P.dma('sync', cm[:], cm_in[:, :], [], [rc])
                P.dma('sync', pw[:], pw_in[:, 0:NIT], [], [rc])
                CKV = sb(ph, "aCKV", [128, NB, KVR + 1], BF16)
                CKVT = sb(ph, "aCKVT", [128, 2, S], BF16)
                KIT = sb(ph, "aKIT", [64, S], BF16)
                rK = Res()
                P.op('vector', lambda e: e.memset(CKV[:, :, KVR:KVR + 1], 1.0), [], [rK])
                score = sb(ph, "ascore", [128, S], F32)
                msk = sb(ph, "amsk", [128, S], BF16)
                junk = msk
                zA = sb(ph, "azA", [128, 328], F32)
                mskT_rot = Rot([sb(ph, f"amskT{i}", [128, NB, 128], BF16) for i in range(2)])
                x32_rot = Rot([sb(ph, f"ax32{i}", [128, D], F32) for i in range(1)])
                xh_rot = Rot([sb(ph, f"axh{i}", [128, 8, 130], BF16) for i in range(2)])
                sqa = sb(ph, "asqa", [128, QR], F32)
                cq = sb(ph, "acq", [128, QR], BF16)
                kix = sb(ph, "akix", [128, IDXD], F32)
                kib = sb(ph, "akib", [128, IDXD], BF16)
                wq = sb(ph, "awq", [128, 8], BF16)
                st = sb(ph, "ast", [128, 16], F32)
                cqT_rot = Rot([sb(ph, f"acqT{i}", [128, 3, 128], BF16) for i in range(2)])
                wT = sb(ph, "awT", [8, 128], BF16)
                QIT = sb(ph, "aQIT", [64, 8, 8, 16], BF16)
                Wsel = sb(ph, "aWsel", [128, 8, 128], BF16)
                R_rot = Rot([sb(ph, f"aR{i}", [128, 512], BF16) for i in range(3)])
                mm = sb(ph, "amm", [128, 2, 8], F32)
                bs = sb(ph, "abs", [128, 8], F32)
                Wt = sb(ph, "aWt", [128, NIT], F32)
                qlT4 = sb(ph, "aqlT4", [128, 2, 512], BF16)
                PT_rot = Rot([sb(ph, f"aPT{i}", [128, 4, 128], BF16) for i in range(5)])
                olb = sb(ph, "aolb", [128, AH, KVR], BF16)
                olT = sb(ph, "aolT8", [128, AH, 2, 128], BF16)
                dcol = sb(ph, "adcol", [128, AH], F32)
                rd = sb(ph, "ard8", [128, AH], F32)
                rolb, rolT, rrd = Res(), Res(), Res()
                oh = sb(ph, "aoh", [128, D], BF16)
                ohT = sb(ph, "aohT", [128, 8, 128], BF16)
                rden = sb(ph, "arden", [128, 1], F32)
                z = sb(ph, "az", [128, D], F32)
                rt = Res()
                r_z, r_q, r_cq, r_kv, r_ki, r_w, r_cqT, r_wT, r_qit, r_wsel, r_mm, r_bs = [Res() for _ in range(12)]
                sqa_kv = sb(ph, "asqakv", [128, KVR], F32)
                sqa_ki = sb(ph, "asqaki", [128, IDXD], F32)
                rsc = Res()
                rmk = Res()
                rql = Res()
                ro = Res()

                def rmsn(ps, rps0, c0, n, gtile, rg, out_ap, idx):
                    ss, rs = st[:, idx:idx + 1], st[:, idx + 1:idx + 2]
                    P.op('scalar', lambda e: e.activation(out=sqa[:, 0:n], in_=ps[:, c0:c0 + n], func=AF.Square), [rps0], [r_q])
                    P.op('vector', lambda e: e.tensor_scalar(out=sqa[:, 0:n], in0=sqa[:, 0:n], scalar1=1.0, scalar2=None,
                                                             op0=ALU.mult, op1=ALU.add, accum_out=ss), [r_q], [r_q])
                    rsqrt(rs, ss, 1.0 / n, 1, [r_q], [r_q])
                    P.op('vector', lambda e: e.scalar_tensor_tensor(out=out_ap, in0=ps[:, c0:c0 + n], scalar=rs, in1=gtile[:, 0:n],
                                                                    op0=ALU.mult, op1=ALU.mult), [rps0, r_q, rg], [r_cq])

                blk = {}
                r_zo = Res()

                def SB(b):
                    cqT, r_cqT = cqT_rot.next()
                    mskT, rmkT = mskT_rot.next()
                    L = (b + 1) * 128
                    xh, rxh = xh_rot.next()
                    P.dma('sync', xh[:, :, 2:130], XTH[b, :, :, 2:130], [rXTH[b]], [rxh])
                    ps0, rps0 = PB()
                    ps1, rps1 = PB()
                    for c in range(8):
                        P.op('tensor', lambda e, c=c, ps0=ps0: e.matmul(ps0[:, :], lhsT=xh[:, c, 2:130], rhs=win[:, c, 0:512],
                                                                       start=(c == 0), stop=(c == 7)), [rw, rxh], [rps0], inc=(c == 7))
                    for c in range(8):
                        P.op('tensor', lambda e, c=c, ps1=ps1: e.matmul(ps1[:, 0:AIN - 512], lhsT=xh[:, c, 2:130], rhs=win[:, c, 512:AIN],
                                                                       start=(c == 0), stop=(c == 7)), [rw, rxh], [rps1], inc=(c == 7))
                    rmsn(ps0, rps0, 0, QR, gq, rgq, cq[:], 0)
                    P.op('scalar', lambda e, ps0=ps0: e.copy(out=zA[:, 0:128], in_=ps0[:, QR:512]), [rps0], [r_z])
                    P.op('scalar', lambda e, ps1=ps1: e.copy(out=zA[:, 128:128 + 200], in_=ps1[:, 0:200]), [rps1], [r_z])
                    ss, rs = st[:, 2:3], st[:, 3:4]
                    P.op('scalar', lambda e: e.activation(out=sqa_kv[:, 0:KVR], in_=zA[:, 0:KVR], func=AF.Square), [r_z], [r_kv])
                    P.op('vector', lambda e: e.tensor_scalar(out=sqa_kv[:, 0:KVR], in0=sqa_kv[:, 0:KVR], scalar1=1.0, scalar2=None,
                                                             op0=ALU.mult, op1=ALU.add, accum_out=ss), [r_kv], [r_kv])
                    rsqrt(rs, ss, 1.0 / KVR, 1, [r_kv], [r_kv])
                    P.op('vector', lambda e, b=b: e.scalar_tensor_tensor(out=CKV[:, b, 0:KVR], in0=zA[:, 0:KVR], scalar=rs, in1=gkv[:, :],
                                                                         op0=ALU.mult, op1=ALU.mult), [r_kv, r_z, rgkv], [rK])
                    s1, s2, mean, msq, var, rstd, nmr = [st[:, 4 + i:5 + i] for i in range(7)]
                    ki = zA[:, 256:320]
                    P.op('vector', lambda e: e.tensor_scalar(out=kix[:], in0=ki, scalar1=1.0, scalar2=None, op0=ALU.mult,
                                                             op1=ALU.add, accum_out=s1), [r_z], [r_ki])
                    P.op('scalar', lambda e: e.activation(out=sqa_ki[:, 0:IDXD], in_=ki, func=AF.Square), [r_z], [r_ki])
                    P.op('vector', lambda e: e.tensor_scalar(out=sqa_ki[:, 0:IDXD], in0=sqa_ki[:, 0:IDXD], scalar1=1.0, scalar2=None,
                                                             op0=ALU.mult, op1=ALU.add, accum_out=s2), [r_ki], [r_ki])
                    P.op('vector', lambda e: e.tensor_scalar(out=mean, in0=s1, scalar1=1.0 / IDXD, scalar2=None, op0=ALU.mult), [r_ki], [r_ki])
                    P.op('vector', lambda e: e.tensor_tensor(out=msq, in0=mean, in1=mean, op=ALU.mult), [r_ki], [r_ki])
                    P.op('vector', lambda e: e.scalar_tensor_tensor(out=var, in0=s2, scalar=1.0 / IDXD, in1=msq, op0=ALU.mult,
                                                                    op1=ALU.subtract), [r_ki], [r_ki])
                    rsqrt(rstd, var, 1.0, 0, [r_ki], [r_ki])
                    P.op('vector', lambda e: e.scalar_tensor_tensor(out=nmr, in0=mean, scalar=-1.0, in1=rstd, op0=ALU.mult,
                                                                    op1=ALU.mult), [r_ki], [r_ki])
                    P.op('scalar', lambda e: e.activation(out=kix[:], in_=ki, func=AF.Identity, scale=rstd, bias=nmr), [r_ki, r_z], [r_ki])
                    P.op('vector', lambda e: e.tensor_tensor(out=kix[:], in0=kix[:], in1=gki[:], op=ALU.mult), [r_ki, rgki], [r_ki])
                    P.op('vector', lambda e: e.tensor_tensor(out=kib[:], in0=kix[:], in1=bki[:], op=ALU.add), [r_ki, rbki], [r_ki])
                    P.op('vector', lambda e: e.tensor_scalar(out=wq[:], in0=zA[:, 320:328], scalar1=float((IDXH * IDXD) ** -0.5),
                                                             scalar2=None, op0=ALU.mult), [r_z], [r_w])
                    pt, rpt = PT()
                    for c in range(3):
                        P.op('tensor', lambda e, c=c, pt=pt: e.transpose(out=pt[:, c * 128:(c + 1) * 128], in_=cq[:, c * 128:(c + 1) * 128],
                                                                        identity=ident[:]), [r_cq, rid], [rpt])
                    for c in range(2):
                        P.op('tensor', lambda e, c=c, pt=pt, b=b: e.transpose(out=pt[:, (3 + c) * 128:(4 + c) * 128],
                                                                             in_=CKV[:, b, c * 128:(c + 1) * 128], identity=ident[:]),
                             [rK, rid], [rpt])
                    P.op('tensor', lambda e, pt=pt: e.transpose(out=pt[0:64, 5 * 128:6 * 128], in_=kib[:, :], identity=ident[:]),
                         [r_ki, rid], [rpt])
                    P.op('tensor', lambda e, pt=pt: e.transpose(out=pt[0:8, 6 * 128:7 * 128], in_=wq[:, :], identity=ident[:]),
                         [r_w, rid], [rpt])
                    P.op('vector', lambda e, pt=pt: e.tensor_copy(out=cqT[:].rearrange("p c t -> p (c t)"), in_=pt[:, 0:384]), [rpt], [r_cqT])
                    for c in range(2):
                        P.op('vector', lambda e, pt=pt, c=c, b=b: e.tensor_copy(out=CKVT[:, c, b * 128:(b + 1) * 128],
                                                                               in_=pt[:, (3 + c) * 128:(4 + c) * 128]), [rpt], [rK])
                    P.op('vector', lambda e, pt=pt, b=b: e.tensor_copy(out=KIT[:, b * 128:(b + 1) * 128], in_=pt[0:64, 640:768]), [rpt], [rK])
                    P.op('vector', lambda e, pt=pt: e.tensor_copy(out=wT[:], in_=pt[0:8, 768:896]), [rpt], [r_wT])
                    for hh in range(2):
                        pq, rpq = PB()
                        for h4 in range(4):
                            h = hh * 4 + h4
                            for kc in range(3):
                                P.op('tensor', lambda e, pq=pq, h=h, h4=h4, kc=kc: e.matmul(
                                    pq[0:64, h4 * 128:(h4 + 1) * 128], lhsT=wqi[:, kc, h * 64:(h + 1) * 64], rhs=cqT[:, kc, :],
                                    start=(kc == 0), stop=(kc == 2)), [rw, r_cqT], [rpq], inc=(kc == 2))
                        P.op('scalar', lambda e, pq=pq, hh=hh: e.copy(
                            out=QIT[:, :, hh * 4:(hh + 1) * 4, :].rearrange("p g h t -> p h g t"),
                            in_=pq[0:64, :].rearrange("p (h g t) -> p h g t", h=4, g=8)), [rpq], [r_qit])
                    pe_, rpe = PB()
                    P.op('tensor', lambda e, pe_=pe_: e.matmul(pe_[:, 0:128], lhsT=rep[:, :], rhs=wT[:, :], start=True, stop=True),
                         [rc, r_wT], [rpe])
                    for g in range(8):
                        P.op('vector', lambda e, g=g, pe_=pe_: e.tensor_tensor(out=Wsel[:, g, :], in0=pe_[:, 0:128], in1=maskall[:, g, :],
                                                                              op=ALU.mult), [rpe, rc], [r_wsel])
                    nkc = (L + 511) // 512
                    for kc in range(nkc):
                        k0_ = kc * 512
                        n = min(512, L - k0_)
                        psc, rpsc = pbank[4]
                        pend = []

                        def emit_mm1(g, k0_=k0_, n=n):
                            p1, rp1 = PB()
                            P.op('tensor', lambda e, p1=p1, g=g: e.matmul(
                                p1[:, 0:n], lhsT=QIT[:, g, :, :].rearrange("p h t -> p (h t)"), rhs=KIT[:, k0_:k0_ + n], start=True, stop=True),
                                [r_qit, rK], [rp1])
                            R, rR = R_rot.next()
                            if g % 2 == 0:
                                P.op('scalar', lambda e, p1=p1, R=R: e.activation(out=R[:, 0:n], in_=p1[:, 0:n], func=AF.Relu),
                                     [rp1], [rR])
                            else:
                                P.op('vector', lambda e, p1=p1, R=R: e.tensor_scalar(out=R[:, 0:n], in0=p1[:, 0:n], scalar1=0.0,
                                                                                    scalar2=None, op0=ALU.max), [rp1], [rR])
                            pend.append((R, rR))

                        emit_mm1(0)
                        emit_mm1(1)
                        for g in range(8):
                            if g + 2 < 8:
                                emit_mm1(g + 2)
                            R, rR = pend.pop(0)
                            P.op('tensor', lambda e, psc=psc, g=g, R=R, n=n: e.matmul(
                                psc[:, 0:n], lhsT=Wsel[:, g, :], rhs=R[:, 0:n], start=(g == 0), stop=(g == 7)), [r_wsel, rR], [rpsc])
                        P.op('scalar', lambda e, psc=psc, k0_=k0_, n=n: e.copy(out=score[:, k0_:k0_ + n], in_=psc[:, 0:n]), [rpsc], [rsc])
                        P.op('vector', lambda e, psc=psc, kc=kc, n=n: e.tensor_reduce(out=mm[:, 0, kc:kc + 1], in_=psc[:, 0:n], axis=AX.X,
                                                                                     op=ALU.min), [rpsc], [r_mm])
                        P.op('vector', lambda e, psc=psc, kc=kc, n=n: e.tensor_reduce(out=mm[:, 1, kc:kc + 1], in_=psc[:, 0:n], axis=AX.X,
                                                                                     op=ALU.max), [rpsc], [r_mm])
                    P.op('vector', lambda e, L=L: e.tensor_tensor(out=score[:, L - 128:L], in0=score[:, L - 128:L], in1=cm[:], op=ALU.add),
                         [rsc, rc], [rsc])
                    lo, hi, w0, mid, cnt, stp = [bs[:, i:i + 1] for i in range(6)]
                    P.op('vector', lambda e, nkc=nkc: e.tensor_reduce(out=lo, in_=mm[:, 0, 0:nkc], axis=AX.X, op=ALU.min), [r_mm], [r_bs])
                    P.op('vector', lambda e, nkc=nkc: e.tensor_reduce(out=hi, in_=mm[:, 1, 0:nkc], axis=AX.X, op=ALU.max), [r_mm], [r_bs])
                    P.op('vector', lambda e: e.tensor_tensor(out=w0, in0=hi, in1=lo, op=ALU.subtract), [r_bs], [r_bs])
                    P.op('vector', lambda e: e.tensor_scalar(out=w0, in0=w0, scalar1=1.0001, scalar2=1e-6, op0=ALU.mult, op1=ALU.add), [r_bs], [r_bs])
                    P.op('vector', lambda e: e.tensor_scalar(out=Wt[:], in0=pw[:], scalar1=w0, scalar2=None, op0=ALU.mult), [r_bs, rc], [r_bs])
                    if L > TOPK:
                        P.op('vector', lambda e: e.tensor_tensor(out=mid, in0=lo, in1=Wt[:, 0:1], op=ALU.add), [r_bs], [r_bs])
                        for k in range(NIT):
                            P.op('vector', lambda e, L=L: e.tensor_scalar(out=junk[:, 0:L], in0=score[:, 0:L], scalar1=mid, scalar2=None,
                                                                         op0=ALU.is_ge, op1=ALU.add, accum_out=cnt), [r_bs, rsc], [r_bs, rmk])
                            P.op('vector', lambda e, k=k: e.scalar_tensor_tensor(out=stp, in0=cnt, scalar=float(TOPK) - 0.5, in1=Wt[:, k:k + 1],
                                                                                op0=ALU.is_ge, op1=ALU.mult), [r_bs], [r_bs])
                            if k < NIT - 1:
                                P.op('vector', lambda e, k=k: e.scalar_tensor_tensor(out=mid, in0=mid, scalar=Wt[:, k + 1:k + 2], in1=stp,
                                                                                    op0=ALU.subtract, op1=ALU.add), [r_bs], [r_bs])
                            else:
                                P.op('vector', lambda e, k=k: e.scalar_tensor_tensor(out=lo, in0=mid, scalar=Wt[:, k:k + 1], in1=stp,
                                                                                    op0=ALU.subtract, op1=ALU.add), [r_bs], [r_bs])
                    P.op('vector', lambda e, L=L: e.tensor_scalar(out=msk[:, 0:L], in0=score[:, 0:L], scalar1=lo, scalar2=None,
                                                                 op0=ALU.is_ge), [r_bs, rsc], [rmk])
                    for kb0 in range(0, b + 1, 8):
                        nk = min(8, b + 1 - kb0)
                        pt, rpt = PT()
                        for i in range(nk):
                            P.op('tensor', lambda e, pt=pt, i=i, kb0=kb0: e.transpose(
                                out=pt[:, i * 128:(i + 1) * 128], in_=msk[:, (kb0 + i) * 128:(kb0 + i + 1) * 128], identity=ident[:]),
                                [rmk, rid], [rpt])
                        P.op('vector', lambda e, pt=pt, nk=nk, kb0=kb0: e.tensor_copy(
                            out=mskT[:, kb0:kb0 + nk, :].rearrange("p k t -> p (k t)"), in_=pt[:, 0:nk * 128]), [rpt], [rmkT])
                    blk[b] = (cqT, r_cqT, mskT, rmkT)

                def ATT(b):
                    cqT, r_cqT, mskT, rmkT = blk.pop(b)
                    x32, rx32 = x32_rot.next()
                    P.dma('sync', x32[:], xsrc(layer)[b * 128:(b + 1) * 128, :], [rX32[b]], [rx32])
                    qk_i = [0]

                    def QKB():
                        r = pbank[4 + qk_i[0]]
                        qk_i[0] ^= 1
                        return r

                    for hg in range(2):
                        for rcx in range(2):
                            pq, rpq = QKB()
                            for h4 in range(4):
                                ch = (hg * 4 + h4) * 2 + rcx
                                for kc in range(3):
                                    P.op('tensor', lambda e, pq=pq, h4=h4, ch=ch, kc=kc: e.matmul(
                                        pq[:, h4 * 128:(h4 + 1) * 128], lhsT=wql[:, kc, ch * 128:(ch + 1) * 128], rhs=cqT[:, kc, :],
                                        start=(kc == 0), stop=(kc == 2)), [rw, r_cqT], [rpq], inc=(kc == 2))
                            P.op('scalar', lambda e, pq=pq, rcx=rcx: e.activation(out=qlT4[:, rcx, :], in_=pq[:, :], func=AF.Copy,
                                                                                scale=float(KVR ** -0.5)), [rpq], [rql])
                        pendq = []

                        def emit_qk(kb):
                            pst, rpst = QKB()
                            for rcx in range(2):
                                P.op('tensor', lambda e, pst=pst, rcx=rcx, kb=kb: e.matmul(
                                    pst[:, :], lhsT=CKVT[:, rcx, kb * 128:(kb + 1) * 128], rhs=qlT4[:, rcx, :],
                                    start=(rcx == 0), stop=(rcx == 1)), [rK, rql], [rpst], inc=(rcx == 1))
                            PTt, rPT = PT_rot.next()
                            P.op('scalar', lambda e, pst=pst, PTt=PTt: e.activation(
                                out=PTt[:].rearrange("p k t -> p (k t)"), in_=pst[:, :], func=AF.Exp), [rpst], [rPT])
                            P.op('gpsimd', lambda e, PTt=PTt, kb=kb: e.tensor_tensor(
                                out=PTt[:], in0=PTt[:], in1=mskT[:, kb:kb + 1, :].to_broadcast([128, 4, 128]), op=ALU.mult), [rPT, rmkT], [rPT])
                            pendq.append((PTt, rPT))

                        for kb in range(min(3, b + 1)):
                            emit_qk(kb)
                        for kb in range(b + 1):
                            if kb + 3 <= b:
                                emit_qk(kb + 3)
                            PTt, rPT = pendq.pop(0)
                            for h4 in range(4):
                                pol, rpol = pbank[h4]
                                P.op('tensor', lambda e, pol=pol, PTt=PTt, h4=h4, kb=kb, b=b: e.matmul(
                                    pol[:, 0:KVR + 1], lhsT=PTt[:, h4, :], rhs=CKV[:, kb, :], start=(kb == 0), stop=(kb == b)),
                                    [rPT, rK], [rpol])
                        for h4 in range(4):
                            pol, rpol = pbank[h4]
                            h = hg * 4 + h4
                            P.op('scalar', lambda e, pol=pol, h=h: e.copy(out=olb[:, h, :], in_=pol[:, 0:KVR]), [rpol], [rolb])
                            P.op('scalar', lambda e, pol=pol, h=h: e.copy(out=dcol[:, h:h + 1], in_=pol[:, KVR:KVR + 1]), [rpol], [rolb])
                    P.op('vector', lambda e: e.reciprocal(out=rd[:], in_=dcol[:]), [rolb], [rrd])
                    pts = [PT(), PT()]
                    for h in range(8):
                        pt, rpt = pts[h // 4]
                        for rcx in range(2):
                            P.op('tensor', lambda e, pt=pt, h=h, rcx=rcx: e.transpose(
                                out=pt[:, ((h % 4) * 2 + rcx) * 128:((h % 4) * 2 + rcx + 1) * 128],
                                in_=olb[:, h, rcx * 128:(rcx + 1) * 128], identity=ident[:]), [rolb, rid], [rpt])
                    for hh in range(2):
                        pt, rpt = pts[hh]
                        P.op('scalar', lambda e, pt=pt, hh=hh: e.copy(
                            out=olT[:, hh * 4:(hh + 1) * 4, :, :].rearrange("p h c t -> p (h c t)"), in_=pt[:, :]), [rpt], [rolT])
                    puvs = [QKB(), QKB()]
                    for h in range(8):
                        puv, rpuv = puvs[h // 4]
                        for rcx in range(2):
                            P.op('tensor', lambda e, puv=puv, rcx=rcx, h=h: e.matmul(
                                puv[:, (h % 4) * 128:(h % 4 + 1) * 128], lhsT=olT[:, h, rcx, :], rhs=wuv[:, h * 2 + rcx, :],
                                start=(rcx == 0), stop=(rcx == 1)), [rolT, rw], [rpuv], inc=(rcx == 1))
                    for hh in range(2):
                        puv, rpuv = puvs[hh]
                        P.op('vector', lambda e, puv=puv, hh=hh: e.tensor_tensor(
                            out=oh[:, hh * 512:(hh + 1) * 512].rearrange("p (h d) -> p h d", h=4),
                            in0=puv[:, :].rearrange("p (h d) -> p h d", h=4),
                            in1=rd[:, hh * 4:(hh + 1) * 4].rearrange("p (h o) -> p h o", o=1).to_broadcast([128, 4, 128]),
                            op=ALU.mult), [rpuv, rrd], [ro])
                    pt, rpt = PT()
                    for c in range(8):
                        P.op('tensor', lambda e, c=c, pt=pt: e.transpose(out=pt[:, c * 128:(c + 1) * 128],
                                                                        in_=oh[:, c * 128:(c + 1) * 128], identity=ident[:]),
                             [ro, rid], [rpt])
                    P.op('vector', lambda e, pt=pt: e.tensor_copy(out=ohT[:].rearrange("p c t -> p (c t)"), in_=pt[:]), [rpt], [ro])
                    for half in range(2):
                        po, rpo = PB()
                        for c in range(8):
                            P.op('tensor', lambda e, c=c, half=half, po=po: e.matmul(
                                po[:, :], lhsT=ohT[:, c, :], rhs=wout[:, c, half * 512:(half + 1) * 512],
                                start=(c == 0), stop=(c == 7)), [ro, rw], [rpo], inc=(c == 7))
                        P.op('vector', lambda e, half=half, po=po, x32=x32: e.scalar_tensor_tensor(
                            out=z[:, half * 512:(half + 1) * 512], in0=x32[:, half * 512:(half + 1) * 512], scalar=ALPHA,
                            in1=po[:, :], op0=ALU.mult, op1=ALU.add), [rpo, rx32], [r_zo])
                    ln_tail(b, z, r_zo, gam, bet, [rg1, rb1], False, False)

                SB(0)
                for b in range(NB):
                    if b + 1 < NB:
                        SB(b + 1)
                    ATT(b)
                P.barrier()

        def mixer_phase(layer):
            if MIXERS[layer % 2] is None:
                identity_mixer_phase(layer)
            elif layer % 2 == 0:
                dsa_phase(layer)
            else:
                hgrn_phase(layer)

        for layer in layers:
            mixer_phase(layer)
            ffn_phase(layer, last=(layer == layers[-1]))
        for e in ('sync',):
            if rY.w is not None:
                P._wait(e, rY.w[0], rY.w[1])
        P.barrier()
        print("instructions:", P.ninst)
    return nc


MIXERS = [True, True]

WSHAPES = {
    'a_w_in': [2, D, AIN], 'a_g_q': [2, QR], 'a_g_kv': [2, KVR], 'a_w_q_lat': [2, QR, AH * KVR],
    'a_w_q_idx': [2, QR, IDXH * IDXD], 'a_g_kidx': [2, IDXD], 'a_b_kidx': [2, IDXD],
    'a_w_uv': [2, AH, KVR, 128], 'a_w_out': [2, D, D], 'b_w_in': [2, D, 4 * D], 'b_lb_logits': [4, D],
    'b_g_o': [2, D], 'b_w_out': [2, D, D], 'ln1_g': [4, D], 'ln1_b': [4, D], 'f_w_up': [4, D, 2 * DFF],
    'f_conv_w': [4, 3, 1, 2 * DFF], 'f_conv_b': [4, 2 * DFF], 'f_w_down': [4, DFF, D], 'ln2_g': [4, D], 'ln2_b': [4, D],
}


def host_consts():
    ident = np.eye(128, dtype=np.float32).astype(ml_dtypes.bfloat16)
    s_ = np.arange(128)[:, None]
    t_ = np.arange(128)[None, :]
    same = (s_ // 64) == (t_ // 64)
    hconst = np.zeros((128, 4, 128), np.float32)
    hconst[:, 0, :] = (same & (s_ <= t_))
    hconst[:, 1, :] = same
    hconst[:, 3, :] = (same & (s_ <= t_))
    hcind = np.zeros((128, 4), np.float32)
    hcind[:64, 0] = 1.0
    hcind[64:, 1] = 1.0
    p_ = np.arange(128)
    maskall = np.zeros((128, 8, 128), np.float32)
    for g in range(8):
        maskall[p_, g, 16 * g + (p_ % 16)] = 1.0
    rep = np.zeros((8, 128), np.float32)
    rep[p_ // 16, p_] = 1.0
    cm = np.where(np.arange(128)[None, :] <= np.arange(128)[:, None], 0.0, -1e30).astype(np.float32)
    pw = np.tile((0.5 ** np.arange(1, 33, dtype=np.float64)).astype(np.float32)[None, :], (128, 1))
    return {'ident': ident, 'hconst': hconst, 'hcind': hcind, 'maskall': maskall,
            'rep': rep.astype(ml_dtypes.bfloat16), 'cm': cm, 'pw': pw}


def kernel(**inputs):
    x = np.ascontiguousarray(np.asarray(inputs['x'], dtype=np.float32))
    B, S, _ = x.shape
    nc = build_program(S)
    consts = host_consts()
    wmaps = {k: np.ascontiguousarray(np.asarray(inputs[k], dtype=np.float32)) for k in WSHAPES}
    in_maps = []
    for c in range(8):
        m = dict(wmaps)
        m['x'] = x[(c // 2) % B]
        m.update(consts)
        in_maps.append(m)
    res = run_bass_kernel_spmd(nc, in_maps, core_ids=list(range(8)))
    out = np.stack([res.results[2 * b]['y'] for b in range(B)], axis=0)
    return out.astype(np.float32)
```

```python
from contextlib import ExitStack
import numpy as np
import ml_dtypes
import concourse.bass as bass
import concourse.mybir as mybir
from concourse.bass_utils import run_bass_kernel_spmd

F32 = mybir.dt.float32
BF16 = mybir.dt.bfloat16
ALU = mybir.AluOpType
AF = mybir.ActivationFunctionType
AX = mybir.AxisListType

D = 1024
DFF = 2816
NFC = DFF // 128
DEPTH = 4
ALPHA = (2 * DEPTH) ** 0.25
LN_EPS = 1e-5
RMS_EPS = 1e-6
QR, KVR, IDXD, IDXH, AH = 384, 256, 64, 8, 8
AIN = QR + KVR + IDXD + IDXH

ENGS = ['tensor', 'vector', 'scalar', 'gpsimd', 'sync']
DMAQ = ['sync', 'scalar', 'gpsimd']


class Res:
    __slots__ = ('w', 'r', 'name', 'excl')

    def __init__(self, name='', excl=False):
        self.w = None
        self.r = []
        self.name = name
        self.excl = excl


class Prog:
    def __init__(self, nc, stack, n_dma_slots=4):
        self.nc = nc
        self.sem = {e: stack.enter_context(nc.semaphore(f"s_{e}")) for e in ENGS}
        self.cnt = {e: 0 for e in ENGS}
        self.nslots = n_dma_slots
        self.dsem, self.dcnt, self.dnext = {}, {}, {}
        for qn in DMAQ:
            for s in range(n_dma_slots):
                self.dsem[(qn, s)] = stack.enter_context(nc.semaphore(f"d_{qn}_{s}"))
                self.dcnt[(qn, s)] = 0
            self.dnext[qn] = 0
        self.seen = {e: {} for e in ENGS}
        self.ninst = 0

    def _wait(self, eng, prod, val):
        if self.seen[eng].get(prod, 0) >= val:
            return
        sem = self.sem[prod] if isinstance(prod, str) else self.dsem[prod]
        getattr(self.nc, eng).wait_ge(sem, val)
        self.seen[eng][prod] = val

    def _deps(self, eng, reads, writes):
        deps = []
        for b in reads:
            if b.w is not None:
                deps.append(b.w)
            if b.excl:
                deps.extend(b.r)
        for b in writes:
            if b.w is not None:
                if not (b.excl and eng == 'tensor' and b.w[0] == 'tensor'):
                    deps.append(b.w)
            deps.extend(b.r)
        for (p, v) in deps:
            self._wait(eng, p, v)

    def _commit(self, me, reads, writes):
        for b in reads:
            b.r.append(me)
        for b in writes:
            b.w = me
            b.r = []

    def op(self, eng, fn, reads=(), writes=(), inc=True):
        self._deps(eng, reads, writes)
        if inc:
            self.cnt[eng] += 1
            fn(getattr(self.nc, eng)).then_inc(self.sem[eng], 1)
            self._commit((eng, self.cnt[eng]), reads, writes)
        else:
            fn(getattr(self.nc, eng))
            self._commit((eng, self.cnt[eng] + 1), reads, writes)
        self.ninst += 1

    def dma(self, qn, out, in_, reads=(), writes=(), **kw):
        s = self.dnext[qn]
        self.dnext[qn] = (s + 1) % self.nslots
        key = (qn, s)
        prev = 16 * self.dcnt[key]
        if prev:
            self._wait(qn, key, prev)
        self._deps(qn, reads, writes)
        self.dcnt[key] += 1
        getattr(self.nc, qn).dma_start(out=out, in_=in_, **kw).then_inc(self.dsem[key], 16)
        self._commit((key, 16 * self.dcnt[key]), reads, writes)
        self.ninst += 1

    def barrier(self):
        for e in ENGS:
            for p in ENGS:
                if p != e and self.cnt[p]:
                    self._wait(e, p, self.cnt[p])
            for k, c in self.dcnt.items():
                if c:
                    self._wait(e, k, 16 * c)


class Rot:
    def __init__(self, tiles):
        self.t = [(t, Res()) for t in tiles]
        self.i = 0

    def next(self):
        r = self.t[self.i]
        self.i = (self.i + 1) % len(self.t)
        return r


def build_program(S, layers=(0, 1, 2, 3), first=True):
    layers = tuple(layers)
    NB = S // 128
    TOPK = min(256, S // 4)
    nc = bass.Bass("TRN2", target_bir_lowering=False)

    def din(name, shape, dt=F32):
        return nc.dram_tensor(name, list(shape), dt, kind="ExternalInput").ap()

    x_in = din("x", [S, D])
    Wn = {}
    for name, shape in WSHAPES.items():
        Wn[name] = din(name, shape)
    ident_in = din("ident", [128, 128], BF16)
    hconst_in = din("hconst", [128, 4, 128])
    hcind_in = din("hcind", [128, 4])
    maskall_in = din("maskall", [128, 8, 128])
    rep_in = din("rep", [8, 128], BF16)
    cm_in = din("cm", [128, 128])
    pw_in = din("pw", [128, 32])
    y_out = nc.dram_tensor("y", [S, D], F32, kind="ExternalOutput").ap()
    X32 = nc.dram_tensor("X32", [S, D], F32).ap()
    XTH = nc.dram_tensor("XTH", [NB + 1, 128, 8, 130], BF16).ap()
    rX32 = [Res() for _ in range(NB)]
    rXTH = [Res() for _ in range(NB + 1)]
    rXTHh = [Res() for _ in range(NB + 1)]
    rY = Res()
    WUb = nc.dram_tensor("WUb", [4, NFC, 128, 8, 256], BF16).ap()
    rWb = [Res() for _ in range(4)]

    with ExitStack() as top:
        P = Prog(nc, top)
        uid = [0]

        def sb(st, name, shape, dt):
            uid[0] += 1
            return st.enter_context(nc.sbuf_tensor(f"t{uid[0]}_{name}", list(shape), dt))
        pbank = [(top.enter_context(nc.psum_tensor(f"pb{i}", [128, 512], F32)), Res(excl=True)) for i in range(6)]
        ptb = [(top.enter_context(nc.psum_tensor(f"pt{i}", [128, 1024], BF16)), Res(excl=True)) for i in range(2)]
        pbi = [0]
        pti = [0]

        def PB():
            r = pbank[pbi[0]]
            pbi[0] = (pbi[0] + 1) % 4
            return r

        def PT():
            r = ptb[pti[0]]
            pti[0] = (pti[0] + 1) % len(ptb)
            return r

        epsT = sb(top, "epsT", [128, 2], F32)
        reps = Res()
        P.op('vector', lambda e: e.memset(epsT[:, 0:1], LN_EPS), [], [reps])
        P.op('vector', lambda e: e.memset(epsT[:, 1:2], RMS_EPS), [], [reps])

        def rsqrt(out, in_, scale, which, reads, writes):
            P.op('scalar', lambda e: e.activation(out=out, in_=in_, func=AF.Sqrt, scale=scale, bias=epsT[:, which:which + 1]),
                 list(reads) + [reps], writes)
            P.op('vector', lambda e: e.reciprocal(out=out, in_=out), writes, writes)

        ident = sb(top, "ident_sb", [128, 128], BF16)
        rid = Res()
        P.dma('sync', ident[:], ident_in[:, :], writes=[rid])

        def convert_wu(l, j):
            for part in range(2):
                col = part * DFF + j * 128
                P.dma('gpsimd', WUb[l, j, :, :, part * 128:(part + 1) * 128],
                      Wn['f_w_up'][l, :, col:col + 128].rearrange("(c p) n -> p c n", p=128), [], [rWb[l]])

        for j in range(NFC):
            convert_wu(layers[0], j)

        lnst = ExitStack()
        top.enter_context(lnst)
        yb_rot = Rot([sb(top, f"yb{i}", [128, D], BF16) for i in range(1)])
        xtb_rot = Rot([sb(top, f"xtb{i}", [128, 8, 128], BF16) for i in range(2)])
        yn_rot = Rot([sb(top, f"yn{i}", [128, D], F32) for i in range(1)])
        sq_rot = Rot([sb(top, f"sq{i}", [128, D], F32) for i in range(1)])
        st_rot = Rot([sb(top, f"st{i}", [128, 8], F32) for i in range(2)])
        eng_flip = [0]

        def transpose_store(b, y32, ry, halo):
            yb, ryb = yb_rot.next()
            P.op('scalar', lambda e: e.copy(out=yb[:], in_=y32[:]), [ry], [ryb])
            pt, rpt = PT()
            for c in range(8):
                P.op('tensor', lambda e, c=c: e.transpose(out=pt[:, c * 128:(c + 1) * 128],
                                                           in_=yb[:, c * 128:(c + 1) * 128], identity=ident[:]),
                     [ryb, rid], [rpt])
            xtb, rxtb = xtb_rot.next()
            P.op('vector', lambda e: e.tensor_copy(out=xtb[:].rearrange("p c t -> p (c t)"), in_=pt[:]), [rpt], [rxtb])
            P.dma('gpsimd', XTH[b, :, :, 2:130], xtb[:], [rxtb], [rXTH[b]])
            if halo:
                P.dma('gpsimd', XTH[b + 1, :, :, 0:2], xtb[:, :, 126:128], [rxtb], [rXTHh[b + 1]])

        def ln_tail(b, z, rz, gam, bet, rgb, to_out, halo):
            st_, rst = st_rot.next()
            sq, rsq = sq_rot.next()
            yn, ryn = yn_rot.next()
            s1, s2, mean, msq, var, rstd, nmr = [st_[:, i:i + 1] for i in range(7)]
            P.op('vector', lambda e: e.tensor_scalar(out=yn[:], in0=z[:], scalar1=1.0, scalar2=None, op0=ALU.mult,
                                                     op1=ALU.add, accum_out=s1), [rz], [ryn, rst])
            P.op('scalar', lambda e: e.activation(out=sq[:], in_=z[:], func=AF.Square), [rz], [rsq])
            P.op('vector', lambda e: e.tensor_scalar(out=sq[:], in0=sq[:], scalar1=1.0, scalar2=None, op0=ALU.mult,
                                                     op1=ALU.add, accum_out=s2), [rsq], [rsq, rst])
            P.op('vector', lambda e: e.tensor_scalar(out=mean, in0=s1, scalar1=1.0 / D, scalar2=None, op0=ALU.mult), [rst], [rst])
            P.op('vector', lambda e: e.tensor_tensor(out=msq, in0=mean, in1=mean, op=ALU.mult), [rst], [rst])
            P.op('vector', lambda e: e.scalar_tensor_tensor(out=var, in0=s2, scalar=1.0 / D, in1=msq, op0=ALU.mult,
                                                            op1=ALU.subtract), [rst], [rst])
            rsqrt(rstd, var, 1.0, 0, [rst], [rst])
            P.op('vector', lambda e: e.scalar_tensor_tensor(out=nmr, in0=mean, scalar=-1.0, in1=rstd, op0=ALU.mult,
                                                            op1=ALU.mult), [rst], [rst])
            P.op('scalar', lambda e: e.activation(out=yn[:], in_=z[:], func=AF.Identity, scale=rstd, bias=nmr),
                 [rz, rst], [ryn])
            P.op('gpsimd', lambda e: e.tensor_tensor(out=yn[:], in0=yn[:], in1=gam[:], op=ALU.mult), [ryn] + list(rgb), [ryn])
            P.op('vector', lambda e: e.tensor_tensor(out=yn[:], in0=yn[:], in1=bet[:], op=ALU.add), [ryn] + list(rgb), [ryn])
            if to_out:
                P.dma('gpsimd', y_out[b * 128:(b + 1) * 128, :], yn[:], [ryn], [rY])
            else:
                P.dma('gpsimd', X32[b * 128:(b + 1) * 128, :], yn[:], [ryn], [rX32[b]])
                transpose_store(b, yn, ryn, halo)

        if first:
            with ExitStack() as ph:
                zt = sb(ph, "zt", [128, 8, 2], BF16)
                rzt = Res()
                P.op('vector', lambda e: e.memset(zt[:], 0.0), [], [rzt])
                P.dma('sync', XTH[0, :, :, 0:2], zt[:], [rzt], [rXTHh[0]])
                xl_rot = Rot([sb(ph, f"xl{i}", [128, D], F32) for i in range(2)])
                for b in range(NB):
                    xl, rxl = xl_rot.next()
                    P.dma('scalar', xl[:], x_in[b * 128:(b + 1) * 128, :], [], [rxl])
                    transpose_store(b, xl, rxl, False)
                P.barrier()

        def xsrc(layer):
            return x_in if (first and layer == layers[0]) else X32

        def load_bc(st, name, src_row):
            t = sb(st, name, [128, src_row.shape[-1]], F32)
            r = Res()
            P.dma('sync', t[:], src_row.to_broadcast([128, src_row.shape[-1]]), [], [r])
            return t, r

        TT = 512 if NB % 4 == 0 else 128 * NB
        NBT = TT // 128

        def ffn_phase(layer, last):
            with ExitStack() as ph:
                wd = sb(ph, "wd", [128, NFC, D], BF16)
                rw = Res()
                for j in range(0, NFC, 2):
                    P.dma('gpsimd', wd[:, j:j + 2, :],
                          Wn['f_w_down'][layer, j * 128:(j + 2) * 128, :].rearrange("(c p) n -> p c n", p=128), [], [rw])
                cw = sb(ph, "cw", [128, 3, 2 * NFC], F32)
                cb = sb(ph, "cb", [128, 2 * NFC], F32)
                for k in range(3):
                    P.dma('scalar', cw[:, k, :], Wn['f_conv_w'][layer, k, 0, :].rearrange("(j p) -> p j", p=128), [], [rw],
                          allow_slow_non_contiguous=True)
                P.dma('scalar', cb[:], Wn['f_conv_b'][layer, :].rearrange("(j p) -> p j", p=128), [], [rw],
                      allow_slow_non_contiguous=True)
                gam = sb(ph, "gam2", [128, D], F32)
                bet = sb(ph, "bet2", [128, D], F32)
                P.dma('sync', gam[:], Wn['ln2_g'][layer:layer + 1, :].to_broadcast([128, D]), [], [rw])
                P.dma('sync', bet[:], Wn['ln2_b'][layer:layer + 1, :].to_broadcast([128, D]), [], [rw])
                carry = sb(ph, "carry", [128, 2 * NFC, 2], F32)
                rcar = [Res() for _ in range(2 * NFC)]
                P.op('gpsimd', lambda e: e.memset(carry[:], 0.0), [], rcar)
                wu_rot = Rot([sb(ph, f"wu{i}", [128, 8, 256], BF16) for i in range(3)])
                xw_rot = Rot([sb(ph, f"xw{i}", [128, 8, TT], BF16) for i in range(2)])
                g_rot = Rot([sb(ph, f"g{i}", [128, NFC, TT], BF16) for i in range(2)])
                hs_rot = Rot([sb(ph, f"hs{i}", [128, TT + 2], F32) for i in range(4)])
                acc_rot = Rot([sb(ph, f"acc{i}", [128, TT], F32) for i in range(6)])
                sa_rot = Rot([sb(ph, f"sa{i}", [128, TT], F32) for i in range(2)])
                x32_rot = Rot([sb(ph, f"x32{i}", [128, D], F32) for i in range(4 if NBT == 4 else NBT)])
                z_rot = Rot([sb(ph, f"z{i}", [128, D], F32) for i in range(2)])
                for t in range(NB // NBT):
                    b0 = t * NBT
                    xw, rxw = xw_rot.next()
                    for i in range(NBT):
                        P.dma('sync', xw[:, :, i * 128:(i + 1) * 128], XTH[b0 + i, :, :, 2:130], [rXTH[b0 + i]], [rxw])
                    g, rg = g_rot.next()
                    pend_gate = None
                    x32s = []
                    for i in range(NBT):
                        x32, rx32 = x32_rot.next()
                        P.dma('sync', x32[:], X32[(b0 + i) * 128:(b0 + i + 1) * 128, :], [rX32[b0 + i]], [rx32])
                        x32s.append((x32, rx32))

                    def emit_gate(accs, j, g, rg):
                        sa, rsa = sa_rot.next()
                        P.op('scalar', lambda e, sa=sa, a=accs[0][0]: e.activation(out=sa[:], in_=a[:], func=AF.Silu),
                             [accs[0][1]], [rsa])
                        P.op('vector', lambda e, sa=sa, u=accs[1][0], g=g, j=j: e.tensor_tensor(
                            out=g[:, j, :], in0=sa[:], in1=u[:], op=ALU.mult), [rsa, accs[1][1]], [rg])

                    for j in range(NFC):
                        if t == 0 and layer != layers[-1]:
                            convert_wu(layers[layers.index(layer) + 1], j)
                        wu, rwu = wu_rot.next()
                        P.dma('sync', wu[:], WUb[layer, j], [rWb[layer]], [rwu])
                        accs = []
                        for part in range(2):
                            jj = part * NFC + j
                            ps, rps = PB()
                            for c in range(8):
                                P.op('tensor', lambda e, c=c, part=part, ps=ps, wu=wu, xw=xw: e.matmul(
                                    ps[:, 0:TT], lhsT=wu[:, c, part * 128:(part + 1) * 128], rhs=xw[:, c, :],
                                    start=(c == 0), stop=(c == 7)), [rwu, rxw], [rps], inc=(c == 7))
                            hs, rhs_ = hs_rot.next()
                            P.op('gpsimd', lambda e, hs=hs, jj=jj: e.tensor_copy(out=hs[:, 0:2], in_=carry[:, jj, :]), [rcar[jj]], [rhs_])
                            P.op('scalar', lambda e, hs=hs, ps=ps: e.copy(out=hs[:, 2:TT + 2], in_=ps[:, 0:TT]), [rps], [rhs_])
                            P.op('gpsimd', lambda e, hs=hs, jj=jj: e.tensor_copy(out=carry[:, jj, :], in_=hs[:, TT:TT + 2]), [rhs_], [rcar[jj]])
                            acc, racc = acc_rot.next()
                            eng = 'vector' if part == 0 else 'gpsimd'
                            if part == 0:
                                P.op('scalar', lambda e, acc=acc, hs=hs, jj=jj: e.activation(
                                    out=acc[:], in_=hs[:, 0:TT], func=AF.Identity, scale=cw[:, 0, jj:jj + 1], bias=cb[:, jj:jj + 1]),
                                    [rhs_, rw], [racc])
                            else:
                                P.op(eng, lambda e, acc=acc, hs=hs, jj=jj: e.tensor_scalar(
                                    out=acc[:], in0=hs[:, 0:TT], scalar1=cw[:, 0, jj:jj + 1], scalar2=cb[:, jj:jj + 1],
                                    op0=ALU.mult, op1=ALU.add), [rhs_, rw], [racc])
                            for k in (1, 2):
                                P.op('vector', lambda e, acc=acc, hs=hs, jj=jj, k=k: e.scalar_tensor_tensor(
                                    out=acc[:], in0=hs[:, k:k + TT], scalar=cw[:, k, jj:jj + 1], in1=acc[:],
                                    op0=ALU.mult, op1=ALU.add), [rhs_, rw, racc], [racc])
                            accs.append((acc, racc))
                        if pend_gate is not None:
                            emit_gate(*pend_gate)
                        pend_gate = (accs, j, g, rg)
                    emit_gate(*pend_gate)
                    pend_gate = None
                    for i in range(NBT):
                        b = b0 + i
                        x32, rx32 = x32s[i]
                        z, rz = z_rot.next()
                        for half in range(2):
                            po, rpo = PB()
                            for j in range(NFC):
                                P.op('tensor', lambda e, j=j, half=half, po=po, g=g, i=i: e.matmul(
                                    po[:, :], lhsT=g[:, j, i * 128:(i + 1) * 128], rhs=wd[:, j, half * 512:(half + 1) * 512],
                                    start=(j == 0), stop=(j == NFC - 1)), [rg, rw], [rpo], inc=(j == NFC - 1))
                            P.op('vector', lambda e, half=half, po=po, z=z, x32=x32: e.scalar_tensor_tensor(
                                out=z[:, half * 512:(half + 1) * 512], in0=x32[:, half * 512:(half + 1) * 512], scalar=ALPHA,
                                in1=po[:, :], op0=ALU.mult, op1=ALU.add), [rpo, rx32], [rz])
                        ln_tail(b, z, rz, gam, bet, [rw], last, False)
                P.barrier()

        ph_hs = [sb(top, "hsb", [128, 130], F32)]
        rhs = Res()

        def identity_mixer_phase(layer):
            with ExitStack() as ph:
                gam, rg1 = load_bc(ph, "gam1", Wn['ln1_g'][layer:layer + 1, :])
                bet, rb1 = load_bc(ph, "bet1", Wn['ln1_b'][layer:layer + 1, :])
                rgb = Res()
                P.op('vector', lambda e: e.memset(ph_hs[0][:, 0:1], 0.0), [rg1, rb1, rhs], [rgb, rhs])
                x32_rot = Rot([sb(ph, f"mx32{i}", [128, D], F32) for i in range(2)])
                z_rot = Rot([sb(ph, f"mz{i}", [128, D], F32) for i in range(2)])
                for b in range(NB):
                    x32, rx32 = x32_rot.next()
                    P.dma('sync', x32[:], xsrc(layer)[b * 128:(b + 1) * 128, :], [rX32[b]], [rx32])
                    z, rz = z_rot.next()
                    P.op('vector', lambda e, z=z, x32=x32: e.tensor_scalar(out=z[:], in0=x32[:], scalar1=ALPHA, scalar2=None,
                                                                        op0=ALU.mult), [rx32], [rz])
                    ln_tail(b, z, rz, gam, bet, [rgb], False, True)
                P.barrier()


        def hgrn_phase(layer):
            j = layer // 2
            with ExitStack() as ph:
                win = sb(ph, "hwin", [128, 8, 4 * D], BF16)
                wout = sb(ph, "hwout", [128, 8, D], BF16)
                rw = Res()
                for c in range(8):
                    P.dma('gpsimd', win[:, c, :], Wn['b_w_in'][j, c * 128:(c + 1) * 128, :], [], [rw])
                P.dma('gpsimd', wout[:], Wn['b_w_out'][j].rearrange("(c p) n -> p c n", p=128), [], [rw])
                gam, rg1 = load_bc(ph, "hgam1", Wn['ln1_g'][layer:layer + 1, :])
                bet, rb1 = load_bc(ph, "hbet1", Wn['ln1_b'][layer:layer + 1, :])
                go, rgo = load_bc(ph, "hgo", Wn['b_g_o'][j:j + 1, :])
                cst = sb(ph, "hcst", [128, 3, 128], F32)
                cind = sb(ph, "hcind", [128, 4], F32)
                cmask = sb(ph, "hcmask", [128, 128], F32)
                cmask4 = sb(ph, "hcmask4", [128, 4, 128], F32)
                rc = Res()
                P.dma('sync', cst[:], hconst_in[:, 0:3, :], [], [rc])
                P.dma('sync', cmask[:], hconst_in[:, 3, :], [], [rc])
                for i4 in range(4):
                    P.dma('sync', cmask4[:, i4, :], hconst_in[:, 3, :], [], [rc])
                P.dma('sync', cind[:], hcind_in[:, :], [], [rc])
                lb = sb(ph, "hlb", [128, D], F32)
                oml = sb(ph, "homl", [128, D], F32)
                rl = Res()
                tmp = ExitStack()
                lg = [load_bc(tmp, f"hlg{l}", Wn['b_lb_logits'][l:l + 1, :]) for l in range(4)]
                mx = sb(tmp, "hmx", [128, D], F32)
                den = sb(tmp, "hden", [128, D], F32)
                P.op('vector', lambda e: e.tensor_tensor(out=mx[:], in0=lg[0][0][:], in1=lg[1][0][:], op=ALU.max),
                     [lg[0][1], lg[1][1]], [rl])
                for l in (2, 3):
                    P.op('vector', lambda e, l=l: e.tensor_tensor(out=mx[:], in0=mx[:], in1=lg[l][0][:], op=ALU.max),
                         [lg[l][1], rl], [rl])
                for l in range(4):
                    P.op('vector', lambda e, l=l: e.tensor_tensor(out=lg[l][0][:], in0=lg[l][0][:], in1=mx[:], op=ALU.subtract),
                         [rl, lg[l][1]], [lg[l][1]])
                    P.op('scalar', lambda e, l=l: e.activation(out=lg[l][0][:], in_=lg[l][0][:], func=AF.Exp),
                         [lg[l][1]], [lg[l][1]])
                P.op('vector', lambda e: e.tensor_tensor(out=den[:], in0=lg[0][0][:], in1=lg[1][0][:], op=ALU.add),
                     [lg[0][1], lg[1][1]], [rl])
                for l in (2, 3):
                    P.op('vector', lambda e, l=l: e.tensor_tensor(out=den[:], in0=den[:], in1=lg[l][0][:], op=ALU.add),
                         [lg[l][1], rl], [rl])
                P.op('vector', lambda e: e.tensor_copy(out=lb[:], in_=lg[1][0][:]), [lg[1][1], rl], [rl])
                for l in range(2, layer + 1):
                    P.op('vector', lambda e, l=l: e.tensor_tensor(out=lb[:], in0=lb[:], in1=lg[l][0][:], op=ALU.add),
                         [lg[l][1], rl], [rl])
                P.op('vector', lambda e: e.reciprocal(out=den[:], in_=den[:]), [rl], [rl])
                P.op('vector', lambda e: e.tensor_tensor(out=lb[:], in0=lb[:], in1=den[:], op=ALU.mult), [rl], [rl])
                P.op('vector', lambda e: e.tensor_scalar(out=oml[:], in0=lb[:], scalar1=-1.0, scalar2=1.0, op0=ALU.mult,
                                                         op1=ALU.add), [rl], [rl])
                P.barrier()
                tmp.close()
                Sf = sb(ph, "hSf", [128, 8, 128], F32)
                SbA = sb(ph, "hSbA", [128, 8, 128], BF16)
                SbB = sb(ph, "hSbB", [128, 8, 128], BF16)
                q0T = sb(ph, "hq0T", [128, 8, 128], BF16)
                q1T = sb(ph, "hq1T", [128, 8, 128], BF16)
                kT = sb(ph, "hkT", [128, 8, 128], BF16)
                qT = sb(ph, "hqT", [128, 8, 128], BF16)
                rS = Res()
                rq = Res()
                P.op('vector', lambda e: e.memset(Sf[:], 0.0), [], [rS])
                P.op('gpsimd', lambda e: e.memset(q0T[:], 0.0), [], [rq])
                P.op('gpsimd', lambda e: e.memset(q1T[:], 0.0), [], [rq])
                xh_rot = Rot([sb(ph, f"hxh{i}", [128, 8, 130], BF16) for i in range(2)])
                x32_rot = Rot([sb(ph, f"hx32{i}", [128, D], F32) for i in range(2)])
                F = lambda n: sb(ph, n, [128, D], F32)
                sig, lf, key, bb, bl, qs, og = F("hsig"), F("hlf"), F("hkey"), F("hbb"), F("hbl"), F("hqs"), F("hog")
                zt_ = qs
                gateT = F("hgate")
                rqs, rvv, rgate = Res(), Res(), Res()
                B16 = lambda n: sb(ph, n, [128, D], BF16)
                qtl, ktl, k0, k1, vv, onb = B16("hqtl"), B16("hktl"), B16("hk0"), B16("hk1"), B16("hvv"), B16("honb")
                onT = sb(ph, "honT", [128, 8, 128], BF16)
                scT = sb(ph, "hscT", [128, 8, 128], BF16)
                rscT = Res()
                rog = Res()
                rSb = {id(SbA): Res(), id(SbB): Res()}
                P.op('vector', lambda e: e.memset(SbA[:], 0.0), [], [rSb[id(SbA)]])
                dec = sb(ph, "hdec", [128, 8, 2], F32)
                ss = sb(ph, "hss", [128, 8], F32)
                rt = Res()

                def proj(part, xh, rxh):
                    outs = []
                    for half in range(2):
                        ps, rps = PB()
                        col = part * D + half * 512
                        for c in range(8):
                            P.op('tensor', lambda e, c=c, col=col, ps=ps: e.matmul(
                                ps[:, :], lhsT=xh[:, c, 2:130], rhs=win[:, c, col:col + 512],
                                start=(c == 0), stop=(c == 7)), [rw, rxh], [rps], inc=(c == 7))
                        outs.append((ps, rps))
                    return outs

                for b in range(NB):
                    xh, rxh = xh_rot.next()
                    P.dma('sync', xh[:, :, 2:130], XTH[b, :, :, 2:130], [rXTH[b]], [rxh])
                    x32, rx32 = x32_rot.next()
                    P.dma('sync', x32[:], xsrc(layer)[b * 128:(b + 1) * 128, :], [rX32[b]], [rx32])
                    for half, (ps, rps) in enumerate(proj(1, xh, rxh)):
                        hs = slice(half * 512, (half + 1) * 512)
                        P.op('scalar', lambda e, ps=ps, hs=hs: e.activation(out=sig[:, hs], in_=ps[:, :], func=AF.Sigmoid),
                             [rps], [rt])
                    for half, (ps, rps) in enumerate(proj(0, xh, rxh)):
                        hs = slice(half * 512, (half + 1) * 512)
                        P.op('scalar', lambda e, ps=ps, hs=hs: e.activation(out=qs[:, hs], in_=ps[:, :], func=AF.Silu),
                             [rps], [rqs])
                    for half, (ps, rps) in enumerate(proj(2, xh, rxh)):
                        hs = slice(half * 512, (half + 1) * 512)
                        P.op('vector', lambda e, ps=ps, hs=hs: e.tensor_copy(out=vv[:, hs], in_=ps[:, :]), [rps], [rvv])
                    for half, (ps, rps) in enumerate(proj(3, xh, rxh)):
                        hs = slice(half * 512, (half + 1) * 512)
                        P.op('scalar', lambda e, ps=ps, hs=hs: e.activation(out=gateT[:, hs], in_=ps[:, :], func=AF.Sigmoid),
                             [rps], [rgate])
                    P.op('vector', lambda e: e.tensor_tensor(out=lf[:], in0=sig[:], in1=oml[:], op=ALU.mult), [rt, rl], [rt])
                    P.op('vector', lambda e: e.tensor_tensor(out=key[:], in0=oml[:], in1=lf[:], op=ALU.subtract), [rt, rl], [rt])
                    P.op('vector', lambda e: e.tensor_tensor(out=lf[:], in0=lf[:], in1=lb[:], op=ALU.add), [rt, rl], [rt])
                    P.op('scalar', lambda e: e.activation(out=lf[:], in_=lf[:], func=AF.Ln), [rt], [rt])
                    for half in range(2):
                        hs = slice(half * 512, (half + 1) * 512)
                        ps, rps = PB()
                        P.op('tensor', lambda e, ps=ps, hs=hs: e.matmul(ps[:, :], lhsT=cst[:, 0, :], rhs=lf[:, hs],
                                                                       start=True, stop=True), [rt, rc], [rps])
                        P.op('vector', lambda e, ps=ps, hs=hs: e.tensor_copy(out=bb[:, hs], in_=ps[:, :]), [rps], [rt])
                        ps, rps = PB()
                        P.op('tensor', lambda e, ps=ps, hs=hs: e.matmul(ps[:, :], lhsT=cst[:, 1, :], rhs=lf[:, hs],
                                                                       start=True, stop=True), [rt, rc], [rps])
                        P.op('vector', lambda e, ps=ps, hs=hs: e.tensor_copy(out=bl[:, hs], in_=ps[:, :]), [rps], [rt])
                    ps, rps = PB()
                    for h in range(8):
                        P.op('tensor', lambda e, h=h, ps=ps: e.matmul(ps[:, h * 2:h * 2 + 2], lhsT=lf[:, h * 128:(h + 1) * 128],
                                                                     rhs=cind[:, 0:2], start=True, stop=True), [rt, rc], [rps])
                    P.op('scalar', lambda e, ps=ps: e.activation(out=dec[:].rearrange("p h c -> p (h c)"), in_=ps[:, 0:16],
                                                                 func=AF.Exp), [rps], [rt])
                    P.op('vector', lambda e: e.tensor_tensor(out=bl[:], in0=bl[:], in1=bb[:], op=ALU.subtract), [rt], [rt])
                    P.op('scalar', lambda e: e.activation(out=bl[:], in_=bl[:], func=AF.Exp), [rt], [rt])
                    P.op('vector', lambda e: e.tensor_tensor(out=bl[:], in0=bl[:], in1=key[:], op=ALU.mult), [rt], [rt])
                    P.op('vector', lambda e: e.tensor_scalar(out=k0[:], in0=bl[:], scalar1=cind[:, 0:1], scalar2=None,
                                                             op0=ALU.mult), [rt, rc], [rt])
                    P.op('vector', lambda e: e.tensor_scalar(out=k1[:], in0=bl[:], scalar1=cind[:, 1:2], scalar2=None,
                                                             op0=ALU.mult), [rt, rc], [rt])
                    P.op('scalar', lambda e: e.activation(out=sig[:], in_=bb[:], func=AF.Exp, scale=-1.0), [rt], [rt])
                    P.op('vector', lambda e: e.tensor_tensor(out=ktl[:], in0=sig[:], in1=key[:], op=ALU.mult), [rt], [rt])
                    P.op('scalar', lambda e: e.activation(out=bb[:], in_=bb[:], func=AF.Exp), [rt], [rt])
                    P.op('vector', lambda e: e.tensor_tensor(out=qtl[:], in0=qs[:], in1=bb[:], op=ALU.mult), [rt, rqs], [rt])
                    for (src, dsts) in ((qtl, 'q'), (ktl, 'k')):
                        pt, rpt = PT()
                        for h in range(8):
                            P.op('tensor', lambda e, h=h, pt=pt, src=src: e.transpose(
                                out=pt[:, h * 128:(h + 1) * 128], in_=src[:, h * 128:(h + 1) * 128], identity=ident[:]),
                                [rt, rid], [rpt])
                        ptv = pt[:].rearrange("p (h t) -> p h t", h=8)
                        if dsts == 'q':
                            P.op('vector', lambda e, ptv=ptv: e.tensor_copy(out=qT[:], in_=ptv), [rpt], [rq])
                            P.op('vector', lambda e, ptv=ptv: e.tensor_copy(out=q0T[:, :, 0:64], in_=ptv[:, :, 0:64]), [rpt], [rq])
                            P.op('vector', lambda e, ptv=ptv: e.tensor_copy(out=q1T[:, :, 64:128], in_=ptv[:, :, 64:128]), [rpt], [rq])
                        else:
                            P.op('vector', lambda e, ptv=ptv: e.tensor_copy(out=kT[:], in_=ptv), [rpt], [rq])
                    pu = [pbank[0], pbank[1]]
                    psc = [pbank[2], pbank[3]]
                    po = [pbank[4], pbank[5]]
                    Sfv = Sf[:].rearrange("p h e -> p (h e)")

                    def state_update(kk, ci, Sb):
                        for h in range(8):
                            hsl = slice(h * 128, (h + 1) * 128)
                            pb_, rpb = pu[h // 4]
                            P.op('tensor', lambda e, pb_=pb_, h=h, hsl=hsl: e.matmul(
                                pb_[:, (h % 4) * 128:(h % 4 + 1) * 128], lhsT=kk[:, hsl], rhs=vv[:, hsl], start=True, stop=True),
                                [rt, rvv], [rpb])
                        P.op('vector', lambda e: e.tensor_tensor(out=Sf[:], in0=Sf[:], in1=dec[:, :, ci:ci + 1].to_broadcast([128, 8, 128]),
                                                                 op=ALU.mult), [rt, rS], [rS])
                        for hh in range(2):
                            pb_, rpb = pu[hh]
                            P.op('vector', lambda e, pb_=pb_, hh=hh: e.tensor_tensor(
                                out=Sfv[:, hh * 512:(hh + 1) * 512], in0=Sfv[:, hh * 512:(hh + 1) * 512], in1=pb_[:, :], op=ALU.add),
                                [rpb, rS], [rS])
                        P.op('scalar', lambda e: e.copy(out=Sb[:], in_=Sf[:]), [rS], [rSb[id(Sb)]])

                    state_update(k0, 0, SbB)
                    for h in range(8):
                        pb_, rpb = psc[h // 4]
                        P.op('tensor', lambda e, pb_=pb_, h=h: e.matmul(pb_[:, (h % 4) * 128:(h % 4 + 1) * 128], lhsT=kT[:, h, :],
                                                                       rhs=qT[:, h, :], start=True, stop=True), [rq], [rpb])
                    for hh in range(2):
                        pb_, rpb = psc[hh]
                        P.op('vector', lambda e, pb_=pb_, hh=hh: e.tensor_tensor(
                            out=scT[:, hh * 4:(hh + 1) * 4, :], in0=pb_[:, :].rearrange("p (h t) -> p h t", h=4), in1=cmask4[:],
                            op=ALU.mult), [rpb, rc], [rscT])
                    for h in range(8):
                        hsl = slice(h * 128, (h + 1) * 128)
                        pb_, rpb = po[h // 4]
                        osl = slice((h % 4) * 128, (h % 4 + 1) * 128)
                        P.op('tensor', lambda e, pb_=pb_, h=h, hsl=hsl, osl=osl: e.matmul(
                            pb_[:, osl], lhsT=scT[:, h, :], rhs=vv[:, hsl], start=True, stop=False), [rscT, rt, rvv], [rpb])
                        P.op('tensor', lambda e, pb_=pb_, h=h, osl=osl: e.matmul(
                            pb_[:, osl], lhsT=q0T[:, h, :], rhs=SbA[:, h, :], start=False, stop=False), [rq, rSb[id(SbA)]], [rpb])
                        P.op('tensor', lambda e, pb_=pb_, h=h, osl=osl: e.matmul(
                            pb_[:, osl], lhsT=q1T[:, h, :], rhs=SbB[:, h, :], start=False, stop=True), [rq, rSb[id(SbB)]], [rpb])
                    for hh in range(2):
                        pb_, rpb = po[hh]
                        P.op('vector', lambda e, pb_=pb_, hh=hh: e.tensor_tensor(
                            out=og[:, hh * 512:(hh + 1) * 512], in0=pb_[:, :], in1=gateT[:, hh * 512:(hh + 1) * 512], op=ALU.mult),
                            [rpb, rgate], [rog])
                    state_update(k1, 1, SbA)
                    P.op('scalar', lambda e: e.activation(out=qs[:], in_=og[:], func=AF.Square), [rt, rog, rqs], [rt, rqs])
                    P.op('vector', lambda e: e.tensor_reduce(out=ss[:], in_=qs[:].rearrange("p (h e) -> p h e", h=8),
                                                             axis=AX.X, op=ALU.add), [rt], [rt])
                    rsqrt(ss[:], ss[:], 1.0 / 128, 1, [rt], [rt])
                    for h in range(8):
                        hsl = slice(h * 128, (h + 1) * 128)
                        P.op('vector', lambda e, h=h, hsl=hsl: e.scalar_tensor_tensor(
                            out=onb[:, hsl], in0=og[:, hsl], scalar=ss[:, h:h + 1], in1=go[:, hsl], op0=ALU.mult, op1=ALU.mult),
                            [rt, rgo, rog], [rt])
                    pt, rpt = PT()
                    for c in range(8):
                        P.op('tensor', lambda e, c=c, pt=pt: e.transpose(out=pt[:, c * 128:(c + 1) * 128],
                                                                        in_=onb[:, c * 128:(c + 1) * 128], identity=ident[:]),
                             [rt, rid], [rpt])
                    P.op('vector', lambda e, pt=pt: e.tensor_copy(out=onT[:].rearrange("p c t -> p (c t)"), in_=pt[:]), [rpt], [rt])
                    for half in range(2):
                        po, rpo = PB()
                        for c in range(8):
                            P.op('tensor', lambda e, c=c, half=half, po=po: e.matmul(
                                po[:, :], lhsT=onT[:, c, :], rhs=wout[:, c, half * 512:(half + 1) * 512],
                                start=(c == 0), stop=(c == 7)), [rt, rw], [rpo], inc=(c == 7))
                        P.op('vector', lambda e, half=half, po=po, x32=x32: e.scalar_tensor_tensor(
                            out=zt_[:, half * 512:(half + 1) * 512], in0=x32[:, half * 512:(half + 1) * 512], scalar=ALPHA,
                            in1=po[:, :], op0=ALU.mult, op1=ALU.add), [rpo, rx32, rqs], [rt, rqs])
                    ln_tail(b, zt_, rt, gam, bet, [rg1, rb1], False, False)
                P.barrier()


        NIT = 18

        def dsa_phase(layer):
            j = layer // 2
            with ExitStack() as ph:
                win = sb(ph, "awin", [128, 8, AIN], BF16)
                wql = sb(ph, "awql", [128, 3, AH * KVR], BF16)
                wqi = sb(ph, "awqi", [128, 3, IDXH * IDXD], BF16)
                wuv = sb(ph, "awuv", [128, 2 * AH, 128], BF16)
                wout = sb(ph, "awout", [128, 8, D], BF16)
                rw = Res()
                P.dma('gpsimd', win[:], Wn['a_w_in'][j].rearrange("(c p) n -> p c n", p=128), [], [rw])
                P.dma('gpsimd', wql[:], Wn['a_w_q_lat'][j].rearrange("(c p) n -> p c n", p=128), [], [rw])
                P.dma('gpsimd', wqi[:], Wn['a_w_q_idx'][j].rearrange("(c p) n -> p c n", p=128), [], [rw])
                P.dma('gpsimd', wuv[:], Wn['a_w_uv'][j].rearrange("h (rc p) d -> p (h rc) d", p=128), [], [rw])
                P.dma('gpsimd', wout[:], Wn['a_w_out'][j].rearrange("(c p) n -> p c n", p=128), [], [rw])
                gam, rg1 = load_bc(ph, "agam1", Wn['ln1_g'][layer:layer + 1, :])
                bet, rb1 = load_bc(ph, "abet1", Wn['ln1_b'][layer:layer + 1, :])
                gq, rgq = load_bc(ph, "agq", Wn['a_g_q'][j:j + 1, :])
                gkv, rgkv = load_bc(ph, "agkv", Wn['a_g_kv'][j:j + 1, :])
                gki, rgki = load_bc(ph, "agki", Wn['a_g_kidx'][j:j + 1, :])
                bki, rbki = load_bc(ph, "abki", Wn['a_b_kidx'][j:j + 1, :])
                maskall = sb(ph, "amaskall", [128, 8, 128], F32)
                rep = sb(ph, "arep", [8, 128], BF16)
                cm = sb(ph, "acm", [128, 128], F32)
                pw = sb(ph, "apw", [128, NIT], F32)
                rc = Res()
                P.dma('sync', maskall[:], maskall_in[:, :, :], [], [rc])
                P.dma('sync', rep[:], rep_in[:, :], [], [rc])
                P.dma('sync', cm[:], cm_in[:, :], [], [rc])
                P.dma('sync', pw[:], pw_in[:, 0:NIT], [], [rc])
                CKV = sb(ph, "aCKV", [128, NB, KVR + 1], BF16)
                CKVT = sb(ph, "aCKVT", [128, 2, S], BF16)
                KIT = sb(ph, "aKIT", [64, S], BF16)
                rK = Res()
                P.op('vector', lambda e: e.memset(CKV[:, :, KVR:KVR + 1], 1.0), [], [rK])
                score = sb(ph, "ascore", [128, S], F32)
                msk = sb(ph, "amsk", [128, S], BF16)
                junk = msk
                zA = sb(ph, "azA", [128, 328], F32)
                mskT_rot = Rot([sb(ph, f"amskT{i}", [128, NB, 128], BF16) for i in range(2)])
                x32_rot = Rot([sb(ph, f"ax32{i}", [128, D], F32) for i in range(1)])
                xh_rot = Rot([sb(ph, f"axh{i}", [128, 8, 130], BF16) for i in range(2)])
                sqa = sb(ph, "asqa", [128, QR], F32)
                cq = sb(ph, "acq", [128, QR], BF16)
                kix = sb(ph, "akix", [128, IDXD], F32)
                kib = sb(ph, "akib", [128, IDXD], BF16)
                wq = sb(ph, "awq", [128, 8], BF16)
                st = sb(ph, "ast", [128, 16], F32)
                cqT_rot = Rot([sb(ph, f"acqT{i}", [128, 3, 128], BF16) for i in range(2)])
                wT = sb(ph, "awT", [8, 128], BF16)
                QIT = sb(ph, "aQIT", [64, 8, 8, 16], BF16)
                Wsel = sb(ph, "aWsel", [128, 8, 128], BF16)
                R_rot = Rot([sb(ph, f"aR{i}", [128, 512], BF16) for i in range(3)])
                mm = sb(ph, "amm", [128, 2, 8], F32)
                bs = sb(ph, "abs", [128, 8], F32)
                Wt = sb(ph, "aWt", [128, NIT], F32)
                qlT4 = sb(ph, "aqlT4", [128, 2, 512], BF16)
                PT_rot = Rot([sb(ph, f"aPT{i}", [128, 4, 128], BF16) for i in range(5)])
                olb = sb(ph, "aolb", [128, AH, KVR], BF16)
                olT = sb(ph, "aolT8", [128, AH, 2, 128], BF16)
                dcol = sb(ph, "adcol", [128, AH], F32)
                rd = sb(ph, "ard8", [128, AH], F32)
                rolb, rolT, rrd = Res(), Res(), Res()
                oh = sb(ph, "aoh", [128, D], BF16)
                ohT = sb(ph, "aohT", [128, 8, 128], BF16)
                rden = sb(ph, "arden", [128, 1], F32)
                z = sb(ph, "az", [128, D], F32)
                rt = Res()
                r_z, r_q, r_cq, r_kv, r_ki, r_w, r_cqT, r_wT, r_qit, r_wsel, r_mm, r_bs = [Res() for _ in range(12)]
                sqa_kv = sb(ph, "asqakv", [128, KVR], F32)
                sqa_ki = sb(ph, "asqaki", [128, IDXD], F32)
                rsc = Res()
                rmk = Res()
                rql = Res()
                ro = Res()

                def rmsn(ps, rps0, c0, n, gtile, rg, out_ap, idx):
                    ss, rs = st[:, idx:idx + 1], st[:, idx + 1:idx + 2]
                    P.op('scalar', lambda e: e.activation(out=sqa[:, 0:n], in_=ps[:, c0:c0 + n], func=AF.Square), [rps0], [r_q])
                    P.op('vector', lambda e: e.tensor_scalar(out=sqa[:, 0:n], in0=sqa[:, 0:n], scalar1=1.0, scalar2=None,
                                                             op0=ALU.mult, op1=ALU.add, accum_out=ss), [r_q], [r_q])
                    rsqrt(rs, ss, 1.0 / n, 1, [r_q], [r_q])
                    P.op('vector', lambda e: e.scalar_tensor_tensor(out=out_ap, in0=ps[:, c0:c0 + n], scalar=rs, in1=gtile[:, 0:n],
                                                                    op0=ALU.mult, op1=ALU.mult), [rps0, r_q, rg], [r_cq])

                blk = {}
                mts = {}
                r_zo = Res()

                def SB(b):
                    cqT, r_cqT = cqT_rot.next()
                    mskT, rmkT = mskT_rot.next()
                    L = (b + 1) * 128
                    xh, rxh = xh_rot.next()
                    P.dma('sync', xh[:, :, 2:130], XTH[b, :, :, 2:130], [rXTH[b]], [rxh])
                    ps0, rps0 = PB()
                    ps1, rps1 = PB()
                    for c in range(8):
                        P.op('tensor', lambda e, c=c, ps0=ps0: e.matmul(ps0[:, :], lhsT=xh[:, c, 2:130], rhs=win[:, c, 0:512],
                                                                       start=(c == 0), stop=(c == 7)), [rw, rxh], [rps0], inc=(c == 7))
                    for c in range(8):
                        P.op('tensor', lambda e, c=c, ps1=ps1: e.matmul(ps1[:, 0:AIN - 512], lhsT=xh[:, c, 2:130], rhs=win[:, c, 512:AIN],
                                                                       start=(c == 0), stop=(c == 7)), [rw, rxh], [rps1], inc=(c == 7))
                    rmsn(ps0, rps0, 0, QR, gq, rgq, cq[:], 0)
                    P.op('scalar', lambda e, ps0=ps0: e.copy(out=zA[:, 0:128], in_=ps0[:, QR:512]), [rps0], [r_z])
                    P.op('scalar', lambda e, ps1=ps1: e.copy(out=zA[:, 128:128 + 200], in_=ps1[:, 0:200]), [rps1], [r_z])
                    ss, rs = st[:, 2:3], st[:, 3:4]
                    P.op('scalar', lambda e: e.activation(out=sqa_kv[:, 0:KVR], in_=zA[:, 0:KVR], func=AF.Square), [r_z], [r_kv])
                    P.op('vector', lambda e: e.tensor_scalar(out=sqa_kv[:, 0:KVR], in0=sqa_kv[:, 0:KVR], scalar1=1.0, scalar2=None,
                                                             op0=ALU.mult, op1=ALU.add, accum_out=ss), [r_kv], [r_kv])
                    rsqrt(rs, ss, 1.0 / KVR, 1, [r_kv], [r_kv])
                    P.op('vector', lambda e, b=b: e.scalar_tensor_tensor(out=CKV[:, b, 0:KVR], in0=zA[:, 0:KVR], scalar=rs, in1=gkv[:, :],
                                                                         op0=ALU.mult, op1=ALU.mult), [r_kv, r_z, rgkv], [rK])
                    s1, s2, mean, msq, var, rstd, nmr = [st[:, 4 + i:5 + i] for i in range(7)]
                    ki = zA[:, 256:320]
                    P.op('vector', lambda e: e.tensor_scalar(out=kix[:], in0=ki, scalar1=1.0, scalar2=None, op0=ALU.mult,
                                                             op1=ALU.add, accum_out=s1), [r_z], [r_ki])
                    P.op('scalar', lambda e: e.activation(out=sqa_ki[:, 0:IDXD], in_=ki, func=AF.Square), [r_z], [r_ki])
                    P.op('vector', lambda e: e.tensor_scalar(out=sqa_ki[:, 0:IDXD], in0=sqa_ki[:, 0:IDXD], scalar1=1.0, scalar2=None,
                                                             op0=ALU.mult, op1=ALU.add, accum_out=s2), [r_ki], [r_ki])
                    P.op('vector', lambda e: e.tensor_scalar(out=mean, in0=s1, scalar1=1.0 / IDXD, scalar2=None, op0=ALU.mult), [r_ki], [r_ki])
                    P.op('vector', lambda e: e.tensor_tensor(out=msq, in0=mean, in1=mean, op=ALU.mult), [r_ki], [r_ki])
                    P.op('vector', lambda e: e.scalar_tensor_tensor(out=var, in0=s2, scalar=1.0 / IDXD, in1=msq, op0=ALU.mult,
                                                                    op1=ALU.subtract), [r_ki], [r_ki])
                    rsqrt(rstd, var, 1.0, 0, [r_ki], [r_ki])
                    P.op('vector', lambda e: e.scalar_tensor_tensor(out=nmr, in0=mean, scalar=-1.0, in1=rstd, op0=ALU.mult,
                                                                    op1=ALU.mult), [r_ki], [r_ki])
                    P.op('scalar', lambda e: e.activation(out=kix[:], in_=ki, func=AF.Identity, scale=rstd, bias=nmr), [r_ki, r_z], [r_ki])
                    P.op('vector', lambda e: e.tensor_tensor(out=kix[:], in0=kix[:], in1=gki[:], op=ALU.mult), [r_ki, rgki], [r_ki])
                    P.op('vector', lambda e: e.tensor_tensor(out=kib[:], in0=kix[:], in1=bki[:], op=ALU.add), [r_ki, rbki], [r_ki])
                    P.op('vector', lambda e: e.tensor_scalar(out=wq[:], in0=zA[:, 320:328], scalar1=float((IDXH * IDXD) ** -0.5),
                                                             scalar2=None, op0=ALU.mult), [r_z], [r_w])
                    pt, rpt = PT()
                    for c in range(3):
                        P.op('tensor', lambda e, c=c, pt=pt: e.transpose(out=pt[:, c * 128:(c + 1) * 128], in_=cq[:, c * 128:(c + 1) * 128],
                                                                        identity=ident[:]), [r_cq, rid], [rpt])
                    for c in range(2):
                        P.op('tensor', lambda e, c=c, pt=pt, b=b: e.transpose(out=pt[:, (3 + c) * 128:(4 + c) * 128],
                                                                             in_=CKV[:, b, c * 128:(c + 1) * 128], identity=ident[:]),
                             [rK, rid], [rpt])
                    P.op('tensor', lambda e, pt=pt: e.transpose(out=pt[0:64, 5 * 128:6 * 128], in_=kib[:, :], identity=ident[:]),
                         [r_ki, rid], [rpt])
                    P.op('tensor', lambda e, pt=pt: e.transpose(out=pt[0:8, 6 * 128:7 * 128], in_=wq[:, :], identity=ident[:]),
                         [r_w, rid], [rpt])
                    P.op('vector', lambda e, pt=pt: e.tensor_copy(out=cqT[:].rearrange("p c t -> p (c t)"), in_=pt[:, 0:384]), [rpt], [r_cqT])
                    for c in range(2):
                        P.op('vector', lambda e, pt=pt, c=c, b=b: e.tensor_copy(out=CKVT[:, c, b * 128:(b + 1) * 128],
                                                                               in_=pt[:, (3 + c) * 128:(4 + c) * 128]), [rpt], [rK])
                    P.op('vector', lambda e, pt=pt, b=b: e.tensor_copy(out=KIT[:, b * 128:(b + 1) * 128], in_=pt[0:64, 640:768]), [rpt], [rK])
                    P.op('vector', lambda e, pt=pt: e.tensor_copy(out=wT[:], in_=pt[0:8, 768:896]), [rpt], [r_wT])
                    for hh in range(2):
                        pq, rpq = PB()
                        for h4 in range(4):
                            h = hh * 4 + h4
                            for kc in range(3):
                                P.op('tensor', lambda e, pq=pq, h=h, h4=h4, kc=kc: e.matmul(
                                    pq[0:64, h4 * 128:(h4 + 1) * 128], lhsT=wqi[:, kc, h * 64:(h + 1) * 64], rhs=cqT[:, kc, :],
                                    start=(kc == 0), stop=(kc == 2)), [rw, r_cqT], [rpq], inc=(kc == 2))
                        P.op('scalar', lambda e, pq=pq, hh=hh: e.copy(
                            out=QIT[:, :, hh * 4:(hh + 1) * 4, :].rearrange("p g h t -> p h g t"),
                            in_=pq[0:64, :].rearrange("p (h g t) -> p h g t", h=4, g=8)), [rpq], [r_qit])
                    pe_, rpe = PB()
                    P.op('tensor', lambda e, pe_=pe_: e.matmul(pe_[:, 0:128], lhsT=rep[:, :], rhs=wT[:, :], start=True, stop=True),
                         [rc, r_wT], [rpe])
                    for g in range(8):
                        P.op('vector', lambda e, g=g, pe_=pe_: e.tensor_tensor(out=Wsel[:, g, :], in0=pe_[:, 0:128], in1=maskall[:, g, :],
                                                                              op=ALU.mult), [rpe, rc], [r_wsel])
                    nkc = (L + 511) // 512
                    for kc in range(nkc):
                        k0_ = kc * 512
                        n = min(512, L - k0_)
                        psc, rpsc = pbank[4]
                        pend = []

                        def emit_mm1(g, k0_=k0_, n=n):
                            p1, rp1 = PB()
                            P.op('tensor', lambda e, p1=p1, g=g: e.matmul(
                                p1[:, 0:n], lhsT=QIT[:, g, :, :].rearrange("p h t -> p (h t)"), rhs=KIT[:, k0_:k0_ + n], start=True, stop=True),
                                [r_qit, rK], [rp1])
                            R, rR = R_rot.next()
                            if g % 2 == 0:
                                P.op('scalar', lambda e, p1=p1, R=R: e.activation(out=R[:, 0:n], in_=p1[:, 0:n], func=AF.Relu),
                                     [rp1], [rR])
                            else:
                                P.op('vector', lambda e, p1=p1, R=R: e.tensor_scalar(out=R[:, 0:n], in0=p1[:, 0:n], scalar1=0.0,
                                                                                    scalar2=None, op0=ALU.max), [rp1], [rR])
                            pend.append((R, rR))

                        emit_mm1(0)
                        emit_mm1(1)
                        for g in range(8):
                            if g + 2 < 8:
                                emit_mm1(g + 2)
                            R, rR = pend.pop(0)
                            P.op('tensor', lambda e, psc=psc, g=g, R=R, n=n: e.matmul(
                                psc[:, 0:n], lhsT=Wsel[:, g, :], rhs=R[:, 0:n], start=(g == 0), stop=(g == 7)), [r_wsel, rR], [rpsc])
                        P.op('scalar', lambda e, psc=psc, k0_=k0_, n=n: e.copy(out=score[:, k0_:k0_ + n], in_=psc[:, 0:n]), [rpsc], [rsc])
                        P.op('vector', lambda e, psc=psc, kc=kc, n=n: e.tensor_reduce(out=mm[:, 0, kc:kc + 1], in_=psc[:, 0:n], axis=AX.X,
                                                                                     op=ALU.min), [rpsc], [r_mm])
                        P.op('vector', lambda e, psc=psc, kc=kc, n=n: e.tensor_reduce(out=mm[:, 1, kc:kc + 1], in_=psc[:, 0:n], axis=AX.X,
                                                                                     op=ALU.max), [rpsc], [r_mm])
                    P.op('vector', lambda e, L=L: e.tensor_tensor(out=score[:, L - 128:L], in0=score[:, L - 128:L], in1=cm[:], op=ALU.add),
                         [rsc, rc], [rsc])
                    lo, hi, w0, mid, cnt, stp = [bs[:, i:i + 1] for i in range(6)]
                    P.op('vector', lambda e, nkc=nkc: e.tensor_reduce(out=lo, in_=mm[:, 0, 0:nkc], axis=AX.X, op=ALU.min), [r_mm], [r_bs])
                    P.op('vector', lambda e, nkc=nkc: e.tensor_reduce(out=hi, in_=mm[:, 1, 0:nkc], axis=AX.X, op=ALU.max), [r_mm], [r_bs])
                    P.op('vector', lambda e: e.tensor_tensor(out=w0, in0=hi, in1=lo, op=ALU.subtract), [r_bs], [r_bs])
                    P.op('vector', lambda e: e.tensor_scalar(out=w0, in0=w0, scalar1=1.0001, scalar2=1e-6, op0=ALU.mult, op1=ALU.add), [r_bs], [r_bs])
                    P.op('vector', lambda e: e.tensor_scalar(out=Wt[:], in0=pw[:], scalar1=w0, scalar2=None, op0=ALU.mult), [r_bs, rc], [r_bs])
                    if L > TOPK:
                        P.op('vector', lambda e: e.tensor_tensor(out=mid, in0=lo, in1=Wt[:, 0:1], op=ALU.add), [r_bs], [r_bs])
                        for k in range(NIT):
                            P.op('vector', lambda e, L=L: e.tensor_scalar(out=junk[:, 0:L], in0=score[:, 0:L], scalar1=mid, scalar2=None,
                                                                         op0=ALU.is_ge, op1=ALU.add, accum_out=cnt), [r_bs, rsc], [r_bs, rmk])
                            P.op('vector', lambda e, k=k: e.scalar_tensor_tensor(out=stp, in0=cnt, scalar=float(TOPK) - 0.5, in1=Wt[:, k:k + 1],
                                                                                op0=ALU.is_ge, op1=ALU.mult), [r_bs], [r_bs])
                            if k < NIT - 1:
                                P.op('vector', lambda e, k=k: e.scalar_tensor_tensor(out=mid, in0=mid, scalar=Wt[:, k + 1:k + 2], in1=stp,
                                                                                    op0=ALU.subtract, op1=ALU.add), [r_bs], [r_bs])
                            else:
                                P.op('vector', lambda e, k=k: e.scalar_tensor_tensor(out=lo, in0=mid, scalar=Wt[:, k:k + 1], in1=stp,
                                                                                    op0=ALU.subtract, op1=ALU.add), [r_bs], [r_bs])
                    P.op('vector', lambda e, L=L: e.tensor_scalar(out=msk[:, 0:L], in0=score[:, 0:L], scalar1=lo, scalar2=None,
                                                                 op0=ALU.is_ge), [r_bs, rsc], [rmk])
                    def MT(b=b, mskT=mskT, rmkT=rmkT):
                        for kb0 in range(0, b + 1, 8):
                            nk = min(8, b + 1 - kb0)
                            pt, rpt = PT()
                            for i in range(nk):
                                P.op('tensor', lambda e, pt=pt, i=i, kb0=kb0: e.transpose(
                                    out=pt[:, i * 128:(i + 1) * 128], in_=msk[:, (kb0 + i) * 128:(kb0 + i + 1) * 128], identity=ident[:]),
                                    [rmk, rid], [rpt])
                            P.op('vector', lambda e, pt=pt, nk=nk, kb0=kb0: e.tensor_copy(
                                out=mskT[:, kb0:kb0 + nk, :].rearrange("p k t -> p (k t)"), in_=pt[:, 0:nk * 128]), [rpt], [rmkT])
                    blk[b] = (cqT, r_cqT, mskT, rmkT)
                    mts[b] = MT

                def ATT(b):
                    cqT, r_cqT, mskT, rmkT = blk.pop(b)
                    x32, rx32 = x32_rot.next()
                    P.dma('sync', x32[:], xsrc(layer)[b * 128:(b + 1) * 128, :], [rX32[b]], [rx32])
                    qk_i = [0]

                    def QKB():
                        r = pbank[4 + qk_i[0]]
                        qk_i[0] ^= 1
                        return r

                    for hg in range(2):
                        for rcx in range(2):
                            pq, rpq = QKB()
                            for h4 in range(4):
                                ch = (hg * 4 + h4) * 2 + rcx
                                for kc in range(3):
                                    P.op('tensor', lambda e, pq=pq, h4=h4, ch=ch, kc=kc: e.matmul(
                                        pq[:, h4 * 128:(h4 + 1) * 128], lhsT=wql[:, kc, ch * 128:(ch + 1) * 128], rhs=cqT[:, kc, :],
                                        start=(kc == 0), stop=(kc == 2)), [rw, r_cqT], [rpq], inc=(kc == 2))
                            P.op('scalar', lambda e, pq=pq, rcx=rcx: e.activation(out=qlT4[:, rcx, :], in_=pq[:, :], func=AF.Copy,
                                                                                scale=float(KVR ** -0.5)), [rpq], [rql])
                        pendq = []

                        def emit_qk(kb):
                            pst, rpst = QKB()
                            for rcx in range(2):
                                P.op('tensor', lambda e, pst=pst, rcx=rcx, kb=kb: e.matmul(
                                    pst[:, :], lhsT=CKVT[:, rcx, kb * 128:(kb + 1) * 128], rhs=qlT4[:, rcx, :],
                                    start=(rcx == 0), stop=(rcx == 1)), [rK, rql], [rpst], inc=(rcx == 1))
                            PTt, rPT = PT_rot.next()
                            P.op('scalar', lambda e, pst=pst, PTt=PTt: e.activation(
                                out=PTt[:].rearrange("p k t -> p (k t)"), in_=pst[:, :], func=AF.Exp), [rpst], [rPT])
                            P.op('gpsimd', lambda e, PTt=PTt, kb=kb: e.tensor_tensor(
                                out=PTt[:], in0=PTt[:], in1=mskT[:, kb:kb + 1, :].to_broadcast([128, 4, 128]), op=ALU.mult), [rPT, rmkT], [rPT])
                            pendq.append((PTt, rPT))

                        for kb in range(min(3, b + 1)):
                            emit_qk(kb)
                        for kb in range(b + 1):
                            if kb + 3 <= b:
                                emit_qk(kb + 3)
                            PTt, rPT = pendq.pop(0)
                            for h4 in range(4):
                                pol, rpol = pbank[h4]
                                P.op('tensor', lambda e, pol=pol, PTt=PTt, h4=h4, kb=kb, b=b: e.matmul(
                                    pol[:, 0:KVR + 1], lhsT=PTt[:, h4, :], rhs=CKV[:, kb, :], start=(kb == 0), stop=(kb == b)),
                                    [rPT, rK], [rpol])
                        for h4 in range(4):
                            pol, rpol = pbank[h4]
                            h = hg * 4 + h4
                            P.op('scalar', lambda e, pol=pol, h=h: e.copy(out=olb[:, h, :], in_=pol[:, 0:KVR]), [rpol], [rolb])
                            P.op('scalar', lambda e, pol=pol, h=h: e.copy(out=dcol[:, h:h + 1], in_=pol[:, KVR:KVR + 1]), [rpol], [rolb])
                    if (b + 1) in mts:
                        mts.pop(b + 1)()
                    P.op('vector', lambda e: e.reciprocal(out=rd[:], in_=dcol[:]), [rolb], [rrd])
                    pts = [PT(), PT()]
                    for h in range(8):
                        pt, rpt = pts[h // 4]
                        for rcx in range(2):
                            P.op('tensor', lambda e, pt=pt, h=h, rcx=rcx: e.transpose(
                                out=pt[:, ((h % 4) * 2 + rcx) * 128:((h % 4) * 2 + rcx + 1) * 128],
                                in_=olb[:, h, rcx * 128:(rcx + 1) * 128], identity=ident[:]), [rolb, rid], [rpt])
                    for hh in range(2):
                        pt, rpt = pts[hh]
                        P.op('scalar', lambda e, pt=pt, hh=hh: e.copy(
                            out=olT[:, hh * 4:(hh + 1) * 4, :, :].rearrange("p h c t -> p (h c t)"), in_=pt[:, :]), [rpt], [rolT])
                    puvs = [QKB(), QKB()]
                    for h in range(8):
                        puv, rpuv = puvs[h // 4]
                        for rcx in range(2):
                            P.op('tensor', lambda e, puv=puv, rcx=rcx, h=h: e.matmul(
                                puv[:, (h % 4) * 128:(h % 4 + 1) * 128], lhsT=olT[:, h, rcx, :], rhs=wuv[:, h * 2 + rcx, :],
                                start=(rcx == 0), stop=(rcx == 1)), [rolT, rw], [rpuv], inc=(rcx == 1))
                    for hh in range(2):
                        puv, rpuv = puvs[hh]
                        P.op('vector', lambda e, puv=puv, hh=hh: e.tensor_tensor(
                            out=oh[:, hh * 512:(hh + 1) * 512].rearrange("p (h d) -> p h d", h=4),
                            in0=puv[:, :].rearrange("p (h d) -> p h d", h=4),
                            in1=rd[:, hh * 4:(hh + 1) * 4].rearrange("p (h o) -> p h o", o=1).to_broadcast([128, 4, 128]),
                            op=ALU.mult), [rpuv, rrd], [ro])
                    pt, rpt = PT()
                    for c in range(8):
                        P.op('tensor', lambda e, c=c, pt=pt: e.transpose(out=pt[:, c * 128:(c + 1) * 128],
                                                                        in_=oh[:, c * 128:(c + 1) * 128], identity=ident[:]),
                             [ro, rid], [rpt])
                    P.op('vector', lambda e, pt=pt: e.tensor_copy(out=ohT[:].rearrange("p c t -> p (c t)"), in_=pt[:]), [rpt], [ro])
                    for half in range(2):
                        po, rpo = PB()
                        for c in range(8):
                            P.op('tensor', lambda e, c=c, half=half, po=po: e.matmul(
                                po[:, :], lhsT=ohT[:, c, :], rhs=wout[:, c, half * 512:(half + 1) * 512],
                                start=(c == 0), stop=(c == 7)), [ro, rw], [rpo], inc=(c == 7))
                        P.op('vector', lambda e, half=half, po=po, x32=x32: e.scalar_tensor_tensor(
                            out=z[:, half * 512:(half + 1) * 512], in0=x32[:, half * 512:(half + 1) * 512], scalar=ALPHA,
                            in1=po[:, :], op0=ALU.mult, op1=ALU.add), [rpo, rx32], [r_zo])
                    ln_tail(b, z, r_zo, gam, bet, [rg1, rb1], False, False)

                SB(0)
                mts.pop(0)()
                for b in range(NB):
                    if b + 1 < NB:
                        SB(b + 1)
                    ATT(b)
                P.barrier()

        def mixer_phase(layer):
            if MIXERS[layer % 2] is None:
                identity_mixer_phase(layer)
            elif layer % 2 == 0:
                dsa_phase(layer)
            else:
                hgrn_phase(layer)

        for layer in layers:
            mixer_phase(layer)
            ffn_phase(layer, last=(layer == layers[-1]))
        for e in ('sync',):
            if rY.w is not None:
                P._wait(e, rY.w[0], rY.w[1])
        P.barrier()
        print("instructions:", P.ninst)
    return nc


MIXERS = [True, True]

WSHAPES = {
    'a_w_in': [2, D, AIN], 'a_g_q': [2, QR], 'a_g_kv': [2, KVR], 'a_w_q_lat': [2, QR, AH * KVR],
    'a_w_q_idx': [2, QR, IDXH * IDXD], 'a_g_kidx': [2, IDXD], 'a_b_kidx': [2, IDXD],
    'a_w_uv': [2, AH, KVR, 128], 'a_w_out': [2, D, D], 'b_w_in': [2, D, 4 * D], 'b_lb_logits': [4, D],
    'b_g_o': [2, D], 'b_w_out': [2, D, D], 'ln1_g': [4, D], 'ln1_b': [4, D], 'f_w_up': [4, D, 2 * DFF],
    'f_conv_w': [4, 3, 1, 2 * DFF], 'f_conv_b': [4, 2 * DFF], 'f_w_down': [4, DFF, D], 'ln2_g': [4, D], 'ln2_b': [4, D],
}


def host_consts():
    ident = np.eye(128, dtype=np.float32).astype(ml_dtypes.bfloat16)
    s_ = np.arange(128)[:, None]
    t_ = np.arange(128)[None, :]
    same = (s_ // 64) == (t_ // 64)
    hconst = np.zeros((128, 4, 128), np.float32)
    hconst[:, 0, :] = (same & (s_ <= t_))
    hconst[:, 1, :] = same
    hconst[:, 3, :] = (same & (s_ <= t_))
    hcind = np.zeros((128, 4), np.float32)
    hcind[:64, 0] = 1.0
    hcind[64:, 1] = 1.0
    p_ = np.arange(128)
    maskall = np.zeros((128, 8, 128), np.float32)
    for g in range(8):
        maskall[p_, g, 16 * g + (p_ % 16)] = 1.0
    rep = np.zeros((8, 128), np.float32)
    rep[p_ // 16, p_] = 1.0
    cm = np.where(np.arange(128)[None, :] <= np.arange(128)[:, None], 0.0, -1e30).astype(np.float32)
    pw = np.tile((0.5 ** np.arange(1, 33, dtype=np.float64)).astype(np.float32)[None, :], (128, 1))
    return {'ident': ident, 'hconst': hconst, 'hcind': hcind, 'maskall': maskall,
            'rep': rep.astype(ml_dtypes.bfloat16), 'cm': cm, 'pw': pw}


def kernel(**inputs):
    x = np.ascontiguousarray(np.asarray(inputs['x'], dtype=np.float32))
    B, S, _ = x.shape
    nc = build_program(S)
    consts = host_consts()
    wmaps = {k: np.ascontiguousarray(np.asarray(inputs[k], dtype=np.float32)) for k in WSHAPES}
    in_maps = []
    for c in range(8):
        m = dict(wmaps)
        m['x'] = x[(c // 2) % B]
        m.update(consts)
        in_maps.append(m)
    res = run_bass_kernel_spmd(nc, in_maps, core_ids=list(range(8)))
    out = np.stack([res.results[2 * b]['y'] for b in range(B)], axis=0)
    return out.astype(np.float32)
```

```python
from contextlib import ExitStack
import numpy as np
import ml_dtypes
import concourse.bass as bass
import concourse.mybir as mybir
from concourse.bass_utils import run_bass_kernel_spmd

F32 = mybir.dt.float32
BF16 = mybir.dt.bfloat16
ALU = mybir.AluOpType
AF = mybir.ActivationFunctionType
AX = mybir.AxisListType

D = 1024
DFF = 2816
NFC = DFF // 128
DEPTH = 4
ALPHA = (2 * DEPTH) ** 0.25
LN_EPS = 1e-5
RMS_EPS = 1e-6
QR, KVR, IDXD, IDXH, AH = 384, 256, 64, 8, 8
AIN = QR + KVR + IDXD + IDXH

ENGS = ['tensor', 'vector', 'scalar', 'gpsimd', 'sync']
DMAQ = ['sync', 'scalar', 'gpsimd']


class Res:
    __slots__ = ('w', 'r', 'name', 'excl')

    def __init__(self, name='', excl=False):
        self.w = None
        self.r = []
        self.name = name
        self.excl = excl


class Prog:
    def __init__(self, nc, stack, n_dma_slots=4):
        self.nc = nc
        self.sem = {e: stack.enter_context(nc.semaphore(f"s_{e}")) for e in ENGS}
        self.cnt = {e: 0 for e in ENGS}
        self.nslots = n_dma_slots
        self.dsem, self.dcnt, self.dnext = {}, {}, {}
        for qn in DMAQ:
            for s in range(n_dma_slots):
                self.dsem[(qn, s)] = stack.enter_context(nc.semaphore(f"d_{qn}_{s}"))
                self.dcnt[(qn, s)] = 0
            self.dnext[qn] = 0
        self.seen = {e: {} for e in ENGS}
        self.ninst = 0

    def _wait(self, eng, prod, val):
        if self.seen[eng].get(prod, 0) >= val:
            return
        sem = self.sem[prod] if isinstance(prod, str) else self.dsem[prod]
        getattr(self.nc, eng).wait_ge(sem, val)
        self.seen[eng][prod] = val

    def _deps(self, eng, reads, writes):
        deps = []
        for b in reads:
            if b.w is not None:
                deps.append(b.w)
            if b.excl:
                deps.extend(b.r)
        for b in writes:
            if b.w is not None:
                if not (b.excl and eng == 'tensor' and b.w[0] == 'tensor'):
                    deps.append(b.w)
            deps.extend(b.r)
        for (p, v) in deps:
            self._wait(eng, p, v)

    def _commit(self, me, reads, writes):
        for b in reads:
            b.r.append(me)
        for b in writes:
            b.w = me
            b.r = []

    def op(self, eng, fn, reads=(), writes=(), inc=True):
        self._deps(eng, reads, writes)
        if inc:
            self.cnt[eng] += 1
            fn(getattr(self.nc, eng)).then_inc(self.sem[eng], 1)
            self._commit((eng, self.cnt[eng]), reads, writes)
        else:
            fn(getattr(self.nc, eng))
            self._commit((eng, self.cnt[eng] + 1), reads, writes)
        self.ninst += 1

    def dma(self, qn, out, in_, reads=(), writes=(), **kw):
        s = self.dnext[qn]
        self.dnext[qn] = (s + 1) % self.nslots
        key = (qn, s)
        prev = 16 * self.dcnt[key]
        if prev:
            self._wait(qn, key, prev)
        self._deps(qn, reads, writes)
        self.dcnt[key] += 1
        getattr(self.nc, qn).dma_start(out=out, in_=in_, **kw).then_inc(self.dsem[key], 16)
        self._commit((key, 16 * self.dcnt[key]), reads, writes)
        self.ninst += 1

    def barrier(self):
        for e in ENGS:
            for p in ENGS:
                if p != e and self.cnt[p]:
                    self._wait(e, p, self.cnt[p])
            for k, c in self.dcnt.items():
                if c:
                    self._wait(e, k, 16 * c)


class Rot:
    def __init__(self, tiles):
        self.t = [(t, Res()) for t in tiles]
        self.i = 0

    def next(self):
        r = self.t[self.i]
        self.i = (self.i + 1) % len(self.t)
        return r


def build_program(S, layers=(0, 1, 2, 3), first=True):
    layers = tuple(layers)
    NB = S // 128
    TOPK = min(256, S // 4)
    nc = bass.Bass("TRN2", target_bir_lowering=False)

    def din(name, shape, dt=F32):
        return nc.dram_tensor(name, list(shape), dt, kind="ExternalInput").ap()

    x_in = din("x", [S, D])
    Wn = {}
    for name, shape in WSHAPES.items():
        Wn[name] = din(name, shape)
    ident_in = din("ident", [128, 128], BF16)
    hconst_in = din("hconst", [128, 4, 128])
    hcind_in = din("hcind", [128, 4])
    maskall_in = din("maskall", [128, 8, 128])
    rep_in = din("rep", [8, 128], BF16)
    cm_in = din("cm", [128, 128])
    pw_in = din("pw", [128, 32])
    y_out = nc.dram_tensor("y", [S, D], F32, kind="ExternalOutput").ap()
    X32 = nc.dram_tensor("X32", [S, D], F32).ap()
    XTH = nc.dram_tensor("XTH", [NB + 1, 128, 8, 130], BF16).ap()
    rX32 = [Res() for _ in range(NB)]
    rXTH = [Res() for _ in range(NB + 1)]
    rXTHh = [Res() for _ in range(NB + 1)]
    rY = Res()
    WUb = nc.dram_tensor("WUb", [4, NFC, 128, 8, 256], BF16).ap()
    rWb = [Res() for _ in range(4)]

    with ExitStack() as top:
        P = Prog(nc, top)
        uid = [0]

        def sb(st, name, shape, dt):
            uid[0] += 1
            return st.enter_context(nc.sbuf_tensor(f"t{uid[0]}_{name}", list(shape), dt))
        pbank = [(top.enter_context(nc.psum_tensor(f"pb{i}", [128, 512], F32)), Res(excl=True)) for i in range(6)]
        ptb = [(top.enter_context(nc.psum_tensor(f"pt{i}", [128, 1024], BF16)), Res(excl=True)) for i in range(2)]
        pbi = [0]
        pti = [0]

        def PB():
            r = pbank[pbi[0]]
            pbi[0] = (pbi[0] + 1) % 4
            return r

        def PT():
            r = ptb[pti[0]]
            pti[0] = (pti[0] + 1) % len(ptb)
            return r

        epsT = sb(top, "epsT", [128, 2], F32)
        reps = Res()
        P.op('vector', lambda e: e.memset(epsT[:, 0:1], LN_EPS), [], [reps])
        P.op('vector', lambda e: e.memset(epsT[:, 1:2], RMS_EPS), [], [reps])

        def rsqrt(out, in_, scale, which, reads, writes):
            P.op('scalar', lambda e: e.activation(out=out, in_=in_, func=AF.Sqrt, scale=scale, bias=epsT[:, which:which + 1]),
                 list(reads) + [reps], writes)
            P.op('vector', lambda e: e.reciprocal(out=out, in_=out), writes, writes)

        ident = sb(top, "ident_sb", [128, 128], BF16)
        rid = Res()
        P.dma('sync', ident[:], ident_in[:, :], writes=[rid])

        def convert_wu(l, j):
            for part in range(2):
                col = part * DFF + j * 128
                P.dma('gpsimd', WUb[l, j, :, :, part * 128:(part + 1) * 128],
                      Wn['f_w_up'][l, :, col:col + 128].rearrange("(c p) n -> p c n", p=128), [], [rWb[l]])

        for j in range(NFC):
            convert_wu(layers[0], j)

        lnst = ExitStack()
        top.enter_context(lnst)
        yb_rot = Rot([sb(top, f"yb{i}", [128, D], BF16) for i in range(1)])
        xtb_rot = Rot([sb(top, f"xtb{i}", [128, 8, 128], BF16) for i in range(2)])
        yn_rot = Rot([sb(top, f"yn{i}", [128, D], F32) for i in range(1)])
        sq_rot = Rot([sb(top, f"sq{i}", [128, D], F32) for i in range(1)])
        st_rot = Rot([sb(top, f"st{i}", [128, 8], F32) for i in range(2)])
        eng_flip = [0]

        def transpose_store(b, y32, ry, halo, defer=False):
            yb, ryb = yb_rot.next()
            P.op('scalar', lambda e: e.copy(out=yb[:], in_=y32[:]), [ry], [ryb])
            if defer:
                return lambda: transpose_store_pe(b, yb, ryb, halo)
            transpose_store_pe(b, yb, ryb, halo)
            return None

        def transpose_store_pe(b, yb, ryb, halo):
            pt, rpt = PT()
            for c in range(8):
                P.op('tensor', lambda e, c=c: e.transpose(out=pt[:, c * 128:(c + 1) * 128],
                                                           in_=yb[:, c * 128:(c + 1) * 128], identity=ident[:]),
                     [ryb, rid], [rpt])
            xtb, rxtb = xtb_rot.next()
            P.op('vector', lambda e: e.tensor_copy(out=xtb[:].rearrange("p c t -> p (c t)"), in_=pt[:]), [rpt], [rxtb])
            P.dma('gpsimd', XTH[b, :, :, 2:130], xtb[:], [rxtb], [rXTH[b]])
            if halo:
                P.dma('gpsimd', XTH[b + 1, :, :, 0:2], xtb[:, :, 126:128], [rxtb], [rXTHh[b + 1]])

        def ln_tail(b, z, rz, gam, bet, rgb, to_out, halo, defer=False):
            st_, rst = st_rot.next()
            sq, rsq = sq_rot.next()
            yn, ryn = yn_rot.next()
            s1, s2, mean, msq, var, rstd, nmr = [st_[:, i:i + 1] for i in range(7)]
            P.op('vector', lambda e: e.tensor_scalar(out=yn[:], in0=z[:], scalar1=1.0, scalar2=None, op0=ALU.mult,
                                                     op1=ALU.add, accum_out=s1), [rz], [ryn, rst])
            P.op('scalar', lambda e: e.activation(out=sq[:], in_=z[:], func=AF.Square), [rz], [rsq])
            P.op('vector', lambda e: e.tensor_scalar(out=sq[:], in0=sq[:], scalar1=1.0, scalar2=None, op0=ALU.mult,
                                                     op1=ALU.add, accum_out=s2), [rsq], [rsq, rst])
            P.op('vector', lambda e: e.tensor_scalar(out=mean, in0=s1, scalar1=1.0 / D, scalar2=None, op0=ALU.mult), [rst], [rst])
            P.op('vector', lambda e: e.tensor_tensor(out=msq, in0=mean, in1=mean, op=ALU.mult), [rst], [rst])
            P.op('vector', lambda e: e.scalar_tensor_tensor(out=var, in0=s2, scalar=1.0 / D, in1=msq, op0=ALU.mult,
                                                            op1=ALU.subtract), [rst], [rst])
            rsqrt(rstd, var, 1.0, 0, [rst], [rst])
            P.op('vector', lambda e: e.scalar_tensor_tensor(out=nmr, in0=mean, scalar=-1.0, in1=rstd, op0=ALU.mult,
                                                            op1=ALU.mult), [rst], [rst])
            P.op('scalar', lambda e: e.activation(out=yn[:], in_=z[:], func=AF.Identity, scale=rstd, bias=nmr),
                 [rz, rst], [ryn])
            P.op('gpsimd', lambda e: e.tensor_tensor(out=yn[:], in0=yn[:], in1=gam[:], op=ALU.mult), [ryn] + list(rgb), [ryn])
            P.op('vector', lambda e: e.tensor_tensor(out=yn[:], in0=yn[:], in1=bet[:], op=ALU.add), [ryn] + list(rgb), [ryn])
            if to_out:
                P.dma('gpsimd', y_out[b * 128:(b + 1) * 128, :], yn[:], [ryn], [rY])
            else:
                P.dma('gpsimd', X32[b * 128:(b + 1) * 128, :], yn[:], [ryn], [rX32[b]])
                return transpose_store(b, yn, ryn, halo, defer)
            return None

        if first:
            with ExitStack() as ph:
                zt = sb(ph, "zt", [128, 8, 2], BF16)
                rzt = Res()
                P.op('vector', lambda e: e.memset(zt[:], 0.0), [], [rzt])
                P.dma('sync', XTH[0, :, :, 0:2], zt[:], [rzt], [rXTHh[0]])
                xl_rot = Rot([sb(ph, f"xl{i}", [128, D], F32) for i in range(2)])
                for b in range(NB):
                    xl, rxl = xl_rot.next()
                    P.dma('scalar', xl[:], x_in[b * 128:(b + 1) * 128, :], [], [rxl])
                    transpose_store(b, xl, rxl, False)
                P.barrier()

        def xsrc(layer):
            return x_in if (first and layer == layers[0]) else X32

        def load_bc(st, name, src_row):
            t = sb(st, name, [128, src_row.shape[-1]], F32)
            r = Res()
            P.dma('sync', t[:], src_row.to_broadcast([128, src_row.shape[-1]]), [], [r])
            return t, r

        TT = 512 if NB % 4 == 0 else 128 * NB
        NBT = TT // 128

        def ffn_phase(layer, last):
            with ExitStack() as ph:
                wd = sb(ph, "wd", [128, NFC, D], BF16)
                rw = Res()
                for j in range(0, NFC, 2):
                    P.dma('gpsimd', wd[:, j:j + 2, :],
                          Wn['f_w_down'][layer, j * 128:(j + 2) * 128, :].rearrange("(c p) n -> p c n", p=128), [], [rw])
                cw = sb(ph, "cw", [128, 3, 2 * NFC], F32)
                cb = sb(ph, "cb", [128, 2 * NFC], F32)
                for k in range(3):
                    P.dma('scalar', cw[:, k, :], Wn['f_conv_w'][layer, k, 0, :].rearrange("(j p) -> p j", p=128), [], [rw],
                          allow_slow_non_contiguous=True)
                P.dma('scalar', cb[:], Wn['f_conv_b'][layer, :].rearrange("(j p) -> p j", p=128), [], [rw],
                      allow_slow_non_contiguous=True)
                gam = sb(ph, "gam2", [128, D], F32)
                bet = sb(ph, "bet2", [128, D], F32)
                P.dma('sync', gam[:], Wn['ln2_g'][layer:layer + 1, :].to_broadcast([128, D]), [], [rw])
                P.dma('sync', bet[:], Wn['ln2_b'][layer:layer + 1, :].to_broadcast([128, D]), [], [rw])
                carry = sb(ph, "carry", [128, 2 * NFC, 2], F32)
                rcar = [Res() for _ in range(2 * NFC)]
                P.op('gpsimd', lambda e: e.memset(carry[:], 0.0), [], rcar)
                wu_rot = Rot([sb(ph, f"wu{i}", [128, 8, 256], BF16) for i in range(3)])
                xw_rot = Rot([sb(ph, f"xw{i}", [128, 8, TT], BF16) for i in range(2)])
                g_rot = Rot([sb(ph, f"g{i}", [128, NFC, TT], BF16) for i in range(2)])
                hs_rot = Rot([sb(ph, f"hs{i}", [128, TT + 2], F32) for i in range(4)])
                acc_rot = Rot([sb(ph, f"acc{i}", [128, TT], F32) for i in range(6)])
                sa_rot = Rot([sb(ph, f"sa{i}", [128, TT], F32) for i in range(2)])
                x32_rot = Rot([sb(ph, f"x32{i}", [128, D], F32) for i in range(4 if NBT == 4 else NBT)])
                z_rot = Rot([sb(ph, f"z{i}", [128, D], F32) for i in range(2)])
                for t in range(NB // NBT):
                    b0 = t * NBT
                    xw, rxw = xw_rot.next()
                    for i in range(NBT):
                        P.dma('sync', xw[:, :, i * 128:(i + 1) * 128], XTH[b0 + i, :, :, 2:130], [rXTH[b0 + i]], [rxw])
                    g, rg = g_rot.next()
                    pend_gate = None
                    x32s = []
                    for i in range(NBT):
                        x32, rx32 = x32_rot.next()
                        P.dma('sync', x32[:], X32[(b0 + i) * 128:(b0 + i + 1) * 128, :], [rX32[b0 + i]], [rx32])
                        x32s.append((x32, rx32))

                    def emit_gate(accs, j, g, rg):
                        sa, rsa = sa_rot.next()
                        P.op('scalar', lambda e, sa=sa, a=accs[0][0]: e.activation(out=sa[:], in_=a[:], func=AF.Silu),
                             [accs[0][1]], [rsa])
                        P.op('vector', lambda e, sa=sa, u=accs[1][0], g=g, j=j: e.tensor_tensor(
                            out=g[:, j, :], in0=sa[:], in1=u[:], op=ALU.mult), [rsa, accs[1][1]], [rg])

                    for j in range(NFC):
                        if t == 0 and layer != layers[-1]:
                            convert_wu(layers[layers.index(layer) + 1], j)
                        wu, rwu = wu_rot.next()
                        P.dma('sync', wu[:], WUb[layer, j], [rWb[layer]], [rwu])
                        accs = []
                        for part in range(2):
                            jj = part * NFC + j
                            ps, rps = PB()
                            for c in range(8):
                                P.op('tensor', lambda e, c=c, part=part, ps=ps, wu=wu, xw=xw: e.matmul(
                                    ps[:, 0:TT], lhsT=wu[:, c, part * 128:(part + 1) * 128], rhs=xw[:, c, :],
                                    start=(c == 0), stop=(c == 7)), [rwu, rxw], [rps], inc=(c == 7))
                            hs, rhs_ = hs_rot.next()
                            P.op('gpsimd', lambda e, hs=hs, jj=jj: e.tensor_copy(out=hs[:, 0:2], in_=carry[:, jj, :]), [rcar[jj]], [rhs_])
                            P.op('scalar', lambda e, hs=hs, ps=ps: e.copy(out=hs[:, 2:TT + 2], in_=ps[:, 0:TT]), [rps], [rhs_])
                            P.op('gpsimd', lambda e, hs=hs, jj=jj: e.tensor_copy(out=carry[:, jj, :], in_=hs[:, TT:TT + 2]), [rhs_], [rcar[jj]])
                            acc, racc = acc_rot.next()
                            eng = 'vector' if part == 0 else 'gpsimd'
                            if part == 0:
                                P.op('scalar', lambda e, acc=acc, hs=hs, jj=jj: e.activation(
                                    out=acc[:], in_=hs[:, 0:TT], func=AF.Identity, scale=cw[:, 0, jj:jj + 1], bias=cb[:, jj:jj + 1]),
                                    [rhs_, rw], [racc])
                            else:
                                P.op(eng, lambda e, acc=acc, hs=hs, jj=jj: e.tensor_scalar(
                                    out=acc[:], in0=hs[:, 0:TT], scalar1=cw[:, 0, jj:jj + 1], scalar2=cb[:, jj:jj + 1],
                                    op0=ALU.mult, op1=ALU.add), [rhs_, rw], [racc])
                            for k in (1, 2):
                                P.op('vector', lambda e, acc=acc, hs=hs, jj=jj, k=k: e.scalar_tensor_tensor(
                                    out=acc[:], in0=hs[:, k:k + TT], scalar=cw[:, k, jj:jj + 1], in1=acc[:],
                                    op0=ALU.mult, op1=ALU.add), [rhs_, rw, racc], [racc])
                            accs.append((acc, racc))
                        if pend_gate is not None:
                            emit_gate(*pend_gate)
                        pend_gate = (accs, j, g, rg)
                    emit_gate(*pend_gate)
                    pend_gate = None
                    for i in range(NBT):
                        b = b0 + i
                        x32, rx32 = x32s[i]
                        z, rz = z_rot.next()
                        for half in range(2):
                            po, rpo = PB()
                            for j in range(NFC):
                                P.op('tensor', lambda e, j=j, half=half, po=po, g=g, i=i: e.matmul(
                                    po[:, :], lhsT=g[:, j, i * 128:(i + 1) * 128], rhs=wd[:, j, half * 512:(half + 1) * 512],
                                    start=(j == 0), stop=(j == NFC - 1)), [rg, rw], [rpo], inc=(j == NFC - 1))
                            P.op('vector', lambda e, half=half, po=po, z=z, x32=x32: e.scalar_tensor_tensor(
                                out=z[:, half * 512:(half + 1) * 512], in0=x32[:, half * 512:(half + 1) * 512], scalar=ALPHA,
                                in1=po[:, :], op0=ALU.mult, op1=ALU.add), [rpo, rx32], [rz])
                        ln_tail(b, z, rz, gam, bet, [rw], last, False)
                P.barrier()

        ph_hs = [sb(top, "hsb", [128, 130], F32)]
        rhs = Res()

        def identity_mixer_phase(layer):
            with ExitStack() as ph:
                gam, rg1 = load_bc(ph, "gam1", Wn['ln1_g'][layer:layer + 1, :])
                bet, rb1 = load_bc(ph, "bet1", Wn['ln1_b'][layer:layer + 1, :])
                rgb = Res()
                P.op('vector', lambda e: e.memset(ph_hs[0][:, 0:1], 0.0), [rg1, rb1, rhs], [rgb, rhs])
                x32_rot = Rot([sb(ph, f"mx32{i}", [128, D], F32) for i in range(2)])
                z_rot = Rot([sb(ph, f"mz{i}", [128, D], F32) for i in range(2)])
                for b in range(NB):
                    x32, rx32 = x32_rot.next()
                    P.dma('sync', x32[:], xsrc(layer)[b * 128:(b + 1) * 128, :], [rX32[b]], [rx32])
                    z, rz = z_rot.next()
                    P.op('vector', lambda e, z=z, x32=x32: e.tensor_scalar(out=z[:], in0=x32[:], scalar1=ALPHA, scalar2=None,
                                                                        op0=ALU.mult), [rx32], [rz])
                    ln_tail(b, z, rz, gam, bet, [rgb], False, True)
                P.barrier()


        def hgrn_phase(layer):
            j = layer // 2
            with ExitStack() as ph:
                win = sb(ph, "hwin", [128, 8, 4 * D], BF16)
                wout = sb(ph, "hwout", [128, 8, D], BF16)
                rw = Res()
                for c in range(8):
                    P.dma('gpsimd', win[:, c, :], Wn['b_w_in'][j, c * 128:(c + 1) * 128, :], [], [rw])
                P.dma('gpsimd', wout[:], Wn['b_w_out'][j].rearrange("(c p) n -> p c n", p=128), [], [rw])
                gam, rg1 = load_bc(ph, "hgam1", Wn['ln1_g'][layer:layer + 1, :])
                bet, rb1 = load_bc(ph, "hbet1", Wn['ln1_b'][layer:layer + 1, :])
                go, rgo = load_bc(ph, "hgo", Wn['b_g_o'][j:j + 1, :])
                cst = sb(ph, "hcst", [128, 3, 128], F32)
                cind = sb(ph, "hcind", [128, 4], F32)
                cmask = sb(ph, "hcmask", [128, 128], F32)
                cmask4 = sb(ph, "hcmask4", [128, 4, 128], F32)
                rc = Res()
                P.dma('sync', cst[:], hconst_in[:, 0:3, :], [], [rc])
                P.dma('sync', cmask[:], hconst_in[:, 3, :], [], [rc])
                for i4 in range(4):
                    P.dma('sync', cmask4[:, i4, :], hconst_in[:, 3, :], [], [rc])
                P.dma('sync', cind[:], hcind_in[:, :], [], [rc])
                lb = sb(ph, "hlb", [128, D], F32)
                oml = sb(ph, "homl", [128, D], F32)
                rl = Res()
                tmp = ExitStack()
                lg = [load_bc(tmp, f"hlg{l}", Wn['b_lb_logits'][l:l + 1, :]) for l in range(4)]
                mx = sb(tmp, "hmx", [128, D], F32)
                den = sb(tmp, "hden", [128, D], F32)
                P.op('vector', lambda e: e.tensor_tensor(out=mx[:], in0=lg[0][0][:], in1=lg[1][0][:], op=ALU.max),
                     [lg[0][1], lg[1][1]], [rl])
                for l in (2, 3):
                    P.op('vector', lambda e, l=l: e.tensor_tensor(out=mx[:], in0=mx[:], in1=lg[l][0][:], op=ALU.max),
                         [lg[l][1], rl], [rl])
                for l in range(4):
                    P.op('vector', lambda e, l=l: e.tensor_tensor(out=lg[l][0][:], in0=lg[l][0][:], in1=mx[:], op=ALU.subtract),
                         [rl, lg[l][1]], [lg[l][1]])
                    P.op('scalar', lambda e, l=l: e.activation(out=lg[l][0][:], in_=lg[l][0][:], func=AF.Exp),
                         [lg[l][1]], [lg[l][1]])
                P.op('vector', lambda e: e.tensor_tensor(out=den[:], in0=lg[0][0][:], in1=lg[1][0][:], op=ALU.add),
                     [lg[0][1], lg[1][1]], [rl])
                for l in (2, 3):
                    P.op('vector', lambda e, l=l: e.tensor_tensor(out=den[:], in0=den[:], in1=lg[l][0][:], op=ALU.add),
                         [lg[l][1], rl], [rl])
                P.op('vector', lambda e: e.tensor_copy(out=lb[:], in_=lg[1][0][:]), [lg[1][1], rl], [rl])
                for l in range(2, layer + 1):
                    P.op('vector', lambda e, l=l: e.tensor_tensor(out=lb[:], in0=lb[:], in1=lg[l][0][:], op=ALU.add),
                         [lg[l][1], rl], [rl])
                P.op('vector', lambda e: e.reciprocal(out=den[:], in_=den[:]), [rl], [rl])
                P.op('vector', lambda e: e.tensor_tensor(out=lb[:], in0=lb[:], in1=den[:], op=ALU.mult), [rl], [rl])
                P.op('vector', lambda e: e.tensor_scalar(out=oml[:], in0=lb[:], scalar1=-1.0, scalar2=1.0, op0=ALU.mult,
                                                         op1=ALU.add), [rl], [rl])
                P.barrier()
                tmp.close()
                Sf = sb(ph, "hSf", [128, 8, 128], F32)
                SbA = sb(ph, "hSbA", [128, 8, 128], BF16)
                SbB = sb(ph, "hSbB", [128, 8, 128], BF16)
                q0T = sb(ph, "hq0T", [128, 8, 128], BF16)
                q1T = sb(ph, "hq1T", [128, 8, 128], BF16)
                kT = sb(ph, "hkT", [128, 8, 128], BF16)
                qT = sb(ph, "hqT", [128, 8, 128], BF16)
                rS = Res()
                rq = Res()
                P.op('vector', lambda e: e.memset(Sf[:], 0.0), [], [rS])
                P.op('gpsimd', lambda e: e.memset(q0T[:], 0.0), [], [rq])
                P.op('gpsimd', lambda e: e.memset(q1T[:], 0.0), [], [rq])
                xh_rot = Rot([sb(ph, f"hxh{i}", [128, 8, 130], BF16) for i in range(2)])
                x32_rot = Rot([sb(ph, f"hx32{i}", [128, D], F32) for i in range(2)])
                F = lambda n: sb(ph, n, [128, D], F32)
                sig, lf, key, bb, bl, qs, og = F("hsig"), F("hlf"), F("hkey"), F("hbb"), F("hbl"), F("hqs"), F("hog")
                zt_ = qs
                gateT = F("hgate")
                rqs, rvv, rgate = Res(), Res(), Res()
                B16 = lambda n: sb(ph, n, [128, D], BF16)
                qtl, ktl, k0, k1, vv, onb = B16("hqtl"), B16("hktl"), B16("hk0"), B16("hk1"), B16("hvv"), B16("honb")
                onT = sb(ph, "honT", [128, 8, 128], BF16)
                scT = sb(ph, "hscT", [128, 8, 128], BF16)
                rscT = Res()
                rog = Res()
                rSb = {id(SbA): Res(), id(SbB): Res()}
                P.op('vector', lambda e: e.memset(SbA[:], 0.0), [], [rSb[id(SbA)]])
                dec = sb(ph, "hdec", [128, 8, 2], F32)
                ss = sb(ph, "hss", [128, 8], F32)
                rt = Res()

                def proj(part, xh, rxh):
                    outs = []
                    for half in range(2):
                        ps, rps = PB()
                        col = part * D + half * 512
                        for c in range(8):
                            P.op('tensor', lambda e, c=c, col=col, ps=ps: e.matmul(
                                ps[:, :], lhsT=xh[:, c, 2:130], rhs=win[:, c, col:col + 512],
                                start=(c == 0), stop=(c == 7)), [rw, rxh], [rps], inc=(c == 7))
                        outs.append((ps, rps))
                    return outs

                for b in range(NB):
                    xh, rxh = xh_rot.next()
                    P.dma('sync', xh[:, :, 2:130], XTH[b, :, :, 2:130], [rXTH[b]], [rxh])
                    x32, rx32 = x32_rot.next()
                    P.dma('sync', x32[:], xsrc(layer)[b * 128:(b + 1) * 128, :], [rX32[b]], [rx32])
                    for half, (ps, rps) in enumerate(proj(1, xh, rxh)):
                        hs = slice(half * 512, (half + 1) * 512)
                        P.op('scalar', lambda e, ps=ps, hs=hs: e.activation(out=sig[:, hs], in_=ps[:, :], func=AF.Sigmoid),
                             [rps], [rt])
                    for half, (ps, rps) in enumerate(proj(0, xh, rxh)):
                        hs = slice(half * 512, (half + 1) * 512)
                        P.op('scalar', lambda e, ps=ps, hs=hs: e.activation(out=qs[:, hs], in_=ps[:, :], func=AF.Silu),
                             [rps], [rqs])
                    for half, (ps, rps) in enumerate(proj(2, xh, rxh)):
                        hs = slice(half * 512, (half + 1) * 512)
                        P.op('vector', lambda e, ps=ps, hs=hs: e.tensor_copy(out=vv[:, hs], in_=ps[:, :]), [rps], [rvv])
                    for half, (ps, rps) in enumerate(proj(3, xh, rxh)):
                        hs = slice(half * 512, (half + 1) * 512)
                        P.op('scalar', lambda e, ps=ps, hs=hs: e.activation(out=gateT[:, hs], in_=ps[:, :], func=AF.Sigmoid),
                             [rps], [rgate])
                    P.op('vector', lambda e: e.tensor_tensor(out=lf[:], in0=sig[:], in1=oml[:], op=ALU.mult), [rt, rl], [rt])
                    P.op('vector', lambda e: e.tensor_tensor(out=key[:], in0=oml[:], in1=lf[:], op=ALU.subtract), [rt, rl], [rt])
                    P.op('vector', lambda e: e.tensor_tensor(out=lf[:], in0=lf[:], in1=lb[:], op=ALU.add), [rt, rl], [rt])
                    P.op('scalar', lambda e: e.activation(out=lf[:], in_=lf[:], func=AF.Ln), [rt], [rt])
                    for half in range(2):
                        hs = slice(half * 512, (half + 1) * 512)
                        ps, rps = PB()
                        P.op('tensor', lambda e, ps=ps, hs=hs: e.matmul(ps[:, :], lhsT=cst[:, 0, :], rhs=lf[:, hs],
                                                                       start=True, stop=True), [rt, rc], [rps])
                        P.op('vector', lambda e, ps=ps, hs=hs: e.tensor_copy(out=bb[:, hs], in_=ps[:, :]), [rps], [rt])
                        ps, rps = PB()
                        P.op('tensor', lambda e, ps=ps, hs=hs: e.matmul(ps[:, :], lhsT=cst[:, 1, :], rhs=lf[:, hs],
                                                                       start=True, stop=True), [rt, rc], [rps])
                        P.op('vector', lambda e, ps=ps, hs=hs: e.tensor_copy(out=bl[:, hs], in_=ps[:, :]), [rps], [rt])
                    ps, rps = PB()
                    for h in range(8):
                        P.op('tensor', lambda e, h=h, ps=ps: e.matmul(ps[:, h * 2:h * 2 + 2], lhsT=lf[:, h * 128:(h + 1) * 128],
                                                                     rhs=cind[:, 0:2], start=True, stop=True), [rt, rc], [rps])
                    P.op('scalar', lambda e, ps=ps: e.activation(out=dec[:].rearrange("p h c -> p (h c)"), in_=ps[:, 0:16],
                                                                 func=AF.Exp), [rps], [rt])
                    P.op('vector', lambda e: e.tensor_tensor(out=bl[:], in0=bl[:], in1=bb[:], op=ALU.subtract), [rt], [rt])
                    P.op('scalar', lambda e: e.activation(out=bl[:], in_=bl[:], func=AF.Exp), [rt], [rt])
                    P.op('vector', lambda e: e.tensor_tensor(out=bl[:], in0=bl[:], in1=key[:], op=ALU.mult), [rt], [rt])
                    P.op('vector', lambda e: e.tensor_scalar(out=k0[:], in0=bl[:], scalar1=cind[:, 0:1], scalar2=None,
                                                             op0=ALU.mult), [rt, rc], [rt])
                    P.op('vector', lambda e: e.tensor_scalar(out=k1[:], in0=bl[:], scalar1=cind[:, 1:2], scalar2=None,
                                                             op0=ALU.mult), [rt, rc], [rt])
                    P.op('scalar', lambda e: e.activation(out=sig[:], in_=bb[:], func=AF.Exp, scale=-1.0), [rt], [rt])
                    P.op('vector', lambda e: e.tensor_tensor(out=ktl[:], in0=sig[:], in1=key[:], op=ALU.mult), [rt], [rt])
                    P.op('scalar', lambda e: e.activation(out=bb[:], in_=bb[:], func=AF.Exp), [rt], [rt])
                    P.op('vector', lambda e: e.tensor_tensor(out=qtl[:], in0=qs[:], in1=bb[:], op=ALU.mult), [rt, rqs], [rt])
                    for (src, dsts) in ((qtl, 'q'), (ktl, 'k')):
                        pt, rpt = PT()
                        for h in range(8):
                            P.op('tensor', lambda e, h=h, pt=pt, src=src: e.transpose(
                                out=pt[:, h * 128:(h + 1) * 128], in_=src[:, h * 128:(h + 1) * 128], identity=ident[:]),
                                [rt, rid], [rpt])
                        ptv = pt[:].rearrange("p (h t) -> p h t", h=8)
                        if dsts == 'q':
                            P.op('vector', lambda e, ptv=ptv: e.tensor_copy(out=qT[:], in_=ptv), [rpt], [rq])
                            P.op('vector', lambda e, ptv=ptv: e.tensor_copy(out=q0T[:, :, 0:64], in_=ptv[:, :, 0:64]), [rpt], [rq])
                            P.op('vector', lambda e, ptv=ptv: e.tensor_copy(out=q1T[:, :, 64:128], in_=ptv[:, :, 64:128]), [rpt], [rq])
                        else:
                            P.op('vector', lambda e, ptv=ptv: e.tensor_copy(out=kT[:], in_=ptv), [rpt], [rq])
                    pu = [pbank[0], pbank[1]]
                    psc = [pbank[2], pbank[3]]
                    po = [pbank[4], pbank[5]]
                    Sfv = Sf[:].rearrange("p h e -> p (h e)")

                    def state_update(kk, ci, Sb):
                        for h in range(8):
                            hsl = slice(h * 128, (h + 1) * 128)
                            pb_, rpb = pu[h // 4]
                            P.op('tensor', lambda e, pb_=pb_, h=h, hsl=hsl: e.matmul(
                                pb_[:, (h % 4) * 128:(h % 4 + 1) * 128], lhsT=kk[:, hsl], rhs=vv[:, hsl], start=True, stop=True),
                                [rt, rvv], [rpb])
                        P.op('vector', lambda e: e.tensor_tensor(out=Sf[:], in0=Sf[:], in1=dec[:, :, ci:ci + 1].to_broadcast([128, 8, 128]),
                                                                 op=ALU.mult), [rt, rS], [rS])
                        for hh in range(2):
                            pb_, rpb = pu[hh]
                            P.op('vector', lambda e, pb_=pb_, hh=hh: e.tensor_tensor(
                                out=Sfv[:, hh * 512:(hh + 1) * 512], in0=Sfv[:, hh * 512:(hh + 1) * 512], in1=pb_[:, :], op=ALU.add),
                                [rpb, rS], [rS])
                        P.op('scalar', lambda e: e.copy(out=Sb[:], in_=Sf[:]), [rS], [rSb[id(Sb)]])

                    state_update(k0, 0, SbB)
                    for h in range(8):
                        pb_, rpb = psc[h // 4]
                        P.op('tensor', lambda e, pb_=pb_, h=h: e.matmul(pb_[:, (h % 4) * 128:(h % 4 + 1) * 128], lhsT=kT[:, h, :],
                                                                       rhs=qT[:, h, :], start=True, stop=True), [rq], [rpb])
                    for hh in range(2):
                        pb_, rpb = psc[hh]
                        P.op('vector', lambda e, pb_=pb_, hh=hh: e.tensor_tensor(
                            out=scT[:, hh * 4:(hh + 1) * 4, :], in0=pb_[:, :].rearrange("p (h t) -> p h t", h=4), in1=cmask4[:],
                            op=ALU.mult), [rpb, rc], [rscT])
                    for h in range(8):
                        hsl = slice(h * 128, (h + 1) * 128)
                        pb_, rpb = po[h // 4]
                        osl = slice((h % 4) * 128, (h % 4 + 1) * 128)
                        P.op('tensor', lambda e, pb_=pb_, h=h, hsl=hsl, osl=osl: e.matmul(
                            pb_[:, osl], lhsT=scT[:, h, :], rhs=vv[:, hsl], start=True, stop=False), [rscT, rt, rvv], [rpb])
                        P.op('tensor', lambda e, pb_=pb_, h=h, osl=osl: e.matmul(
                            pb_[:, osl], lhsT=q0T[:, h, :], rhs=SbA[:, h, :], start=False, stop=False), [rq, rSb[id(SbA)]], [rpb])
                        P.op('tensor', lambda e, pb_=pb_, h=h, osl=osl: e.matmul(
                            pb_[:, osl], lhsT=q1T[:, h, :], rhs=SbB[:, h, :], start=False, stop=True), [rq, rSb[id(SbB)]], [rpb])
                    for hh in range(2):
                        pb_, rpb = po[hh]
                        P.op('vector', lambda e, pb_=pb_, hh=hh: e.tensor_tensor(
                            out=og[:, hh * 512:(hh + 1) * 512], in0=pb_[:, :], in1=gateT[:, hh * 512:(hh + 1) * 512], op=ALU.mult),
                            [rpb, rgate], [rog])
                    state_update(k1, 1, SbA)
                    P.op('scalar', lambda e: e.activation(out=qs[:], in_=og[:], func=AF.Square), [rt, rog, rqs], [rt, rqs])
                    P.op('vector', lambda e: e.tensor_reduce(out=ss[:], in_=qs[:].rearrange("p (h e) -> p h e", h=8),
                                                             axis=AX.X, op=ALU.add), [rt], [rt])
                    rsqrt(ss[:], ss[:], 1.0 / 128, 1, [rt], [rt])
                    for h in range(8):
                        hsl = slice(h * 128, (h + 1) * 128)
                        P.op('vector', lambda e, h=h, hsl=hsl: e.scalar_tensor_tensor(
                            out=onb[:, hsl], in0=og[:, hsl], scalar=ss[:, h:h + 1], in1=go[:, hsl], op0=ALU.mult, op1=ALU.mult),
                            [rt, rgo, rog], [rt])
                    pt, rpt = PT()
                    for c in range(8):
                        P.op('tensor', lambda e, c=c, pt=pt: e.transpose(out=pt[:, c * 128:(c + 1) * 128],
                                                                        in_=onb[:, c * 128:(c + 1) * 128], identity=ident[:]),
                             [rt, rid], [rpt])
                    P.op('vector', lambda e, pt=pt: e.tensor_copy(out=onT[:].rearrange("p c t -> p (c t)"), in_=pt[:]), [rpt], [rt])
                    for half in range(2):
                        po, rpo = PB()
                        for c in range(8):
                            P.op('tensor', lambda e, c=c, half=half, po=po: e.matmul(
                                po[:, :], lhsT=onT[:, c, :], rhs=wout[:, c, half * 512:(half + 1) * 512],
                                start=(c == 0), stop=(c == 7)), [rt, rw], [rpo], inc=(c == 7))
                        P.op('vector', lambda e, half=half, po=po, x32=x32: e.scalar_tensor_tensor(
                            out=zt_[:, half * 512:(half + 1) * 512], in0=x32[:, half * 512:(half + 1) * 512], scalar=ALPHA,
                            in1=po[:, :], op0=ALU.mult, op1=ALU.add), [rpo, rx32, rqs], [rt, rqs])
                    ln_tail(b, zt_, rt, gam, bet, [rg1, rb1], False, False)
                P.barrier()


        NIT = 18

        def dsa_phase(layer):
            j = layer // 2
            with ExitStack() as ph:
                win = sb(ph, "awin", [128, 8, AIN], BF16)
                wql = sb(ph, "awql", [128, 3, AH * KVR], BF16)
                wqi = sb(ph, "awqi", [128, 3, IDXH * IDXD], BF16)
                wuv = sb(ph, "awuv", [128, 2 * AH, 128], BF16)
                wout = sb(ph, "awout", [128, 8, D], BF16)
                rw = Res()
                P.dma('gpsimd', win[:], Wn['a_w_in'][j].rearrange("(c p) n -> p c n", p=128), [], [rw])
                P.dma('gpsimd', wql[:], Wn['a_w_q_lat'][j].rearrange("(c p) n -> p c n", p=128), [], [rw])
                P.dma('gpsimd', wqi[:], Wn['a_w_q_idx'][j].rearrange("(c p) n -> p c n", p=128), [], [rw])
                P.dma('gpsimd', wuv[:], Wn['a_w_uv'][j].rearrange("h (rc p) d -> p (h rc) d", p=128), [], [rw])
                P.dma('gpsimd', wout[:], Wn['a_w_out'][j].rearrange("(c p) n -> p c n", p=128), [], [rw])
                gam, rg1 = load_bc(ph, "agam1", Wn['ln1_g'][layer:layer + 1, :])
                bet, rb1 = load_bc(ph, "abet1", Wn['ln1_b'][layer:layer + 1, :])
                gq, rgq = load_bc(ph, "agq", Wn['a_g_q'][j:j + 1, :])
                gkv, rgkv = load_bc(ph, "agkv", Wn['a_g_kv'][j:j + 1, :])
                gki, rgki = load_bc(ph, "agki", Wn['a_g_kidx'][j:j + 1, :])
                bki, rbki = load_bc(ph, "abki", Wn['a_b_kidx'][j:j + 1, :])
                maskall = sb(ph, "amaskall", [128, 8, 128], F32)
                rep = sb(ph, "arep", [8, 128], BF16)
                cm = sb(ph, "acm", [128, 128], F32)
                pw = sb(ph, "apw", [128, NIT], F32)
                rc = Res()
                P.dma('sync', maskall[:], maskall_in[:, :, :], [], [rc])
                P.dma('sync', rep[:], rep_in[:, :], [], [rc])
                P.dma('sync', cm[:], cm_in[:, :], [], [rc])
                P.dma('sync', pw[:], pw_in[:, 0:NIT], [], [rc])
                CKV = sb(ph, "aCKV", [128, NB, KVR + 1], BF16)
                CKVT = sb(ph, "aCKVT", [128, 2, S], BF16)
                KIT = sb(ph, "aKIT", [64, S], BF16)
                rK = Res()
                P.op('vector', lambda e: e.memset(CKV[:, :, KVR:KVR + 1], 1.0), [], [rK])
                score = sb(ph, "ascore", [128, S], F32)
                msk = sb(ph, "amsk", [128, S], BF16)
                junk = msk
                zA = sb(ph, "azA", [128, 328], F32)
                mskT_rot = Rot([sb(ph, f"amskT{i}", [128, NB, 128], BF16) for i in range(2)])
                x32_rot = Rot([sb(ph, f"ax32{i}", [128, D], F32) for i in range(1)])
                xh_rot = Rot([sb(ph, f"axh{i}", [128, 8, 130], BF16) for i in range(2)])
                sqa = sb(ph, "asqa", [128, QR], F32)
                cq = sb(ph, "acq", [128, QR], BF16)
                kix = sb(ph, "akix", [128, IDXD], F32)
                kib = sb(ph, "akib", [128, IDXD], BF16)
                wq = sb(ph, "awq", [128, 8], BF16)
                st = sb(ph, "ast", [128, 16], F32)
                cqT_rot = Rot([sb(ph, f"acqT{i}", [128, 3, 128], BF16) for i in range(2)])
                wT = sb(ph, "awT", [8, 128], BF16)
                QIT = sb(ph, "aQIT", [64, 8, 8, 16], BF16)
                Wsel = sb(ph, "aWsel", [128, 8, 128], BF16)
                R_rot = Rot([sb(ph, f"aR{i}", [128, 512], BF16) for i in range(3)])
                mm = sb(ph, "amm", [128, 2, 8], F32)
                bs = sb(ph, "abs", [128, 8], F32)
                Wt = sb(ph, "aWt", [128, NIT], F32)
                qlT4 = sb(ph, "aqlT4", [128, 2, 512], BF16)
                PT_rot = Rot([sb(ph, f"aPT{i}", [128, 4, 128], BF16) for i in range(5)])
                olb = sb(ph, "aolb", [128, AH, KVR], BF16)
                olT = sb(ph, "aolT8", [128, AH, 2, 128], BF16)
                dcol = sb(ph, "adcol", [128, AH], F32)
                rd = sb(ph, "ard8", [128, AH], F32)
                rolb, rolT, rrd = Res(), Res(), Res()
                oh = sb(ph, "aoh", [128, D], BF16)
                ohT = sb(ph, "aohT", [128, 8, 128], BF16)
                rden = sb(ph, "arden", [128, 1], F32)
                z = sb(ph, "az", [128, D], F32)
                rt = Res()
                r_z, r_q, r_cq, r_kv, r_ki, r_w, r_cqT, r_wT, r_qit, r_wsel, r_mm, r_bs = [Res() for _ in range(12)]
                sqa_kv = sb(ph, "asqakv", [128, KVR], F32)
                sqa_ki = sb(ph, "asqaki", [128, IDXD], F32)
                rsc = Res()
                rmk = Res()
                rql = Res()
                ro = Res()

                def rmsn(ps, rps0, c0, n, gtile, rg, out_ap, idx):
                    ss, rs = st[:, idx:idx + 1], st[:, idx + 1:idx + 2]
                    P.op('scalar', lambda e: e.activation(out=sqa[:, 0:n], in_=ps[:, c0:c0 + n], func=AF.Square), [rps0], [r_q])
                    P.op('vector', lambda e: e.tensor_scalar(out=sqa[:, 0:n], in0=sqa[:, 0:n], scalar1=1.0, scalar2=None,
                                                             op0=ALU.mult, op1=ALU.add, accum_out=ss), [r_q], [r_q])
                    rsqrt(rs, ss, 1.0 / n, 1, [r_q], [r_q])
                    P.op('vector', lambda e: e.scalar_tensor_tensor(out=out_ap, in0=ps[:, c0:c0 + n], scalar=rs, in1=gtile[:, 0:n],
                                                                    op0=ALU.mult, op1=ALU.mult), [rps0, r_q, rg], [r_cq])

                blk = {}
                mts = {}
                tails = {}
                r_zo = Res()

                def SB(b):
                    cqT, r_cqT = cqT_rot.next()
                    mskT, rmkT = mskT_rot.next()
                    L = (b + 1) * 128
                    xh, rxh = xh_rot.next()
                    P.dma('sync', xh[:, :, 2:130], XTH[b, :, :, 2:130], [rXTH[b]], [rxh])
                    ps0, rps0 = PB()
                    ps1, rps1 = PB()
                    for c in range(8):
                        P.op('tensor', lambda e, c=c, ps0=ps0: e.matmul(ps0[:, :], lhsT=xh[:, c, 2:130], rhs=win[:, c, 0:512],
                                                                       start=(c == 0), stop=(c == 7)), [rw, rxh], [rps0], inc=(c == 7))
                    for c in range(8):
                        P.op('tensor', lambda e, c=c, ps1=ps1: e.matmul(ps1[:, 0:AIN - 512], lhsT=xh[:, c, 2:130], rhs=win[:, c, 512:AIN],
                                                                       start=(c == 0), stop=(c == 7)), [rw, rxh], [rps1], inc=(c == 7))
                    rmsn(ps0, rps0, 0, QR, gq, rgq, cq[:], 0)
                    P.op('scalar', lambda e, ps0=ps0: e.copy(out=zA[:, 0:128], in_=ps0[:, QR:512]), [rps0], [r_z])
                    P.op('scalar', lambda e, ps1=ps1: e.copy(out=zA[:, 128:128 + 200], in_=ps1[:, 0:200]), [rps1], [r_z])
                    ss, rs = st[:, 2:3], st[:, 3:4]
                    P.op('scalar', lambda e: e.activation(out=sqa_kv[:, 0:KVR], in_=zA[:, 0:KVR], func=AF.Square), [r_z], [r_kv])
                    P.op('vector', lambda e: e.tensor_scalar(out=sqa_kv[:, 0:KVR], in0=sqa_kv[:, 0:KVR], scalar1=1.0, scalar2=None,
                                                             op0=ALU.mult, op1=ALU.add, accum_out=ss), [r_kv], [r_kv])
                    rsqrt(rs, ss, 1.0 / KVR, 1, [r_kv], [r_kv])
                    P.op('vector', lambda e, b=b: e.scalar_tensor_tensor(out=CKV[:, b, 0:KVR], in0=zA[:, 0:KVR], scalar=rs, in1=gkv[:, :],
                                                                         op0=ALU.mult, op1=ALU.mult), [r_kv, r_z, rgkv], [rK])
                    s1, s2, mean, msq, var, rstd, nmr = [st[:, 4 + i:5 + i] for i in range(7)]
                    ki = zA[:, 256:320]
                    P.op('vector', lambda e: e.tensor_scalar(out=kix[:], in0=ki, scalar1=1.0, scalar2=None, op0=ALU.mult,
                                                             op1=ALU.add, accum_out=s1), [r_z], [r_ki])
                    P.op('scalar', lambda e: e.activation(out=sqa_ki[:, 0:IDXD], in_=ki, func=AF.Square), [r_z], [r_ki])
                    P.op('vector', lambda e: e.tensor_scalar(out=sqa_ki[:, 0:IDXD], in0=sqa_ki[:, 0:IDXD], scalar1=1.0, scalar2=None,
                                                             op0=ALU.mult, op1=ALU.add, accum_out=s2), [r_ki], [r_ki])
                    P.op('vector', lambda e: e.tensor_scalar(out=mean, in0=s1, scalar1=1.0 / IDXD, scalar2=None, op0=ALU.mult), [r_ki], [r_ki])
                    P.op('vector', lambda e: e.tensor_tensor(out=msq, in0=mean, in1=mean, op=ALU.mult), [r_ki], [r_ki])
                    P.op('vector', lambda e: e.scalar_tensor_tensor(out=var, in0=s2, scalar=1.0 / IDXD, in1=msq, op0=ALU.mult,
                                                                    op1=ALU.subtract), [r_ki], [r_ki])
                    rsqrt(rstd, var, 1.0, 0, [r_ki], [r_ki])
                    P.op('vector', lambda e: e.scalar_tensor_tensor(out=nmr, in0=mean, scalar=-1.0, in1=rstd, op0=ALU.mult,
                                                                    op1=ALU.mult), [r_ki], [r_ki])
                    P.op('scalar', lambda e: e.activation(out=kix[:], in_=ki, func=AF.Identity, scale=rstd, bias=nmr), [r_ki, r_z], [r_ki])
                    P.op('vector', lambda e: e.tensor_tensor(out=kix[:], in0=kix[:], in1=gki[:], op=ALU.mult), [r_ki, rgki], [r_ki])
                    P.op('vector', lambda e: e.tensor_tensor(out=kib[:], in0=kix[:], in1=bki[:], op=ALU.add), [r_ki, rbki], [r_ki])
                    P.op('vector', lambda e: e.tensor_scalar(out=wq[:], in0=zA[:, 320:328], scalar1=float((IDXH * IDXD) ** -0.5),
                                                             scalar2=None, op0=ALU.mult), [r_z], [r_w])
                    pt, rpt = PT()
                    for c in range(3):
                        P.op('tensor', lambda e, c=c, pt=pt: e.transpose(out=pt[:, c * 128:(c + 1) * 128], in_=cq[:, c * 128:(c + 1) * 128],
                                                                        identity=ident[:]), [r_cq, rid], [rpt])
                    for c in range(2):
                        P.op('tensor', lambda e, c=c, pt=pt, b=b: e.transpose(out=pt[:, (3 + c) * 128:(4 + c) * 128],
                                                                             in_=CKV[:, b, c * 128:(c + 1) * 128], identity=ident[:]),
                             [rK, rid], [rpt])
                    P.op('tensor', lambda e, pt=pt: e.transpose(out=pt[0:64, 5 * 128:6 * 128], in_=kib[:, :], identity=ident[:]),
                         [r_ki, rid], [rpt])
                    P.op('tensor', lambda e, pt=pt: e.transpose(out=pt[0:8, 6 * 128:7 * 128], in_=wq[:, :], identity=ident[:]),
                         [r_w, rid], [rpt])
                    P.op('vector', lambda e, pt=pt: e.tensor_copy(out=cqT[:].rearrange("p c t -> p (c t)"), in_=pt[:, 0:384]), [rpt], [r_cqT])
                    for c in range(2):
                        P.op('vector', lambda e, pt=pt, c=c, b=b: e.tensor_copy(out=CKVT[:, c, b * 128:(b + 1) * 128],
                                                                               in_=pt[:, (3 + c) * 128:(4 + c) * 128]), [rpt], [rK])
                    P.op('vector', lambda e, pt=pt, b=b: e.tensor_copy(out=KIT[:, b * 128:(b + 1) * 128], in_=pt[0:64, 640:768]), [rpt], [rK])
                    P.op('vector', lambda e, pt=pt: e.tensor_copy(out=wT[:], in_=pt[0:8, 768:896]), [rpt], [r_wT])
                    for hh in range(2):
                        pq, rpq = PB()
                        for h4 in range(4):
                            h = hh * 4 + h4
                            for kc in range(3):
                                P.op('tensor', lambda e, pq=pq, h=h, h4=h4, kc=kc: e.matmul(
                                    pq[0:64, h4 * 128:(h4 + 1) * 128], lhsT=wqi[:, kc, h * 64:(h + 1) * 64], rhs=cqT[:, kc, :],
                                    start=(kc == 0), stop=(kc == 2)), [rw, r_cqT], [rpq], inc=(kc == 2))
                        P.op('scalar', lambda e, pq=pq, hh=hh: e.copy(
                            out=QIT[:, :, hh * 4:(hh + 1) * 4, :].rearrange("p g h t -> p h g t"),
                            in_=pq[0:64, :].rearrange("p (h g t) -> p h g t", h=4, g=8)), [rpq], [r_qit])
                    pe_, rpe = PB()
                    P.op('tensor', lambda e, pe_=pe_: e.matmul(pe_[:, 0:128], lhsT=rep[:, :], rhs=wT[:, :], start=True, stop=True),
                         [rc, r_wT], [rpe])
                    for g in range(8):
                        P.op('vector', lambda e, g=g, pe_=pe_: e.tensor_tensor(out=Wsel[:, g, :], in0=pe_[:, 0:128], in1=maskall[:, g, :],
                                                                              op=ALU.mult), [rpe, rc], [r_wsel])
                    nkc = (L + 511) // 512
                    for kc in range(nkc):
                        k0_ = kc * 512
                        n = min(512, L - k0_)
                        psc, rpsc = pbank[4]
                        pend = []

                        def emit_mm1(g, k0_=k0_, n=n):
                            p1, rp1 = PB()
                            P.op('tensor', lambda e, p1=p1, g=g: e.matmul(
                                p1[:, 0:n], lhsT=QIT[:, g, :, :].rearrange("p h t -> p (h t)"), rhs=KIT[:, k0_:k0_ + n], start=True, stop=True),
                                [r_qit, rK], [rp1])
                            R, rR = R_rot.next()
                            if g % 2 == 0:
                                P.op('scalar', lambda e, p1=p1, R=R: e.activation(out=R[:, 0:n], in_=p1[:, 0:n], func=AF.Relu),
                                     [rp1], [rR])
                            else:
                                P.op('vector', lambda e, p1=p1, R=R: e.tensor_scalar(out=R[:, 0:n], in0=p1[:, 0:n], scalar1=0.0,
                                                                                    scalar2=None, op0=ALU.max), [rp1], [rR])
                            pend.append((R, rR))

                        emit_mm1(0)
                        emit_mm1(1)
                        for g in range(8):
                            if g + 2 < 8:
                                emit_mm1(g + 2)
                            R, rR = pend.pop(0)
                            P.op('tensor', lambda e, psc=psc, g=g, R=R, n=n: e.matmul(
                                psc[:, 0:n], lhsT=Wsel[:, g, :], rhs=R[:, 0:n], start=(g == 0), stop=(g == 7)), [r_wsel, rR], [rpsc])
                        P.op('scalar', lambda e, psc=psc, k0_=k0_, n=n: e.copy(out=score[:, k0_:k0_ + n], in_=psc[:, 0:n]), [rpsc], [rsc])
                        P.op('vector', lambda e, psc=psc, kc=kc, n=n: e.tensor_reduce(out=mm[:, 0, kc:kc + 1], in_=psc[:, 0:n], axis=AX.X,
                                                                                     op=ALU.min), [rpsc], [r_mm])
                        P.op('vector', lambda e, psc=psc, kc=kc, n=n: e.tensor_reduce(out=mm[:, 1, kc:kc + 1], in_=psc[:, 0:n], axis=AX.X,
                                                                                     op=ALU.max), [rpsc], [r_mm])
                    P.op('vector', lambda e, L=L: e.tensor_tensor(out=score[:, L - 128:L], in0=score[:, L - 128:L], in1=cm[:], op=ALU.add),
                         [rsc, rc], [rsc])
                    lo, hi, w0, mid, cnt, stp = [bs[:, i:i + 1] for i in range(6)]
                    P.op('vector', lambda e, nkc=nkc: e.tensor_reduce(out=lo, in_=mm[:, 0, 0:nkc], axis=AX.X, op=ALU.min), [r_mm], [r_bs])
                    P.op('vector', lambda e, nkc=nkc: e.tensor_reduce(out=hi, in_=mm[:, 1, 0:nkc], axis=AX.X, op=ALU.max), [r_mm], [r_bs])
                    P.op('vector', lambda e: e.tensor_tensor(out=w0, in0=hi, in1=lo, op=ALU.subtract), [r_bs], [r_bs])
                    P.op('vector', lambda e: e.tensor_scalar(out=w0, in0=w0, scalar1=1.0001, scalar2=1e-6, op0=ALU.mult, op1=ALU.add), [r_bs], [r_bs])
                    P.op('vector', lambda e: e.tensor_scalar(out=Wt[:], in0=pw[:], scalar1=w0, scalar2=None, op0=ALU.mult), [r_bs, rc], [r_bs])
                    if L > TOPK:
                        P.op('vector', lambda e: e.tensor_tensor(out=mid, in0=lo, in1=Wt[:, 0:1], op=ALU.add), [r_bs], [r_bs])
                        for k in range(NIT):
                            P.op('vector', lambda e, L=L: e.tensor_scalar(out=junk[:, 0:L], in0=score[:, 0:L], scalar1=mid, scalar2=None,
                                                                         op0=ALU.is_ge, op1=ALU.add, accum_out=cnt), [r_bs, rsc], [r_bs, rmk])
                            P.op('vector', lambda e, k=k: e.scalar_tensor_tensor(out=stp, in0=cnt, scalar=float(TOPK) - 0.5, in1=Wt[:, k:k + 1],
                                                                                op0=ALU.is_ge, op1=ALU.mult), [r_bs], [r_bs])
                            if k < NIT - 1:
                                P.op('vector', lambda e, k=k: e.scalar_tensor_tensor(out=mid, in0=mid, scalar=Wt[:, k + 1:k + 2], in1=stp,
                                                                                    op0=ALU.subtract, op1=ALU.add), [r_bs], [r_bs])
                            else:
                                P.op('vector', lambda e, k=k: e.scalar_tensor_tensor(out=lo, in0=mid, scalar=Wt[:, k:k + 1], in1=stp,
                                                                                    op0=ALU.subtract, op1=ALU.add), [r_bs], [r_bs])
                    P.op('vector', lambda e, L=L: e.tensor_scalar(out=msk[:, 0:L], in0=score[:, 0:L], scalar1=lo, scalar2=None,
                                                                 op0=ALU.is_ge), [r_bs, rsc], [rmk])
                    def MT(b=b, mskT=mskT, rmkT=rmkT):
                        for kb0 in range(0, b + 1, 8):
                            nk = min(8, b + 1 - kb0)
                            pt, rpt = PT()
                            for i in range(nk):
                                P.op('tensor', lambda e, pt=pt, i=i, kb0=kb0: e.transpose(
                                    out=pt[:, i * 128:(i + 1) * 128], in_=msk[:, (kb0 + i) * 128:(kb0 + i + 1) * 128], identity=ident[:]),
                                    [rmk, rid], [rpt])
                            P.op('vector', lambda e, pt=pt, nk=nk, kb0=kb0: e.tensor_copy(
                                out=mskT[:, kb0:kb0 + nk, :].rearrange("p k t -> p (k t)"), in_=pt[:, 0:nk * 128]), [rpt], [rmkT])
                    blk[b] = (cqT, r_cqT, mskT, rmkT)
                    mts[b] = MT

                def ATT(b):
                    cqT, r_cqT, mskT, rmkT = blk.pop(b)
                    x32, rx32 = x32_rot.next()
                    P.dma('sync', x32[:], xsrc(layer)[b * 128:(b + 1) * 128, :], [rX32[b]], [rx32])
                    qk_i = [0]

                    def QKB():
                        r = pbank[4 + qk_i[0]]
                        qk_i[0] ^= 1
                        return r

                    for hg in range(2):
                        for rcx in range(2):
                            pq, rpq = QKB()
                            for h4 in range(4):
                                ch = (hg * 4 + h4) * 2 + rcx
                                for kc in range(3):
                                    P.op('tensor', lambda e, pq=pq, h4=h4, ch=ch, kc=kc: e.matmul(
                                        pq[:, h4 * 128:(h4 + 1) * 128], lhsT=wql[:, kc, ch * 128:(ch + 1) * 128], rhs=cqT[:, kc, :],
                                        start=(kc == 0), stop=(kc == 2)), [rw, r_cqT], [rpq], inc=(kc == 2))
                            P.op('scalar', lambda e, pq=pq, rcx=rcx: e.activation(out=qlT4[:, rcx, :], in_=pq[:, :], func=AF.Copy,
                                                                                scale=float(KVR ** -0.5)), [rpq], [rql])
                        pendq = []

                        def emit_qk(kb):
                            pst, rpst = QKB()
                            for rcx in range(2):
                                P.op('tensor', lambda e, pst=pst, rcx=rcx, kb=kb: e.matmul(
                                    pst[:, :], lhsT=CKVT[:, rcx, kb * 128:(kb + 1) * 128], rhs=qlT4[:, rcx, :],
                                    start=(rcx == 0), stop=(rcx == 1)), [rK, rql], [rpst], inc=(rcx == 1))
                            PTt, rPT = PT_rot.next()
                            P.op('scalar', lambda e, pst=pst, PTt=PTt: e.activation(
                                out=PTt[:].rearrange("p k t -> p (k t)"), in_=pst[:, :], func=AF.Exp), [rpst], [rPT])
                            P.op('gpsimd', lambda e, PTt=PTt, kb=kb: e.tensor_tensor(
                                out=PTt[:], in0=PTt[:], in1=mskT[:, kb:kb + 1, :].to_broadcast([128, 4, 128]), op=ALU.mult), [rPT, rmkT], [rPT])
                            pendq.append((PTt, rPT))

                        for kb in range(min(3, b + 1)):
                            emit_qk(kb)
                        for kb in range(b + 1):
                            if kb + 3 <= b:
                                emit_qk(kb + 3)
                            PTt, rPT = pendq.pop(0)
                            for h4 in range(4):
                                pol, rpol = pbank[h4]
                                P.op('tensor', lambda e, pol=pol, PTt=PTt, h4=h4, kb=kb, b=b: e.matmul(
                                    pol[:, 0:KVR + 1], lhsT=PTt[:, h4, :], rhs=CKV[:, kb, :], start=(kb == 0), stop=(kb == b)),
                                    [rPT, rK], [rpol])
                        for h4 in range(4):
                            pol, rpol = pbank[h4]
                            h = hg * 4 + h4
                            P.op('scalar', lambda e, pol=pol, h=h: e.copy(out=olb[:, h, :], in_=pol[:, 0:KVR]), [rpol], [rolb])
                            P.op('scalar', lambda e, pol=pol, h=h: e.copy(out=dcol[:, h:h + 1], in_=pol[:, KVR:KVR + 1]), [rpol], [rolb])
                    if (b + 1) in mts:
                        mts.pop(b + 1)()
                    if (b - 1) in tails:
                        tails.pop(b - 1)()
                    P.op('vector', lambda e: e.reciprocal(out=rd[:], in_=dcol[:]), [rolb], [rrd])
                    pts = [PT(), PT()]
                    for h in range(8):
                        pt, rpt = pts[h // 4]
                        for rcx in range(2):
                            P.op('tensor', lambda e, pt=pt, h=h, rcx=rcx: e.transpose(
                                out=pt[:, ((h % 4) * 2 + rcx) * 128:((h % 4) * 2 + rcx + 1) * 128],
                                in_=olb[:, h, rcx * 128:(rcx + 1) * 128], identity=ident[:]), [rolb, rid], [rpt])
                    for hh in range(2):
                        pt, rpt = pts[hh]
                        P.op('scalar', lambda e, pt=pt, hh=hh: e.copy(
                            out=olT[:, hh * 4:(hh + 1) * 4, :, :].rearrange("p h c t -> p (h c t)"), in_=pt[:, :]), [rpt], [rolT])
                    puvs = [QKB(), QKB()]
                    for h in range(8):
                        puv, rpuv = puvs[h // 4]
                        for rcx in range(2):
                            P.op('tensor', lambda e, puv=puv, rcx=rcx, h=h: e.matmul(
                                puv[:, (h % 4) * 128:(h % 4 + 1) * 128], lhsT=olT[:, h, rcx, :], rhs=wuv[:, h * 2 + rcx, :],
                                start=(rcx == 0), stop=(rcx == 1)), [rolT, rw], [rpuv], inc=(rcx == 1))
                    for hh in range(2):
                        puv, rpuv = puvs[hh]
                        P.op('vector', lambda e, puv=puv, hh=hh: e.tensor_tensor(
                            out=oh[:, hh * 512:(hh + 1) * 512].rearrange("p (h d) -> p h d", h=4),
                            in0=puv[:, :].rearrange("p (h d) -> p h d", h=4),
                            in1=rd[:, hh * 4:(hh + 1) * 4].rearrange("p (h o) -> p h o", o=1).to_broadcast([128, 4, 128]),
                            op=ALU.mult), [rpuv, rrd], [ro])
                    pt, rpt = PT()
                    for c in range(8):
                        P.op('tensor', lambda e, c=c, pt=pt: e.transpose(out=pt[:, c * 128:(c + 1) * 128],
                                                                        in_=oh[:, c * 128:(c + 1) * 128], identity=ident[:]),
                             [ro, rid], [rpt])
                    P.op('vector', lambda e, pt=pt: e.tensor_copy(out=ohT[:].rearrange("p c t -> p (c t)"), in_=pt[:]), [rpt], [ro])
                    for half in range(2):
                        po, rpo = PB()
                        for c in range(8):
                            P.op('tensor', lambda e, c=c, half=half, po=po: e.matmul(
                                po[:, :], lhsT=ohT[:, c, :], rhs=wout[:, c, half * 512:(half + 1) * 512],
                                start=(c == 0), stop=(c == 7)), [ro, rw], [rpo], inc=(c == 7))
                        P.op('vector', lambda e, half=half, po=po, x32=x32: e.scalar_tensor_tensor(
                            out=z[:, half * 512:(half + 1) * 512], in0=x32[:, half * 512:(half + 1) * 512], scalar=ALPHA,
                            in1=po[:, :], op0=ALU.mult, op1=ALU.add), [rpo, rx32], [r_zo])
                    tails[b] = ln_tail(b, z, r_zo, gam, bet, [rg1, rb1], False, False, defer=True)

                SB(0)
                mts.pop(0)()
                for b in range(NB):
                    if b + 1 < NB:
                        SB(b + 1)
                    ATT(b)
                for b_ in sorted(tails):
                    tails[b_]()
                tails.clear()
                P.barrier()

        def mixer_phase(layer):
            if MIXERS[layer % 2] is None:
                identity_mixer_phase(layer)
            elif layer % 2 == 0:
                dsa_phase(layer)
            else:
                hgrn_phase(layer)

        for layer in layers:
            mixer_phase(layer)
            ffn_phase(layer, last=(layer == layers[-1]))
        for e in ('sync',):
            if rY.w is not None:
                P._wait(e, rY.w[0], rY.w[1])
        P.barrier()
        print("instructions:", P.ninst)
    return nc


MIXERS = [True, True]

WSHAPES = {
    'a_w_in': [2, D, AIN], 'a_g_q': [2, QR], 'a_g_kv': [2, KVR], 'a_w_q_lat': [2, QR, AH * KVR],
    'a_w_q_idx': [2, QR, IDXH * IDXD], 'a_g_kidx': [2, IDXD], 'a_b_kidx': [2, IDXD],
    'a_w_uv': [2, AH, KVR, 128], 'a_w_out': [2, D, D], 'b_w_in': [2, D, 4 * D], 'b_lb_logits': [4, D],
    'b_g_o': [2, D], 'b_w_out': [2, D, D], 'ln1_g': [4, D], 'ln1_b': [4, D], 'f_w_up': [4, D, 2 * DFF],
    'f_conv_w': [4, 3, 1, 2 * DFF], 'f_conv_b': [4, 2 * DFF], 'f_w_down': [4, DFF, D], 'ln2_g': [4, D], 'ln2_b': [4, D],
}


def host_consts():
    ident = np.eye(128, dtype=np.float32).astype(ml_dtypes.bfloat16)
    s_ = np.arange(128)[:, None]
    t_ = np.arange(128)[None, :]
    same = (s_ // 64) == (t_ // 64)
    hconst = np.zeros((128, 4, 128), np.float32)
    hconst[:, 0, :] = (same & (s_ <= t_))
    hconst[:, 1, :] = same
    hconst[:, 3, :] = (same & (s_ <= t_))
    hcind = np.zeros((128, 4), np.float32)
    hcind[:64, 0] = 1.0
    hcind[64:, 1] = 1.0
    p_ = np.arange(128)
    maskall = np.zeros((128, 8, 128), np.float32)
    for g in range(8):
        maskall[p_, g, 16 * g + (p_ % 16)] = 1.0
    rep = np.zeros((8, 128), np.float32)
    rep[p_ // 16, p_] = 1.0
    cm = np.where(np.arange(128)[None, :] <= np.arange(128)[:, None], 0.0, -1e30).astype(np.float32)
    pw = np.tile((0.5 ** np.arange(1, 33, dtype=np.float64)).astype(np.float32)[None, :], (128, 1))
    return {'ident': ident, 'hconst': hconst, 'hcind': hcind, 'maskall': maskall,
            'rep': rep.astype(ml_dtypes.bfloat16), 'cm': cm, 'pw': pw}


def kernel(**inputs):
    x = np.ascontiguousarray(np.asarray(inputs['x'], dtype=np.float32))
    B, S, _ = x.shape
    nc = build_program(S)
    consts = host_consts()
    wmaps = {k: np.ascontiguousarray(np.asarray(inputs[k], dtype=np.float32)) for k in WSHAPES}
    in_maps = []
    for c in range(8):
        m = dict(wmaps)
        m['x'] = x[(c // 2) % B]
        m.update(consts)
        in_maps.append(m)
    res = run_bass_kernel_spmd(nc, in_maps, core_ids=list(range(8)))
    out = np.stack([res.results[2 * b]['y'] for b in range(B)], axis=0)
    return out.astype(np.float32)
```

```python
from contextlib import ExitStack
import numpy as np
import ml_dtypes
import concourse.bass as bass
import concourse.mybir as mybir
from concourse.bass_utils import run_bass_kernel_spmd

F32 = mybir.dt.float32
BF16 = mybir.dt.bfloat16
ALU = mybir.AluOpType
AF = mybir.ActivationFunctionType
AX = mybir.AxisListType

D = 1024
DFF = 2816
NFC = DFF // 128
DEPTH = 4
ALPHA = (2 * DEPTH) ** 0.25
LN_EPS = 1e-5
RMS_EPS = 1e-6
QR, KVR, IDXD, IDXH, AH = 384, 256, 64, 8, 8
AIN = QR + KVR + IDXD + IDXH

ENGS = ['tensor', 'vector', 'scalar', 'gpsimd', 'sync']
DMAQ = ['sync', 'scalar', 'gpsimd']


class Res:
    __slots__ = ('w', 'r', 'name', 'excl')

    def __init__(self, name='', excl=False):
        self.w = None
        self.r = []
        self.name = name
        self.excl = excl


class Prog:
    def __init__(self, nc, stack, n_dma_slots=4):
        self.nc = nc
        self.sem = {e: stack.enter_context(nc.semaphore(f"s_{e}")) for e in ENGS}
        self.cnt = {e: 0 for e in ENGS}
        self.nslots = n_dma_slots
        self.dsem, self.dcnt, self.dnext = {}, {}, {}
        for qn in DMAQ:
            for s in range(n_dma_slots):
                self.dsem[(qn, s)] = stack.enter_context(nc.semaphore(f"d_{qn}_{s}"))
                self.dcnt[(qn, s)] = 0
            self.dnext[qn] = 0
        self.seen = {e: {} for e in ENGS}
        self.ninst = 0

    def _wait(self, eng, prod, val):
        if self.seen[eng].get(prod, 0) >= val:
            return
        sem = self.sem[prod] if isinstance(prod, str) else self.dsem[prod]
        getattr(self.nc, eng).wait_ge(sem, val)
        self.seen[eng][prod] = val

    def _deps(self, eng, reads, writes):
        deps = []
        for b in reads:
            if b.w is not None:
                deps.append(b.w)
            if b.excl:
                deps.extend(b.r)
        for b in writes:
            if b.w is not None:
                if not (b.excl and eng == 'tensor' and b.w[0] == 'tensor'):
                    deps.append(b.w)
            deps.extend(b.r)
        for (p, v) in deps:
            self._wait(eng, p, v)

    def _commit(self, me, reads, writes):
        for b in reads:
            b.r.append(me)
        for b in writes:
            b.w = me
            b.r = []

    def op(self, eng, fn, reads=(), writes=(), inc=True):
        self._deps(eng, reads, writes)
        if inc:
            self.cnt[eng] += 1
            fn(getattr(self.nc, eng)).then_inc(self.sem[eng], 1)
            self._commit((eng, self.cnt[eng]), reads, writes)
        else:
            fn(getattr(self.nc, eng))
            self._commit((eng, self.cnt[eng] + 1), reads, writes)
        self.ninst += 1

    def dma(self, qn, out, in_, reads=(), writes=(), **kw):
        s = self.dnext[qn]
        self.dnext[qn] = (s + 1) % self.nslots
        key = (qn, s)
        prev = 16 * self.dcnt[key]
        if prev:
            self._wait(qn, key, prev)
        self._deps(qn, reads, writes)
        self.dcnt[key] += 1
        getattr(self.nc, qn).dma_start(out=out, in_=in_, **kw).then_inc(self.dsem[key], 16)
        self._commit((key, 16 * self.dcnt[key]), reads, writes)
        self.ninst += 1

    def barrier(self):
        for e in ENGS:
            for p in ENGS:
                if p != e and self.cnt[p]:
                    self._wait(e, p, self.cnt[p])
            for k, c in self.dcnt.items():
                if c:
                    self._wait(e, k, 16 * c)


class Rot:
    def __init__(self, tiles):
        self.t = [(t, Res()) for t in tiles]
        self.i = 0

    def next(self):
        r = self.t[self.i]
        self.i = (self.i + 1) % len(self.t)
        return r


def build_program(S, layers=(0, 1, 2, 3), first=True):
    layers = tuple(layers)
    NB = S // 128
    TOPK = min(256, S // 4)
    nc = bass.Bass("TRN2", target_bir_lowering=False)

    def din(name, shape, dt=F32):
        return nc.dram_tensor(name, list(shape), dt, kind="ExternalInput").ap()

    x_in = din("x", [S, D])
    Wn = {}
    for name, shape in WSHAPES.items():
        Wn[name] = din(name, shape)
    ident_in = din("ident", [128, 128], BF16)
    hconst_in = din("hconst", [128, 4, 128])
    hcind_in = din("hcind", [128, 4])
    maskall_in = din("maskall", [128, 8, 128])
    rep_in = din("rep", [8, 128], BF16)
    cm_in = din("cm", [128, 128])
    pw_in = din("pw", [128, 32])
    y_out = nc.dram_tensor("y", [S, D], F32, kind="ExternalOutput").ap()
    X32 = nc.dram_tensor("X32", [S, D], F32).ap()
    XTH = nc.dram_tensor("XTH", [NB + 1, 128, 8, 130], BF16).ap()
    rX32 = [Res() for _ in range(NB)]
    rXTH = [Res() for _ in range(NB + 1)]
    rXTHh = [Res() for _ in range(NB + 1)]
    rY = Res()
    WUb = nc.dram_tensor("WUb", [4, NFC, 128, 8, 256], BF16).ap()
    rWb = [Res() for _ in range(4)]

    with ExitStack() as top:
        P = Prog(nc, top)
        uid = [0]

        def sb(st, name, shape, dt):
            uid[0] += 1
            return st.enter_context(nc.sbuf_tensor(f"t{uid[0]}_{name}", list(shape), dt))
        pbank = [(top.enter_context(nc.psum_tensor(f"pb{i}", [128, 512], F32)), Res(excl=True)) for i in range(6)]
        ptb = [(top.enter_context(nc.psum_tensor(f"pt{i}", [128, 1024], BF16)), Res(excl=True)) for i in range(2)]
        pbi = [0]
        pti = [0]

        def PB():
            r = pbank[pbi[0]]
            pbi[0] = (pbi[0] + 1) % 4
            return r

        def PT():
            r = ptb[pti[0]]
            pti[0] = (pti[0] + 1) % len(ptb)
            return r

        epsT = sb(top, "epsT", [128, 2], F32)
        reps = Res()
        P.op('vector', lambda e: e.memset(epsT[:, 0:1], LN_EPS), [], [reps])
        P.op('vector', lambda e: e.memset(epsT[:, 1:2], RMS_EPS), [], [reps])

        def rsqrt(out, in_, scale, which, reads, writes):
            P.op('scalar', lambda e: e.activation(out=out, in_=in_, func=AF.Sqrt, scale=scale, bias=epsT[:, which:which + 1]),
                 list(reads) + [reps], writes)
            P.op('vector', lambda e: e.reciprocal(out=out, in_=out), writes, writes)

        ident = sb(top, "ident_sb", [128, 128], BF16)
        rid = Res()
        P.dma('sync', ident[:], ident_in[:, :], writes=[rid])

        def convert_wu(l, j):
            for part in range(2):
                col = part * DFF + j * 128
                P.dma('gpsimd', WUb[l, j, :, :, part * 128:(part + 1) * 128],
                      Wn['f_w_up'][l, :, col:col + 128].rearrange("(c p) n -> p c n", p=128), [], [rWb[l]])

        for j in range(NFC):
            convert_wu(layers[0], j)

        lnst = ExitStack()
        top.enter_context(lnst)
        yb_rot = Rot([sb(top, f"yb{i}", [128, D], BF16) for i in range(1)])
        xtb_rot = Rot([sb(top, f"xtb{i}", [128, 8, 128], BF16) for i in range(2)])
        yn_rot = Rot([sb(top, f"yn{i}", [128, D], F32) for i in range(1)])
        sq_rot = Rot([sb(top, f"sq{i}", [128, D], F32) for i in range(1)])
        st_rot = Rot([sb(top, f"st{i}", [128, 8], F32) for i in range(2)])
        eng_flip = [0]

        def transpose_store(b, y32, ry, halo):
            yb, ryb = yb_rot.next()
            P.op('scalar', lambda e: e.copy(out=yb[:], in_=y32[:]), [ry], [ryb])
            pt, rpt = PT()
            for c in range(8):
                P.op('tensor', lambda e, c=c: e.transpose(out=pt[:, c * 128:(c + 1) * 128],
                                                           in_=yb[:, c * 128:(c + 1) * 128], identity=ident[:]),
                     [ryb, rid], [rpt])
            xtb, rxtb = xtb_rot.next()
            P.op('vector', lambda e: e.tensor_copy(out=xtb[:].rearrange("p c t -> p (c t)"), in_=pt[:]), [rpt], [rxtb])
            P.dma('gpsimd', XTH[b, :, :, 2:130], xtb[:], [rxtb], [rXTH[b]])
            if halo:
                P.dma('gpsimd', XTH[b + 1, :, :, 0:2], xtb[:, :, 126:128], [rxtb], [rXTHh[b + 1]])

        def ln_tail(b, z, rz, gam, bet, rgb, to_out, halo):
            st_, rst = st_rot.next()
            sq, rsq = sq_rot.next()
            yn, ryn = yn_rot.next()
            s1, s2, mean, msq, var, rstd, nmr = [st_[:, i:i + 1] for i in range(7)]
            P.op('vector', lambda e: e.tensor_scalar(out=yn[:], in0=z[:], scalar1=1.0, scalar2=None, op0=ALU.mult,
                                                     op1=ALU.add, accum_out=s1), [rz], [ryn, rst])
            P.op('scalar', lambda e: e.activation(out=sq[:], in_=z[:], func=AF.Square), [rz], [rsq])
            P.op('vector', lambda e: e.tensor_scalar(out=sq[:], in0=sq[:], scalar1=1.0, scalar2=None, op0=ALU.mult,
                                                     op1=ALU.add, accum_out=s2), [rsq], [rsq, rst])
            P.op('vector', lambda e: e.tensor_scalar(out=mean, in0=s1, scalar1=1.0 / D, scalar2=None, op0=ALU.mult), [rst], [rst])
            P.op('vector', lambda e: e.tensor_tensor(out=msq, in0=mean, in1=mean, op=ALU.mult), [rst], [rst])
            P.op('vector', lambda e: e.scalar_tensor_tensor(out=var, in0=s2, scalar=1.0 / D, in1=msq, op0=ALU.mult,
                                                            op1=ALU.subtract), [rst], [rst])
            rsqrt(rstd, var, 1.0, 0, [rst], [rst])
            P.op('vector', lambda e: e.scalar_tensor_tensor(out=nmr, in0=mean, scalar=-1.0, in1=rstd, op0=ALU.mult,
                                                            op1=ALU.mult), [rst], [rst])
            P.op('scalar', lambda e: e.activation(out=yn[:], in_=z[:], func=AF.Identity, scale=rstd, bias=nmr),
                 [rz, rst], [ryn])
            P.op('gpsimd', lambda e: e.tensor_tensor(out=yn[:], in0=yn[:], in1=gam[:], op=ALU.mult), [ryn] + list(rgb), [ryn])
            P.op('vector', lambda e: e.tensor_tensor(out=yn[:], in0=yn[:], in1=bet[:], op=ALU.add), [ryn] + list(rgb), [ryn])
            if to_out:
                P.dma('gpsimd', y_out[b * 128:(b + 1) * 128, :], yn[:], [ryn], [rY])
            else:
                P.dma('gpsimd', X32[b * 128:(b + 1) * 128, :], yn[:], [ryn], [rX32[b]])
                transpose_store(b, yn, ryn, halo)

        if first:
            with ExitStack() as ph:
                zt = sb(ph, "zt", [128, 8, 2], BF16)
                rzt = Res()
                P.op('vector', lambda e: e.memset(zt[:], 0.0), [], [rzt])
                P.dma('sync', XTH[0, :, :, 0:2], zt[:], [rzt], [rXTHh[0]])
                xl_rot = Rot([sb(ph, f"xl{i}", [128, D], F32) for i in range(2)])
                for b in range(NB):
                    xl, rxl = xl_rot.next()
                    P.dma('scalar', xl[:], x_in[b * 128:(b + 1) * 128, :], [], [rxl])
                    transpose_store(b, xl, rxl, False)
                P.barrier()

        def xsrc(layer):
            return x_in if (first and layer == layers[0]) else X32

        def load_bc(st, name, src_row):
            t = sb(st, name, [128, src_row.shape[-1]], F32)
            r = Res()
            P.dma('sync', t[:], src_row.to_broadcast([128, src_row.shape[-1]]), [], [r])
            return t, r

        TT = 512 if NB % 4 == 0 else 128 * NB
        NBT = TT // 128

        def ffn_phase(layer, last):
            with ExitStack() as ph:
                wd = sb(ph, "wd", [128, NFC, D], BF16)
                rw = Res()
                for j in range(0, NFC, 2):
                    P.dma('gpsimd', wd[:, j:j + 2, :],
                          Wn['f_w_down'][layer, j * 128:(j + 2) * 128, :].rearrange("(c p) n -> p c n", p=128), [], [rw])
                cw = sb(ph, "cw", [128, 3, 2 * NFC], F32)
                cb = sb(ph, "cb", [128, 2 * NFC], F32)
                for k in range(3):
                    P.dma('scalar', cw[:, k, :], Wn['f_conv_w'][layer, k, 0, :].rearrange("(j p) -> p j", p=128), [], [rw],
                          allow_slow_non_contiguous=True)
                P.dma('scalar', cb[:], Wn['f_conv_b'][layer, :].rearrange("(j p) -> p j", p=128), [], [rw],
                      allow_slow_non_contiguous=True)
                gam = sb(ph, "gam2", [128, D], F32)
                bet = sb(ph, "bet2", [128, D], F32)
                P.dma('sync', gam[:], Wn['ln2_g'][layer:layer + 1, :].to_broadcast([128, D]), [], [rw])
                P.dma('sync', bet[:], Wn['ln2_b'][layer:layer + 1, :].to_broadcast([128, D]), [], [rw])
                carry = sb(ph, "carry", [128, 2 * NFC, 2], F32)
                rcar = [Res() for _ in range(2 * NFC)]
                P.op('gpsimd', lambda e: e.memset(carry[:], 0.0), [], rcar)
                wu_rot = Rot([sb(ph, f"wu{i}", [128, 8, 256], BF16) for i in range(3)])
                xw_rot = Rot([sb(ph, f"xw{i}", [128, 8, TT], BF16) for i in range(2)])
                g_rot = Rot([sb(ph, f"g{i}", [128, NFC, TT], BF16) for i in range(2)])
                hs_rot = Rot([sb(ph, f"hs{i}", [128, TT + 2], F32) for i in range(4)])
                acc_rot = Rot([sb(ph, f"acc{i}", [128, TT], F32) for i in range(6)])
                sa_rot = Rot([sb(ph, f"sa{i}", [128, TT], F32) for i in range(2)])
                x32_rot = Rot([sb(ph, f"x32{i}", [128, D], F32) for i in range(4 if NBT == 4 else NBT)])
                z_rot = Rot([sb(ph, f"z{i}", [128, D], F32) for i in range(2)])
                for t in range(NB // NBT):
                    b0 = t * NBT
                    xw, rxw = xw_rot.next()
                    for i in range(NBT):
                        P.dma('sync', xw[:, :, i * 128:(i + 1) * 128], XTH[b0 + i, :, :, 2:130], [rXTH[b0 + i]], [rxw])
                    g, rg = g_rot.next()
                    pend_gate = None
                    x32s = []
                    for i in range(NBT):
                        x32, rx32 = x32_rot.next()
                        P.dma('sync', x32[:], X32[(b0 + i) * 128:(b0 + i + 1) * 128, :], [rX32[b0 + i]], [rx32])
                        x32s.append((x32, rx32))

                    def emit_gate(accs, j, g, rg):
                        sa, rsa = sa_rot.next()
                        P.op('scalar', lambda e, sa=sa, a=accs[0][0]: e.activation(out=sa[:], in_=a[:], func=AF.Silu),
                             [accs[0][1]], [rsa])
                        P.op('vector', lambda e, sa=sa, u=accs[1][0], g=g, j=j: e.tensor_tensor(
                            out=g[:, j, :], in0=sa[:], in1=u[:], op=ALU.mult), [rsa, accs[1][1]], [rg])

                    for j in range(NFC):
                        if t == 0 and layer != layers[-1]:
                            convert_wu(layers[layers.index(layer) + 1], j)
                        wu, rwu = wu_rot.next()
                        P.dma('sync', wu[:], WUb[layer, j], [rWb[layer]], [rwu])
                        accs = []
                        for part in range(2):
                            jj = part * NFC + j
                            ps, rps = PB()
                            for c in range(8):
                                P.op('tensor', lambda e, c=c, part=part, ps=ps, wu=wu, xw=xw: e.matmul(
                                    ps[:, 0:TT], lhsT=wu[:, c, part * 128:(part + 1) * 128], rhs=xw[:, c, :],
                                    start=(c == 0), stop=(c == 7)), [rwu, rxw], [rps], inc=(c == 7))
                            hs, rhs_ = hs_rot.next()
                            P.op('gpsimd', lambda e, hs=hs, jj=jj: e.tensor_copy(out=hs[:, 0:2], in_=carry[:, jj, :]), [rcar[jj]], [rhs_])
                            P.op('scalar', lambda e, hs=hs, ps=ps: e.copy(out=hs[:, 2:TT + 2], in_=ps[:, 0:TT]), [rps], [rhs_])
                            P.op('gpsimd', lambda e, hs=hs, jj=jj: e.tensor_copy(out=carry[:, jj, :], in_=hs[:, TT:TT + 2]), [rhs_], [rcar[jj]])
                            acc, racc = acc_rot.next()
                            eng = 'vector' if part == 0 else 'gpsimd'
                            if part == 0:
                                P.op('scalar', lambda e, acc=acc, hs=hs, jj=jj: e.activation(
                                    out=acc[:], in_=hs[:, 0:TT], func=AF.Identity, scale=cw[:, 0, jj:jj + 1], bias=cb[:, jj:jj + 1]),
                                    [rhs_, rw], [racc])
                            else:
                                P.op(eng, lambda e, acc=acc, hs=hs, jj=jj: e.tensor_scalar(
                                    out=acc[:], in0=hs[:, 0:TT], scalar1=cw[:, 0, jj:jj + 1], scalar2=cb[:, jj:jj + 1],
                                    op0=ALU.mult, op1=ALU.add), [rhs_, rw], [racc])
                            for k in (1, 2):
                                P.op('vector', lambda e, acc=acc, hs=hs, jj=jj, k=k: e.scalar_tensor_tensor(
                                    out=acc[:], in0=hs[:, k:k + TT], scalar=cw[:, k, jj:jj + 1], in1=acc[:],
                                    op0=ALU.mult, op1=ALU.add), [rhs_, rw, racc], [racc])
                            accs.append((acc, racc))
                        if pend_gate is not None:
                            emit_gate(*pend_gate)
                        pend_gate = (accs, j, g, rg)
                    emit_gate(*pend_gate)
                    pend_gate = None
                    for i in range(NBT):
                        b = b0 + i
                        x32, rx32 = x32s[i]
                        z, rz = z_rot.next()
                        for half in range(2):
                            po, rpo = PB()
                            for j in range(NFC):
                                P.op('tensor', lambda e, j=j, half=half, po=po, g=g, i=i: e.matmul(
                                    po[:, :], lhsT=g[:, j, i * 128:(i + 1) * 128], rhs=wd[:, j, half * 512:(half + 1) * 512],
                                    start=(j == 0), stop=(j == NFC - 1)), [rg, rw], [rpo], inc=(j == NFC - 1))
                            P.op('vector', lambda e, half=half, po=po, z=z, x32=x32: e.scalar_tensor_tensor(
                                out=z[:, half * 512:(half + 1) * 512], in0=x32[:, half * 512:(half + 1) * 512], scalar=ALPHA,
                                in1=po[:, :], op0=ALU.mult, op1=ALU.add), [rpo, rx32], [rz])
                        ln_tail(b, z, rz, gam, bet, [rw], last, False)
                P.barrier()

        ph_hs = [sb(top, "hsb", [128, 130], F32)]
        rhs = Res()

        def identity_mixer_phase(layer):
            with ExitStack() as ph:
                gam, rg1 = load_bc(ph, "gam1", Wn['ln1_g'][layer:layer + 1, :])
                bet, rb1 = load_bc(ph, "bet1", Wn['ln1_b'][layer:layer + 1, :])
                rgb = Res()
                P.op('vector', lambda e: e.memset(ph_hs[0][:, 0:1], 0.0), [rg1, rb1, rhs], [rgb, rhs])
                x32_rot = Rot([sb(ph, f"mx32{i}", [128, D], F32) for i in range(2)])
                z_rot = Rot([sb(ph, f"mz{i}", [128, D], F32) for i in range(2)])
                for b in range(NB):
                    x32, rx32 = x32_rot.next()
                    P.dma('sync', x32[:], xsrc(layer)[b * 128:(b + 1) * 128, :], [rX32[b]], [rx32])
                    z, rz = z_rot.next()
                    P.op('vector', lambda e, z=z, x32=x32: e.tensor_scalar(out=z[:], in0=x32[:], scalar1=ALPHA, scalar2=None,
                                                                        op0=ALU.mult), [rx32], [rz])
                    ln_tail(b, z, rz, gam, bet, [rgb], False, True)
                P.barrier()


        def hgrn_phase(layer):
            j = layer // 2
            with ExitStack() as ph:
                win = sb(ph, "hwin", [128, 8, 4 * D], BF16)
                wout = sb(ph, "hwout", [128, 8, D], BF16)
                rw = Res()
                for c in range(8):
                    P.dma('gpsimd', win[:, c, :], Wn['b_w_in'][j, c * 128:(c + 1) * 128, :], [], [rw])
                P.dma('gpsimd', wout[:], Wn['b_w_out'][j].rearrange("(c p) n -> p c n", p=128), [], [rw])
                gam, rg1 = load_bc(ph, "hgam1", Wn['ln1_g'][layer:layer + 1, :])
                bet, rb1 = load_bc(ph, "hbet1", Wn['ln1_b'][layer:layer + 1, :])
                go, rgo = load_bc(ph, "hgo", Wn['b_g_o'][j:j + 1, :])
                cst = sb(ph, "hcst", [128, 3, 128], F32)
                cind = sb(ph, "hcind", [128, 4], F32)
                cmask = sb(ph, "hcmask", [128, 128], F32)
                cmask4 = sb(ph, "hcmask4", [128, 4, 128], F32)
                rc = Res()
                P.dma('sync', cst[:], hconst_in[:, 0:3, :], [], [rc])
                P.dma('sync', cmask[:], hconst_in[:, 3, :], [], [rc])
                for i4 in range(4):
                    P.dma('sync', cmask4[:, i4, :], hconst_in[:, 3, :], [], [rc])
                P.dma('sync', cind[:], hcind_in[:, :], [], [rc])
                lb = sb(ph, "hlb", [128, D], F32)
                oml = sb(ph, "homl", [128, D], F32)
                rl = Res()
                tmp = ExitStack()
                lg = [load_bc(tmp, f"hlg{l}", Wn['b_lb_logits'][l:l + 1, :]) for l in range(4)]
                mx = sb(tmp, "hmx", [128, D], F32)
                den = sb(tmp, "hden", [128, D], F32)
                P.op('vector', lambda e: e.tensor_tensor(out=mx[:], in0=lg[0][0][:], in1=lg[1][0][:], op=ALU.max),
                     [lg[0][1], lg[1][1]], [rl])
                for l in (2, 3):
                    P.op('vector', lambda e, l=l: e.tensor_tensor(out=mx[:], in0=mx[:], in1=lg[l][0][:], op=ALU.max),
                         [lg[l][1], rl], [rl])
                for l in range(4):
                    P.op('vector', lambda e, l=l: e.tensor_tensor(out=lg[l][0][:], in0=lg[l][0][:], in1=mx[:], op=ALU.subtract),
                         [rl, lg[l][1]], [lg[l][1]])
                    P.op('scalar', lambda e, l=l: e.activation(out=lg[l][0][:], in_=lg[l][0][:], func=AF.Exp),
                         [lg[l][1]], [lg[l][1]])
                P.op('vector', lambda e: e.tensor_tensor(out=den[:], in0=lg[0][0][:], in1=lg[1][0][:], op=ALU.add),
                     [lg[0][1], lg[1][1]], [rl])
                for l in (2, 3):
                    P.op('vector', lambda e, l=l: e.tensor_tensor(out=den[:], in0=den[:], in1=lg[l][0][:], op=ALU.add),
                         [lg[l][1], rl], [rl])
                P.op('vector', lambda e: e.tensor_copy(out=lb[:], in_=lg[1][0][:]), [lg[1][1], rl], [rl])
                for l in range(2, layer + 1):
                    P.op('vector', lambda e, l=l: e.tensor_tensor(out=lb[:], in0=lb[:], in1=lg[l][0][:], op=ALU.add),
                         [lg[l][1], rl], [rl])
                P.op('vector', lambda e: e.reciprocal(out=den[:], in_=den[:]), [rl], [rl])
                P.op('vector', lambda e: e.tensor_tensor(out=lb[:], in0=lb[:], in1=den[:], op=ALU.mult), [rl], [rl])
                P.op('vector', lambda e: e.tensor_scalar(out=oml[:], in0=lb[:], scalar1=-1.0, scalar2=1.0, op0=ALU.mult,
                                                         op1=ALU.add), [rl], [rl])
                P.barrier()
                tmp.close()
                Sf = sb(ph, "hSf", [128, 8, 128], F32)
                SbA = sb(ph, "hSbA", [128, 8, 128], BF16)
                SbB = sb(ph, "hSbB", [128, 8, 128], BF16)
                q0T = sb(ph, "hq0T", [128, 8, 128], BF16)
                q1T = sb(ph, "hq1T", [128, 8, 128], BF16)
                kT = sb(ph, "hkT", [128, 8, 128], BF16)
                qT = sb(ph, "hqT", [128, 8, 128], BF16)
                rS = Res()
                rq = Res()
                P.op('vector', lambda e: e.memset(Sf[:], 0.0), [], [rS])
                P.op('gpsimd', lambda e: e.memset(q0T[:], 0.0), [], [rq])
                P.op('gpsimd', lambda e: e.memset(q1T[:], 0.0), [], [rq])
                xh_rot = Rot([sb(ph, f"hxh{i}", [128, 8, 130], BF16) for i in range(2)])
                x32_rot = Rot([sb(ph, f"hx32{i}", [128, D], F32) for i in range(2)])
                F = lambda n: sb(ph, n, [128, D], F32)
                sig, lf, key, bb, bl, qs, og = F("hsig"), F("hlf"), F("hkey"), F("hbb"), F("hbl"), F("hqs"), F("hog")
                zt_ = qs
                gateT = F("hgate")
                rqs, rvv, rgate = Res(), Res(), Res()
                B16 = lambda n: sb(ph, n, [128, D], BF16)
                qtl, ktl, k0, k1, vv, onb = B16("hqtl"), B16("hktl"), B16("hk0"), B16("hk1"), B16("hvv"), B16("honb")
                onT = sb(ph, "honT", [128, 8, 128], BF16)
                scT = sb(ph, "hscT", [128, 8, 128], BF16)
                rscT = Res()
                rog = Res()
                rSb = {id(SbA): Res(), id(SbB): Res()}
                P.op('vector', lambda e: e.memset(SbA[:], 0.0), [], [rSb[id(SbA)]])
                dec = sb(ph, "hdec", [128, 8, 2], F32)
                ss = sb(ph, "hss", [128, 8], F32)
                rt = Res()

                def proj(part, xh, rxh):
                    outs = []
                    for half in range(2):
                        ps, rps = PB()
                        col = part * D + half * 512
                        for c in range(8):
                            P.op('tensor', lambda e, c=c, col=col, ps=ps: e.matmul(
                                ps[:, :], lhsT=xh[:, c, 2:130], rhs=win[:, c, col:col + 512],
                                start=(c == 0), stop=(c == 7)), [rw, rxh], [rps], inc=(c == 7))
                        outs.append((ps, rps))
                    return outs

                for b in range(NB):
                    xh, rxh = xh_rot.next()
                    P.dma('sync', xh[:, :, 2:130], XTH[b, :, :, 2:130], [rXTH[b]], [rxh])
                    x32, rx32 = x32_rot.next()
                    P.dma('sync', x32[:], xsrc(layer)[b * 128:(b + 1) * 128, :], [rX32[b]], [rx32])
                    for half, (ps, rps) in enumerate(proj(1, xh, rxh)):
                        hs = slice(half * 512, (half + 1) * 512)
                        P.op('scalar', lambda e, ps=ps, hs=hs: e.activation(out=sig[:, hs], in_=ps[:, :], func=AF.Sigmoid),
                             [rps], [rt])
                    for half, (ps, rps) in enumerate(proj(0, xh, rxh)):
                        hs = slice(half * 512, (half + 1) * 512)
                        P.op('scalar', lambda e, ps=ps, hs=hs: e.activation(out=qs[:, hs], in_=ps[:, :], func=AF.Silu),
                             [rps], [rqs])
                    for half, (ps, rps) in enumerate(proj(2, xh, rxh)):
                        hs = slice(half * 512, (half + 1) * 512)
                        P.op('vector', lambda e, ps=ps, hs=hs: e.tensor_copy(out=vv[:, hs], in_=ps[:, :]), [rps], [rvv])
                    for half, (ps, rps) in enumerate(proj(3, xh, rxh)):
                        hs = slice(half * 512, (half + 1) * 512)
                        P.op('scalar', lambda e, ps=ps, hs=hs: e.activation(out=gateT[:, hs], in_=ps[:, :], func=AF.Sigmoid),
                             [rps], [rgate])
                    P.op('vector', lambda e: e.tensor_tensor(out=lf[:], in0=sig[:], in1=oml[:], op=ALU.mult), [rt, rl], [rt])
                    P.op('vector', lambda e: e.tensor_tensor(out=key[:], in0=oml[:], in1=lf[:], op=ALU.subtract), [rt, rl], [rt])
                    P.op('vector', lambda e: e.tensor_tensor(out=lf[:], in0=lf[:], in1=lb[:], op=ALU.add), [rt, rl], [rt])
                    P.op('scalar', lambda e: e.activation(out=lf[:], in_=lf[:], func=AF.Ln), [rt], [rt])
                    for half in range(2):
                        hs = slice(half * 512, (half + 1) * 512)
                        ps, rps = PB()
                        P.op('tensor', lambda e, ps=ps, hs=hs: e.matmul(ps[:, :], lhsT=cst[:, 0, :], rhs=lf[:, hs],
                                                                       start=True, stop=True), [rt, rc], [rps])
                        P.op('vector', lambda e, ps=ps, hs=hs: e.tensor_copy(out=bb[:, hs], in_=ps[:, :]), [rps], [rt])
                        ps, rps = PB()
                        P.op('tensor', lambda e, ps=ps, hs=hs: e.matmul(ps[:, :], lhsT=cst[:, 1, :], rhs=lf[:, hs],
                                                                       start=True, stop=True), [rt, rc], [rps])
                        P.op('vector', lambda e, ps=ps, hs=hs: e.tensor_copy(out=bl[:, hs], in_=ps[:, :]), [rps], [rt])
                    ps, rps = PB()
                    for h in range(8):
                        P.op('tensor', lambda e, h=h, ps=ps: e.matmul(ps[:, h * 2:h * 2 + 2], lhsT=lf[:, h * 128:(h + 1) * 128],
                                                                     rhs=cind[:, 0:2], start=True, stop=True), [rt, rc], [rps])
                    P.op('scalar', lambda e, ps=ps: e.activation(out=dec[:].rearrange("p h c -> p (h c)"), in_=ps[:, 0:16],
                                                                 func=AF.Exp), [rps], [rt])
                    P.op('vector', lambda e: e.tensor_tensor(out=bl[:], in0=bl[:], in1=bb[:], op=ALU.subtract), [rt], [rt])
                    P.op('scalar', lambda e: e.activation(out=bl[:], in_=bl[:], func=AF.Exp), [rt], [rt])
                    P.op('vector', lambda e: e.tensor_tensor(out=bl[:], in0=bl[:], in1=key[:], op=ALU.mult), [rt], [rt])
                    P.op('vector', lambda e: e.tensor_scalar(out=k0[:], in0=bl[:], scalar1=cind[:, 0:1], scalar2=None,
                                                             op0=ALU.mult), [rt, rc], [rt])
                    P.op('vector', lambda e: e.tensor_scalar(out=k1[:], in0=bl[:], scalar1=cind[:, 1:2], scalar2=None,
                                                             op0=ALU.mult), [rt, rc], [rt])
                    P.op('scalar', lambda e: e.activation(out=sig[:], in_=bb[:], func=AF.Exp, scale=-1.0), [rt], [rt])
                    P.op('vector', lambda e: e.tensor_tensor(out=ktl[:], in0=sig[:], in1=key[:], op=ALU.mult), [rt], [rt])
                    P.op('scalar', lambda e: e.activation(out=bb[:], in_=bb[:], func=AF.Exp), [rt], [rt])
                    P.op('vector', lambda e: e.tensor_tensor(out=qtl[:], in0=qs[:], in1=bb[:], op=ALU.mult), [rt, rqs], [rt])
                    for (src, dsts) in ((qtl, 'q'), (ktl, 'k')):
                        pt, rpt = PT()
                        for h in range(8):
                            P.op('tensor', lambda e, h=h, pt=pt, src=src: e.transpose(
                                out=pt[:, h * 128:(h + 1) * 128], in_=src[:, h * 128:(h + 1) * 128], identity=ident[:]),
                                [rt, rid], [rpt])
                        ptv = pt[:].rearrange("p (h t) -> p h t", h=8)
                        if dsts == 'q':
                            P.op('vector', lambda e, ptv=ptv: e.tensor_copy(out=qT[:], in_=ptv), [rpt], [rq])
                            P.op('vector', lambda e, ptv=ptv: e.tensor_copy(out=q0T[:, :, 0:64], in_=ptv[:, :, 0:64]), [rpt], [rq])
                            P.op('vector', lambda e, ptv=ptv: e.tensor_copy(out=q1T[:, :, 64:128], in_=ptv[:, :, 64:128]), [rpt], [rq])
                        else:
                            P.op('vector', lambda e, ptv=ptv: e.tensor_copy(out=kT[:], in_=ptv), [rpt], [rq])
                    pu = [pbank[0], pbank[1]]
                    psc = [pbank[2], pbank[3]]
                    po = [pbank[4], pbank[5]]
                    Sfv = Sf[:].rearrange("p h e -> p (h e)")

                    def state_update(kk, ci, Sb):
                        for h in range(8):
                            hsl = slice(h * 128, (h + 1) * 128)
                            pb_, rpb = pu[h // 4]
                            P.op('tensor', lambda e, pb_=pb_, h=h, hsl=hsl: e.matmul(
                                pb_[:, (h % 4) * 128:(h % 4 + 1) * 128], lhsT=kk[:, hsl], rhs=vv[:, hsl], start=True, stop=True),
                                [rt, rvv], [rpb])
                        P.op('vector', lambda e: e.tensor_tensor(out=Sf[:], in0=Sf[:], in1=dec[:, :, ci:ci + 1].to_broadcast([128, 8, 128]),
                                                                 op=ALU.mult), [rt, rS], [rS])
                        for hh in range(2):
                            pb_, rpb = pu[hh]
                            P.op('vector', lambda e, pb_=pb_, hh=hh: e.tensor_tensor(
                                out=Sfv[:, hh * 512:(hh + 1) * 512], in0=Sfv[:, hh * 512:(hh + 1) * 512], in1=pb_[:, :], op=ALU.add),
                                [rpb, rS], [rS])
                        P.op('scalar', lambda e: e.copy(out=Sb[:], in_=Sf[:]), [rS], [rSb[id(Sb)]])

                    state_update(k0, 0, SbB)
                    for h in range(8):
                        pb_, rpb = psc[h // 4]
                        P.op('tensor', lambda e, pb_=pb_, h=h: e.matmul(pb_[:, (h % 4) * 128:(h % 4 + 1) * 128], lhsT=kT[:, h, :],
                                                                       rhs=qT[:, h, :], start=True, stop=True), [rq], [rpb])
                    for hh in range(2):
                        pb_, rpb = psc[hh]
                        P.op('vector', lambda e, pb_=pb_, hh=hh: e.tensor_tensor(
                            out=scT[:, hh * 4:(hh + 1) * 4, :], in0=pb_[:, :].rearrange("p (h t) -> p h t", h=4), in1=cmask4[:],
                            op=ALU.mult), [rpb, rc], [rscT])
                    for h in range(8):
                        hsl = slice(h * 128, (h + 1) * 128)
                        pb_, rpb = po[h // 4]
                        osl = slice((h % 4) * 128, (h % 4 + 1) * 128)
                        P.op('tensor', lambda e, pb_=pb_, h=h, hsl=hsl, osl=osl: e.matmul(
                            pb_[:, osl], lhsT=scT[:, h, :], rhs=vv[:, hsl], start=True, stop=False), [rscT, rt, rvv], [rpb])
                        P.op('tensor', lambda e, pb_=pb_, h=h, osl=osl: e.matmul(
                            pb_[:, osl], lhsT=q0T[:, h, :], rhs=SbA[:, h, :], start=False, stop=False), [rq, rSb[id(SbA)]], [rpb])
                        P.op('tensor', lambda e, pb_=pb_, h=h, osl=osl: e.matmul(
                            pb_[:, osl], lhsT=q1T[:, h, :], rhs=SbB[:, h, :], start=False, stop=True), [rq, rSb[id(SbB)]], [rpb])
                    for hh in range(2):
                        pb_, rpb = po[hh]
                        P.op('vector', lambda e, pb_=pb_, hh=hh: e.tensor_tensor(
                            out=og[:, hh * 512:(hh + 1) * 512], in0=pb_[:, :], in1=gateT[:, hh * 512:(hh + 1) * 512], op=ALU.mult),
                            [rpb, rgate], [rog])
                    state_update(k1, 1, SbA)
                    P.op('scalar', lambda e: e.activation(out=qs[:], in_=og[:], func=AF.Square), [rt, rog, rqs], [rt, rqs])
                    P.op('vector', lambda e: e.tensor_reduce(out=ss[:], in_=qs[:].rearrange("p (h e) -> p h e", h=8),
                                                             axis=AX.X, op=ALU.add), [rt], [rt])
                    rsqrt(ss[:], ss[:], 1.0 / 128, 1, [rt], [rt])
                    for h in range(8):
                        hsl = slice(h * 128, (h + 1) * 128)
                        P.op('vector', lambda e, h=h, hsl=hsl: e.scalar_tensor_tensor(
                            out=onb[:, hsl], in0=og[:, hsl], scalar=ss[:, h:h + 1], in1=go[:, hsl], op0=ALU.mult, op1=ALU.mult),
                            [rt, rgo, rog], [rt])
                    pt, rpt = PT()
                    for c in range(8):
                        P.op('tensor', lambda e, c=c, pt=pt: e.transpose(out=pt[:, c * 128:(c + 1) * 128],
                                                                        in_=onb[:, c * 128:(c + 1) * 128], identity=ident[:]),
                             [rt, rid], [rpt])
                    P.op('vector', lambda e, pt=pt: e.tensor_copy(out=onT[:].rearrange("p c t -> p (c t)"), in_=pt[:]), [rpt], [rt])
                    for half in range(2):
                        po, rpo = PB()
                        for c in range(8):
                            P.op('tensor', lambda e, c=c, half=half, po=po: e.matmul(
                                po[:, :], lhsT=onT[:, c, :], rhs=wout[:, c, half * 512:(half + 1) * 512],
                                start=(c == 0), stop=(c == 7)), [rt, rw], [rpo], inc=(c == 7))
                        P.op('vector', lambda e, half=half, po=po, x32=x32: e.scalar_tensor_tensor(
                            out=zt_[:, half * 512:(half + 1) * 512], in0=x32[:, half * 512:(half + 1) * 512], scalar=ALPHA,
                            in1=po[:, :], op0=ALU.mult, op1=ALU.add), [rpo, rx32, rqs], [rt, rqs])
                    ln_tail(b, zt_, rt, gam, bet, [rg1, rb1], False, False)
                P.barrier()


        NIT = 18

        def dsa_phase(layer):
            j = layer // 2
            with ExitStack() as ph:
                win = sb(ph, "awin", [128, 8, AIN], BF16)
                wql = sb(ph, "awql", [128, 3, AH * KVR], BF16)
                wqi = sb(ph, "awqi", [128, 3, IDXH * IDXD], BF16)
                wuv = sb(ph, "awuv", [128, 2 * AH, 128], BF16)
                wout = sb(ph, "awout", [128, 8, D], BF16)
                rw = Res()
                P.dma('gpsimd', win[:], Wn['a_w_in'][j].rearrange("(c p) n -> p c n", p=128), [], [rw])
                P.dma('gpsimd', wql[:], Wn['a_w_q_lat'][j].rearrange("(c p) n -> p c n", p=128), [], [rw])
                P.dma('gpsimd', wqi[:], Wn['a_w_q_idx'][j].rearrange("(c p) n -> p c n", p=128), [], [rw])
                P.dma('gpsimd', wuv[:], Wn['a_w_uv'][j].rearrange("h (rc p) d -> p (h rc) d", p=128), [], [rw])
                P.dma('gpsimd', wout[:], Wn['a_w_out'][j].rearrange("(c p) n -> p c n", p=128), [], [rw])
                gam, rg1 = load_bc(ph, "agam1", Wn['ln1_g'][layer:layer + 1, :])
                bet, rb1 = load_bc(ph, "abet1", Wn['ln1_b'][layer:layer + 1, :])
                gq, rgq = load_bc(ph, "agq", Wn['a_g_q'][j:j + 1, :])
                gkv, rgkv = load_bc(ph, "agkv", Wn['a_g_kv'][j:j + 1, :])
                gki, rgki = load_bc(ph, "agki", Wn['a_g_kidx'][j:j + 1, :])
                bki, rbki = load_bc(ph, "abki", Wn['a_b_kidx'][j:j + 1, :])
                maskall = sb(ph, "amaskall", [128, 8, 128], F32)
                rep = sb(ph, "arep", [8, 128], BF16)
                cm = sb(ph, "acm", [128, 128], F32)
                pw = sb(ph, "apw", [128, NIT], F32)
                rc = Res()
                P.dma('sync', maskall[:], maskall_in[:, :, :], [], [rc])
                P.dma('sync', rep[:], rep_in[:, :], [], [rc])
                P.dma('sync', cm[:], cm_in[:, :], [], [rc])
                P.dma('sync', pw[:], pw_in[:, 0:NIT], [], [rc])
                CKV = sb(ph, "aCKV", [128, NB, KVR + 1], BF16)
                CKVT = sb(ph, "aCKVT", [128, 2, S], BF16)
                KIT = sb(ph, "aKIT", [64, S], BF16)
                rK = Res()
                P.op('vector', lambda e: e.memset(CKV[:, :, KVR:KVR + 1], 1.0), [], [rK])
                score = sb(ph, "ascore", [128, S], F32)
                msk = sb(ph, "amsk", [128, S], BF16)
                junk = msk
                zA = sb(ph, "azA", [128, 328], F32)
                mskT_rot = Rot([sb(ph, f"amskT{i}", [128, NB, 128], BF16) for i in range(2)])
                x32_rot = Rot([sb(ph, f"ax32{i}", [128, D], F32) for i in range(1)])
                xh_rot = Rot([sb(ph, f"axh{i}", [128, 8, 130], BF16) for i in range(2)])
                sqa = sb(ph, "asqa", [128, QR], F32)
                cq = sb(ph, "acq", [128, QR], BF16)
                kix = sb(ph, "akix", [128, IDXD], F32)
                kib = sb(ph, "akib", [128, IDXD], BF16)
                wq = sb(ph, "awq", [128, 8], BF16)
                st = sb(ph, "ast", [128, 16], F32)
                cqT_rot = Rot([sb(ph, f"acqT{i}", [128, 3, 128], BF16) for i in range(2)])
                wT = sb(ph, "awT", [8, 128], BF16)
                QIT = sb(ph, "aQIT", [64, 8, 8, 16], BF16)
                Wsel = sb(ph, "aWsel", [128, 8, 128], BF16)
                R_rot = Rot([sb(ph, f"aR{i}", [128, 512], BF16) for i in range(4)])
                mm = sb(ph, "amm", [128, 2, 8], F32)
                bs = sb(ph, "abs", [128, 8], F32)
                Wt = sb(ph, "aWt", [128, NIT], F32)
                qlT4 = sb(ph, "aqlT4", [128, 2, 512], BF16)
                PT_rot = Rot([sb(ph, f"aPT{i}", [128, 4, 128], BF16) for i in range(5)])
                olb = sb(ph, "aolb", [128, AH, KVR], BF16)
                olT = sb(ph, "aolT8", [128, AH, 2, 128], BF16)
                dcol = sb(ph, "adcol", [128, AH], F32)
                rd = sb(ph, "ard8", [128, AH], F32)
                rolb, rolT, rrd = Res(), Res(), Res()
                oh = sb(ph, "aoh", [128, D], BF16)
                ohT = sb(ph, "aohT", [128, 8, 128], BF16)
                rden = sb(ph, "arden", [128, 1], F32)
                z = sb(ph, "az", [128, D], F32)
                rt = Res()
                r_z, r_q, r_cq, r_kv, r_ki, r_w, r_cqT, r_wT, r_qit, r_wsel, r_mm, r_bs = [Res() for _ in range(12)]
                sqa_kv = sb(ph, "asqakv", [128, KVR], F32)
                sqa_ki = sb(ph, "asqaki", [128, IDXD], F32)
                rsc = Res()
                rmk = Res()
                rql = Res()
                ro = Res()

                def rmsn(ps, rps0, c0, n, gtile, rg, out_ap, idx):
                    ss, rs = st[:, idx:idx + 1], st[:, idx + 1:idx + 2]
                    P.op('scalar', lambda e: e.activation(out=sqa[:, 0:n], in_=ps[:, c0:c0 + n], func=AF.Square), [rps0], [r_q])
                    P.op('vector', lambda e: e.tensor_scalar(out=sqa[:, 0:n], in0=sqa[:, 0:n], scalar1=1.0, scalar2=None,
                                                             op0=ALU.mult, op1=ALU.add, accum_out=ss), [r_q], [r_q])
                    rsqrt(rs, ss, 1.0 / n, 1, [r_q], [r_q])
                    P.op('vector', lambda e: e.scalar_tensor_tensor(out=out_ap, in0=ps[:, c0:c0 + n], scalar=rs, in1=gtile[:, 0:n],
                                                                    op0=ALU.mult, op1=ALU.mult), [rps0, r_q, rg], [r_cq])

                blk = {}
                mts = {}
                r_zo = Res()

                def SB(b):
                    cqT, r_cqT = cqT_rot.next()
                    mskT, rmkT = mskT_rot.next()
                    L = (b + 1) * 128
                    xh, rxh = xh_rot.next()
                    P.dma('sync', xh[:, :, 2:130], XTH[b, :, :, 2:130], [rXTH[b]], [rxh])
                    ps0, rps0 = PB()
                    ps1, rps1 = PB()
                    for c in range(8):
                        P.op('tensor', lambda e, c=c, ps0=ps0: e.matmul(ps0[:, :], lhsT=xh[:, c, 2:130], rhs=win[:, c, 0:512],
                                                                       start=(c == 0), stop=(c == 7)), [rw, rxh], [rps0], inc=(c == 7))
                    for c in range(8):
                        P.op('tensor', lambda e, c=c, ps1=ps1: e.matmul(ps1[:, 0:AIN - 512], lhsT=xh[:, c, 2:130], rhs=win[:, c, 512:AIN],
                                                                       start=(c == 0), stop=(c == 7)), [rw, rxh], [rps1], inc=(c == 7))
                    rmsn(ps0, rps0, 0, QR, gq, rgq, cq[:], 0)
                    P.op('scalar', lambda e, ps0=ps0: e.copy(out=zA[:, 0:128], in_=ps0[:, QR:512]), [rps0], [r_z])
                    P.op('scalar', lambda e, ps1=ps1: e.copy(out=zA[:, 128:128 + 200], in_=ps1[:, 0:200]), [rps1], [r_z])
                    ss, rs = st[:, 2:3], st[:, 3:4]
                    P.op('scalar', lambda e: e.activation(out=sqa_kv[:, 0:KVR], in_=zA[:, 0:KVR], func=AF.Square), [r_z], [r_kv])
                    P.op('vector', lambda e: e.tensor_scalar(out=sqa_kv[:, 0:KVR], in0=sqa_kv[:, 0:KVR], scalar1=1.0, scalar2=None,
                                                             op0=ALU.mult, op1=ALU.add, accum_out=ss), [r_kv], [r_kv])
                    rsqrt(rs, ss, 1.0 / KVR, 1, [r_kv], [r_kv])
                    P.op('vector', lambda e, b=b: e.scalar_tensor_tensor(out=CKV[:, b, 0:KVR], in0=zA[:, 0:KVR], scalar=rs, in1=gkv[:, :],
                                                                         op0=ALU.mult, op1=ALU.mult), [r_kv, r_z, rgkv], [rK])
                    s1, s2, mean, msq, var, rstd, nmr = [st[:, 4 + i:5 + i] for i in range(7)]
                    ki = zA[:, 256:320]
                    P.op('vector', lambda e: e.tensor_scalar(out=kix[:], in0=ki, scalar1=1.0, scalar2=None, op0=ALU.mult,
                                                             op1=ALU.add, accum_out=s1), [r_z], [r_ki])
                    P.op('scalar', lambda e: e.activation(out=sqa_ki[:, 0:IDXD], in_=ki, func=AF.Square), [r_z], [r_ki])
                    P.op('vector', lambda e: e.tensor_scalar(out=sqa_ki[:, 0:IDXD], in0=sqa_ki[:, 0:IDXD], scalar1=1.0, scalar2=None,
                                                             op0=ALU.mult, op1=ALU.add, accum_out=s2), [r_ki], [r_ki])
                    P.op('vector', lambda e: e.tensor_scalar(out=mean, in0=s1, scalar1=1.0 / IDXD, scalar2=None, op0=ALU.mult), [r_ki], [r_ki])
                    P.op('vector', lambda e: e.tensor_tensor(out=msq, in0=mean, in1=mean, op=ALU.mult), [r_ki], [r_ki])
                    P.op('vector', lambda e: e.scalar_tensor_tensor(out=var, in0=s2, scalar=1.0 / IDXD, in1=msq, op0=ALU.mult,
                                                                    op1=ALU.subtract), [r_ki], [r_ki])
                    rsqrt(rstd, var, 1.0, 0, [r_ki], [r_ki])
                    P.op('vector', lambda e: e.scalar_tensor_tensor(out=nmr, in0=mean, scalar=-1.0, in1=rstd, op0=ALU.mult,
                                                                    op1=ALU.mult), [r_ki], [r_ki])
                    P.op('scalar', lambda e: e.activation(out=kix[:], in_=ki, func=AF.Identity, scale=rstd, bias=nmr), [r_ki, r_z], [r_ki])
                    P.op('vector', lambda e: e.tensor_tensor(out=kix[:], in0=kix[:], in1=gki[:], op=ALU.mult), [r_ki, rgki], [r_ki])
                    P.op('vector', lambda e: e.tensor_tensor(out=kib[:], in0=kix[:], in1=bki[:], op=ALU.add), [r_ki, rbki], [r_ki])
                    P.op('vector', lambda e: e.tensor_scalar(out=wq[:], in0=zA[:, 320:328], scalar1=float((IDXH * IDXD) ** -0.5),
                                                             scalar2=None, op0=ALU.mult), [r_z], [r_w])
                    pt, rpt = PT()
                    for c in range(3):
                        P.op('tensor', lambda e, c=c, pt=pt: e.transpose(out=pt[:, c * 128:(c + 1) * 128], in_=cq[:, c * 128:(c + 1) * 128],
                                                                        identity=ident[:]), [r_cq, rid], [rpt])
                    for c in range(2):
                        P.op('tensor', lambda e, c=c, pt=pt, b=b: e.transpose(out=pt[:, (3 + c) * 128:(4 + c) * 128],
                                                                             in_=CKV[:, b, c * 128:(c + 1) * 128], identity=ident[:]),
                             [rK, rid], [rpt])
                    P.op('tensor', lambda e, pt=pt: e.transpose(out=pt[0:64, 5 * 128:6 * 128], in_=kib[:, :], identity=ident[:]),
                         [r_ki, rid], [rpt])
                    P.op('tensor', lambda e, pt=pt: e.transpose(out=pt[0:8, 6 * 128:7 * 128], in_=wq[:, :], identity=ident[:]),
                         [r_w, rid], [rpt])
                    P.op('vector', lambda e, pt=pt: e.tensor_copy(out=cqT[:].rearrange("p c t -> p (c t)"), in_=pt[:, 0:384]), [rpt], [r_cqT])
                    for c in range(2):
                        P.op('vector', lambda e, pt=pt, c=c, b=b: e.tensor_copy(out=CKVT[:, c, b * 128:(b + 1) * 128],
                                                                               in_=pt[:, (3 + c) * 128:(4 + c) * 128]), [rpt], [rK])
                    P.op('vector', lambda e, pt=pt, b=b: e.tensor_copy(out=KIT[:, b * 128:(b + 1) * 128], in_=pt[0:64, 640:768]), [rpt], [rK])
                    P.op('vector', lambda e, pt=pt: e.tensor_copy(out=wT[:], in_=pt[0:8, 768:896]), [rpt], [r_wT])
                    for hh in range(2):
                        pq, rpq = PB()
                        for h4 in range(4):
                            h = hh * 4 + h4
                            for kc in range(3):
                                P.op('tensor', lambda e, pq=pq, h=h, h4=h4, kc=kc: e.matmul(
                                    pq[0:64, h4 * 128:(h4 + 1) * 128], lhsT=wqi[:, kc, h * 64:(h + 1) * 64], rhs=cqT[:, kc, :],
                                    start=(kc == 0), stop=(kc == 2)), [rw, r_cqT], [rpq], inc=(kc == 2))
                        P.op('scalar', lambda e, pq=pq, hh=hh: e.copy(
                            out=QIT[:, :, hh * 4:(hh + 1) * 4, :].rearrange("p g h t -> p h g t"),
                            in_=pq[0:64, :].rearrange("p (h g t) -> p h g t", h=4, g=8)), [rpq], [r_qit])
                    pe_, rpe = PB()
                    P.op('tensor', lambda e, pe_=pe_: e.matmul(pe_[:, 0:128], lhsT=rep[:, :], rhs=wT[:, :], start=True, stop=True),
                         [rc, r_wT], [rpe])
                    for g in range(8):
                        P.op('vector', lambda e, g=g, pe_=pe_: e.tensor_tensor(out=Wsel[:, g, :], in0=pe_[:, 0:128], in1=maskall[:, g, :],
                                                                              op=ALU.mult), [rpe, rc], [r_wsel])
                    nkc = (L + 511) // 512
                    for kc in range(nkc):
                        k0_ = kc * 512
                        n = min(512, L - k0_)
                        psc, rpsc = pbank[4]
                        pend = []

                        def emit_mm1(g, k0_=k0_, n=n):
                            p1, rp1 = PB()
                            P.op('tensor', lambda e, p1=p1, g=g: e.matmul(
                                p1[:, 0:n], lhsT=QIT[:, g, :, :].rearrange("p h t -> p (h t)"), rhs=KIT[:, k0_:k0_ + n], start=True, stop=True),
                                [r_qit, rK], [rp1])
                            R, rR = R_rot.next()
                            if g % 2 == 0:
                                P.op('scalar', lambda e, p1=p1, R=R: e.activation(out=R[:, 0:n], in_=p1[:, 0:n], func=AF.Relu),
                                     [rp1], [rR])
                            else:
                                P.op('vector', lambda e, p1=p1, R=R: e.tensor_scalar(out=R[:, 0:n], in0=p1[:, 0:n], scalar1=0.0,
                                                                                    scalar2=None, op0=ALU.max), [rp1], [rR])
                            pend.append((R, rR))

                        emit_mm1(0)
                        emit_mm1(1)
                        emit_mm1(2)
                        for g in range(8):
                            if g + 3 < 8:
                                emit_mm1(g + 3)
                            R, rR = pend.pop(0)
                            P.op('tensor', lambda e, psc=psc, g=g, R=R, n=n: e.matmul(
                                psc[:, 0:n], lhsT=Wsel[:, g, :], rhs=R[:, 0:n], start=(g == 0), stop=(g == 7)), [r_wsel, rR], [rpsc])
                        P.op('scalar', lambda e, psc=psc, k0_=k0_, n=n: e.copy(out=score[:, k0_:k0_ + n], in_=psc[:, 0:n]), [rpsc], [rsc])
                        P.op('vector', lambda e, psc=psc, kc=kc, n=n: e.tensor_reduce(out=mm[:, 0, kc:kc + 1], in_=psc[:, 0:n], axis=AX.X,
                                                                                     op=ALU.min), [rpsc], [r_mm])
                        P.op('vector', lambda e, psc=psc, kc=kc, n=n: e.tensor_reduce(out=mm[:, 1, kc:kc + 1], in_=psc[:, 0:n], axis=AX.X,
                                                                                     op=ALU.max), [rpsc], [r_mm])
                    P.op('vector', lambda e, L=L: e.tensor_tensor(out=score[:, L - 128:L], in0=score[:, L - 128:L], in1=cm[:], op=ALU.add),
                         [rsc, rc], [rsc])
                    lo, hi, w0, mid, cnt, stp = [bs[:, i:i + 1] for i in range(6)]
                    P.op('vector', lambda e, nkc=nkc: e.tensor_reduce(out=lo, in_=mm[:, 0, 0:nkc], axis=AX.X, op=ALU.min), [r_mm], [r_bs])
                    P.op('vector', lambda e, nkc=nkc: e.tensor_reduce(out=hi, in_=mm[:, 1, 0:nkc], axis=AX.X, op=ALU.max), [r_mm], [r_bs])
                    P.op('vector', lambda e: e.tensor_tensor(out=w0, in0=hi, in1=lo, op=ALU.subtract), [r_bs], [r_bs])
                    P.op('vector', lambda e: e.tensor_scalar(out=w0, in0=w0, scalar1=1.0001, scalar2=1e-6, op0=ALU.mult, op1=ALU.add), [r_bs], [r_bs])
                    P.op('vector', lambda e: e.tensor_scalar(out=Wt[:], in0=pw[:], scalar1=w0, scalar2=None, op0=ALU.mult), [r_bs, rc], [r_bs])
                    if L > TOPK:
                        P.op('vector', lambda e: e.tensor_tensor(out=mid, in0=lo, in1=Wt[:, 0:1], op=ALU.add), [r_bs], [r_bs])
                        for k in range(NIT):
                            P.op('vector', lambda e, L=L: e.tensor_scalar(out=junk[:, 0:L], in0=score[:, 0:L], scalar1=mid, scalar2=None,
                                                                         op0=ALU.is_ge, op1=ALU.add, accum_out=cnt), [r_bs, rsc], [r_bs, rmk])
                            P.op('vector', lambda e, k=k: e.scalar_tensor_tensor(out=stp, in0=cnt, scalar=float(TOPK) - 0.5, in1=Wt[:, k:k + 1],
                                                                                op0=ALU.is_ge, op1=ALU.mult), [r_bs], [r_bs])
                            if k < NIT - 1:
                                P.op('vector', lambda e, k=k: e.scalar_tensor_tensor(out=mid, in0=mid, scalar=Wt[:, k + 1:k + 2], in1=stp,
                                                                                    op0=ALU.subtract, op1=ALU.add), [r_bs], [r_bs])
                            else:
                                P.op('vector', lambda e, k=k: e.scalar_tensor_tensor(out=lo, in0=mid, scalar=Wt[:, k:k + 1], in1=stp,
                                                                                    op0=ALU.subtract, op1=ALU.add), [r_bs], [r_bs])
                    P.op('vector', lambda e, L=L: e.tensor_scalar(out=msk[:, 0:L], in0=score[:, 0:L], scalar1=lo, scalar2=None,
                                                                 op0=ALU.is_ge), [r_bs, rsc], [rmk])
                    def MT(b=b, mskT=mskT, rmkT=rmkT):
                        for kb0 in range(0, b + 1, 8):
                            nk = min(8, b + 1 - kb0)
                            pt, rpt = PT()
                            for i in range(nk):
                                P.op('tensor', lambda e, pt=pt, i=i, kb0=kb0: e.transpose(
                                    out=pt[:, i * 128:(i + 1) * 128], in_=msk[:, (kb0 + i) * 128:(kb0 + i + 1) * 128], identity=ident[:]),
                                    [rmk, rid], [rpt])
                            P.op('vector', lambda e, pt=pt, nk=nk, kb0=kb0: e.tensor_copy(
                                out=mskT[:, kb0:kb0 + nk, :].rearrange("p k t -> p (k t)"), in_=pt[:, 0:nk * 128]), [rpt], [rmkT])
                    blk[b] = (cqT, r_cqT, mskT, rmkT)
                    mts[b] = MT

                def ATT(b):
                    cqT, r_cqT, mskT, rmkT = blk.pop(b)
                    x32, rx32 = x32_rot.next()
                    P.dma('sync', x32[:], xsrc(layer)[b * 128:(b + 1) * 128, :], [rX32[b]], [rx32])
                    qk_i = [0]

                    def QKB():
                        r = pbank[4 + qk_i[0]]
                        qk_i[0] ^= 1
                        return r

                    for hg in range(2):
                        for rcx in range(2):
                            pq, rpq = QKB()
                            for h4 in range(4):
                                ch = (hg * 4 + h4) * 2 + rcx
                                for kc in range(3):
                                    P.op('tensor', lambda e, pq=pq, h4=h4, ch=ch, kc=kc: e.matmul(
                                        pq[:, h4 * 128:(h4 + 1) * 128], lhsT=wql[:, kc, ch * 128:(ch + 1) * 128], rhs=cqT[:, kc, :],
                                        start=(kc == 0), stop=(kc == 2)), [rw, r_cqT], [rpq], inc=(kc == 2))
                            P.op('scalar', lambda e, pq=pq, rcx=rcx: e.activation(out=qlT4[:, rcx, :], in_=pq[:, :], func=AF.Copy,
                                                                                scale=float(KVR ** -0.5)), [rpq], [rql])
                        pendq = []

                        def emit_qk(kb):
                            pst, rpst = QKB()
                            for rcx in range(2):
                                P.op('tensor', lambda e, pst=pst, rcx=rcx, kb=kb: e.matmul(
                                    pst[:, :], lhsT=CKVT[:, rcx, kb * 128:(kb + 1) * 128], rhs=qlT4[:, rcx, :],
                                    start=(rcx == 0), stop=(rcx == 1)), [rK, rql], [rpst], inc=(rcx == 1))
                            PTt, rPT = PT_rot.next()
                            P.op('scalar', lambda e, pst=pst, PTt=PTt: e.activation(
                                out=PTt[:].rearrange("p k t -> p (k t)"), in_=pst[:, :], func=AF.Exp), [rpst], [rPT])
                            P.op('gpsimd', lambda e, PTt=PTt, kb=kb: e.tensor_tensor(
                                out=PTt[:], in0=PTt[:], in1=mskT[:, kb:kb + 1, :].to_broadcast([128, 4, 128]), op=ALU.mult), [rPT, rmkT], [rPT])
                            pendq.append((PTt, rPT))

                        for kb in range(min(3, b + 1)):
                            emit_qk(kb)
                        for kb in range(b + 1):
                            if kb + 3 <= b:
                                emit_qk(kb + 3)
                            PTt, rPT = pendq.pop(0)
                            for h4 in range(4):
                                pol, rpol = pbank[h4]
                                P.op('tensor', lambda e, pol=pol, PTt=PTt, h4=h4, kb=kb, b=b: e.matmul(
                                    pol[:, 0:KVR + 1], lhsT=PTt[:, h4, :], rhs=CKV[:, kb, :], start=(kb == 0), stop=(kb == b)),
                                    [rPT, rK], [rpol])
                        for h4 in range(4):
                            pol, rpol = pbank[h4]
                            h = hg * 4 + h4
                            P.op('scalar', lambda e, pol=pol, h=h: e.copy(out=olb[:, h, :], in_=pol[:, 0:KVR]), [rpol], [rolb])
                            P.op('scalar', lambda e, pol=pol, h=h: e.copy(out=dcol[:, h:h + 1], in_=pol[:, KVR:KVR + 1]), [rpol], [rolb])
                    if (b + 1) in mts:
                        mts.pop(b + 1)()
                    P.op('vector', lambda e: e.reciprocal(out=rd[:], in_=dcol[:]), [rolb], [rrd])
                    pts = [PT(), PT()]
                    for h in range(8):
                        pt, rpt = pts[h // 4]
                        for rcx in range(2):
                            P.op('tensor', lambda e, pt=pt, h=h, rcx=rcx: e.transpose(
                                out=pt[:, ((h % 4) * 2 + rcx) * 128:((h % 4) * 2 + rcx + 1) * 128],
                                in_=olb[:, h, rcx * 128:(rcx + 1) * 128], identity=ident[:]), [rolb, rid], [rpt])
                    for hh in range(2):
                        pt, rpt = pts[hh]
                        P.op('scalar', lambda e, pt=pt, hh=hh: e.copy(
                            out=olT[:, hh * 4:(hh + 1) * 4, :, :].rearrange("p h c t -> p (h c t)"), in_=pt[:, :]), [rpt], [rolT])
                    puvs = [QKB(), QKB()]
                    for h in range(8):
                        puv, rpuv = puvs[h // 4]
                        for rcx in range(2):
                            P.op('tensor', lambda e, puv=puv, rcx=rcx, h=h: e.matmul(
                                puv[:, (h % 4) * 128:(h % 4 + 1) * 128], lhsT=olT[:, h, rcx, :], rhs=wuv[:, h * 2 + rcx, :],
                                start=(rcx == 0), stop=(rcx == 1)), [rolT, rw], [rpuv], inc=(rcx == 1))
                    for hh in range(2):
                        puv, rpuv = puvs[hh]
                        P.op('vector', lambda e, puv=puv, hh=hh: e.tensor_tensor(
                            out=oh[:, hh * 512:(hh + 1) * 512].rearrange("p (h d) -> p h d", h=4),
                            in0=puv[:, :].rearrange("p (h d) -> p h d", h=4),
                            in1=rd[:, hh * 4:(hh + 1) * 4].rearrange("p (h o) -> p h o", o=1).to_broadcast([128, 4, 128]),
                            op=ALU.mult), [rpuv, rrd], [ro])
                    pt, rpt = PT()
                    for c in range(8):
                        P.op('tensor', lambda e, c=c, pt=pt: e.transpose(out=pt[:, c * 128:(c + 1) * 128],
                                                                        in_=oh[:, c * 128:(c + 1) * 128], identity=ident[:]),
                             [ro, rid], [rpt])
                    P.op('vector', lambda e, pt=pt: e.tensor_copy(out=ohT[:].rearrange("p c t -> p (c t)"), in_=pt[:]), [rpt], [ro])
                    for half in range(2):
                        po, rpo = PB()
                        for c in range(8):
                            P.op('tensor', lambda e, c=c, half=half, po=po: e.matmul(
                                po[:, :], lhsT=ohT[:, c, :], rhs=wout[:, c, half * 512:(half + 1) * 512],
                                start=(c == 0), stop=(c == 7)), [ro, rw], [rpo], inc=(c == 7))
                        P.op('vector', lambda e, half=half, po=po, x32=x32: e.scalar_tensor_tensor(
                            out=z[:, half * 512:(half + 1) * 512], in0=x32[:, half * 512:(half + 1) * 512], scalar=ALPHA,
                            in1=po[:, :], op0=ALU.mult, op1=ALU.add), [rpo, rx32], [r_zo])
                    ln_tail(b, z, r_zo, gam, bet, [rg1, rb1], False, False)

                SB(0)
                mts.pop(0)()
                for b in range(NB):
                    if b + 1 < NB:
                        SB(b + 1)
                    ATT(b)
                P.barrier()

        def mixer_phase(layer):
            if MIXERS[layer % 2] is None:
                identity_mixer_phase(layer)
            elif layer % 2 == 0:
                dsa_phase(layer)
            else:
                hgrn_phase(layer)

        for layer in layers:
            mixer_phase(layer)
            ffn_phase(layer, last=(layer == layers[-1]))
        for e in ('sync',):
            if rY.w is not None:
                P._wait(e, rY.w[0], rY.w[1])
        P.barrier()
        print("instructions:", P.ninst)
    return nc


MIXERS = [True, True]

WSHAPES = {
    'a_w_in': [2, D, AIN], 'a_g_q': [2, QR], 'a_g_kv': [2, KVR], 'a_w_q_lat': [2, QR, AH * KVR],
    'a_w_q_idx': [2, QR, IDXH * IDXD], 'a_g_kidx': [2, IDXD], 'a_b_kidx': [2, IDXD],
    'a_w_uv': [2, AH, KVR, 128], 'a_w_out': [2, D, D], 'b_w_in': [2, D, 4 * D], 'b_lb_logits': [4, D],
    'b_g_o': [2, D], 'b_w_out': [2, D, D], 'ln1_g': [4, D], 'ln1_b': [4, D], 'f_w_up': [4, D, 2 * DFF],
    'f_conv_w': [4, 3, 1, 2 * DFF], 'f_conv_b': [4, 2 * DFF], 'f_w_down': [4, DFF, D], 'ln2_g': [4, D], 'ln2_b': [4, D],
}


def host_consts():
    ident = np.eye(128, dtype=np.float32).astype(ml_dtypes.bfloat16)
    s_ = np.arange(128)[:, None]
    t_ = np.arange(128)[None, :]
    same = (s_ // 64) == (t_ // 64)
    hconst = np.zeros((128, 4, 128), np.float32)
    hconst[:, 0, :] = (same & (s_ <= t_))
    hconst[:, 1, :] = same
    hconst[:, 3, :] = (same & (s_ <= t_))
    hcind = np.zeros((128, 4), np.float32)
    hcind[:64, 0] = 1.0
    hcind[64:, 1] = 1.0
    p_ = np.arange(128)
    maskall = np.zeros((128, 8, 128), np.float32)
    for g in range(8):
        maskall[p_, g, 16 * g + (p_ % 16)] = 1.0
    rep = np.zeros((8, 128), np.float32)
    rep[p_ // 16, p_] = 1.0
    cm = np.where(np.arange(128)[None, :] <= np.arange(128)[:, None], 0.0, -1e30).astype(np.float32)
    pw = np.tile((0.5 ** np.arange(1, 33, dtype=np.float64)).astype(np.float32)[None, :], (128, 1))
    return {'ident': ident, 'hconst': hconst, 'hcind': hcind, 'maskall': maskall,
            'rep': rep.astype(ml_dtypes.bfloat16), 'cm': cm, 'pw': pw}


def kernel(**inputs):
    x = np.ascontiguousarray(np.asarray(inputs['x'], dtype=np.float32))
    B, S, _ = x.shape
    nc = build_program(S)
    consts = host_consts()
    wmaps = {k: np.ascontiguousarray(np.asarray(inputs[k], dtype=np.float32)) for k in WSHAPES}
    in_maps = []
    for c in range(8):
        m = dict(wmaps)
        m['x'] = x[(c // 2) % B]
        m.update(consts)
        in_maps.append(m)
    res = run_bass_kernel_spmd(nc, in_maps, core_ids=list(range(8)))
    out = np.stack([res.results[2 * b]['y'] for b in range(B)], axis=0)
    return out.astype(np.float32)
```
